# Optimizing a Trainium2 kernel written in Bass

```python
import jax, jax.numpy as jnp
from jax import lax
import numpy as np

D_MODEL = 1024
BATCH = 4
SEQ = 8192
DEPTH = 1

CHUNK = 64
D_MIX = D_MODEL
HEAD_DIM = 64
N_HEADS_A = 8
N_HEADS_B = 8
D_A = N_HEADS_A * HEAD_DIM
D_B = N_HEADS_B * HEAD_DIM
SG_BLOCK = 128
Q_BLOCK = 128
D_FF = 2816
CONV_W = 3
EPS = 1e-6
D_IN = 2 * D_A + 3 * D_B + N_HEADS_B

kernel_name = "hymba_gmlp_fox_convffn"


def rms_norm(x, g):
    xf = x.astype(jnp.float32)
    y = xf * lax.rsqrt(jnp.mean(xf * xf, axis=-1, keepdims=True) + EPS)
    return (y * g.astype(jnp.float32)).astype(x.dtype)


def spatial_gating(u, v, ln_g, w_s, b_s):
    B, S, _ = u.shape
    n = S // SG_BLOCK
    v = v.reshape(B, n, SG_BLOCK, N_HEADS_A, HEAD_DIM)
    vf = v.astype(jnp.float32)
    mu = jnp.mean(vf, axis=-1, keepdims=True)
    var = jnp.mean(jnp.square(vf - mu), axis=-1, keepdims=True)
    vn = ((vf - mu) * lax.rsqrt(var + EPS) * ln_g.astype(jnp.float32)).astype(u.dtype)
    pos_chunk = jnp.arange(SG_BLOCK) // CHUNK
    mask = pos_chunk[:, None] >= pos_chunk[None, :]
    w = jnp.where(mask[None], w_s, 0).astype(u.dtype)
    mixed = jnp.einsum('hts,bnshd->bnthd', w, vn) + b_s.T.astype(u.dtype)[None, None, :, :, None]
    out = u.reshape(B, n, SG_BLOCK, N_HEADS_A, HEAD_DIM) * mixed
    return out.reshape(B, S, D_A)


def forgetting_attention(q, k, v, f_logit):
    B, S, H, Dh = q.shape
    n = S // Q_BLOCK
    scale = Dh ** -0.5
    c = jnp.cumsum(jax.nn.log_sigmoid(f_logit.astype(jnp.float32)), axis=1)
    c = c.transpose(0, 2, 1)
    qb = q.reshape(B, n, Q_BLOCK, H, Dh).transpose(1, 0, 3, 2, 4)
    cb = c.reshape(B, H, n, Q_BLOCK).transpose(2, 0, 1, 3)
    kpos = jnp.arange(S)

    def block(args):
        i, qi, ci = args
        s = jnp.einsum('bhtd,bshd->bhts', qi, k, preferred_element_type=jnp.float32) * scale
        s = s + ci[..., :, None] - c[:, :, None, :]
        qpos = i * Q_BLOCK + jnp.arange(Q_BLOCK)
        s = jnp.where(kpos[None, :] <= qpos[:, None], s, -jnp.inf)
        p = jax.nn.softmax(s, axis=-1)
        return jnp.einsum('bhts,bshd->bthd', p.astype(v.dtype), v)

    out = lax.map(block, (jnp.arange(n), qb, cb))
    return out.transpose(1, 0, 2, 3, 4).reshape(B, S, H * Dh)


def conv_ffn(h, w_up, w_conv, b_conv, w_down):
    a = h @ w_up
    C = a.shape[-1]
    a = lax.conv_general_dilated(
        a, w_conv[:, None, :].astype(a.dtype), window_strides=(1,),
        padding=[(CONV_W - 1, 0)], dimension_numbers=('NWC', 'WIO', 'NWC'),
        feature_group_count=C) + b_conv.astype(a.dtype)
    g, val = jnp.split(a, 2, axis=-1)
    return (jax.nn.silu(g) * val) @ w_down


def setup_inputs(seed: int = 0) -> dict:
    key = jax.random.key(seed)
    ks = jax.random.split(key, 14)
    L = DEPTH
    nrm = jax.random.normal
    return {
        "x": nrm(ks[0], (BATCH, SEQ, D_MODEL), jnp.float32),
        "norm_mix_g": 1.0 + 0.02 * nrm(ks[1], (L, D_MODEL), jnp.float32),
        "w_in": nrm(ks[2], (L, D_MODEL, D_IN), jnp.float32) * D_MODEL ** -0.5,
        "f_bias": 3.0 + 0.5 * nrm(ks[3], (L, N_HEADS_B), jnp.float32),
        "sg_ln_g": 1.0 + 0.02 * nrm(ks[4], (L, N_HEADS_A, HEAD_DIM), jnp.float32),
        "sg_w": nrm(ks[5], (L, N_HEADS_A, SG_BLOCK, SG_BLOCK), jnp.float32) * SG_BLOCK ** -0.5,
        "sg_b": 1.0 + 0.1 * nrm(ks[6], (L, N_HEADS_A, SG_BLOCK), jnp.float32),
        "w_out": nrm(ks[7], (L, D_MIX, D_MODEL), jnp.float32) * D_MIX ** -0.5,
        "norm_ffn_g": 1.0 + 0.02 * nrm(ks[8], (L, D_MODEL), jnp.float32),
        "w_up": nrm(ks[9], (L, D_MODEL, 2 * D_FF), jnp.float32) * D_MODEL ** -0.5,
        "w_conv": nrm(ks[10], (L, CONV_W, 2 * D_FF), jnp.float32) * CONV_W ** -0.5,
        "b_conv": 0.02 * nrm(ks[11], (L, 2 * D_FF), jnp.float32),
        "w_down": nrm(ks[12], (L, D_FF, D_MODEL), jnp.float32) * D_FF ** -0.5,
        "norm_final_g": 1.0 + 0.02 * nrm(ks[13], (D_MODEL,), jnp.float32),
    }


def reference(x, norm_mix_g, w_in, f_bias, sg_ln_g, sg_w, sg_b, w_out,
              norm_ffn_g, w_up, w_conv, b_conv, w_down, norm_final_g):
    B, S, _ = x.shape
    for l in range(DEPTH):
        h = rms_norm(x, norm_mix_g[l])
        z = h @ w_in[l]
        o = 0
        u_a = jax.nn.gelu(z[..., o:o + D_A], approximate=False); o += D_A
        v_a = jax.nn.gelu(z[..., o:o + D_A], approximate=False); o += D_A
        q_b = z[..., o:o + D_B].reshape(B, S, N_HEADS_B, HEAD_DIM); o += D_B
        k_b = z[..., o:o + D_B].reshape(B, S, N_HEADS_B, HEAD_DIM); o += D_B
        v_b = z[..., o:o + D_B].reshape(B, S, N_HEADS_B, HEAD_DIM); o += D_B
        f_logit = z[..., o:o + N_HEADS_B] + f_bias[l].astype(z.dtype)
        out_a = spatial_gating(u_a, v_a, sg_ln_g[l], sg_w[l], sg_b[l])
        out_b = forgetting_attention(q_b, k_b, v_b, f_logit)
        x = x + jnp.concatenate([out_a, out_b], axis=-1) @ w_out[l]
        x = x + conv_ffn(rms_norm(x, norm_ffn_g[l]), w_up[l], w_conv[l], b_conv[l], w_down[l])
    return rms_norm(x, norm_final_g)
```

```python
import contextlib
import numpy as np
import concourse.bass as bass
import concourse.mybir as mybir
from concourse.bass_utils import run_bass_kernel_spmd

F32 = mybir.dt.float32
BF16 = mybir.dt.bfloat16
AF = mybir.ActivationFunctionType
ALU = mybir.AluOpType
AX = mybir.AxisListType

SAME_ENGINE_SYNC = True
NDMASEM = 8

D = 1024
SEQ = 8192
NB = 64
G = 256
NGRP = 32
FIRST_FULL = 15
DFF = 2816
NCH = 22
EPS = 1e-6
NEG = -30000.0

PC = {}
_off = 0
for _n, _w in [("eye", 128), ("tri", 128), ("sel127", 128), ("selA", 128), ("selB", 128),
               ("cm0", 256), ("cm1", 256), ("fb", 8), ("lng", 512), ("gfin", 1024),
               ("g1T", 8), ("g2T", 8), ("wc", 132), ("bc", 44), ("bsT", 8), ("kmask", 64),
               ("sgwT", 1024)]:
    PC[_n] = (_off, _off + _w)
    _off += _w
NPAR = _off

S_WIN, S_WOUT, S_WUP, S_WDN = 0, 10, 14, 36
NSLOT = 47


class Op:
    __slots__ = ("eng", "fn", "deps", "idx", "signaled", "ev", "is_dma", "dq", "dslot")

    def __init__(self, eng, fn, idx, is_dma):
        self.eng = eng
        self.fn = fn
        self.idx = idx
        self.deps = []
        self.signaled = False
        self.ev = None
        self.is_dma = is_dma


class Prog:
    ENGS = ("sp", "act", "dve", "pool", "pe")

    def __init__(self, nc):
        self.nc = nc
        self.ops = []
        self.last_writer = {}
        self.readers = {}
        self.dma_count = {e: 0 for e in self.ENGS}

    def op(self, eng, fn, reads=(), writes=(), dma=False):
        o = Op(eng, fn, len(self.ops), dma)
        deps = {}
        for r in reads:
            w = self.last_writer.get(r)
            if w is not None:
                deps[w.idx] = w
        for r in writes:
            w = self.last_writer.get(r)
            if w is not None:
                deps[w.idx] = w
            for rd in self.readers.get(r, ()):
                deps[rd.idx] = rd
        o.deps = list(deps.values())
        for r in reads:
            self.readers.setdefault(r, []).append(o)
        for r in writes:
            self.last_writer[r] = o
            self.readers[r] = []
        if dma:
            o.dq = eng
            o.dslot = self.dma_count[eng]
            self.dma_count[eng] += 1
        self.ops.append(o)
        return o

    def emit(self):
        nc = self.nc
        ops = self.ops

        def skip(d, o):
            return (not d.is_dma) and (not o.is_dma) and d.eng == o.eng and (d.eng == "pe" or not SAME_ENGINE_SYNC)

        for o in ops:
            for d in o.deps:
                if d.is_dma or skip(d, o):
                    continue
                d.signaled = True
        with contextlib.ExitStack() as es:
            esem = {e: es.enter_context(nc.semaphore("c_" + e)) for e in self.ENGS}
            dsem = {}
            for e in self.ENGS:
                if self.dma_count[e] > 0:
                    dsem[e] = [es.enter_context(nc.semaphore("d_%s_%d" % (e, i))) for i in range(NDMASEM)]
            cnt = {e: 0 for e in self.ENGS}
            for o in ops:
                if o.is_dma:
                    s = dsem[o.dq][o.dslot % NDMASEM]
                    o.ev = (s, 16 * (o.dslot // NDMASEM + 1))
                elif o.signaled:
                    cnt[o.eng] += 1
                    o.ev = (esem[o.eng], cnt[o.eng])
            self.stats = dict(cnt)
            block = es.enter_context(nc.Block())

            def body(engname, eng):
                seen = {}
                for o in ops:
                    if o.eng != engname:
                        continue
                    waits = []
                    for d in o.deps:
                        if d.ev is None or skip(d, o):
                            continue
                        waits.append(d.ev)
                    if o.is_dma and o.dslot >= NDMASEM:
                        s = dsem[o.dq][o.dslot % NDMASEM]
                        waits.append((s, 16 * (o.dslot // NDMASEM)))
                    for (s, v) in waits:
                        k = id(s)
                        if seen.get(k, 0) >= v:
                            continue
                        seen[k] = v
                        eng.wait_ge(s, v)
                    if o.fn is None:
                        continue
                    inst = o.fn(eng)
                    if o.is_dma:
                        inst.then_inc(o.ev[0], 16)
                    elif o.signaled:
                        inst.then_inc(o.ev[0], 1)

            @block.sync
            def _(e):
                body("sp", e)

            @block.scalar
            def _(e):
                body("act", e)

            @block.vector
            def _(e):
                body("dve", e)

            @block.gpsimd
            def _(e):
                body("pool", e)

            @block.tensor
            def _(e):
                body("pe", e)


def build_nc(n_full=17, dbg=False, stage=99, n_ctx=FIRST_FULL):
    nc = bass.Bass("TRN2", target_bir_lowering=False)
    x_d = nc.dram_tensor("x", [SEQ, D], F32, kind="ExternalInput").ap()
    par_d = nc.dram_tensor("par", [128, NPAR], F32, kind="ExternalInput").ap()
    win_d = nc.dram_tensor("w_in", [D, 2568], F32, kind="ExternalInput").ap()
    wout_d = nc.dram_tensor("w_out", [D, D], F32, kind="ExternalInput").ap()
    wup_d = nc.dram_tensor("w_up", [D, 2 * DFF], F32, kind="ExternalInput").ap()
    wdn_d = nc.dram_tensor("w_down", [DFF, D], F32, kind="ExternalInput").ap()
    out_d = nc.dram_tensor("out", [4096, D], F32, kind="ExternalOutput").ap()
    wsc = nc.dram_tensor("wsc", [NSLOT, 128, 2048], BF16, kind="Internal").ap()
    ksc = nc.dram_tensor("ksc", [4, 128, SEQ], BF16, kind="Internal").ap()
    dbg_d = {}
    if dbg:
        for nm, shp in [("d_x1", [256, D]), ("d_outT", [128, 8 * 256]), ("d_ckb", [128, 64 * 8]),
                        ("d_qT", [128, 4 * 256]), ("d_oa", [128, 2 * 512])]:
            dbg_d[nm] = nc.dram_tensor(nm, shp, F32, kind="ExternalOutput").ap()

    with contextlib.ExitStack() as es:
        def T(name, shape, dt):
            return es.enter_context(nc.sbuf_tensor(name, shape, dt))

        def PS(name, shape, dt):
            return es.enter_context(nc.psum_tensor(name, shape, dt))

        P = Prog(nc)
        KSTG = T("KSTG", [128, 4, G], BF16)
        KCH = [T("KCH%d" % i, [128, 1024], BF16) for i in range(3)]
        V = T("V", [128, NB, 4, 132], BF16)
        XT = T("XT", [128, 2, D], F32)
        HB = T("HB", [128, 2, D], BF16)
        HT = T("HT", [128, 8, G], BF16)
        FA = T("FA", [128, 2056], F32)
        QT = FA[:, 0:512].bitcast(BF16).rearrange("p (a t) -> p a t", a=4)
        NPT = 8
        PT = [FA[:, 512 + 128 * i:512 + 128 * (i + 1)].bitcast(BF16) for i in range(NPT)]
        OSB = [FA[:, 1536 + 256 * i:1536 + 256 * (i + 1)] for i in range(2)]
        AGV = [FA[:, 516 * i:516 * (i + 1)].rearrange("p (a c) -> p a c", a=2) for i in range(2)]
        YGV = [FA[:, 1032 + 512 * i:1032 + 512 * (i + 1)].rearrange("p (a c) -> p a c", a=2) for i in range(2)]
        ARENA = T("ARENA", [128, 3072], F32)
        OUTF = ARENA[:, 0:2048].rearrange("p (b f) -> p b f", b=2)
        U = ARENA[:, 0:1024].rearrange("p (b f) -> p b f", b=2)
        VA = ARENA[:, 1024:2048].rearrange("p (b f) -> p b f", b=2)
        ABF = ARENA[:].bitcast(BF16)
        VN = ABF[:, 4096:5120].rearrange("p (b f) -> p b f", b=2)
        OA = ABF[:, 5120:6144].rearrange("p (b f) -> p b f", b=2)
        ACTT = ABF[:, 0:NCH * G].rearrange("p (c t) -> p c t", c=NCH)
        OUTT = HT
        NWP = 20
        WP = [T("WP%d" % i, [128, 2048], BF16) for i in range(NWP)]
        RBT = T("RBT", [128, 512], F32)
        RB = [RBT[:, i * G:(i + 1) * G] for i in range(2)]
        SQ = RBT
        CKB = T("CKB", [128, NB, 8], F32)
        BIAS = [T("BIAS%d" % i, [128, NB], F32) for i in range(2)]
        WF = T("WF", [128, 8, 8], BF16)
        CARRY = T("CARRY", [128, NCH, 2, 2], F32)
        ident = T("ident", [128, 128], BF16)
        CMASK = T("CMASK", [128, 2, G], BF16)
        WMT = T("WMT", [128, 8, 128], BF16)
        TRI = T("TRI", [128, 128], F32)
        SEL127 = T("SEL127", [128, 128], F32)
        SELA = T("SELA", [128, 128], F32)
        SELB = T("SELB", [128, 128], F32)
        FB = T("FB", [128, 8], F32)
        LNG = T("LNG", [128, 512], F32)
        GFIN = T("GFIN", [128, D], F32)
        GT = T("GT", [128, 16], F32)
        WC = T("WC", [128, 2 * NCH, 3], F32)
        BC = T("BC", [128, 2 * NCH], F32)
        BST = T("BST", [128, 8], F32)
        KM = T("KM", [128, NB], F32)
        ST = T("ST", [128, 64], F32)
        CCAR = T("CCAR", [128, 8], F32)
        RG = T("RG", [128, 8], F32)
        CB = T("CB", [128, 8], F32)
        SP_ = T("SP_", [128, 8], F32)
        EPSC = T("EPSC", [128, 2], F32)
        PGB = [PS("PGB%d" % i, [128, 512], F32) for i in range(3)]
        ACCB = [PS("ACCB%d" % i, [128, 512], F32) for i in range(4)]
        PSTB = PS("PSTB", [128, 1024], BF16)
        print("sbuf bytes remaining:", nc.sbuf_bytes_remaining)

        rot = {"pg": 0, "pt": 0}
        att_rot = [0]
        kch_rot = [0]
        PSTF = PSTB[:, :].bitcast(F32)

        def next_pg():
            i = rot["pg"] % 3
            rot["pg"] += 1
            return i

        def next_pt():
            i = rot["pt"] % NPT
            rot["pt"] += 1
            return i

        def pc(name):
            a, b = PC[name]
            return par_d[:, a:b]

        def dma(out, in_, reads, writes):
            return P.op("sp", lambda e: e.dma_start(out=out, in_=in_), reads=reads, writes=writes, dma=True)

        ARENA_ALL = ["U", "VA", "VN", "OA"] + [("ACTT", c) for c in range(NCH)]

        dma(TRI[:], pc("tri"), [], ["TRI"])
        dma(SEL127[:], pc("sel127"), [], ["SEL127"])
        dma(SELA[:], pc("selA"), [], ["SELA"])
        dma(SELB[:], pc("selB"), [], ["SELB"])
        dma(FB[:], pc("fb"), [], ["FB"])
        dma(LNG[:], pc("lng"), [], ["LNG"])
        dma(GFIN[:], pc("gfin"), [], ["GFIN"])
        dma(GT[:, 0:8], pc("g1T"), [], ["GT"])
        dma(GT[:, 8:16], pc("g2T"), [], ["GT"])
        dma(WC[:].rearrange("p c j -> p (c j)"), pc("wc"), [], ["WC"])
        dma(BC[:], pc("bc"), [], ["BC"])
        dma(BST[:], pc("bsT"), [], ["BST"])
        dma(KM[:], pc("kmask"), [], ["KM"])
        XTf = XT[:].rearrange("p b f -> p (b f)")
        HBf = HB[:].rearrange("p b f -> p (b f)")
        HTf = HT[:].rearrange("p k t -> p (k t)")
        a0 = PC["eye"][0]
        dma(XTf[:, 0:128], pc("eye"), [], [("XT", 0), ("XT", 1), "XT"])
        dma(XTf[:, 128:640], par_d[:, PC["cm0"][0]:PC["cm1"][1]], [], [("XT", 0), ("XT", 1), "XT"])
        dma(XTf[:, 640:1664], pc("sgwT"), [], [("XT", 0), ("XT", 1), "XT"])
        P.op("dve", lambda e: e.tensor_copy(out=ident[:], in_=XTf[:, 0:128]), reads=[("XT", 0), ("XT", 1), "XT"], writes=["ident"])
        P.op("dve", lambda e: e.tensor_copy(out=CMASK[:].rearrange("p a t -> p (a t)"), in_=XTf[:, 128:640]), reads=[("XT", 0), ("XT", 1), "XT"], writes=["CMASK"])
        P.op("dve", lambda e: e.memset(XTf[64:128, 640:1664].rearrange("p (h t) -> p h t", h=8)[:, :, 0:64], 0.0), reads=[], writes=[("XT", 0), ("XT", 1), "XT"])
        P.op("dve", lambda e: e.tensor_copy(out=WMT[:].rearrange("p h t -> p (h t)"), in_=XTf[:, 640:1664]), reads=[("XT", 0), ("XT", 1), "XT"], writes=["WMT"])
        P.op("dve", lambda e: e.memset(EPSC[:, 0:1], EPS), writes=["EPSC"])
        P.op("dve", lambda e: e.memset(EPSC[:, 1:2], 1.0), writes=["EPSC"])
        P.op("dve", lambda e: e.memset(CCAR[:], 0.0), writes=["CCAR"])
        P.op("dve", lambda e: e.memset(CARRY[:].rearrange("p c a j -> p (c a j)"), 0.0), writes=["CARRY"])
        import os
        if os.environ.get("NOVMEM") != "1":
            P.op("dve", lambda e: e.memset(V[:].rearrange("p b a c -> p (b a) c")[:, :, 64:66], 1.0), writes=[("V", b, s) for b in range(NB) for s in range(2)])

        NCONV = 99 if stage >= 2 else 0
        XT_ALL = [("XT", 0), ("XT", 1), "XT"]
        HB_ALL = [("HB", 0), ("HB", 1)]
        HT_ALL = [("HT", 0), ("HT", 1)]
        stage_f = [(XTf, XT_ALL), (ARENA[:, 0:2048], ARENA_ALL)]
        stage_b = [(HBf, HB_ALL), (HTf, HT_ALL)]
        conv_engs = ["dve", "act"]
        cnt = [0]

        def convert(slot, pieces, gain_col, bg=False):
            if cnt[0] >= NCONV:
                return
            i = cnt[0] % 2
            cnt[0] += 1
            if bg:
                sf, sfn = ARENA[:, 0:2048], ["cvF"]
                sb, sbn = ABF[:, 4096:6144], ["cvB"]
            else:
                sf, sfn = stage_f[i]
                sb, sbn = stage_b[i]
            wr = list(sfn)
            for (dst, src) in pieces:
                dma(dst(sf), src, [], wr)
            eng = conv_engs[i]
            if gain_col is None:
                if eng == "dve":
                    P.op(eng, lambda e: e.tensor_copy(out=sb[:, :], in_=sf[:, :]), reads=wr, writes=list(sbn))
                else:
                    P.op(eng, lambda e: e.copy(out=sb[:, :], in_=sf[:, :]), reads=wr, writes=list(sbn))
            else:
                for k in range(8):
                    if eng == "dve":
                        P.op(eng, lambda e, k=k: e.tensor_scalar(out=sb[:, k * 256:(k + 1) * 256], in0=sf[:, k * 256:(k + 1) * 256],
                                                               scalar1=GT[:, gain_col + k:gain_col + k + 1], scalar2=None, op0=ALU.mult),
                             reads=wr + ["GT"], writes=list(sbn))
                    else:
                        P.op(eng, lambda e, k=k: e.mul(out=sb[:, k * 256:(k + 1) * 256], in_=sf[:, k * 256:(k + 1) * 256],
                                                     mul=GT[:, gain_col + k:gain_col + k + 1]),
                             reads=wr + ["GT"], writes=list(sbn))
            out_fn = lambda: dma(wsc[slot], sb[:, :], list(sbn), [("wsc", slot)])
            if bg:
                return out_fn
            out_fn()
            return None

        def v3(sf, a, b):
            return sf[:, 0:2048].rearrange("p (a b) -> p a b", a=a)

        conv_jobs = {}
        for s in range(10):
            conv_jobs[S_WIN + s] = ([(lambda sf: v3(sf, 8, 256), win_d[:, s * 256:(s + 1) * 256].rearrange("(k p) c -> p k c", p=128))], 0)
        for s in range(4):
            conv_jobs[S_WOUT + s] = ([(lambda sf: v3(sf, 8, 256), wout_d[:, s * 256:(s + 1) * 256].rearrange("(k p) c -> p k c", p=128))], None)
        for j in range(NCH):
            conv_jobs[S_WUP + j] = ([(lambda sf: v3(sf, 8, 256)[:, :, 0:128], wup_d[:, j * 128:(j + 1) * 128].rearrange("(k p) c -> p k c", p=128)),
                                     (lambda sf: v3(sf, 8, 256)[:, :, 128:256], wup_d[:, DFF + j * 128:DFF + (j + 1) * 128].rearrange("(k p) c -> p k c", p=128))], 8)
        for i in range(11):
            conv_jobs[S_WDN + i] = ([(lambda sf: v3(sf, 2, 1024), wdn_d[i * 256:(i + 1) * 256, :].rearrange("(c p) n -> p c n", p=128))], None)
        for slot in (6, 7, 8, 9):
            convert(slot, *conv_jobs[slot])
        dma(XTf[:, 0:64].rearrange("p (k c) -> p k c", k=8), win_d[:, 2560:2568].rearrange("(k p) c -> p k c", p=128), [], [("XT", 0), ("XT", 1), "XT"])
        for k in range(8):
            P.op("dve", lambda e, k=k: e.tensor_scalar(out=WF[:, k, :], in0=XTf[:, k * 8:(k + 1) * 8], scalar1=GT[:, k:k + 1], scalar2=None, op0=ALU.mult),
                 reads=[("XT", 0), ("XT", 1), "XT", "GT"], writes=["WF"])
        P.op("dve", lambda e: e.memset(ST[:, 60:61], 0.0), reads=[], writes=ARENA_ALL + ["cvF", "cvB"])
        bg_slots = [sl for sl in list(range(0, 6)) + list(range(10, NSLOT))]
        bg_state = {"i": 0, "pending_out": None}

        def conv_step():
            if bg_state["pending_out"] is not None:
                bg_state["pending_out"]()
                bg_state["pending_out"] = None
            if stage < 2 or bg_state["i"] >= len(bg_slots):
                return
            slot = bg_slots[bg_state["i"]]
            bg_state["i"] += 1
            bg_state["pending_out"] = convert(slot, *conv_jobs[slot], bg=True)

        def conv_flush():
            while bg_state["i"] < len(bg_slots) or bg_state["pending_out"] is not None:
                conv_step()
            P.op("dve", lambda e: e.memset(ST[:, 61:62], 0.0), reads=[], writes=ARENA_ALL + ["cvF", "cvB"])

        wseq = []
        for ci in range(NGRP):
            full = ci >= FIRST_FULL
            if stage < 3 or (not full and ci >= n_ctx):
                continue
            if full and (ci - FIRST_FULL) >= n_full:
                break
            if full:
                wseq += list(range(0, 10)) + list(range(10, 14)) + list(range(14, 36)) + list(range(36, 47))
            else:
                wseq += [6, 7, 8, 9]
        wstate = {"use": 0, "load": 0}

        def wload_next():
            n = wstate["load"]
            if n >= len(wseq):
                return
            slot = wseq[n]
            buf = n % NWP
            wstate["load"] += 1
            dma(WP[buf][:, :], wsc[slot], [("wsc", slot)], [("WP", buf)])

        def wacquire(slot):
            n = wstate["use"]
            assert wseq[n] == slot, (n, wseq[n], slot)
            return n % NWP

        def wrelease():
            wstate["use"] += 1
            wload_next()

        for _ in range(NWP):
            wload_next()

        evac_rr = [0]

        def rmsnorm_to_HT():
            for b in range(2):
                P.op("dve", lambda e, b=b: e.scalar_tensor_tensor(out=HB[:, b, :], in0=XT[:, b, :], scalar=1.0, in1=XT[:, b, :],
                                                                 op0=ALU.mult, op1=ALU.mult, accum_out=ST[:, b:b + 1]),
                     reads=[("XT", b)], writes=[("HB", b), ("ST", b)])
                P.op("act", lambda e, b=b: e.activation(out=ST[:, 2 + b:3 + b], in_=ST[:, b:b + 1], func=AF.Ln, bias=EPSC[:, 0:1], scale=1.0 / D),
                     reads=[("ST", b), "EPSC"], writes=[("ST", 2 + b)])
                P.op("act", lambda e, b=b: e.activation(out=ST[:, 4 + b:5 + b], in_=ST[:, 2 + b:3 + b], func=AF.Exp, scale=-0.5),
                     reads=[("ST", 2 + b)], writes=[("ST", 4 + b)])
                P.op("dve", lambda e, b=b: e.tensor_scalar(out=HB[:, b, :], in0=XT[:, b, :], scalar1=ST[:, 4 + b:5 + b], scalar2=None, op0=ALU.mult),
                     reads=[("XT", b), ("ST", 4 + b)], writes=[("HB", b)])
                for k in range(8):
                    P.op("pe", lambda e, b=b, k=k: e.transpose(out=PSTB[:, k * 128:(k + 1) * 128], in_=HB[:, b, k * 128:(k + 1) * 128], identity=ident[:]),
                         reads=[("HB", b), "ident"], writes=["PST"])
                P.op("act", lambda e, b=b: e.copy(out=HT[:, :, b * 128:(b + 1) * 128], in_=PSTB[:, :].rearrange("p (k t) -> p k t", k=8)),
                     reads=["PST"], writes=[("HT", b)] + [("OUTT", kq) for kq in range(8)])

        def HT_reads(b=None):
            if b is None:
                return [("HT", 0), ("HT", 1)]
            return [("HT", b)]

        def mm_feat(pi, hf, wbuf, c0):
            for k in range(8):
                P.op("pe", lambda e, k=k: e.matmul(PGB[pi][:, hf * G:(hf + 1) * G], lhsT=WP[wbuf][:, k * 256 + c0:k * 256 + c0 + 128], rhs=HT[:, k, :],
                                                  start=(k == 0), stop=(k == 7)),
                     reads=[("WP", wbuf)] + HT_reads(), writes=[("PG", pi)])

        def mm_tok(pi, hf, wbuf, b, lhs, lhs_reads):
            for k in range(8):
                P.op("pe", lambda e, k=k: e.matmul(PGB[pi][:, hf * G:(hf + 1) * G], lhsT=lhs[:, k, b * 128:(b + 1) * 128], rhs=WP[wbuf][:, k * 256:(k + 1) * 256],
                                                  start=(k == 0), stop=(k == 7)),
                     reads=[("WP", wbuf)] + lhs_reads, writes=[("PG", pi)])

        def emit_group(ci):
            full = ci >= FIRST_FULL
            gi = ci - FIRST_FULL
            b0 = 2 * ci
            T0 = ci * G
            if gi == 0:
                conv_flush()
            dma(XT[:], x_d[T0:T0 + G, :].rearrange("(b p) f -> p b f", p=128), [], [("XT", 0), ("XT", 1), "XT"])
            if not full:
                conv_step()
            rmsnorm_to_HT()
            if full:
                for s in range(4):
                    wb = wacquire(s)
                    dst, dn = (U, "U") if s < 2 else (VA, "VA")
                    hf = s % 2
                    pi = next_pg()
                    for b in range(2):
                        mm_tok(pi, b, wb, b, HT, HT_reads(b))
                    P.op("act", lambda e, pi=pi, dst=dst, hf=hf: e.activation(out=dst[:, :, hf * 256:(hf + 1) * 256], in_=PGB[pi][:, :].rearrange("p (b c) -> p b c", b=2), func=AF.Gelu),
                         reads=[("PG", pi)], writes=[dn] + [("ACTT", c) for c in range(NCH)])
                    wrelease()
                for s in range(2):
                    wb = wacquire(4 + s)
                    pi = next_pg()
                    for m in range(2):
                        mm_feat(pi, m, wb, m * 128)
                    P.op("act", lambda e, pi=pi, s=s: e.mul(out=QT[:, 2 * s:2 * s + 2, :].rearrange("p a t -> p (a t)"), in_=PGB[pi][:, :], mul=0.125),
                         reads=[("PG", pi)], writes=[("QT", 2 * s), ("QT", 2 * s + 1)])
                    wrelease()
            for s in range(2):
                wb = wacquire(6 + s)
                pi = next_pg()
                for m in range(2):
                    mm_feat(pi, m, wb, m * 128)
                for m in range(2):
                    hp = 2 * s + m
                    P.op("act", lambda e, pi=pi, hp=hp, m=m: e.copy(out=KSTG[:, hp, :], in_=PGB[pi][:, m * G:(m + 1) * G]),
                         reads=[("PG", pi)], writes=[("KSTG", hp)])
                wrelease()
            dma(ksc[:, :, T0:T0 + G].rearrange("h p t -> p h t"), KSTG[:, :, :], [("KSTG", hp) for hp in range(4)],
                [("KD", hp, bb) for hp in range(4) for bb in (b0, b0 + 1)])
            if not full:
                conv_step()
            for s in range(2):
                wb = wacquire(8 + s)
                pi = next_pg()
                for b in range(2):
                    mm_tok(pi, b, wb, b, HT, HT_reads(b))
                for b in range(2):
                    for a in range(2):
                        P.op("dve", lambda e, b=b, pi=pi, s=s, a=a: e.tensor_copy(
                            out=V[:, b0 + b, 2 * s + a, :].rearrange("p (e c) -> p e c", e=2)[:, :, 0:64],
                            in_=PGB[pi][:, b * G + a * 128:b * G + (a + 1) * 128].rearrange("p (e c) -> p e c", e=2)),
                             reads=[("PG", pi)], writes=[("V", b0 + b, s)])
                wrelease()
            if not full:
                conv_step()
            for b in range(2):
                pi = next_pg()
                for k in range(8):
                    P.op("pe", lambda e, k=k, b=b, pi=pi: e.matmul(PGB[pi][:, 0:8], lhsT=HT[:, k, b * 128:(b + 1) * 128], rhs=WF[:, k, :], start=(k == 0), stop=(k == 7)),
                         reads=["WF"] + HT_reads(b), writes=[("PG", pi)])
                P.op("dve", lambda e, pi=pi: e.tensor_tensor(out=SP_[:, :], in0=PGB[pi][:, 0:8], in1=FB[:, :], op=ALU.add),
                     reads=[("PG", pi), "FB"], writes=["SP_"])
                P.op("act", lambda e: e.activation(out=SP_[:, :], in_=SP_[:, :], func=AF.Exp, scale=-1.0), reads=["SP_"], writes=["SP_"])
                P.op("act", lambda e: e.activation(out=SP_[:, :], in_=SP_[:, :], func=AF.Ln, bias=EPSC[:, 1:2], scale=1.0), reads=["SP_", "EPSC"], writes=["SP_"])
                pi2 = next_pg()
                P.op("pe", lambda e, pi2=pi2: e.matmul(PGB[pi2][:, 0:8], lhsT=TRI[:, :], rhs=SP_[:, :], start=True, stop=True),
                     reads=["TRI", "SP_"], writes=[("PG", pi2)])
                P.op("dve", lambda e, pi2=pi2: e.tensor_tensor(out=CB[:, :], in0=PGB[pi2][:, 0:8], in1=CCAR[:, :], op=ALU.add),
                     reads=[("PG", pi2), "CCAR"], writes=["CB"])
                P.op("dve", lambda e, b=b: e.tensor_scalar(out=CKB[:, b0 + b, :], in0=CB[:, :], scalar1=KM[:, b0 + b:b0 + b + 1], scalar2=None, op0=ALU.add),
                     reads=["CB", "KM"], writes=[("CKB", b0 + b)])
                pi3 = next_pg()
                P.op("pe", lambda e, pi3=pi3: e.matmul(PGB[pi3][:, 0:8], lhsT=SEL127[:, :], rhs=CB[:, :], start=True, stop=True),
                     reads=["SEL127", "CB"], writes=[("PG", pi3)])
                P.op("dve", lambda e, pi3=pi3: e.tensor_copy(out=CCAR[:, :], in_=PGB[pi3][:, 0:8]), reads=[("PG", pi3)], writes=["CCAR"])
                if b == 0 and full:
                    P.op("dve", lambda e, pi3=pi3: e.tensor_copy(out=RG[:, :], in_=PGB[pi3][:, 0:8]), reads=[("PG", pi3)], writes=["RG"])
            if not full:
                return
            if dbg and gi == dbg_g:
                dma(dbg_d["d_qT"], QT[:].rearrange("p a t -> p (a t)"), [("QT", h) for h in range(4)], ["dq"])
            def emit_sg():
                SQR = [("RB", 0), ("RB", 1)]
                for b in range(2):
                    va3 = VA[:, b, :].rearrange("p (h d) -> p h d", h=8)
                    P.op("dve", lambda e, va3=va3: e.tensor_reduce(out=ST[:, 8:16], in_=va3, axis=AX.X, op=ALU.add), reads=["VA"], writes=[("ST", "s1")])
                    P.op("dve", lambda e, b=b: e.tensor_tensor(out=SQ[:, :], in0=VA[:, b, :], in1=VA[:, b, :], op=ALU.mult), reads=["VA"], writes=SQR)
                    P.op("dve", lambda e: e.tensor_reduce(out=ST[:, 16:24], in_=SQ[:, :].rearrange("p (h d) -> p h d", h=8), axis=AX.X, op=ALU.add),
                         reads=SQR, writes=[("ST", "s2")])
                    P.op("dve", lambda e: e.tensor_scalar(out=ST[:, 8:16], in0=ST[:, 8:16], scalar1=1.0 / 64, scalar2=None, op0=ALU.mult),
                         reads=[("ST", "s1")], writes=[("ST", "s1")])
                    P.op("dve", lambda e: e.tensor_tensor(out=ST[:, 24:32], in0=ST[:, 8:16], in1=ST[:, 8:16], op=ALU.mult),
                         reads=[("ST", "s1")], writes=[("ST", "msq")])
                    P.op("dve", lambda e: e.scalar_tensor_tensor(out=ST[:, 16:24], in0=ST[:, 16:24], scalar=1.0 / 64, in1=ST[:, 24:32], op0=ALU.mult, op1=ALU.subtract),
                         reads=[("ST", "s2"), ("ST", "msq")], writes=[("ST", "s2")])
                    P.op("act", lambda e: e.activation(out=ST[:, 16:24], in_=ST[:, 16:24], func=AF.Ln, bias=EPSC[:, 0:1], scale=1.0),
                         reads=[("ST", "s2"), "EPSC"], writes=[("ST", "s2")])
                    P.op("act", lambda e: e.activation(out=ST[:, 16:24], in_=ST[:, 16:24], func=AF.Exp, scale=-0.5),
                         reads=[("ST", "s2")], writes=[("ST", "s2")])
                    P.op("dve", lambda e, va3=va3: e.tensor_tensor(out=va3, in0=va3, in1=ST[:, 8:16].unsqueeze(2).to_broadcast([128, 8, 64]), op=ALU.subtract),
                         reads=["VA", ("ST", "s1")], writes=["VA"])
                    P.op("dve", lambda e, va3=va3: e.tensor_tensor(out=va3, in0=va3, in1=ST[:, 16:24].unsqueeze(2).to_broadcast([128, 8, 64]), op=ALU.mult),
                         reads=["VA", ("ST", "s2")], writes=["VA"])
                    P.op("dve", lambda e, b=b: e.tensor_tensor(out=VN[:, b, :], in0=VA[:, b, :], in1=LNG[:, :], op=ALU.mult),
                         reads=["VA", "LNG"], writes=["VN"])
                    pi = next_pg()
                    for h in range(8):
                        P.op("pe", lambda e, h=h, b=b, pi=pi: e.matmul(PGB[pi][:, h * 64:(h + 1) * 64], lhsT=WMT[:, h, :], rhs=VN[:, b, h * 64:(h + 1) * 64],
                                                                      start=True, stop=True),
                             reads=["WMT", "VN"], writes=[("PG", pi)])
                    P.op("dve", lambda e, pi=pi: e.tensor_tensor(out=SQ[:, :].rearrange("p (h d) -> p h d", h=8),
                                                                 in0=PGB[pi][:, :].rearrange("p (h d) -> p h d", h=8),
                                                                 in1=BST[:, :].unsqueeze(2).to_broadcast([128, 8, 64]), op=ALU.add),
                         reads=[("PG", pi), "BST"], writes=SQR)
                    P.op("dve", lambda e, b=b: e.tensor_tensor(out=OA[:, b, :], in0=SQ[:, :], in1=U[:, b, :], op=ALU.mult),
                         reads=SQR + ["U"], writes=["OA"])
                    for kk in range(4):
                        P.op("pe", lambda e, b=b, kk=kk: e.transpose(out=PSTB[:, kk * 128:(kk + 1) * 128], in_=OA[:, b, kk * 128:(kk + 1) * 128], identity=ident[:]),
                             reads=["OA", "ident"], writes=["PST"])
                    P.op("act", lambda e, b=b: e.copy(out=OUTT[:, 0:4, b * 128:(b + 1) * 128], in_=PSTB[:, 0:512].rearrange("p (k t) -> p k t", k=4)),
                         reads=["PST"], writes=[("OUTT", k) for k in range(4)] + [("HT", 0), ("HT", 1)])

            nj = b0 + 2
            nchunk = (nj + 7) // 8

            def kload(hp, c):
                nblk = min(8, nj - 8 * c)
                kb = kch_rot[0] % 3
                kch_rot[0] += 1
                dma(KCH[kb][:, 0:nblk * 128], ksc[hp][:, c * 1024:c * 1024 + nblk * 128],
                    [("KD", hp, 8 * c + q) for q in range(nblk)], [("KCH", kb)])
                return kb

            for hp in range(4):
                if hp == 1:
                    emit_sg()
                accs = [(2 * hp) % 4, (2 * hp + 1) % 4]
                kbufs = {0: kload(hp, 0)}
                if nchunk > 1:
                    kbufs[1] = kload(hp, 1)
                for ee in range(2):
                    h = 2 * hp + ee
                    P.op("dve", lambda e, h=h, ee=ee: e.tensor_scalar(out=BIAS[ee][:, 0:nj], in0=CKB[:, 0:nj, h], scalar1=RG[:, h:h + 1], scalar2=None, op0=ALU.subtract),
                         reads=[("CKB", j) for j in range(nj)] + ["RG"], writes=[("BIAS", ee)])

                def qk_step(j, hp=hp):
                    bp = att_rot[0] % 2
                    att_rot[0] += 1
                    banks = [(PGB[0], ("PG", 0)), (PGB[1], ("PG", 1))] if bp == 0 else [(PGB[2], ("PG", 2)), (PSTF, "PST")]
                    for q in range(2):
                        jj = j + q
                        diag = jj >= b0
                        for ee in range(2):
                            kr = slice(ee * 64, (ee + 1) * 64)
                            bk, bkey = banks[ee]
                            kb_ = kbufs[jj // 8]
                            jo = jj % 8
                            P.op("pe", lambda e, ee=ee, kr=kr, bk=bk, jj=jj, q=q, diag=diag, kb_=kb_, jo=jo: e.matmul(bk[:, q * G:(q + 1) * G], lhsT=KCH[kb_][kr, jo * 128:(jo + 1) * 128], rhs=QT[kr, hp, :],
                                                                                                  start=True, stop=not diag),
                                 reads=[("KCH", kb_), ("QT", hp)], writes=[bkey])
                        if diag:
                            for ee in range(2):
                                bk, bkey = banks[ee]
                                P.op("pe", lambda e, bk=bk, jj=jj, q=q: e.matmul(bk[:, q * G:(q + 1) * G], lhsT=ident[:, :], rhs=CMASK[:, jj - b0, :], start=False, stop=True),
                                     reads=["ident", "CMASK"], writes=[bkey])
                    return banks

                def exp_step(j, banks):
                    tis = []
                    for q in range(2):
                        jj = j + q
                        for ee in range(2):
                            bk, bkey = banks[ee]
                            ti = next_pt()
                            P.op("act", lambda e, ee=ee, ti=ti, bk=bk, jj=jj, q=q: e.activation(out=PT[ti][:, :], in_=bk[:, q * G:(q + 1) * G], func=AF.Exp, bias=BIAS[ee][:, jj:jj + 1], scale=1.0),
                                 reads=[bkey, ("BIAS", ee)], writes=[("PT", ti)])
                            tis.append(ti)
                    return tis

                def pv_step(j, tis, hp=hp, accs=accs):
                    for q in range(2):
                        jj = j + q
                        vflat = V[:, jj, hp, :]
                        t0_, t1_ = tis[2 * q], tis[2 * q + 1]
                        P.op("pe", lambda e, jj=jj, vflat=vflat, t0_=t0_: e.matmul(ACCB[accs[0]][0:65, 0:G], lhsT=vflat[:, 0:65], rhs=PT[t0_][:, :], start=(jj == 0), stop=(jj == nj - 1)),
                             reads=[("V", jj, hp // 2), ("PT", t0_)], writes=[("ACC", accs[0])])
                        P.op("pe", lambda e, jj=jj, vflat=vflat, t1_=t1_: e.matmul(ACCB[accs[1]][:, 0:G], lhsT=vflat[:, 2:130], rhs=PT[t1_][:, :], start=(jj == 0), stop=(jj == nj - 1)),
                             reads=[("V", jj, hp // 2), ("PT", t1_)], writes=[("ACC", accs[1])])

                pend = None
                for j in range(0, nj, 2):
                    if j % 8 == 0 and j > 0 and (j // 8 + 1) < nchunk:
                        kbufs[j // 8 + 1] = kload(hp, j // 8 + 1)
                    banks = qk_step(j)
                    tis = exp_step(j, banks)
                    if pend is not None:
                        pv_step(*pend)
                    pend = (j, tis)
                pv_step(*pend)
                for ee in range(2):
                    h = 2 * hp + ee
                    acc = accs[ee]
                    kr = slice(ee * 64, (ee + 1) * 64)
                    oi = ee
                    K = 65 if ee == 0 else 128
                    sel = SELA if ee == 0 else SELB
                    seln = "SELA" if ee == 0 else "SELB"
                    P.op("act", lambda e, oi=oi, K=K, acc=acc: e.copy(out=OSB[oi][0:K, :], in_=ACCB[acc][0:K, 0:G]), reads=[("ACC", acc)], writes=[("OSB", oi)])
                    pi = next_pg()
                    P.op("pe", lambda e, oi=oi, K=K, sel=sel, pi=pi: e.matmul(PGB[pi][:, 0:G], lhsT=sel[0:K, :], rhs=OSB[oi][0:K, :], start=True, stop=True),
                         reads=[seln, ("OSB", oi)], writes=[("PG", pi)])
                    P.op("dve", lambda e, oi=oi, pi=pi: e.tensor_scalar(out=RB[oi][:, :], in0=PGB[pi][:, 0:G], scalar1=1e-30, scalar2=None, op0=ALU.max),
                         reads=[("PG", pi)], writes=[("RB", oi)])
                    P.op("dve", lambda e, oi=oi: e.reciprocal(out=RB[oi][:, :], in_=RB[oi][:, :]), reads=[("RB", oi)], writes=[("RB", oi)])
                    P.op("dve", lambda e, oi=oi, kr=kr, hp=hp: e.tensor_tensor(out=OUTT[kr, 4 + hp, :], in0=OSB[oi][kr, :], in1=RB[oi][kr, :], op=ALU.mult),
                         reads=[("OSB", oi), ("RB", oi)], writes=[("OUTT", 4 + hp), ("HT", 0), ("HT", 1)])
            if dbg and gi == dbg_g:
                for k in range(8):
                    P.op("dve", lambda e, k=k: e.tensor_copy(out=YGV[0][:, 0, :], in_=OUTT[:, k, :]), reads=[("OUTT", k)], writes=["dtmp"])
                    dma(dbg_d["d_outT"][:, k * 256:(k + 1) * 256], YGV[0][:, 0, :], ["dtmp"], ["dtmp2"])
                dma(dbg_d["d_ckb"], CKB[:].rearrange("p b h -> p (b h)"), [("CKB", j) for j in range(NB)], ["dck"])
            for n in range(4):
                wb = wacquire(S_WOUT + n)
                pi = next_pg()
                for b in range(2):
                    mm_tok(pi, b, wb, b, OUTT, [("OUTT", k) for k in range(8)])
                P.op("dve", lambda e, n=n, pi=pi: e.tensor_tensor(out=XT[:, :, n * 256:(n + 1) * 256], in0=XT[:, :, n * 256:(n + 1) * 256],
                                                                  in1=PGB[pi][:, :].rearrange("p (b c) -> p b c", b=2), op=ALU.add),
                     reads=[("PG", pi), ("XT", 0), ("XT", 1)], writes=[("XT", 0), ("XT", 1)])
                wrelease()
            if dbg and gi == dbg_g:
                dma(dbg_d["d_x1"].rearrange("(b p) f -> p b f", p=128), XT[:], [("XT", 0), ("XT", 1)], ["dx1"])
            rmsnorm_to_HT()
            def ffn_tail(j):
                r = j % 2
                P.op("act", lambda e, r=r: e.activation(out=YGV[r][:, 0, :], in_=YGV[r][:, 0, :], func=AF.Silu), reads=[("YGV", r, 0)], writes=[("YGV", r, 0)])
                P.op("pool", lambda e, r=r, j=j: e.tensor_tensor(out=ACTT[:, j, :], in0=YGV[r][:, 0, :], in1=YGV[r][:, 1, :], op=ALU.mult),
                     reads=[("YGV", r, 0), ("YGV", r, 1)], writes=[("ACTT", j), "U", "VA", "VN", "OA"])

            for j in range(NCH):
                wb = wacquire(S_WUP + j)
                r = j % 2
                pi = next_pg()
                for part in range(2):
                    mm_feat(pi, part, wb, part * 128)
                P.op("dve", lambda e, r=r, j=j: e.tensor_copy(out=AGV[r][:, :, 0:2], in_=CARRY[:, j, :, :]),
                     reads=[("CARRY", j)], writes=[("AGV", r, 0), ("AGV", r, 1)])
                P.op("act", lambda e, r=r, pi=pi: e.copy(out=AGV[r][:, :, 2:258], in_=PGB[pi][:, :].rearrange("p (a t) -> p a t", a=2)),
                     reads=[("PG", pi)], writes=[("AGV", r, 0), ("AGV", r, 1)])
                for part in range(2):
                    cj = part * NCH + j
                    P.op("act", lambda e, r=r, part=part, cj=cj, pi=pi: e.activation(out=YGV[r][:, part, :], in_=PGB[pi][:, part * G:(part + 1) * G], func=AF.Identity,
                                                                                 bias=BC[:, cj:cj + 1], scale=WC[:, cj, 2:3]),
                         reads=[("PG", pi), "WC", "BC"], writes=[("YGV", r, part)])
                if j >= 1:
                    ffn_tail(j - 1)
                P.op("dve", lambda e, r=r, j=j: e.tensor_copy(out=CARRY[:, j, :, :], in_=AGV[r][:, :, 256:258]),
                     reads=[("AGV", r, 0), ("AGV", r, 1)], writes=[("CARRY", j)])
                for part in range(2):
                    cj = part * NCH + j
                    for tap in (1, 0):
                        P.op("dve", lambda e, r=r, part=part, cj=cj, tap=tap: e.scalar_tensor_tensor(out=YGV[r][:, part, :], in0=AGV[r][:, part, tap:tap + 256], scalar=WC[:, cj, tap:tap + 1],
                                                                                                  in1=YGV[r][:, part, :], op0=ALU.mult, op1=ALU.add),
                             reads=[("AGV", r, part), "WC", ("YGV", r, part)], writes=[("YGV", r, part)])
                wrelease()
            ffn_tail(NCH - 1)
            for i in range(11):
                wb = wacquire(S_WDN + i)
                for c2 in range(2):
                    ch = 2 * i + c2
                    for b in range(2):
                        for n2 in range(2):
                            acc = b * 2 + n2
                            P.op("pe", lambda e, ch=ch, c2=c2, b=b, n2=n2, acc=acc, wb=wb: e.matmul(ACCB[acc][:, :], lhsT=ACTT[:, ch, b * 128:(b + 1) * 128],
                                                                                                 rhs=WP[wb][:, c2 * 1024 + n2 * 512:c2 * 1024 + (n2 + 1) * 512],
                                                                                                 start=(ch == 0), stop=(ch == NCH - 1)),
                                 reads=[("WP", wb), ("ACTT", ch)], writes=[("ACC", acc)])
                wrelease()
            for b in range(2):
                for n2 in range(2):
                    acc = b * 2 + n2
                    P.op("dve", lambda e, b=b, n2=n2, acc=acc: e.tensor_tensor(out=XT[:, b, n2 * 512:(n2 + 1) * 512], in0=XT[:, b, n2 * 512:(n2 + 1) * 512], in1=ACCB[acc][:, :], op=ALU.add),
                         reads=[("ACC", acc), ("XT", b)], writes=[("XT", b)])
            for b in range(2):
                P.op("dve", lambda e, b=b: e.scalar_tensor_tensor(out=HB[:, b, :], in0=XT[:, b, :], scalar=1.0, in1=XT[:, b, :],
                                                                 op0=ALU.mult, op1=ALU.mult, accum_out=ST[:, 32 + b:33 + b]),
                     reads=[("XT", b)], writes=[("HB", b), ("ST", 32 + b)])
                P.op("act", lambda e, b=b: e.activation(out=ST[:, 34 + b:35 + b], in_=ST[:, 32 + b:33 + b], func=AF.Ln, bias=EPSC[:, 0:1], scale=1.0 / D),
                     reads=[("ST", 32 + b), "EPSC"], writes=[("ST", 34 + b)])
                P.op("act", lambda e, b=b: e.activation(out=ST[:, 36 + b:37 + b], in_=ST[:, 34 + b:35 + b], func=AF.Exp, scale=-0.5),
                     reads=[("ST", 34 + b)], writes=[("ST", 36 + b)])
                P.op("dve", lambda e, b=b: e.scalar_tensor_tensor(out=OUTF[:, b, :], in0=XT[:, b, :], scalar=ST[:, 36 + b:37 + b], in1=GFIN[:, :], op0=ALU.mult, op1=ALU.mult),
                     reads=[("XT", b), ("ST", 36 + b), "GFIN"], writes=ARENA_ALL)
            if gi >= 1:
                r0 = (gi - 1) * G
                dma(out_d[r0:r0 + G, :].rearrange("(b p) f -> p b f", p=128), OUTF, ARENA_ALL, [("out", gi)])

        dbg_g = 1
        for ci in range(NGRP):
            if stage < 3 or (ci < FIRST_FULL and ci >= n_ctx):
                continue
            if ci >= FIRST_FULL and (ci - FIRST_FULL) >= n_full:
                break
            emit_group(ci)
        fin = [("out", gi) for gi in range(1, n_full)]
        if dbg:
            fin += ["dq", "dtmp2", "dck", "dx1"]
        P.op("sp", None, reads=fin)
        P.emit()
        print("ops:", len(P.ops), "signals:", P.stats)
    return nc


def make_par(inputs, core):
    par = np.zeros((128, NPAR), np.float32)

    def put(name, arr):
        a, b = PC[name]
        par[:, a:b] = np.asarray(arr, np.float32).reshape(128, b - a)

    put("eye", np.eye(128, dtype=np.float32))
    s = np.arange(128)
    put("tri", (s[:, None] <= s[None, :]).astype(np.float32))
    sel = np.zeros((128, 128), np.float32); sel[127, :] = 1.0
    put("sel127", sel)
    sel = np.zeros((128, 128), np.float32); sel[64, :] = 1.0
    put("selA", sel)
    sel = np.zeros((128, 128), np.float32); sel[63, :] = 1.0
    put("selB", sel)
    tri_mask = np.where(s[:, None] <= s[None, :], 0.0, NEG).astype(np.float32)
    cm0 = np.concatenate([tri_mask, np.zeros((128, 128), np.float32)], axis=1)
    cm1 = np.concatenate([np.full((128, 128), NEG, np.float32), tri_mask], axis=1)
    put("cm0", cm0)
    put("cm1", cm1)
    put("fb", np.broadcast_to(inputs["f_bias"].reshape(1, 8), (128, 8)))
    put("lng", np.broadcast_to(inputs["sg_ln_g"].reshape(1, 512), (128, 512)))
    put("gfin", np.broadcast_to(inputs["norm_final_g"].reshape(1, D), (128, D)))
    put("g1T", inputs["norm_mix_g"].reshape(8, 128).T)
    put("g2T", inputs["norm_ffn_g"].reshape(8, 128).T)
    wc = inputs["w_conv"].reshape(3, 2 * NCH, 128)
    put("wc", np.transpose(wc, (2, 1, 0)))
    put("bc", inputs["b_conv"].reshape(2 * NCH, 128).T)
    put("bsT", inputs["sg_b"].reshape(8, 128).T)
    km = np.zeros((128, NB), np.float32)
    if core % 2 == 0:
        km[:, 0:32] = NEG
    put("kmask", km)
    sgw = inputs["sg_w"].reshape(8, 128, 128)
    put("sgwT", np.transpose(sgw, (2, 0, 1)))
    return par


_NC_CACHE = {}


def kernel(**inputs):
    inputs = {k: np.asarray(v) for k, v in inputs.items()}
    x = inputs["x"].astype(np.float32, copy=False)
    key = "full"
    if key not in _NC_CACHE:
        _NC_CACHE[key] = build_nc()
    nc = _NC_CACHE[key]
    in_maps = []
    for c in range(8):
        b, half = c // 2, c % 2
        if half == 0:
            xc = np.concatenate([np.zeros((4096, D), np.float32), x[b, 0:4096]], axis=0)
        else:
            xc = x[b]
        in_maps.append({
            "x": np.ascontiguousarray(xc),
            "par": make_par(inputs, c),
            "w_in": np.ascontiguousarray(inputs["w_in"][0]),
            "w_out": np.ascontiguousarray(inputs["w_out"][0]),
            "w_up": np.ascontiguousarray(inputs["w_up"][0]),
            "w_down": np.ascontiguousarray(inputs["w_down"][0]),
        })
    res = run_bass_kernel_spmd(nc, in_maps, core_ids=list(range(8)))
    out = np.empty((4, SEQ, D), np.float32)
    for c in range(8):
        b, half = c // 2, c % 2
        out[b, half * 4096:(half + 1) * 4096] = res.results[c]["out"]
    return out
```

```python
import contextlib
import numpy as np
import concourse.bass as bass
import concourse.mybir as mybir
from concourse.bass_utils import run_bass_kernel_spmd

F32 = mybir.dt.float32
BF16 = mybir.dt.bfloat16
AF = mybir.ActivationFunctionType
ALU = mybir.AluOpType
AX = mybir.AxisListType

SAME_ENGINE_SYNC = True
NDMASEM = 8

D = 1024
SEQ = 8192
NB = 64
G = 256
NGRP = 32
FIRST_FULL = 15
DFF = 2816
NCH = 22
EPS = 1e-6
NEG = -30000.0

PC = {}
_off = 0
for _n, _w in [("eye", 128), ("tri", 128), ("sel127", 128), ("selA", 128), ("selB", 128),
               ("cm0", 256), ("cm1", 256), ("fb", 8), ("lng", 512), ("gfin", 1024),
               ("g1T", 8), ("g2T", 8), ("wc", 132), ("bc", 44), ("bsT", 8), ("kmask", 64),
               ("sgwT", 1024)]:
    PC[_n] = (_off, _off + _w)
    _off += _w
NPAR = _off

S_WIN, S_WOUT, S_WUP, S_WDN = 0, 10, 14, 36
NSLOT = 47


class Op:
    __slots__ = ("eng", "fn", "deps", "idx", "signaled", "ev", "is_dma", "dq", "dslot")

    def __init__(self, eng, fn, idx, is_dma):
        self.eng = eng
        self.fn = fn
        self.idx = idx
        self.deps = []
        self.signaled = False
        self.ev = None
        self.is_dma = is_dma


class Prog:
    ENGS = ("sp", "act", "dve", "pool", "pe")

    def __init__(self, nc):
        self.nc = nc
        self.ops = []
        self.last_writer = {}
        self.readers = {}
        self.dma_count = {e: 0 for e in self.ENGS}

    def op(self, eng, fn, reads=(), writes=(), dma=False):
        o = Op(eng, fn, len(self.ops), dma)
        deps = {}
        for r in reads:
            w = self.last_writer.get(r)
            if w is not None:
                deps[w.idx] = w
        for r in writes:
            w = self.last_writer.get(r)
            if w is not None:
                deps[w.idx] = w
            for rd in self.readers.get(r, ()):
                deps[rd.idx] = rd
        o.deps = list(deps.values())
        for r in reads:
            self.readers.setdefault(r, []).append(o)
        for r in writes:
            self.last_writer[r] = o
            self.readers[r] = []
        if dma:
            o.dq = eng
            o.dslot = self.dma_count[eng]
            self.dma_count[eng] += 1
        self.ops.append(o)
        return o

    def emit(self):
        nc = self.nc
        ops = self.ops

        def skip(d, o):
            return (not d.is_dma) and (not o.is_dma) and d.eng == o.eng and (d.eng == "pe" or not SAME_ENGINE_SYNC)

        for o in ops:
            for d in o.deps:
                if d.is_dma or skip(d, o):
                    continue
                d.signaled = True
        with contextlib.ExitStack() as es:
            esem = {e: es.enter_context(nc.semaphore("c_" + e)) for e in self.ENGS}
            dsem = {}
            for e in self.ENGS:
                if self.dma_count[e] > 0:
                    dsem[e] = [es.enter_context(nc.semaphore("d_%s_%d" % (e, i))) for i in range(NDMASEM)]
            cnt = {e: 0 for e in self.ENGS}
            for o in ops:
                if o.is_dma:
                    s = dsem[o.dq][o.dslot % NDMASEM]
                    o.ev = (s, 16 * (o.dslot // NDMASEM + 1))
                elif o.signaled:
                    cnt[o.eng] += 1
                    o.ev = (esem[o.eng], cnt[o.eng])
            self.stats = dict(cnt)
            block = es.enter_context(nc.Block())

            def body(engname, eng):
                seen = {}
                for o in ops:
                    if o.eng != engname:
                        continue
                    waits = []
                    for d in o.deps:
                        if d.ev is None or skip(d, o):
                            continue
                        waits.append(d.ev)
                    if o.is_dma and o.dslot >= NDMASEM:
                        s = dsem[o.dq][o.dslot % NDMASEM]
                        waits.append((s, 16 * (o.dslot // NDMASEM)))
                    mx = {}
                    for (s, v) in waits:
                        k = id(s)
                        if k not in mx or mx[k][1] < v:
                            mx[k] = (s, v)
                    for k, (s, v) in mx.items():
                        if seen.get(k, 0) >= v:
                            continue
                        seen[k] = v
                        eng.wait_ge(s, v)
                    if o.fn is None:
                        continue
                    inst = o.fn(eng)
                    if o.is_dma:
                        inst.then_inc(o.ev[0], 16)
                    elif o.signaled:
                        inst.then_inc(o.ev[0], 1)

            @block.sync
            def _(e):
                body("sp", e)

            @block.scalar
            def _(e):
                body("act", e)

            @block.vector
            def _(e):
                body("dve", e)

            @block.gpsimd
            def _(e):
                body("pool", e)

            @block.tensor
            def _(e):
                body("pe", e)


def build_nc(n_full=17, dbg=False, stage=99, n_ctx=FIRST_FULL):
    nc = bass.Bass("TRN2", target_bir_lowering=False)
    x_d = nc.dram_tensor("x", [SEQ, D], F32, kind="ExternalInput").ap()
    par_d = nc.dram_tensor("par", [128, NPAR], F32, kind="ExternalInput").ap()
    win_d = nc.dram_tensor("w_in", [D, 2568], F32, kind="ExternalInput").ap()
    wout_d = nc.dram_tensor("w_out", [D, D], F32, kind="ExternalInput").ap()
    wup_d = nc.dram_tensor("w_up", [D, 2 * DFF], F32, kind="ExternalInput").ap()
    wdn_d = nc.dram_tensor("w_down", [DFF, D], F32, kind="ExternalInput").ap()
    out_d = nc.dram_tensor("out", [4096, D], F32, kind="ExternalOutput").ap()
    wsc = nc.dram_tensor("wsc", [NSLOT, 128, 2048], BF16, kind="Internal").ap()
    ksc = nc.dram_tensor("ksc", [4, 128, SEQ], BF16, kind="Internal").ap()
    dbg_d = {}
    if dbg:
        for nm, shp in [("d_x1", [256, D]), ("d_outT", [128, 8 * 256]), ("d_ckb", [128, 64 * 8]),
                        ("d_qT", [128, 4 * 256]), ("d_oa", [128, 2 * 512])]:
            dbg_d[nm] = nc.dram_tensor(nm, shp, F32, kind="ExternalOutput").ap()

    with contextlib.ExitStack() as es:
        def T(name, shape, dt):
            return es.enter_context(nc.sbuf_tensor(name, shape, dt))

        def PS(name, shape, dt):
            return es.enter_context(nc.psum_tensor(name, shape, dt))

        P = Prog(nc)
        KSTG = T("KSTG", [128, 4, G], BF16)
        KCH = [T("KCH%d" % i, [128, 1024], BF16) for i in range(3)]
        V = T("V", [128, NB, 4, 132], BF16)
        XT = T("XT", [128, 2, D], F32)
        HB = T("HB", [128, 2, D], BF16)
        HT = T("HT", [128, 8, G], BF16)
        FA = T("FA", [128, 2056], F32)
        QT = FA[:, 0:512].bitcast(BF16).rearrange("p (a t) -> p a t", a=4)
        NPT = 8
        PT = [FA[:, 512 + 128 * i:512 + 128 * (i + 1)].bitcast(BF16) for i in range(NPT)]
        OSB = [FA[:, 1536 + 256 * i:1536 + 256 * (i + 1)] for i in range(2)]
        AGV = [FA[:, 516 * i:516 * (i + 1)].rearrange("p (a c) -> p a c", a=2) for i in range(2)]
        YGV = [FA[:, 1032 + 512 * i:1032 + 512 * (i + 1)].rearrange("p (a c) -> p a c", a=2) for i in range(2)]
        ARENA = T("ARENA", [128, 3072], F32)
        OUTF = ARENA[:, 0:2048].rearrange("p (b f) -> p b f", b=2)
        U = ARENA[:, 0:1024].rearrange("p (b f) -> p b f", b=2)
        VA = ARENA[:, 1024:2048].rearrange("p (b f) -> p b f", b=2)
        ABF = ARENA[:].bitcast(BF16)
        VN = ABF[:, 4096:5120].rearrange("p (b f) -> p b f", b=2)
        OA = ABF[:, 5120:6144].rearrange("p (b f) -> p b f", b=2)
        ACTT = ABF[:, 0:NCH * G].rearrange("p (c t) -> p c t", c=NCH)
        OUTT = HT
        NWP = 20
        WP = [T("WP%d" % i, [128, 2048], BF16) for i in range(NWP)]
        RBT = T("RBT", [128, 512], F32)
        RB = [RBT[:, i * G:(i + 1) * G] for i in range(2)]
        SQ = RBT
        CKB = T("CKB", [128, NB, 8], F32)
        BIAS = [T("BIAS%d" % i, [128, NB], F32) for i in range(2)]
        WF = T("WF", [128, 8, 8], BF16)
        CARRY = T("CARRY", [128, NCH, 2, 2], F32)
        ident = T("ident", [128, 128], BF16)
        CMASK = T("CMASK", [128, 2, G], BF16)
        WMT = T("WMT", [128, 8, 128], BF16)
        TRI = T("TRI", [128, 128], F32)
        SEL127 = T("SEL127", [128, 128], F32)
        SELA = T("SELA", [128, 128], F32)
        SELB = T("SELB", [128, 128], F32)
        FB = T("FB", [128, 8], F32)
        LNG = T("LNG", [128, 512], F32)
        GFIN = T("GFIN", [128, D], F32)
        GT = T("GT", [128, 16], F32)
        WC = T("WC", [128, 2 * NCH, 3], F32)
        BC = T("BC", [128, 2 * NCH], F32)
        BST = T("BST", [128, 8], F32)
        KM = T("KM", [128, NB], F32)
        ST = T("ST", [128, 64], F32)
        CCAR = T("CCAR", [128, 8], F32)
        RG = T("RG", [128, 8], F32)
        CB = T("CB", [128, 8], F32)
        SP_ = T("SP_", [128, 8], F32)
        EPSC = T("EPSC", [128, 2], F32)
        PGB = [PS("PGB%d" % i, [128, 512], F32) for i in range(3)]
        ACCB = [PS("ACCB%d" % i, [128, 512], F32) for i in range(4)]
        PSTB = PS("PSTB", [128, 1024], BF16)
        print("sbuf bytes remaining:", nc.sbuf_bytes_remaining)

        rot = {"pg": 0, "pt": 0}
        att_rot = [0]
        kch_rot = [0]
        PSTF = PSTB[:, :].bitcast(F32)

        def next_pg():
            i = rot["pg"] % 3
            rot["pg"] += 1
            return i

        def next_pt():
            i = rot["pt"] % NPT
            rot["pt"] += 1
            return i

        def pc(name):
            a, b = PC[name]
            return par_d[:, a:b]

        def dma(out, in_, reads, writes):
            return P.op("sp", lambda e: e.dma_start(out=out, in_=in_), reads=reads, writes=writes, dma=True)

        ARENA_ALL = ["U", "VA", "VN", "OA"] + [("ACTT", c) for c in range(NCH)]

        dma(TRI[:], pc("tri"), [], ["TRI"])
        dma(SEL127[:], pc("sel127"), [], ["SEL127"])
        dma(SELA[:], pc("selA"), [], ["SELA"])
        dma(SELB[:], pc("selB"), [], ["SELB"])
        dma(FB[:], pc("fb"), [], ["FB"])
        dma(LNG[:], pc("lng"), [], ["LNG"])
        dma(GFIN[:], pc("gfin"), [], ["GFIN"])
        dma(GT[:, 0:8], pc("g1T"), [], ["GT"])
        dma(GT[:, 8:16], pc("g2T"), [], ["GT"])
        dma(WC[:].rearrange("p c j -> p (c j)"), pc("wc"), [], ["WC"])
        dma(BC[:], pc("bc"), [], ["BC"])
        dma(BST[:], pc("bsT"), [], ["BST"])
        dma(KM[:], pc("kmask"), [], ["KM"])
        XTf = XT[:].rearrange("p b f -> p (b f)")
        HBf = HB[:].rearrange("p b f -> p (b f)")
        HTf = HT[:].rearrange("p k t -> p (k t)")
        a0 = PC["eye"][0]
        dma(XTf[:, 0:128], pc("eye"), [], [("XT", 0), ("XT", 1), "XT"])
        dma(XTf[:, 128:640], par_d[:, PC["cm0"][0]:PC["cm1"][1]], [], [("XT", 0), ("XT", 1), "XT"])
        dma(XTf[:, 640:1664], pc("sgwT"), [], [("XT", 0), ("XT", 1), "XT"])
        P.op("dve", lambda e: e.tensor_copy(out=ident[:], in_=XTf[:, 0:128]), reads=[("XT", 0), ("XT", 1), "XT"], writes=["ident"])
        P.op("dve", lambda e: e.tensor_copy(out=CMASK[:].rearrange("p a t -> p (a t)"), in_=XTf[:, 128:640]), reads=[("XT", 0), ("XT", 1), "XT"], writes=["CMASK"])
        P.op("dve", lambda e: e.memset(XTf[64:128, 640:1664].rearrange("p (h t) -> p h t", h=8)[:, :, 0:64], 0.0), reads=[], writes=[("XT", 0), ("XT", 1), "XT"])
        P.op("dve", lambda e: e.tensor_copy(out=WMT[:].rearrange("p h t -> p (h t)"), in_=XTf[:, 640:1664]), reads=[("XT", 0), ("XT", 1), "XT"], writes=["WMT"])
        P.op("dve", lambda e: e.memset(EPSC[:, 0:1], EPS), writes=["EPSC"])
        P.op("dve", lambda e: e.memset(EPSC[:, 1:2], 1.0), writes=["EPSC"])
        P.op("dve", lambda e: e.memset(CCAR[:], 0.0), writes=["CCAR"])
        P.op("dve", lambda e: e.memset(CARRY[:].rearrange("p c a j -> p (c a j)"), 0.0), writes=["CARRY"])
        import os
        if os.environ.get("NOVMEM") != "1":
            P.op("dve", lambda e: e.memset(V[:].rearrange("p b a c -> p (b a) c")[:, :, 64:66], 1.0), writes=[("V", b, s) for b in range(NB) for s in range(2)])

        NCONV = 99 if stage >= 2 else 0
        XT_ALL = [("XT", 0), ("XT", 1), "XT"]
        HB_ALL = [("HB", 0), ("HB", 1)]
        HT_ALL = [("HT", 0), ("HT", 1)]
        stage_f = [(XTf, XT_ALL), (ARENA[:, 0:2048], ARENA_ALL)]
        stage_b = [(HBf, HB_ALL), (HTf, HT_ALL)]
        conv_engs = ["dve", "act"]
        cnt = [0]

        def convert(slot, pieces, gain_col, bg=False):
            if cnt[0] >= NCONV:
                return
            i = cnt[0] % 2
            cnt[0] += 1
            if bg:
                sf, sfn = ARENA[:, 0:2048], ["cvF"]
                sb, sbn = ABF[:, 4096:6144], ["cvB"]
            else:
                sf, sfn = stage_f[i]
                sb, sbn = stage_b[i]
            wr = list(sfn)
            for (dst, src) in pieces:
                dma(dst(sf), src, [], wr)
            eng = conv_engs[i]
            if gain_col is None:
                if eng == "dve":
                    P.op(eng, lambda e: e.tensor_copy(out=sb[:, :], in_=sf[:, :]), reads=wr, writes=list(sbn))
                else:
                    P.op(eng, lambda e: e.copy(out=sb[:, :], in_=sf[:, :]), reads=wr, writes=list(sbn))
            else:
                for k in range(8):
                    if eng == "dve":
                        P.op(eng, lambda e, k=k: e.tensor_scalar(out=sb[:, k * 256:(k + 1) * 256], in0=sf[:, k * 256:(k + 1) * 256],
                                                               scalar1=GT[:, gain_col + k:gain_col + k + 1], scalar2=None, op0=ALU.mult),
                             reads=wr + ["GT"], writes=list(sbn))
                    else:
                        P.op(eng, lambda e, k=k: e.mul(out=sb[:, k * 256:(k + 1) * 256], in_=sf[:, k * 256:(k + 1) * 256],
                                                     mul=GT[:, gain_col + k:gain_col + k + 1]),
                             reads=wr + ["GT"], writes=list(sbn))
            out_fn = lambda: dma(wsc[slot], sb[:, :], list(sbn), [("wsc", slot)])
            if bg:
                return out_fn
            out_fn()
            return None

        def v3(sf, a, b):
            return sf[:, 0:2048].rearrange("p (a b) -> p a b", a=a)

        conv_jobs = {}
        for s in range(10):
            conv_jobs[S_WIN + s] = ([(lambda sf: v3(sf, 8, 256), win_d[:, s * 256:(s + 1) * 256].rearrange("(k p) c -> p k c", p=128))], 0)
        for s in range(4):
            conv_jobs[S_WOUT + s] = ([(lambda sf: v3(sf, 8, 256), wout_d[:, s * 256:(s + 1) * 256].rearrange("(k p) c -> p k c", p=128))], None)
        for j in range(NCH):
            conv_jobs[S_WUP + j] = ([(lambda sf: v3(sf, 8, 256)[:, :, 0:128], wup_d[:, j * 128:(j + 1) * 128].rearrange("(k p) c -> p k c", p=128)),
                                     (lambda sf: v3(sf, 8, 256)[:, :, 128:256], wup_d[:, DFF + j * 128:DFF + (j + 1) * 128].rearrange("(k p) c -> p k c", p=128))], 8)
        for i in range(11):
            conv_jobs[S_WDN + i] = ([(lambda sf: v3(sf, 2, 1024), wdn_d[i * 256:(i + 1) * 256, :].rearrange("(c p) n -> p c n", p=128))], None)
        for slot in (6, 7, 8, 9):
            convert(slot, *conv_jobs[slot])
        dma(XTf[:, 0:64].rearrange("p (k c) -> p k c", k=8), win_d[:, 2560:2568].rearrange("(k p) c -> p k c", p=128), [], [("XT", 0), ("XT", 1), "XT"])
        for k in range(8):
            P.op("dve", lambda e, k=k: e.tensor_scalar(out=WF[:, k, :], in0=XTf[:, k * 8:(k + 1) * 8], scalar1=GT[:, k:k + 1], scalar2=None, op0=ALU.mult),
                 reads=[("XT", 0), ("XT", 1), "XT", "GT"], writes=["WF"])
        P.op("dve", lambda e: e.memset(ST[:, 60:61], 0.0), reads=[], writes=ARENA_ALL + ["cvF", "cvB"])
        bg_slots = [sl for sl in list(range(0, 6)) + list(range(10, NSLOT))]
        bg_state = {"i": 0, "pending_out": None}

        def conv_step():
            if bg_state["pending_out"] is not None:
                bg_state["pending_out"]()
                bg_state["pending_out"] = None
            if stage < 2 or bg_state["i"] >= len(bg_slots):
                return
            slot = bg_slots[bg_state["i"]]
            bg_state["i"] += 1
            bg_state["pending_out"] = convert(slot, *conv_jobs[slot], bg=True)

        def conv_flush():
            while bg_state["i"] < len(bg_slots) or bg_state["pending_out"] is not None:
                conv_step()
            P.op("dve", lambda e: e.memset(ST[:, 61:62], 0.0), reads=[], writes=ARENA_ALL + ["cvF", "cvB"])

        wseq = []
        for ci in range(NGRP):
            full = ci >= FIRST_FULL
            if stage < 3 or (not full and ci >= n_ctx):
                continue
            if full and (ci - FIRST_FULL) >= n_full:
                break
            if full:
                wseq += list(range(0, 10)) + list(range(10, 14)) + list(range(14, 36)) + list(range(36, 47))
            else:
                wseq += [6, 7, 8, 9]
        wstate = {"use": 0, "load": 0}

        def wload_next():
            n = wstate["load"]
            if n >= len(wseq):
                return
            slot = wseq[n]
            buf = n % NWP
            wstate["load"] += 1
            dma(WP[buf][:, :], wsc[slot], [("wsc", slot)], [("WP", buf)])

        def wacquire(slot):
            n = wstate["use"]
            assert wseq[n] == slot, (n, wseq[n], slot)
            return n % NWP

        def wrelease():
            wstate["use"] += 1
            wload_next()

        for _ in range(NWP):
            wload_next()

        evac_rr = [0]

        def rmsnorm_to_HT():
            for b in range(2):
                P.op("dve", lambda e, b=b: e.scalar_tensor_tensor(out=HB[:, b, :], in0=XT[:, b, :], scalar=1.0, in1=XT[:, b, :],
                                                                 op0=ALU.mult, op1=ALU.mult, accum_out=ST[:, b:b + 1]),
                     reads=[("XT", b)], writes=[("HB", b), ("ST", b)])
                P.op("act", lambda e, b=b: e.activation(out=ST[:, 2 + b:3 + b], in_=ST[:, b:b + 1], func=AF.Ln, bias=EPSC[:, 0:1], scale=1.0 / D),
                     reads=[("ST", b), "EPSC"], writes=[("ST", 2 + b)])
                P.op("act", lambda e, b=b: e.activation(out=ST[:, 4 + b:5 + b], in_=ST[:, 2 + b:3 + b], func=AF.Exp, scale=-0.5),
                     reads=[("ST", 2 + b)], writes=[("ST", 4 + b)])
                P.op("dve", lambda e, b=b: e.tensor_scalar(out=HB[:, b, :], in0=XT[:, b, :], scalar1=ST[:, 4 + b:5 + b], scalar2=None, op0=ALU.mult),
                     reads=[("XT", b), ("ST", 4 + b)], writes=[("HB", b)])
                for k in range(8):
                    P.op("pe", lambda e, b=b, k=k: e.transpose(out=PSTB[:, k * 128:(k + 1) * 128], in_=HB[:, b, k * 128:(k + 1) * 128], identity=ident[:]),
                         reads=[("HB", b), "ident"], writes=["PST"])
                P.op("act", lambda e, b=b: e.copy(out=HT[:, :, b * 128:(b + 1) * 128], in_=PSTB[:, :].rearrange("p (k t) -> p k t", k=8)),
                     reads=["PST"], writes=[("HT", b)] + [("OUTT", kq) for kq in range(8)])

        def HT_reads(b=None):
            if b is None:
                return [("HT", 0), ("HT", 1)]
            return [("HT", b)]

        def mm_feat(pi, hf, wbuf, c0):
            for k in range(8):
                P.op("pe", lambda e, k=k: e.matmul(PGB[pi][:, hf * G:(hf + 1) * G], lhsT=WP[wbuf][:, k * 256 + c0:k * 256 + c0 + 128], rhs=HT[:, k, :],
                                                  start=(k == 0), stop=(k == 7)),
                     reads=[("WP", wbuf)] + HT_reads(), writes=[("PG", pi)])

        def mm_tok(pi, hf, wbuf, b, lhs, lhs_reads):
            for k in range(8):
                P.op("pe", lambda e, k=k: e.matmul(PGB[pi][:, hf * G:(hf + 1) * G], lhsT=lhs[:, k, b * 128:(b + 1) * 128], rhs=WP[wbuf][:, k * 256:(k + 1) * 256],
                                                  start=(k == 0), stop=(k == 7)),
                     reads=[("WP", wbuf)] + lhs_reads, writes=[("PG", pi)])

        def emit_group(ci):
            full = ci >= FIRST_FULL
            gi = ci - FIRST_FULL
            b0 = 2 * ci
            T0 = ci * G
            if gi == 0:
                conv_flush()
            dma(XT[:], x_d[T0:T0 + G, :].rearrange("(b p) f -> p b f", p=128), [], [("XT", 0), ("XT", 1), "XT"])
            if not full:
                conv_step()
            rmsnorm_to_HT()
            if full:
                for s in range(4):
                    wb = wacquire(s)
                    dst, dn = (U, "U") if s < 2 else (VA, "VA")
                    hf = s % 2
                    pi = next_pg()
                    for b in range(2):
                        mm_tok(pi, b, wb, b, HT, HT_reads(b))
                    P.op("act", lambda e, pi=pi, dst=dst, hf=hf: e.activation(out=dst[:, :, hf * 256:(hf + 1) * 256], in_=PGB[pi][:, :].rearrange("p (b c) -> p b c", b=2), func=AF.Gelu),
                         reads=[("PG", pi)], writes=[dn] + [("ACTT", c) for c in range(NCH)])
                    wrelease()
                for s in range(2):
                    wb = wacquire(4 + s)
                    pi = next_pg()
                    for m in range(2):
                        mm_feat(pi, m, wb, m * 128)
                    P.op("act", lambda e, pi=pi, s=s: e.mul(out=QT[:, 2 * s:2 * s + 2, :].rearrange("p a t -> p (a t)"), in_=PGB[pi][:, :], mul=0.125),
                         reads=[("PG", pi)], writes=[("QT", 2 * s), ("QT", 2 * s + 1)])
                    wrelease()
            for s in range(2):
                wb = wacquire(6 + s)
                pi = next_pg()
                for m in range(2):
                    mm_feat(pi, m, wb, m * 128)
                for m in range(2):
                    hp = 2 * s + m
                    P.op("act", lambda e, pi=pi, hp=hp, m=m: e.copy(out=KSTG[:, hp, :], in_=PGB[pi][:, m * G:(m + 1) * G]),
                         reads=[("PG", pi)], writes=[("KSTG", hp)])
                wrelease()
            dma(ksc[:, :, T0:T0 + G].rearrange("h p t -> p h t"), KSTG[:, :, :], [("KSTG", hp) for hp in range(4)],
                [("KD", hp, bb) for hp in range(4) for bb in (b0, b0 + 1)])
            if not full:
                conv_step()
            for s in range(2):
                wb = wacquire(8 + s)
                pi = next_pg()
                for b in range(2):
                    mm_tok(pi, b, wb, b, HT, HT_reads(b))
                for b in range(2):
                    for a in range(2):
                        P.op("dve", lambda e, b=b, pi=pi, s=s, a=a: e.tensor_copy(
                            out=V[:, b0 + b, 2 * s + a, :].rearrange("p (e c) -> p e c", e=2)[:, :, 0:64],
                            in_=PGB[pi][:, b * G + a * 128:b * G + (a + 1) * 128].rearrange("p (e c) -> p e c", e=2)),
                             reads=[("PG", pi)], writes=[("V", b0 + b, s)])
                wrelease()
            if not full:
                conv_step()
            for b in range(2):
                pi = next_pg()
                for k in range(8):
                    P.op("pe", lambda e, k=k, b=b, pi=pi: e.matmul(PGB[pi][:, 0:8], lhsT=HT[:, k, b * 128:(b + 1) * 128], rhs=WF[:, k, :], start=(k == 0), stop=(k == 7)),
                         reads=["WF"] + HT_reads(b), writes=[("PG", pi)])
                P.op("dve", lambda e, pi=pi: e.tensor_tensor(out=SP_[:, :], in0=PGB[pi][:, 0:8], in1=FB[:, :], op=ALU.add),
                     reads=[("PG", pi), "FB"], writes=["SP_"])
                P.op("act", lambda e: e.activation(out=SP_[:, :], in_=SP_[:, :], func=AF.Exp, scale=-1.0), reads=["SP_"], writes=["SP_"])
                P.op("act", lambda e: e.activation(out=SP_[:, :], in_=SP_[:, :], func=AF.Ln, bias=EPSC[:, 1:2], scale=1.0), reads=["SP_", "EPSC"], writes=["SP_"])
                pi2 = next_pg()
                P.op("pe", lambda e, pi2=pi2: e.matmul(PGB[pi2][:, 0:8], lhsT=TRI[:, :], rhs=SP_[:, :], start=True, stop=True),
                     reads=["TRI", "SP_"], writes=[("PG", pi2)])
                P.op("dve", lambda e, pi2=pi2: e.tensor_tensor(out=CB[:, :], in0=PGB[pi2][:, 0:8], in1=CCAR[:, :], op=ALU.add),
                     reads=[("PG", pi2), "CCAR"], writes=["CB"])
                P.op("dve", lambda e, b=b: e.tensor_scalar(out=CKB[:, b0 + b, :], in0=CB[:, :], scalar1=KM[:, b0 + b:b0 + b + 1], scalar2=None, op0=ALU.add),
                     reads=["CB", "KM"], writes=[("CKB", b0 + b)])
                pi3 = next_pg()
                P.op("pe", lambda e, pi3=pi3: e.matmul(PGB[pi3][:, 0:8], lhsT=SEL127[:, :], rhs=CB[:, :], start=True, stop=True),
                     reads=["SEL127", "CB"], writes=[("PG", pi3)])
                P.op("dve", lambda e, pi3=pi3: e.tensor_copy(out=CCAR[:, :], in_=PGB[pi3][:, 0:8]), reads=[("PG", pi3)], writes=["CCAR"])
                if b == 0 and full:
                    P.op("dve", lambda e, pi3=pi3: e.tensor_copy(out=RG[:, :], in_=PGB[pi3][:, 0:8]), reads=[("PG", pi3)], writes=["RG"])
            if not full:
                return
            if dbg and gi == dbg_g:
                dma(dbg_d["d_qT"], QT[:].rearrange("p a t -> p (a t)"), [("QT", h) for h in range(4)], ["dq"])
            def emit_sg():
                SQR = [("RB", 0), ("RB", 1)]
                for b in range(2):
                    va3 = VA[:, b, :].rearrange("p (h d) -> p h d", h=8)
                    P.op("dve", lambda e, va3=va3: e.tensor_reduce(out=ST[:, 8:16], in_=va3, axis=AX.X, op=ALU.add), reads=["VA"], writes=[("ST", "s1")])
                    P.op("dve", lambda e, b=b: e.tensor_tensor(out=SQ[:, :], in0=VA[:, b, :], in1=VA[:, b, :], op=ALU.mult), reads=["VA"], writes=SQR)
                    P.op("dve", lambda e: e.tensor_reduce(out=ST[:, 16:24], in_=SQ[:, :].rearrange("p (h d) -> p h d", h=8), axis=AX.X, op=ALU.add),
                         reads=SQR, writes=[("ST", "s2")])
                    P.op("dve", lambda e: e.tensor_scalar(out=ST[:, 8:16], in0=ST[:, 8:16], scalar1=1.0 / 64, scalar2=None, op0=ALU.mult),
                         reads=[("ST", "s1")], writes=[("ST", "s1")])
                    P.op("dve", lambda e: e.tensor_tensor(out=ST[:, 24:32], in0=ST[:, 8:16], in1=ST[:, 8:16], op=ALU.mult),
                         reads=[("ST", "s1")], writes=[("ST", "msq")])
                    P.op("dve", lambda e: e.scalar_tensor_tensor(out=ST[:, 16:24], in0=ST[:, 16:24], scalar=1.0 / 64, in1=ST[:, 24:32], op0=ALU.mult, op1=ALU.subtract),
                         reads=[("ST", "s2"), ("ST", "msq")], writes=[("ST", "s2")])
                    P.op("act", lambda e: e.activation(out=ST[:, 16:24], in_=ST[:, 16:24], func=AF.Ln, bias=EPSC[:, 0:1], scale=1.0),
                         reads=[("ST", "s2"), "EPSC"], writes=[("ST", "s2")])
                    P.op("act", lambda e: e.activation(out=ST[:, 16:24], in_=ST[:, 16:24], func=AF.Exp, scale=-0.5),
                         reads=[("ST", "s2")], writes=[("ST", "s2")])
                    P.op("dve", lambda e, va3=va3: e.tensor_tensor(out=va3, in0=va3, in1=ST[:, 8:16].unsqueeze(2).to_broadcast([128, 8, 64]), op=ALU.subtract),
                         reads=["VA", ("ST", "s1")], writes=["VA"])
                    P.op("dve", lambda e, va3=va3: e.tensor_tensor(out=va3, in0=va3, in1=ST[:, 16:24].unsqueeze(2).to_broadcast([128, 8, 64]), op=ALU.mult),
                         reads=["VA", ("ST", "s2")], writes=["VA"])
                    P.op("dve", lambda e, b=b: e.tensor_tensor(out=VN[:, b, :], in0=VA[:, b, :], in1=LNG[:, :], op=ALU.mult),
                         reads=["VA", "LNG"], writes=["VN"])
                    pi = next_pg()
                    for h in range(8):
                        P.op("pe", lambda e, h=h, b=b, pi=pi: e.matmul(PGB[pi][:, h * 64:(h + 1) * 64], lhsT=WMT[:, h, :], rhs=VN[:, b, h * 64:(h + 1) * 64],
                                                                      start=True, stop=True),
                             reads=["WMT", "VN"], writes=[("PG", pi)])
                    P.op("dve", lambda e, pi=pi: e.tensor_tensor(out=SQ[:, :].rearrange("p (h d) -> p h d", h=8),
                                                                 in0=PGB[pi][:, :].rearrange("p (h d) -> p h d", h=8),
                                                                 in1=BST[:, :].unsqueeze(2).to_broadcast([128, 8, 64]), op=ALU.add),
                         reads=[("PG", pi), "BST"], writes=SQR)
                    P.op("dve", lambda e, b=b: e.tensor_tensor(out=OA[:, b, :], in0=SQ[:, :], in1=U[:, b, :], op=ALU.mult),
                         reads=SQR + ["U"], writes=["OA"])
                    for kk in range(4):
                        P.op("pe", lambda e, b=b, kk=kk: e.transpose(out=PSTB[:, kk * 128:(kk + 1) * 128], in_=OA[:, b, kk * 128:(kk + 1) * 128], identity=ident[:]),
                             reads=["OA", "ident"], writes=["PST"])
                    P.op("act", lambda e, b=b: e.copy(out=OUTT[:, 0:4, b * 128:(b + 1) * 128], in_=PSTB[:, 0:512].rearrange("p (k t) -> p k t", k=4)),
                         reads=["PST"], writes=[("OUTT", k) for k in range(4)] + [("HT", 0), ("HT", 1)])

            nj = b0 + 2
            nchunk = (nj + 7) // 8

            def kload(hp, c):
                nblk = min(8, nj - 8 * c)
                kb = kch_rot[0] % 3
                kch_rot[0] += 1
                dma(KCH[kb][:, 0:nblk * 128], ksc[hp][:, c * 1024:c * 1024 + nblk * 128],
                    [("KD", hp, 8 * c + q) for q in range(nblk)], [("KCH", kb)])
                return kb

            for hp in range(4):
                if hp == 1:
                    emit_sg()
                accs = [(2 * hp) % 4, (2 * hp + 1) % 4]
                kbufs = {0: kload(hp, 0)}
                if nchunk > 1:
                    kbufs[1] = kload(hp, 1)
                for ee in range(2):
                    h = 2 * hp + ee
                    P.op("dve", lambda e, h=h, ee=ee: e.tensor_scalar(out=BIAS[ee][:, 0:nj], in0=CKB[:, 0:nj, h], scalar1=RG[:, h:h + 1], scalar2=None, op0=ALU.subtract),
                         reads=[("CKB", j) for j in range(nj)] + ["RG"], writes=[("BIAS", ee)])

                def qk_step(j, hp=hp):
                    bp = att_rot[0] % 2
                    att_rot[0] += 1
                    banks = [(PGB[0], ("PG", 0)), (PGB[1], ("PG", 1))] if bp == 0 else [(PGB[2], ("PG", 2)), (PSTF, "PST")]
                    for q in range(2):
                        jj = j + q
                        diag = jj >= b0
                        for ee in range(2):
                            kr = slice(ee * 64, (ee + 1) * 64)
                            bk, bkey = banks[ee]
                            kb_ = kbufs[jj // 8]
                            jo = jj % 8
                            P.op("pe", lambda e, ee=ee, kr=kr, bk=bk, jj=jj, q=q, diag=diag, kb_=kb_, jo=jo: e.matmul(bk[:, q * G:(q + 1) * G], lhsT=KCH[kb_][kr, jo * 128:(jo + 1) * 128], rhs=QT[kr, hp, :],
                                                                                                  start=True, stop=not diag),
                                 reads=[("KCH", kb_), ("QT", hp)], writes=[bkey])
                        if diag:
                            for ee in range(2):
                                bk, bkey = banks[ee]
                                P.op("pe", lambda e, bk=bk, jj=jj, q=q: e.matmul(bk[:, q * G:(q + 1) * G], lhsT=ident[:, :], rhs=CMASK[:, jj - b0, :], start=False, stop=True),
                                     reads=["ident", "CMASK"], writes=[bkey])
                    return banks

                def exp_step(j, banks):
                    tis = []
                    for q in range(2):
                        jj = j + q
                        for ee in range(2):
                            bk, bkey = banks[ee]
                            ti = next_pt()
                            P.op("act", lambda e, ee=ee, ti=ti, bk=bk, jj=jj, q=q: e.activation(out=PT[ti][:, :], in_=bk[:, q * G:(q + 1) * G], func=AF.Exp, bias=BIAS[ee][:, jj:jj + 1], scale=1.0),
                                 reads=[bkey, ("BIAS", ee)], writes=[("PT", ti)])
                            tis.append(ti)
                    return tis

                def pv_step(j, tis, hp=hp, accs=accs):
                    for q in range(2):
                        jj = j + q
                        vflat = V[:, jj, hp, :]
                        t0_, t1_ = tis[2 * q], tis[2 * q + 1]
                        P.op("pe", lambda e, jj=jj, vflat=vflat, t0_=t0_: e.matmul(ACCB[accs[0]][0:65, 0:G], lhsT=vflat[:, 0:65], rhs=PT[t0_][:, :], start=(jj == 0), stop=(jj == nj - 1)),
                             reads=[("V", jj, hp // 2), ("PT", t0_)], writes=[("ACC", accs[0])])
                        P.op("pe", lambda e, jj=jj, vflat=vflat, t1_=t1_: e.matmul(ACCB[accs[1]][:, 0:G], lhsT=vflat[:, 2:130], rhs=PT[t1_][:, :], start=(jj == 0), stop=(jj == nj - 1)),
                             reads=[("V", jj, hp // 2), ("PT", t1_)], writes=[("ACC", accs[1])])

                pend = None
                for j in range(0, nj, 2):
                    if j % 8 == 0 and j > 0 and (j // 8 + 1) < nchunk:
                        kbufs[j // 8 + 1] = kload(hp, j // 8 + 1)
                    banks = qk_step(j)
                    tis = exp_step(j, banks)
                    if pend is not None:
                        pv_step(*pend)
                    pend = (j, tis)
                pv_step(*pend)
                for ee in range(2):
                    h = 2 * hp + ee
                    acc = accs[ee]
                    kr = slice(ee * 64, (ee + 1) * 64)
                    oi = ee
                    K = 65 if ee == 0 else 128
                    sel = SELA if ee == 0 else SELB
                    seln = "SELA" if ee == 0 else "SELB"
                    P.op("act", lambda e, oi=oi, K=K, acc=acc: e.copy(out=OSB[oi][0:K, :], in_=ACCB[acc][0:K, 0:G]), reads=[("ACC", acc)], writes=[("OSB", oi)])
                    pi = next_pg()
                    P.op("pe", lambda e, oi=oi, K=K, sel=sel, pi=pi: e.matmul(PGB[pi][:, 0:G], lhsT=sel[0:K, :], rhs=OSB[oi][0:K, :], start=True, stop=True),
                         reads=[seln, ("OSB", oi)], writes=[("PG", pi)])
                    P.op("dve", lambda e, oi=oi, pi=pi: e.tensor_scalar(out=RB[oi][:, :], in0=PGB[pi][:, 0:G], scalar1=1e-30, scalar2=None, op0=ALU.max),
                         reads=[("PG", pi)], writes=[("RB", oi)])
                    P.op("dve", lambda e, oi=oi: e.reciprocal(out=RB[oi][:, :], in_=RB[oi][:, :]), reads=[("RB", oi)], writes=[("RB", oi)])
                    P.op("dve", lambda e, oi=oi, kr=kr, hp=hp: e.tensor_tensor(out=OUTT[kr, 4 + hp, :], in0=OSB[oi][kr, :], in1=RB[oi][kr, :], op=ALU.mult),
                         reads=[("OSB", oi), ("RB", oi)], writes=[("OUTT", 4 + hp), ("HT", 0), ("HT", 1)])
            if dbg and gi == dbg_g:
                for k in range(8):
                    P.op("dve", lambda e, k=k: e.tensor_copy(out=YGV[0][:, 0, :], in_=OUTT[:, k, :]), reads=[("OUTT", k)], writes=["dtmp"])
                    dma(dbg_d["d_outT"][:, k * 256:(k + 1) * 256], YGV[0][:, 0, :], ["dtmp"], ["dtmp2"])
                dma(dbg_d["d_ckb"], CKB[:].rearrange("p b h -> p (b h)"), [("CKB", j) for j in range(NB)], ["dck"])
            for n in range(4):
                wb = wacquire(S_WOUT + n)
                pi = next_pg()
                for b in range(2):
                    mm_tok(pi, b, wb, b, OUTT, [("OUTT", k) for k in range(8)])
                P.op("dve", lambda e, n=n, pi=pi: e.tensor_tensor(out=XT[:, :, n * 256:(n + 1) * 256], in0=XT[:, :, n * 256:(n + 1) * 256],
                                                                  in1=PGB[pi][:, :].rearrange("p (b c) -> p b c", b=2), op=ALU.add),
                     reads=[("PG", pi), ("XT", 0), ("XT", 1)], writes=[("XT", 0), ("XT", 1)])
                wrelease()
            if dbg and gi == dbg_g:
                dma(dbg_d["d_x1"].rearrange("(b p) f -> p b f", p=128), XT[:], [("XT", 0), ("XT", 1)], ["dx1"])
            rmsnorm_to_HT()
            def ffn_tail(j):
                r = j % 2
                P.op("act", lambda e, r=r: e.activation(out=YGV[r][:, 0, :], in_=YGV[r][:, 0, :], func=AF.Silu), reads=[("YGV", r, 0)], writes=[("YGV", r, 0)])
                P.op("pool", lambda e, r=r, j=j: e.tensor_tensor(out=ACTT[:, j, :], in0=YGV[r][:, 0, :], in1=YGV[r][:, 1, :], op=ALU.mult),
                     reads=[("YGV", r, 0), ("YGV", r, 1)], writes=[("ACTT", j), "U", "VA", "VN", "OA"])

            for j in range(NCH):
                wb = wacquire(S_WUP + j)
                r = j % 2
                pi = next_pg()
                for part in range(2):
                    mm_feat(pi, part, wb, part * 128)
                P.op("dve", lambda e, r=r, j=j: e.tensor_copy(out=AGV[r][:, :, 0:2], in_=CARRY[:, j, :, :]),
                     reads=[("CARRY", j)], writes=[("AGVc", r)])
                P.op("act", lambda e, r=r, pi=pi: e.copy(out=AGV[r][:, :, 2:258], in_=PGB[pi][:, :].rearrange("p (a t) -> p a t", a=2)),
                     reads=[("PG", pi)], writes=[("AGV", r, 0), ("AGV", r, 1)])
                for part in range(2):
                    cj = part * NCH + j
                    P.op("act", lambda e, r=r, part=part, cj=cj, pi=pi: e.activation(out=YGV[r][:, part, :], in_=PGB[pi][:, part * G:(part + 1) * G], func=AF.Identity,
                                                                                 bias=BC[:, cj:cj + 1], scale=WC[:, cj, 2:3]),
                         reads=[("PG", pi), "WC", "BC"], writes=[("YGV", r, part)])
                if j >= 1:
                    ffn_tail(j - 1)
                P.op("dve", lambda e, r=r, j=j: e.tensor_copy(out=CARRY[:, j, :, :], in_=AGV[r][:, :, 256:258]),
                     reads=[("AGV", r, 0), ("AGV", r, 1)], writes=[("CARRY", j)])
                for tap in (1, 0):
                    for part in range(2):
                        cj = part * NCH + j
                        P.op("dve", lambda e, r=r, part=part, cj=cj, tap=tap: e.scalar_tensor_tensor(out=YGV[r][:, part, :], in0=AGV[r][:, part, tap:tap + 256], scalar=WC[:, cj, tap:tap + 1],
                                                                                                  in1=YGV[r][:, part, :], op0=ALU.mult, op1=ALU.add),
                             reads=[("AGV", r, part), ("AGVc", r), "WC", ("YGV", r, part)], writes=[("YGV", r, part)])
                wrelease()
            ffn_tail(NCH - 1)
            for i in range(11):
                wb = wacquire(S_WDN + i)
                for c2 in range(2):
                    ch = 2 * i + c2
                    for b in range(2):
                        for n2 in range(2):
                            acc = b * 2 + n2
                            P.op("pe", lambda e, ch=ch, c2=c2, b=b, n2=n2, acc=acc, wb=wb: e.matmul(ACCB[acc][:, :], lhsT=ACTT[:, ch, b * 128:(b + 1) * 128],
                                                                                                 rhs=WP[wb][:, c2 * 1024 + n2 * 512:c2 * 1024 + (n2 + 1) * 512],
                                                                                                 start=(ch == 0), stop=(ch == NCH - 1)),
                                 reads=[("WP", wb), ("ACTT", ch)], writes=[("ACC", acc)])
                wrelease()
            for b in range(2):
                for n2 in range(2):
                    acc = b * 2 + n2
                    P.op("dve", lambda e, b=b, n2=n2, acc=acc: e.tensor_tensor(out=XT[:, b, n2 * 512:(n2 + 1) * 512], in0=XT[:, b, n2 * 512:(n2 + 1) * 512], in1=ACCB[acc][:, :], op=ALU.add),
                         reads=[("ACC", acc), ("XT", b)], writes=[("XT", b)])
            for b in range(2):
                P.op("dve", lambda e, b=b: e.scalar_tensor_tensor(out=HB[:, b, :], in0=XT[:, b, :], scalar=1.0, in1=XT[:, b, :],
                                                                 op0=ALU.mult, op1=ALU.mult, accum_out=ST[:, 32 + b:33 + b]),
                     reads=[("XT", b)], writes=[("HB", b), ("ST", 32 + b)])
                P.op("act", lambda e, b=b: e.activation(out=ST[:, 34 + b:35 + b], in_=ST[:, 32 + b:33 + b], func=AF.Ln, bias=EPSC[:, 0:1], scale=1.0 / D),
                     reads=[("ST", 32 + b), "EPSC"], writes=[("ST", 34 + b)])
                P.op("act", lambda e, b=b: e.activation(out=ST[:, 36 + b:37 + b], in_=ST[:, 34 + b:35 + b], func=AF.Exp, scale=-0.5),
                     reads=[("ST", 34 + b)], writes=[("ST", 36 + b)])
                P.op("dve", lambda e, b=b: e.scalar_tensor_tensor(out=OUTF[:, b, :], in0=XT[:, b, :], scalar=ST[:, 36 + b:37 + b], in1=GFIN[:, :], op0=ALU.mult, op1=ALU.mult),
                     reads=[("XT", b), ("ST", 36 + b), "GFIN"], writes=ARENA_ALL)
            if gi >= 1:
                r0 = (gi - 1) * G
                dma(out_d[r0:r0 + G, :].rearrange("(b p) f -> p b f", p=128), OUTF, ARENA_ALL, [("out", gi)])

        dbg_g = 1
        for ci in range(NGRP):
            if stage < 3 or (ci < FIRST_FULL and ci >= n_ctx):
                continue
            if ci >= FIRST_FULL and (ci - FIRST_FULL) >= n_full:
                break
            emit_group(ci)
        fin = [("out", gi) for gi in range(1, n_full)]
        if dbg:
            fin += ["dq", "dtmp2", "dck", "dx1"]
        P.op("sp", None, reads=fin)
        P.emit()
        print("ops:", len(P.ops), "signals:", P.stats)
    return nc


def make_par(inputs, core):
    par = np.zeros((128, NPAR), np.float32)

    def put(name, arr):
        a, b = PC[name]
        par[:, a:b] = np.asarray(arr, np.float32).reshape(128, b - a)

    put("eye", np.eye(128, dtype=np.float32))
    s = np.arange(128)
    put("tri", (s[:, None] <= s[None, :]).astype(np.float32))
    sel = np.zeros((128, 128), np.float32); sel[127, :] = 1.0
    put("sel127", sel)
    sel = np.zeros((128, 128), np.float32); sel[64, :] = 1.0
    put("selA", sel)
    sel = np.zeros((128, 128), np.float32); sel[63, :] = 1.0
    put("selB", sel)
    tri_mask = np.where(s[:, None] <= s[None, :], 0.0, NEG).astype(np.float32)
    cm0 = np.concatenate([tri_mask, np.zeros((128, 128), np.float32)], axis=1)
    cm1 = np.concatenate([np.full((128, 128), NEG, np.float32), tri_mask], axis=1)
    put("cm0", cm0)
    put("cm1", cm1)
    put("fb", np.broadcast_to(inputs["f_bias"].reshape(1, 8), (128, 8)))
    put("lng", np.broadcast_to(inputs["sg_ln_g"].reshape(1, 512), (128, 512)))
    put("gfin", np.broadcast_to(inputs["norm_final_g"].reshape(1, D), (128, D)))
    put("g1T", inputs["norm_mix_g"].reshape(8, 128).T)
    put("g2T", inputs["norm_ffn_g"].reshape(8, 128).T)
    wc = inputs["w_conv"].reshape(3, 2 * NCH, 128)
    put("wc", np.transpose(wc, (2, 1, 0)))
    put("bc", inputs["b_conv"].reshape(2 * NCH, 128).T)
    put("bsT", inputs["sg_b"].reshape(8, 128).T)
    km = np.zeros((128, NB), np.float32)
    if core % 2 == 0:
        km[:, 0:32] = NEG
    put("kmask", km)
    sgw = inputs["sg_w"].reshape(8, 128, 128)
    put("sgwT", np.transpose(sgw, (2, 0, 1)))
    return par


_NC_CACHE = {}


def kernel(**inputs):
    inputs = {k: np.asarray(v) for k, v in inputs.items()}
    x = inputs["x"].astype(np.float32, copy=False)
    key = "full"
    if key not in _NC_CACHE:
        _NC_CACHE[key] = build_nc()
    nc = _NC_CACHE[key]
    in_maps = []
    for c in range(8):
        b, half = c // 2, c % 2
        if half == 0:
            xc = np.concatenate([np.zeros((4096, D), np.float32), x[b, 0:4096]], axis=0)
        else:
            xc = x[b]
        in_maps.append({
            "x": np.ascontiguousarray(xc),
            "par": make_par(inputs, c),
            "w_in": np.ascontiguousarray(inputs["w_in"][0]),
            "w_out": np.ascontiguousarray(inputs["w_out"][0]),
            "w_up": np.ascontiguousarray(inputs["w_up"][0]),
            "w_down": np.ascontiguousarray(inputs["w_down"][0]),
        })
    res = run_bass_kernel_spmd(nc, in_maps, core_ids=list(range(8)))
    out = np.empty((4, SEQ, D), np.float32)
    for c in range(8):
        b, half = c // 2, c % 2
        out[b, half * 4096:(half + 1) * 4096] = res.results[c]["out"]
    return out
```

```python
import contextlib
import numpy as np
import concourse.bass as bass
import concourse.mybir as mybir
from concourse.bass_utils import run_bass_kernel_spmd

F32 = mybir.dt.float32
BF16 = mybir.dt.bfloat16
AF = mybir.ActivationFunctionType
ALU = mybir.AluOpType
AX = mybir.AxisListType

SAME_ENGINE_SYNC = True
NDMASEM = 8

D = 1024
SEQ = 8192
NB = 64
G = 256
NGRP = 32
FIRST_FULL = 15
DFF = 2816
NCH = 22
EPS = 1e-6
NEG = -30000.0

PC = {}
_off = 0
for _n, _w in [("eye", 128), ("tri", 128), ("sel127", 128), ("selA", 128), ("selB", 128),
               ("cm0", 256), ("cm1", 256), ("fb", 8), ("lng", 512), ("gfin", 1024),
               ("g1T", 8), ("g2T", 8), ("wc", 132), ("bc", 44), ("bsT", 8), ("kmask", 64),
               ("sgwT", 1024)]:
    PC[_n] = (_off, _off + _w)
    _off += _w
NPAR = _off

S_WIN, S_WOUT, S_WUP, S_WDN = 0, 10, 14, 36
NSLOT = 47


class Op:
    __slots__ = ("eng", "fn", "deps", "idx", "signaled", "ev", "is_dma", "dq", "dslot")

    def __init__(self, eng, fn, idx, is_dma):
        self.eng = eng
        self.fn = fn
        self.idx = idx
        self.deps = []
        self.signaled = False
        self.ev = None
        self.is_dma = is_dma


class Prog:
    ENGS = ("sp", "act", "dve", "pool", "pe")

    def __init__(self, nc):
        self.nc = nc
        self.ops = []
        self.last_writer = {}
        self.readers = {}
        self.dma_count = {e: 0 for e in self.ENGS}

    def op(self, eng, fn, reads=(), writes=(), dma=False):
        o = Op(eng, fn, len(self.ops), dma)
        deps = {}
        for r in reads:
            w = self.last_writer.get(r)
            if w is not None:
                deps[w.idx] = w
        for r in writes:
            w = self.last_writer.get(r)
            if w is not None:
                deps[w.idx] = w
            for rd in self.readers.get(r, ()):
                deps[rd.idx] = rd
        o.deps = list(deps.values())
        for r in reads:
            self.readers.setdefault(r, []).append(o)
        for r in writes:
            self.last_writer[r] = o
            self.readers[r] = []
        if dma:
            o.dq = eng
            o.dslot = self.dma_count[eng]
            self.dma_count[eng] += 1
        self.ops.append(o)
        return o

    def emit(self):
        nc = self.nc
        ops = self.ops

        def skip(d, o):
            return (not d.is_dma) and (not o.is_dma) and d.eng == o.eng and (d.eng == "pe" or not SAME_ENGINE_SYNC)

        for o in ops:
            for d in o.deps:
                if d.is_dma or skip(d, o):
                    continue
                d.signaled = True
        with contextlib.ExitStack() as es:
            esem = {e: es.enter_context(nc.semaphore("c_" + e)) for e in self.ENGS}
            dsem = {}
            for e in self.ENGS:
                if self.dma_count[e] > 0:
                    dsem[e] = [es.enter_context(nc.semaphore("d_%s_%d" % (e, i))) for i in range(NDMASEM)]
            cnt = {e: 0 for e in self.ENGS}
            for o in ops:
                if o.is_dma:
                    s = dsem[o.dq][o.dslot % NDMASEM]
                    o.ev = (s, 16 * (o.dslot // NDMASEM + 1))
                elif o.signaled:
                    cnt[o.eng] += 1
                    o.ev = (esem[o.eng], cnt[o.eng])
            self.stats = dict(cnt)
            block = es.enter_context(nc.Block())

            def body(engname, eng):
                seen = {}
                for o in ops:
                    if o.eng != engname:
                        continue
                    waits = []
                    for d in o.deps:
                        if d.ev is None or skip(d, o):
                            continue
                        waits.append(d.ev)
                    if o.is_dma and o.dslot >= NDMASEM:
                        s = dsem[o.dq][o.dslot % NDMASEM]
                        waits.append((s, 16 * (o.dslot // NDMASEM)))
                    mx = {}
                    for (s, v) in waits:
                        k = id(s)
                        if k not in mx or mx[k][1] < v:
                            mx[k] = (s, v)
                    for k, (s, v) in mx.items():
                        if seen.get(k, 0) >= v:
                            continue
                        seen[k] = v
                        eng.wait_ge(s, v)
                    if o.fn is None:
                        continue
                    inst = o.fn(eng)
                    if o.is_dma:
                        inst.then_inc(o.ev[0], 16)
                    elif o.signaled:
                        inst.then_inc(o.ev[0], 1)

            @block.sync
            def _(e):
                body("sp", e)

            @block.scalar
            def _(e):
                body("act", e)

            @block.vector
            def _(e):
                body("dve", e)

            @block.gpsimd
            def _(e):
                body("pool", e)

            @block.tensor
            def _(e):
                body("pe", e)


def build_nc(n_full=17, dbg=False, stage=99, n_ctx=FIRST_FULL):
    nc = bass.Bass("TRN2", target_bir_lowering=False)
    x_d = nc.dram_tensor("x", [SEQ, D], F32, kind="ExternalInput").ap()
    par_d = nc.dram_tensor("par", [128, NPAR], F32, kind="ExternalInput").ap()
    win_d = nc.dram_tensor("w_in", [D, 2568], F32, kind="ExternalInput").ap()
    wout_d = nc.dram_tensor("w_out", [D, D], F32, kind="ExternalInput").ap()
    wup_d = nc.dram_tensor("w_up", [D, 2 * DFF], F32, kind="ExternalInput").ap()
    wdn_d = nc.dram_tensor("w_down", [DFF, D], F32, kind="ExternalInput").ap()
    out_d = nc.dram_tensor("out", [4096, D], F32, kind="ExternalOutput").ap()
    wsc = nc.dram_tensor("wsc", [NSLOT, 128, 2048], BF16, kind="Internal").ap()
    ksc = nc.dram_tensor("ksc", [4, 128, SEQ], BF16, kind="Internal").ap()
    dbg_d = {}
    if dbg:
        for nm, shp in [("d_x1", [256, D]), ("d_outT", [128, 8 * 256]), ("d_ckb", [128, 64 * 8]),
                        ("d_qT", [128, 4 * 256]), ("d_oa", [128, 2 * 512])]:
            dbg_d[nm] = nc.dram_tensor(nm, shp, F32, kind="ExternalOutput").ap()

    with contextlib.ExitStack() as es:
        def T(name, shape, dt):
            return es.enter_context(nc.sbuf_tensor(name, shape, dt))

        def PS(name, shape, dt):
            return es.enter_context(nc.psum_tensor(name, shape, dt))

        P = Prog(nc)
        KSTG = T("KSTG", [128, 4, G], BF16)
        KCH = [T("KCH%d" % i, [128, 1024], BF16) for i in range(3)]
        V = T("V", [128, NB, 4, 132], BF16)
        XTs = [T("XT%d" % i, [128, 2, D], F32) for i in range(2)]
        XT = XTs[0]
        HBs = [T("HB%d" % i, [128, 2, D], BF16) for i in range(2)]
        HB = HBs[0]
        HT = T("HT", [128, 8, G], BF16)
        FA = T("FA", [128, 2056], F32)
        QT = FA[:, 0:512].bitcast(BF16).rearrange("p (a t) -> p a t", a=4)
        NPT = 8
        PT = [FA[:, 512 + 128 * i:512 + 128 * (i + 1)].bitcast(BF16) for i in range(NPT)]
        OSB = [FA[:, 1536 + 256 * i:1536 + 256 * (i + 1)] for i in range(2)]
        AGV = [FA[:, 516 * i:516 * (i + 1)].rearrange("p (a c) -> p a c", a=2) for i in range(2)]
        YGV = [FA[:, 1032 + 512 * i:1032 + 512 * (i + 1)].rearrange("p (a c) -> p a c", a=2) for i in range(2)]
        ARENA = T("ARENA", [128, 3072], F32)
        OUTF = ARENA[:, 0:2048].rearrange("p (b f) -> p b f", b=2)
        U = ARENA[:, 0:1024].rearrange("p (b f) -> p b f", b=2)
        VA = ARENA[:, 1024:2048].rearrange("p (b f) -> p b f", b=2)
        ABF = ARENA[:].bitcast(BF16)
        VN = ABF[:, 4096:5120].rearrange("p (b f) -> p b f", b=2)
        OA = ABF[:, 5120:6144].rearrange("p (b f) -> p b f", b=2)
        ACTT = ABF[:, 0:NCH * G].rearrange("p (c t) -> p c t", c=NCH)
        OUTT = HT
        NWP = 17
        WP = [T("WP%d" % i, [128, 2048], BF16) for i in range(NWP)]
        RBT = T("RBT", [128, 512], F32)
        RB = [RBT[:, i * G:(i + 1) * G] for i in range(2)]
        SQ = RBT
        CKB = T("CKB", [128, NB, 8], F32)
        BIAS = [T("BIAS%d" % i, [128, NB], F32) for i in range(2)]
        WF = T("WF", [128, 8, 8], BF16)
        CARRY = T("CARRY", [128, NCH, 2, 2], F32)
        ident = T("ident", [128, 128], BF16)
        CMASK = T("CMASK", [128, 2, G], BF16)
        WMT = T("WMT", [128, 8, 128], BF16)
        TRI = T("TRI", [128, 128], F32)
        SEL127 = T("SEL127", [128, 128], F32)
        SELA = T("SELA", [128, 128], F32)
        SELB = T("SELB", [128, 128], F32)
        FB = T("FB", [128, 8], F32)
        LNG = T("LNG", [128, 512], F32)
        GFIN = T("GFIN", [128, D], F32)
        GT = T("GT", [128, 16], F32)
        WC = T("WC", [128, 2 * NCH, 3], F32)
        BC = T("BC", [128, 2 * NCH], F32)
        BST = T("BST", [128, 8], F32)
        KM = T("KM", [128, NB], F32)
        ST = T("ST", [128, 64], F32)
        CCAR = T("CCAR", [128, 8], F32)
        RG = T("RG", [128, 8], F32)
        CB = T("CB", [128, 8], F32)
        SP_ = T("SP_", [128, 8], F32)
        EPSC = T("EPSC", [128, 2], F32)
        PGB = [PS("PGB%d" % i, [128, 512], F32) for i in range(3)]
        ACCB = [PS("ACCB%d" % i, [128, 512], F32) for i in range(4)]
        PSTB = PS("PSTB", [128, 1024], BF16)
        print("sbuf bytes remaining:", nc.sbuf_bytes_remaining)

        rot = {"pg": 0, "pt": 0}
        att_rot = [0]
        kch_rot = [0]
        PSTF = PSTB[:, :].bitcast(F32)

        def next_pg():
            i = rot["pg"] % 3
            rot["pg"] += 1
            return i

        def next_pt():
            i = rot["pt"] % NPT
            rot["pt"] += 1
            return i

        def pc(name):
            a, b = PC[name]
            return par_d[:, a:b]

        def dma(out, in_, reads, writes):
            return P.op("sp", lambda e: e.dma_start(out=out, in_=in_), reads=reads, writes=writes, dma=True)

        XT_ALL0 = [("XT", 0, 0), ("XT", 0, 1)]
        ARENA_ALL = ["U", "VA", "VN", "OA"] + [("ACTT", c) for c in range(NCH)]

        dma(TRI[:], pc("tri"), [], ["TRI"])
        dma(SEL127[:], pc("sel127"), [], ["SEL127"])
        dma(SELA[:], pc("selA"), [], ["SELA"])
        dma(SELB[:], pc("selB"), [], ["SELB"])
        dma(FB[:], pc("fb"), [], ["FB"])
        dma(LNG[:], pc("lng"), [], ["LNG"])
        dma(GFIN[:], pc("gfin"), [], ["GFIN"])
        dma(GT[:, 0:8], pc("g1T"), [], ["GT"])
        dma(GT[:, 8:16], pc("g2T"), [], ["GT"])
        dma(WC[:].rearrange("p c j -> p (c j)"), pc("wc"), [], ["WC"])
        dma(BC[:], pc("bc"), [], ["BC"])
        dma(BST[:], pc("bsT"), [], ["BST"])
        dma(KM[:], pc("kmask"), [], ["KM"])
        XTf = XT[:].rearrange("p b f -> p (b f)")
        HBf = HB[:].rearrange("p b f -> p (b f)")
        HTf = HT[:].rearrange("p k t -> p (k t)")
        a0 = PC["eye"][0]
        dma(XTf[:, 0:128], pc("eye"), [], XT_ALL0)
        dma(XTf[:, 128:640], par_d[:, PC["cm0"][0]:PC["cm1"][1]], [], XT_ALL0)
        dma(XTf[:, 640:1664], pc("sgwT"), [], XT_ALL0)
        P.op("dve", lambda e: e.tensor_copy(out=ident[:], in_=XTf[:, 0:128]), reads=XT_ALL0, writes=["ident"])
        P.op("dve", lambda e: e.tensor_copy(out=CMASK[:].rearrange("p a t -> p (a t)"), in_=XTf[:, 128:640]), reads=XT_ALL0, writes=["CMASK"])
        P.op("dve", lambda e: e.memset(XTf[64:128, 640:1664].rearrange("p (h t) -> p h t", h=8)[:, :, 0:64], 0.0), reads=[], writes=XT_ALL0)
        P.op("dve", lambda e: e.tensor_copy(out=WMT[:].rearrange("p h t -> p (h t)"), in_=XTf[:, 640:1664]), reads=XT_ALL0, writes=["WMT"])
        P.op("dve", lambda e: e.memset(EPSC[:, 0:1], EPS), writes=["EPSC"])
        P.op("dve", lambda e: e.memset(EPSC[:, 1:2], 1.0), writes=["EPSC"])
        P.op("dve", lambda e: e.memset(CCAR[:], 0.0), writes=["CCAR"])
        P.op("dve", lambda e: e.memset(CARRY[:].rearrange("p c a j -> p (c a j)"), 0.0), writes=["CARRY"])
        import os
        if os.environ.get("NOVMEM") != "1":
            P.op("dve", lambda e: e.memset(V[:].rearrange("p b a c -> p (b a) c")[:, :, 64:66], 1.0), writes=[("V", b, s) for b in range(NB) for s in range(2)])

        NCONV = 99 if stage >= 2 else 0
        HB_ALL = [("HB", 0, 0), ("HB", 0, 1)]
        HT_ALL = [("HT", 0), ("HT", 1)]
        stage_f = [(XTf, XT_ALL0), (ARENA[:, 0:2048], ARENA_ALL)]
        stage_b = [(HBf, HB_ALL), (HTf, HT_ALL)]
        conv_engs = ["dve", "act"]
        cnt = [0]

        def convert(slot, pieces, gain_col, bg=False):
            if cnt[0] >= NCONV:
                return
            i = cnt[0] % 2
            cnt[0] += 1
            if bg:
                sf, sfn = ARENA[:, 0:2048], ["cvF"]
                sb, sbn = ABF[:, 4096:6144], ["cvB"]
            else:
                sf, sfn = stage_f[i]
                sb, sbn = stage_b[i]
            wr = list(sfn)
            for (dst, src) in pieces:
                dma(dst(sf), src, [], wr)
            eng = conv_engs[i]
            if gain_col is None:
                if eng == "dve":
                    P.op(eng, lambda e: e.tensor_copy(out=sb[:, :], in_=sf[:, :]), reads=wr, writes=list(sbn))
                else:
                    P.op(eng, lambda e: e.copy(out=sb[:, :], in_=sf[:, :]), reads=wr, writes=list(sbn))
            else:
                for k in range(8):
                    if eng == "dve":
                        P.op(eng, lambda e, k=k: e.tensor_scalar(out=sb[:, k * 256:(k + 1) * 256], in0=sf[:, k * 256:(k + 1) * 256],
                                                               scalar1=GT[:, gain_col + k:gain_col + k + 1], scalar2=None, op0=ALU.mult),
                             reads=wr + ["GT"], writes=list(sbn))
                    else:
                        P.op(eng, lambda e, k=k: e.mul(out=sb[:, k * 256:(k + 1) * 256], in_=sf[:, k * 256:(k + 1) * 256],
                                                     mul=GT[:, gain_col + k:gain_col + k + 1]),
                             reads=wr + ["GT"], writes=list(sbn))
            out_fn = lambda: dma(wsc[slot], sb[:, :], list(sbn), [("wsc", slot)])
            if bg:
                return out_fn
            out_fn()
            return None

        def v3(sf, a, b):
            return sf[:, 0:2048].rearrange("p (a b) -> p a b", a=a)

        conv_jobs = {}
        for s in range(10):
            conv_jobs[S_WIN + s] = ([(lambda sf: v3(sf, 8, 256), win_d[:, s * 256:(s + 1) * 256].rearrange("(k p) c -> p k c", p=128))], 0)
        for s in range(4):
            conv_jobs[S_WOUT + s] = ([(lambda sf: v3(sf, 8, 256), wout_d[:, s * 256:(s + 1) * 256].rearrange("(k p) c -> p k c", p=128))], None)
        for j in range(NCH):
            conv_jobs[S_WUP + j] = ([(lambda sf: v3(sf, 8, 256)[:, :, 0:128], wup_d[:, j * 128:(j + 1) * 128].rearrange("(k p) c -> p k c", p=128)),
                                     (lambda sf: v3(sf, 8, 256)[:, :, 128:256], wup_d[:, DFF + j * 128:DFF + (j + 1) * 128].rearrange("(k p) c -> p k c", p=128))], 8)
        for i in range(11):
            conv_jobs[S_WDN + i] = ([(lambda sf: v3(sf, 2, 1024), wdn_d[i * 256:(i + 1) * 256, :].rearrange("(c p) n -> p c n", p=128))], None)
        for slot in (6, 7, 8, 9):
            convert(slot, *conv_jobs[slot])
        dma(XTf[:, 0:64].rearrange("p (k c) -> p k c", k=8), win_d[:, 2560:2568].rearrange("(k p) c -> p k c", p=128), [], XT_ALL0)
        for k in range(8):
            P.op("dve", lambda e, k=k: e.tensor_scalar(out=WF[:, k, :], in0=XTf[:, k * 8:(k + 1) * 8], scalar1=GT[:, k:k + 1], scalar2=None, op0=ALU.mult),
                 reads=XT_ALL0 + ["GT"], writes=["WF"])
        P.op("dve", lambda e: e.memset(ST[:, 60:61], 0.0), reads=[], writes=ARENA_ALL + ["cvF", "cvB"])
        bg_slots = [sl for sl in list(range(0, 6)) + list(range(10, NSLOT))]
        bg_state = {"i": 0, "pending_out": None}

        def conv_step():
            if bg_state["pending_out"] is not None:
                bg_state["pending_out"]()
                bg_state["pending_out"] = None
            if stage < 2 or bg_state["i"] >= len(bg_slots):
                return
            slot = bg_slots[bg_state["i"]]
            bg_state["i"] += 1
            bg_state["pending_out"] = convert(slot, *conv_jobs[slot], bg=True)

        def conv_flush():
            while bg_state["i"] < len(bg_slots) or bg_state["pending_out"] is not None:
                conv_step()
            P.op("dve", lambda e: e.memset(ST[:, 61:62], 0.0), reads=[], writes=ARENA_ALL + ["cvF", "cvB"])

        wseq = []
        for ci in range(NGRP):
            full = ci >= FIRST_FULL
            if stage < 3 or (not full and ci >= n_ctx):
                continue
            if full and (ci - FIRST_FULL) >= n_full:
                break
            if full:
                wseq += list(range(0, 10)) + list(range(10, 14)) + list(range(14, 36)) + list(range(36, 47))
            else:
                wseq += [6, 7, 8, 9]
        wstate = {"use": 0, "load": 0}

        def wload_next():
            n = wstate["load"]
            if n >= len(wseq):
                return
            slot = wseq[n]
            buf = n % NWP
            wstate["load"] += 1
            dma(WP[buf][:, :], wsc[slot], [("wsc", slot)], [("WP", buf)])

        def wacquire(slot):
            n = wstate["use"]
            assert wseq[n] == slot, (n, wseq[n], slot)
            return n % NWP

        def wrelease():
            wstate["use"] += 1
            wload_next()

        for _ in range(NWP):
            wload_next()

        evac_rr = [0]

        def rms_front(XT, HB, par, sc):
            for b in range(2):
                P.op("dve", lambda e, b=b: e.scalar_tensor_tensor(out=HB[:, b, :], in0=XT[:, b, :], scalar=1.0, in1=XT[:, b, :],
                                                                 op0=ALU.mult, op1=ALU.mult, accum_out=ST[:, sc + b:sc + b + 1]),
                     reads=[("XT", par, b)], writes=[("HB", par, b), ("ST", sc + b)])
                P.op("act", lambda e, b=b: e.activation(out=ST[:, sc + 2 + b:sc + 3 + b], in_=ST[:, sc + b:sc + b + 1], func=AF.Ln, bias=EPSC[:, 0:1], scale=1.0 / D),
                     reads=[("ST", sc + b), "EPSC"], writes=[("ST", sc + 2 + b)])
                P.op("act", lambda e, b=b: e.activation(out=ST[:, sc + 4 + b:sc + 5 + b], in_=ST[:, sc + 2 + b:sc + 3 + b], func=AF.Exp, scale=-0.5),
                     reads=[("ST", sc + 2 + b)], writes=[("ST", sc + 4 + b)])
                P.op("dve", lambda e, b=b: e.tensor_scalar(out=HB[:, b, :], in0=XT[:, b, :], scalar1=ST[:, sc + 4 + b:sc + 5 + b], scalar2=None, op0=ALU.mult),
                     reads=[("XT", par, b), ("ST", sc + 4 + b)], writes=[("HB", par, b)])

        def rms_back(HB, par):
            for b in range(2):
                for k in range(8):
                    P.op("pe", lambda e, b=b, k=k: e.transpose(out=PSTB[:, k * 128:(k + 1) * 128], in_=HB[:, b, k * 128:(k + 1) * 128], identity=ident[:]),
                         reads=[("HB", par, b), "ident"], writes=["PST"])
                P.op("act", lambda e, b=b: e.copy(out=HT[:, :, b * 128:(b + 1) * 128], in_=PSTB[:, :].rearrange("p (k t) -> p k t", k=8)),
                     reads=["PST"], writes=[("HT", b)] + [("OUTT", kq) for kq in range(8)])

        front_done = set()

        def emit_front(ci):
            front_done.add(ci)
            par = ci % 2
            T0 = ci * G
            dma(XTs[par][:], x_d[T0:T0 + G, :].rearrange("(b p) f -> p b f", p=128), [], [("XT", par, 0), ("XT", par, 1)])
            rms_front(XTs[par], HBs[par], par, 40)

        def HT_reads(b=None):
            if b is None:
                return [("HT", 0), ("HT", 1)]
            return [("HT", b)]

        def mm_feat(pi, hf, wbuf, c0):
            for k in range(8):
                P.op("pe", lambda e, k=k: e.matmul(PGB[pi][:, hf * G:(hf + 1) * G], lhsT=WP[wbuf][:, k * 256 + c0:k * 256 + c0 + 128], rhs=HT[:, k, :],
                                                  start=(k == 0), stop=(k == 7)),
                     reads=[("WP", wbuf)] + HT_reads(), writes=[("PG", pi)])

        def mm_tok(pi, hf, wbuf, b, lhs, lhs_reads):
            for k in range(8):
                P.op("pe", lambda e, k=k: e.matmul(PGB[pi][:, hf * G:(hf + 1) * G], lhsT=lhs[:, k, b * 128:(b + 1) * 128], rhs=WP[wbuf][:, k * 256:(k + 1) * 256],
                                                  start=(k == 0), stop=(k == 7)),
                     reads=[("WP", wbuf)] + lhs_reads, writes=[("PG", pi)])

        def emit_group(ci):
            full = ci >= FIRST_FULL
            gi = ci - FIRST_FULL
            b0 = 2 * ci
            T0 = ci * G
            par = ci % 2
            XT = XTs[par]
            HB = HBs[par]
            if gi == 0:
                conv_flush()
            if not full:
                conv_step()
            rms_back(HB, par)
            nxt = ci + 1
            has_next = (nxt < NGRP) and not (nxt >= FIRST_FULL and (nxt - FIRST_FULL) >= n_full) and not (stage < 3) and not (nxt < FIRST_FULL and nxt >= n_ctx)
            if not full and has_next:
                emit_front(nxt)
            if full:
                for s in range(4):
                    wb = wacquire(s)
                    dst, dn = (U, "U") if s < 2 else (VA, "VA")
                    hf = s % 2
                    pi = next_pg()
                    for b in range(2):
                        mm_tok(pi, b, wb, b, HT, HT_reads(b))
                    P.op("act", lambda e, pi=pi, dst=dst, hf=hf: e.activation(out=dst[:, :, hf * 256:(hf + 1) * 256], in_=PGB[pi][:, :].rearrange("p (b c) -> p b c", b=2), func=AF.Gelu),
                         reads=[("PG", pi)], writes=[dn] + [("ACTT", c) for c in range(NCH)])
                    wrelease()
                for s in range(2):
                    wb = wacquire(4 + s)
                    pi = next_pg()
                    for m in range(2):
                        mm_feat(pi, m, wb, m * 128)
                    P.op("act", lambda e, pi=pi, s=s: e.mul(out=QT[:, 2 * s:2 * s + 2, :].rearrange("p a t -> p (a t)"), in_=PGB[pi][:, :], mul=0.125),
                         reads=[("PG", pi)], writes=[("QT", 2 * s), ("QT", 2 * s + 1)])
                    wrelease()
            for s in range(2):
                wb = wacquire(6 + s)
                pi = next_pg()
                for m in range(2):
                    mm_feat(pi, m, wb, m * 128)
                for m in range(2):
                    hp = 2 * s + m
                    P.op("act", lambda e, pi=pi, hp=hp, m=m: e.copy(out=KSTG[:, hp, :], in_=PGB[pi][:, m * G:(m + 1) * G]),
                         reads=[("PG", pi)], writes=[("KSTG", hp)])
                wrelease()
            dma(ksc[:, :, T0:T0 + G].rearrange("h p t -> p h t"), KSTG[:, :, :], [("KSTG", hp) for hp in range(4)],
                [("KD", hp, bb) for hp in range(4) for bb in (b0, b0 + 1)])
            if not full:
                conv_step()
            for s in range(2):
                wb = wacquire(8 + s)
                pi = next_pg()
                for b in range(2):
                    mm_tok(pi, b, wb, b, HT, HT_reads(b))
                for b in range(2):
                    for a in range(2):
                        P.op("dve", lambda e, b=b, pi=pi, s=s, a=a: e.tensor_copy(
                            out=V[:, b0 + b, 2 * s + a, :].rearrange("p (e c) -> p e c", e=2)[:, :, 0:64],
                            in_=PGB[pi][:, b * G + a * 128:b * G + (a + 1) * 128].rearrange("p (e c) -> p e c", e=2)),
                             reads=[("PG", pi)], writes=[("V", b0 + b, s)])
                wrelease()
            if not full:
                conv_step()
            for b in range(2):
                pi = next_pg()
                for k in range(8):
                    P.op("pe", lambda e, k=k, b=b, pi=pi: e.matmul(PGB[pi][:, 0:8], lhsT=HT[:, k, b * 128:(b + 1) * 128], rhs=WF[:, k, :], start=(k == 0), stop=(k == 7)),
                         reads=["WF"] + HT_reads(b), writes=[("PG", pi)])
                P.op("dve", lambda e, pi=pi: e.tensor_tensor(out=SP_[:, :], in0=PGB[pi][:, 0:8], in1=FB[:, :], op=ALU.add),
                     reads=[("PG", pi), "FB"], writes=["SP_"])
                P.op("act", lambda e: e.activation(out=SP_[:, :], in_=SP_[:, :], func=AF.Exp, scale=-1.0), reads=["SP_"], writes=["SP_"])
                P.op("act", lambda e: e.activation(out=SP_[:, :], in_=SP_[:, :], func=AF.Ln, bias=EPSC[:, 1:2], scale=1.0), reads=["SP_", "EPSC"], writes=["SP_"])
                pi2 = next_pg()
                P.op("pe", lambda e, pi2=pi2: e.matmul(PGB[pi2][:, 0:8], lhsT=TRI[:, :], rhs=SP_[:, :], start=True, stop=True),
                     reads=["TRI", "SP_"], writes=[("PG", pi2)])
                P.op("dve", lambda e, pi2=pi2: e.tensor_tensor(out=CB[:, :], in0=PGB[pi2][:, 0:8], in1=CCAR[:, :], op=ALU.add),
                     reads=[("PG", pi2), "CCAR"], writes=["CB"])
                P.op("dve", lambda e, b=b: e.tensor_scalar(out=CKB[:, b0 + b, :], in0=CB[:, :], scalar1=KM[:, b0 + b:b0 + b + 1], scalar2=None, op0=ALU.add),
                     reads=["CB", "KM"], writes=[("CKB", b0 + b)])
                pi3 = next_pg()
                P.op("pe", lambda e, pi3=pi3: e.matmul(PGB[pi3][:, 0:8], lhsT=SEL127[:, :], rhs=CB[:, :], start=True, stop=True),
                     reads=["SEL127", "CB"], writes=[("PG", pi3)])
                P.op("dve", lambda e, pi3=pi3: e.tensor_copy(out=CCAR[:, :], in_=PGB[pi3][:, 0:8]), reads=[("PG", pi3)], writes=["CCAR"])
                if b == 0 and full:
                    P.op("dve", lambda e, pi3=pi3: e.tensor_copy(out=RG[:, :], in_=PGB[pi3][:, 0:8]), reads=[("PG", pi3)], writes=["RG"])
            if not full:
                return
            if dbg and gi == dbg_g:
                dma(dbg_d["d_qT"], QT[:].rearrange("p a t -> p (a t)"), [("QT", h) for h in range(4)], ["dq"])
            def emit_sg():
                SQR = [("RB", 0), ("RB", 1)]
                for b in range(2):
                    va3 = VA[:, b, :].rearrange("p (h d) -> p h d", h=8)
                    P.op("dve", lambda e, va3=va3: e.tensor_reduce(out=ST[:, 8:16], in_=va3, axis=AX.X, op=ALU.add), reads=["VA"], writes=[("ST", "s1")])
                    P.op("dve", lambda e, b=b: e.tensor_tensor(out=SQ[:, :], in0=VA[:, b, :], in1=VA[:, b, :], op=ALU.mult), reads=["VA"], writes=SQR)
                    P.op("dve", lambda e: e.tensor_reduce(out=ST[:, 16:24], in_=SQ[:, :].rearrange("p (h d) -> p h d", h=8), axis=AX.X, op=ALU.add),
                         reads=SQR, writes=[("ST", "s2")])
                    P.op("dve", lambda e: e.tensor_scalar(out=ST[:, 8:16], in0=ST[:, 8:16], scalar1=1.0 / 64, scalar2=None, op0=ALU.mult),
                         reads=[("ST", "s1")], writes=[("ST", "s1")])
                    P.op("dve", lambda e: e.tensor_tensor(out=ST[:, 24:32], in0=ST[:, 8:16], in1=ST[:, 8:16], op=ALU.mult),
                         reads=[("ST", "s1")], writes=[("ST", "msq")])
                    P.op("dve", lambda e: e.scalar_tensor_tensor(out=ST[:, 16:24], in0=ST[:, 16:24], scalar=1.0 / 64, in1=ST[:, 24:32], op0=ALU.mult, op1=ALU.subtract),
                         reads=[("ST", "s2"), ("ST", "msq")], writes=[("ST", "s2")])
                    P.op("act", lambda e: e.activation(out=ST[:, 16:24], in_=ST[:, 16:24], func=AF.Ln, bias=EPSC[:, 0:1], scale=1.0),
                         reads=[("ST", "s2"), "EPSC"], writes=[("ST", "s2")])
                    P.op("act", lambda e: e.activation(out=ST[:, 16:24], in_=ST[:, 16:24], func=AF.Exp, scale=-0.5),
                         reads=[("ST", "s2")], writes=[("ST", "s2")])
                    P.op("dve", lambda e, va3=va3: e.tensor_tensor(out=va3, in0=va3, in1=ST[:, 8:16].unsqueeze(2).to_broadcast([128, 8, 64]), op=ALU.subtract),
                         reads=["VA", ("ST", "s1")], writes=["VA"])
                    P.op("dve", lambda e, va3=va3: e.tensor_tensor(out=va3, in0=va3, in1=ST[:, 16:24].unsqueeze(2).to_broadcast([128, 8, 64]), op=ALU.mult),
                         reads=["VA", ("ST", "s2")], writes=["VA"])
                    P.op("dve", lambda e, b=b: e.tensor_tensor(out=VN[:, b, :], in0=VA[:, b, :], in1=LNG[:, :], op=ALU.mult),
                         reads=["VA", "LNG"], writes=["VN"])
                    pi = next_pg()
                    for h in range(8):
                        P.op("pe", lambda e, h=h, b=b, pi=pi: e.matmul(PGB[pi][:, h * 64:(h + 1) * 64], lhsT=WMT[:, h, :], rhs=VN[:, b, h * 64:(h + 1) * 64],
                                                                      start=True, stop=True),
                             reads=["WMT", "VN"], writes=[("PG", pi)])
                    P.op("dve", lambda e, pi=pi: e.tensor_tensor(out=SQ[:, :].rearrange("p (h d) -> p h d", h=8),
                                                                 in0=PGB[pi][:, :].rearrange("p (h d) -> p h d", h=8),
                                                                 in1=BST[:, :].unsqueeze(2).to_broadcast([128, 8, 64]), op=ALU.add),
                         reads=[("PG", pi), "BST"], writes=SQR)
                    P.op("dve", lambda e, b=b: e.tensor_tensor(out=OA[:, b, :], in0=SQ[:, :], in1=U[:, b, :], op=ALU.mult),
                         reads=SQR + ["U"], writes=["OA"])
                    for kk in range(4):
                        P.op("pe", lambda e, b=b, kk=kk: e.transpose(out=PSTB[:, kk * 128:(kk + 1) * 128], in_=OA[:, b, kk * 128:(kk + 1) * 128], identity=ident[:]),
                             reads=["OA", "ident"], writes=["PST"])
                    P.op("act", lambda e, b=b: e.copy(out=OUTT[:, 0:4, b * 128:(b + 1) * 128], in_=PSTB[:, 0:512].rearrange("p (k t) -> p k t", k=4)),
                         reads=["PST"], writes=[("OUTT", k) for k in range(4)] + [("HT", 0), ("HT", 1)])

            nj = b0 + 2
            nchunk = (nj + 7) // 8

            def kload(hp, c):
                nblk = min(8, nj - 8 * c)
                kb = kch_rot[0] % 3
                kch_rot[0] += 1
                dma(KCH[kb][:, 0:nblk * 128], ksc[hp][:, c * 1024:c * 1024 + nblk * 128],
                    [("KD", hp, 8 * c + q) for q in range(nblk)], [("KCH", kb)])
                return kb

            for hp in range(4):
                if hp == 1:
                    emit_sg()
                accs = [(2 * hp) % 4, (2 * hp + 1) % 4]
                kbufs = {0: kload(hp, 0)}
                if nchunk > 1:
                    kbufs[1] = kload(hp, 1)
                for ee in range(2):
                    h = 2 * hp + ee
                    P.op("dve", lambda e, h=h, ee=ee: e.tensor_scalar(out=BIAS[ee][:, 0:nj], in0=CKB[:, 0:nj, h], scalar1=RG[:, h:h + 1], scalar2=None, op0=ALU.subtract),
                         reads=[("CKB", j) for j in range(nj)] + ["RG"], writes=[("BIAS", ee)])

                def qk_step(j, hp=hp):
                    bp = att_rot[0] % 2
                    att_rot[0] += 1
                    banks = [(PGB[0], ("PG", 0)), (PGB[1], ("PG", 1))] if bp == 0 else [(PGB[2], ("PG", 2)), (PSTF, "PST")]
                    for q in range(2):
                        jj = j + q
                        diag = jj >= b0
                        for ee in range(2):
                            kr = slice(ee * 64, (ee + 1) * 64)
                            bk, bkey = banks[ee]
                            kb_ = kbufs[jj // 8]
                            jo = jj % 8
                            P.op("pe", lambda e, ee=ee, kr=kr, bk=bk, jj=jj, q=q, diag=diag, kb_=kb_, jo=jo: e.matmul(bk[:, q * G:(q + 1) * G], lhsT=KCH[kb_][kr, jo * 128:(jo + 1) * 128], rhs=QT[kr, hp, :],
                                                                                                  start=True, stop=not diag),
                                 reads=[("KCH", kb_), ("QT", hp)], writes=[bkey])
                        if diag:
                            for ee in range(2):
                                bk, bkey = banks[ee]
                                P.op("pe", lambda e, bk=bk, jj=jj, q=q: e.matmul(bk[:, q * G:(q + 1) * G], lhsT=ident[:, :], rhs=CMASK[:, jj - b0, :], start=False, stop=True),
                                     reads=["ident", "CMASK"], writes=[bkey])
                    return banks

                def exp_step(j, banks):
                    tis = []
                    for q in range(2):
                        jj = j + q
                        for ee in range(2):
                            bk, bkey = banks[ee]
                            ti = next_pt()
                            P.op("act", lambda e, ee=ee, ti=ti, bk=bk, jj=jj, q=q: e.activation(out=PT[ti][:, :], in_=bk[:, q * G:(q + 1) * G], func=AF.Exp, bias=BIAS[ee][:, jj:jj + 1], scale=1.0),
                                 reads=[bkey, ("BIAS", ee)], writes=[("PT", ti)])
                            tis.append(ti)
                    return tis

                def pv_step(j, tis, hp=hp, accs=accs):
                    for q in range(2):
                        jj = j + q
                        vflat = V[:, jj, hp, :]
                        t0_, t1_ = tis[2 * q], tis[2 * q + 1]
                        P.op("pe", lambda e, jj=jj, vflat=vflat, t0_=t0_: e.matmul(ACCB[accs[0]][0:65, 0:G], lhsT=vflat[:, 0:65], rhs=PT[t0_][:, :], start=(jj == 0), stop=(jj == nj - 1)),
                             reads=[("V", jj, hp // 2), ("PT", t0_)], writes=[("ACC", accs[0])])
                        P.op("pe", lambda e, jj=jj, vflat=vflat, t1_=t1_: e.matmul(ACCB[accs[1]][:, 0:G], lhsT=vflat[:, 2:130], rhs=PT[t1_][:, :], start=(jj == 0), stop=(jj == nj - 1)),
                             reads=[("V", jj, hp // 2), ("PT", t1_)], writes=[("ACC", accs[1])])

                pend = None
                for j in range(0, nj, 2):
                    if j % 8 == 0 and j > 0 and (j // 8 + 1) < nchunk:
                        kbufs[j // 8 + 1] = kload(hp, j // 8 + 1)
                    banks = qk_step(j)
                    tis = exp_step(j, banks)
                    if pend is not None:
                        pv_step(*pend)
                    pend = (j, tis)
                pv_step(*pend)
                for ee in range(2):
                    h = 2 * hp + ee
                    acc = accs[ee]
                    kr = slice(ee * 64, (ee + 1) * 64)
                    oi = ee
                    K = 65 if ee == 0 else 128
                    sel = SELA if ee == 0 else SELB
                    seln = "SELA" if ee == 0 else "SELB"
                    P.op("act", lambda e, oi=oi, K=K, acc=acc: e.copy(out=OSB[oi][0:K, :], in_=ACCB[acc][0:K, 0:G]), reads=[("ACC", acc)], writes=[("OSB", oi)])
                    pi = next_pg()
                    P.op("pe", lambda e, oi=oi, K=K, sel=sel, pi=pi: e.matmul(PGB[pi][:, 0:G], lhsT=sel[0:K, :], rhs=OSB[oi][0:K, :], start=True, stop=True),
                         reads=[seln, ("OSB", oi)], writes=[("PG", pi)])
                    P.op("dve", lambda e, oi=oi, pi=pi: e.tensor_scalar(out=RB[oi][:, :], in0=PGB[pi][:, 0:G], scalar1=1e-30, scalar2=None, op0=ALU.max),
                         reads=[("PG", pi)], writes=[("RB", oi)])
                    P.op("dve", lambda e, oi=oi: e.reciprocal(out=RB[oi][:, :], in_=RB[oi][:, :]), reads=[("RB", oi)], writes=[("RB", oi)])
                    P.op("dve", lambda e, oi=oi, kr=kr, hp=hp: e.tensor_tensor(out=OUTT[kr, 4 + hp, :], in0=OSB[oi][kr, :], in1=RB[oi][kr, :], op=ALU.mult),
                         reads=[("OSB", oi), ("RB", oi)], writes=[("OUTT", 4 + hp), ("HT", 0), ("HT", 1)])
            if dbg and gi == dbg_g:
                for k in range(8):
                    P.op("dve", lambda e, k=k: e.tensor_copy(out=YGV[0][:, 0, :], in_=OUTT[:, k, :]), reads=[("OUTT", k)], writes=["dtmp"])
                    dma(dbg_d["d_outT"][:, k * 256:(k + 1) * 256], YGV[0][:, 0, :], ["dtmp"], ["dtmp2"])
                dma(dbg_d["d_ckb"], CKB[:].rearrange("p b h -> p (b h)"), [("CKB", j) for j in range(NB)], ["dck"])
            if has_next:
                emit_front(nxt)
            for n in range(4):
                wb = wacquire(S_WOUT + n)
                pi = next_pg()
                for b in range(2):
                    mm_tok(pi, b, wb, b, OUTT, [("OUTT", k) for k in range(8)])
                P.op("dve", lambda e, n=n, pi=pi: e.tensor_tensor(out=XT[:, :, n * 256:(n + 1) * 256], in0=XT[:, :, n * 256:(n + 1) * 256],
                                                                  in1=PGB[pi][:, :].rearrange("p (b c) -> p b c", b=2), op=ALU.add),
                     reads=[("PG", pi), ("XT", par, 0), ("XT", par, 1)], writes=[("XT", par, 0), ("XT", par, 1)])
                wrelease()
            if dbg and gi == dbg_g:
                dma(dbg_d["d_x1"].rearrange("(b p) f -> p b f", p=128), XT[:], [("XT", par, 0), ("XT", par, 1)], ["dx1"])
            rms_front(XT, HB, par, 0)
            rms_back(HB, par)
            def ffn_tail(j):
                r = j % 2
                P.op("act", lambda e, r=r: e.activation(out=YGV[r][:, 0, :], in_=YGV[r][:, 0, :], func=AF.Silu), reads=[("YGV", r, 0)], writes=[("YGV", r, 0)])
                P.op("pool", lambda e, r=r, j=j: e.tensor_tensor(out=ACTT[:, j, :], in0=YGV[r][:, 0, :], in1=YGV[r][:, 1, :], op=ALU.mult),
                     reads=[("YGV", r, 0), ("YGV", r, 1)], writes=[("ACTT", j), "U", "VA", "VN", "OA"])

            for j in range(NCH):
                wb = wacquire(S_WUP + j)
                r = j % 2
                pi = next_pg()
                for part in range(2):
                    mm_feat(pi, part, wb, part * 128)
                P.op("dve", lambda e, r=r, j=j: e.tensor_copy(out=AGV[r][:, :, 0:2], in_=CARRY[:, j, :, :]),
                     reads=[("CARRY", j)], writes=[("AGVc", r)])
                P.op("act", lambda e, r=r, pi=pi: e.copy(out=AGV[r][:, :, 2:258], in_=PGB[pi][:, :].rearrange("p (a t) -> p a t", a=2)),
                     reads=[("PG", pi)], writes=[("AGV", r, 0), ("AGV", r, 1)])
                for part in range(2):
                    cj = part * NCH + j
                    P.op("act", lambda e, r=r, part=part, cj=cj, pi=pi: e.activation(out=YGV[r][:, part, :], in_=PGB[pi][:, part * G:(part + 1) * G], func=AF.Identity,
                                                                                 bias=BC[:, cj:cj + 1], scale=WC[:, cj, 2:3]),
                         reads=[("PG", pi), "WC", "BC"], writes=[("YGV", r, part)])
                if j >= 1:
                    ffn_tail(j - 1)
                P.op("dve", lambda e, r=r, j=j: e.tensor_copy(out=CARRY[:, j, :, :], in_=AGV[r][:, :, 256:258]),
                     reads=[("AGV", r, 0), ("AGV", r, 1)], writes=[("CARRY", j)])
                for tap in (1, 0):
                    for part in range(2):
                        cj = part * NCH + j
                        P.op("dve", lambda e, r=r, part=part, cj=cj, tap=tap: e.scalar_tensor_tensor(out=YGV[r][:, part, :], in0=AGV[r][:, part, tap:tap + 256], scalar=WC[:, cj, tap:tap + 1],
                                                                                                  in1=YGV[r][:, part, :], op0=ALU.mult, op1=ALU.add),
                             reads=[("AGV", r, part), ("AGVc", r), "WC", ("YGV", r, part)], writes=[("YGV", r, part)])
                wrelease()
            ffn_tail(NCH - 1)
            for i in range(11):
                wb = wacquire(S_WDN + i)
                for c2 in range(2):
                    ch = 2 * i + c2
                    for b in range(2):
                        for n2 in range(2):
                            acc = b * 2 + n2
                            P.op("pe", lambda e, ch=ch, c2=c2, b=b, n2=n2, acc=acc, wb=wb: e.matmul(ACCB[acc][:, :], lhsT=ACTT[:, ch, b * 128:(b + 1) * 128],
                                                                                                 rhs=WP[wb][:, c2 * 1024 + n2 * 512:c2 * 1024 + (n2 + 1) * 512],
                                                                                                 start=(ch == 0), stop=(ch == NCH - 1)),
                                 reads=[("WP", wb), ("ACTT", ch)], writes=[("ACC", acc)])
                wrelease()
            for b in range(2):
                for n2 in range(2):
                    acc = b * 2 + n2
                    P.op("dve", lambda e, b=b, n2=n2, acc=acc: e.tensor_tensor(out=XT[:, b, n2 * 512:(n2 + 1) * 512], in0=XT[:, b, n2 * 512:(n2 + 1) * 512], in1=ACCB[acc][:, :], op=ALU.add),
                         reads=[("ACC", acc), ("XT", par, b)], writes=[("XT", par, b)])
            for b in range(2):
                P.op("dve", lambda e, b=b: e.scalar_tensor_tensor(out=HB[:, b, :], in0=XT[:, b, :], scalar=1.0, in1=XT[:, b, :],
                                                                 op0=ALU.mult, op1=ALU.mult, accum_out=ST[:, 32 + b:33 + b]),
                     reads=[("XT", par, b)], writes=[("HB", par, b), ("ST", 32 + b)])
                P.op("act", lambda e, b=b: e.activation(out=ST[:, 34 + b:35 + b], in_=ST[:, 32 + b:33 + b], func=AF.Ln, bias=EPSC[:, 0:1], scale=1.0 / D),
                     reads=[("ST", 32 + b), "EPSC"], writes=[("ST", 34 + b)])
                P.op("act", lambda e, b=b: e.activation(out=ST[:, 36 + b:37 + b], in_=ST[:, 34 + b:35 + b], func=AF.Exp, scale=-0.5),
                     reads=[("ST", 34 + b)], writes=[("ST", 36 + b)])
                P.op("dve", lambda e, b=b: e.scalar_tensor_tensor(out=OUTF[:, b, :], in0=XT[:, b, :], scalar=ST[:, 36 + b:37 + b], in1=GFIN[:, :], op0=ALU.mult, op1=ALU.mult),
                     reads=[("XT", par, b), ("ST", 36 + b), "GFIN"], writes=ARENA_ALL)
            if gi >= 1:
                r0 = (gi - 1) * G
                dma(out_d[r0:r0 + G, :].rearrange("(b p) f -> p b f", p=128), OUTF, ARENA_ALL, [("out", gi)])

        dbg_g = 1
        first = True
        for ci in range(NGRP):
            if stage < 3 or (ci < FIRST_FULL and ci >= n_ctx):
                continue
            if ci >= FIRST_FULL and (ci - FIRST_FULL) >= n_full:
                break
            if ci not in front_done:
                emit_front(ci)
            emit_group(ci)
        fin = [("out", gi) for gi in range(1, n_full)]
        if dbg:
            fin += ["dq", "dtmp2", "dck", "dx1"]
        P.op("sp", None, reads=fin)
        P.emit()
        print("ops:", len(P.ops), "signals:", P.stats)
    return nc


def make_par(inputs, core):
    par = np.zeros((128, NPAR), np.float32)

    def put(name, arr):
        a, b = PC[name]
        par[:, a:b] = np.asarray(arr, np.float32).reshape(128, b - a)

    put("eye", np.eye(128, dtype=np.float32))
    s = np.arange(128)
    put("tri", (s[:, None] <= s[None, :]).astype(np.float32))
    sel = np.zeros((128, 128), np.float32); sel[127, :] = 1.0
    put("sel127", sel)
    sel = np.zeros((128, 128), np.float32); sel[64, :] = 1.0
    put("selA", sel)
    sel = np.zeros((128, 128), np.float32); sel[63, :] = 1.0
    put("selB", sel)
    tri_mask = np.where(s[:, None] <= s[None, :], 0.0, NEG).astype(np.float32)
    cm0 = np.concatenate([tri_mask, np.zeros((128, 128), np.float32)], axis=1)
    cm1 = np.concatenate([np.full((128, 128), NEG, np.float32), tri_mask], axis=1)
    put("cm0", cm0)
    put("cm1", cm1)
    put("fb", np.broadcast_to(inputs["f_bias"].reshape(1, 8), (128, 8)))
    put("lng", np.broadcast_to(inputs["sg_ln_g"].reshape(1, 512), (128, 512)))
    put("gfin", np.broadcast_to(inputs["norm_final_g"].reshape(1, D), (128, D)))
    put("g1T", inputs["norm_mix_g"].reshape(8, 128).T)
    put("g2T", inputs["norm_ffn_g"].reshape(8, 128).T)
    wc = inputs["w_conv"].reshape(3, 2 * NCH, 128)
    put("wc", np.transpose(wc, (2, 1, 0)))
    put("bc", inputs["b_conv"].reshape(2 * NCH, 128).T)
    put("bsT", inputs["sg_b"].reshape(8, 128).T)
    km = np.zeros((128, NB), np.float32)
    if core % 2 == 0:
        km[:, 0:32] = NEG
    put("kmask", km)
    sgw = inputs["sg_w"].reshape(8, 128, 128)
    put("sgwT", np.transpose(sgw, (2, 0, 1)))
    return par


_NC_CACHE = {}


def kernel(**inputs):
    inputs = {k: np.asarray(v) for k, v in inputs.items()}
    x = inputs["x"].astype(np.float32, copy=False)
    key = "full"
    if key not in _NC_CACHE:
        _NC_CACHE[key] = build_nc()
    nc = _NC_CACHE[key]
    in_maps = []
    for c in range(8):
        b, half = c // 2, c % 2
        if half == 0:
            xc = np.concatenate([np.zeros((4096, D), np.float32), x[b, 0:4096]], axis=0)
        else:
            xc = x[b]
        in_maps.append({
            "x": np.ascontiguousarray(xc),
            "par": make_par(inputs, c),
            "w_in": np.ascontiguousarray(inputs["w_in"][0]),
            "w_out": np.ascontiguousarray(inputs["w_out"][0]),
            "w_up": np.ascontiguousarray(inputs["w_up"][0]),
            "w_down": np.ascontiguousarray(inputs["w_down"][0]),
        })
    res = run_bass_kernel_spmd(nc, in_maps, core_ids=list(range(8)))
    out = np.empty((4, SEQ, D), np.float32)
    for c in range(8):
        b, half = c // 2, c % 2
        out[b, half * 4096:(half + 1) * 4096] = res.results[c]["out"]
    return out
```

```python
import contextlib
import numpy as np
import concourse.bass as bass
import concourse.mybir as mybir
from concourse.bass_utils import run_bass_kernel_spmd

F32 = mybir.dt.float32
BF16 = mybir.dt.bfloat16
AF = mybir.ActivationFunctionType
ALU = mybir.AluOpType
AX = mybir.AxisListType

SAME_ENGINE_SYNC = True
NDMASEM = 8

D = 1024
SEQ = 8192
NB = 64
G = 256
NGRP = 32
FIRST_FULL = 15
DFF = 2816
NCH = 22
EPS = 1e-6
NEG = -30000.0

PC = {}
_off = 0
for _n, _w in [("eye", 128), ("tri", 128), ("sel127", 128), ("selA", 128), ("selB", 128),
               ("cm0", 256), ("cm1", 256), ("fb", 8), ("lng", 512), ("gfin", 1024),
               ("g1T", 8), ("g2T", 8), ("wc", 132), ("bc", 44), ("bsT", 8), ("kmask", 64),
               ("sgwT", 1024)]:
    PC[_n] = (_off, _off + _w)
    _off += _w
NPAR = _off

S_WIN, S_WOUT, S_WUP, S_WDN = 0, 10, 14, 36
NSLOT = 47


class Op:
    __slots__ = ("eng", "fn", "deps", "idx", "signaled", "ev", "is_dma", "dq", "dslot")

    def __init__(self, eng, fn, idx, is_dma):
        self.eng = eng
        self.fn = fn
        self.idx = idx
        self.deps = []
        self.signaled = False
        self.ev = None
        self.is_dma = is_dma


class Prog:
    ENGS = ("sp", "act", "dve", "pool", "pe")

    def __init__(self, nc):
        self.nc = nc
        self.ops = []
        self.last_writer = {}
        self.readers = {}
        self.dma_count = {e: 0 for e in self.ENGS}

    def op(self, eng, fn, reads=(), writes=(), dma=False):
        o = Op(eng, fn, len(self.ops), dma)
        deps = {}
        for r in reads:
            w = self.last_writer.get(r)
            if w is not None:
                deps[w.idx] = w
        for r in writes:
            w = self.last_writer.get(r)
            if w is not None:
                deps[w.idx] = w
            for rd in self.readers.get(r, ()):
                deps[rd.idx] = rd
        o.deps = list(deps.values())
        for r in reads:
            self.readers.setdefault(r, []).append(o)
        for r in writes:
            self.last_writer[r] = o
            self.readers[r] = []
        if dma:
            o.dq = eng
            o.dslot = self.dma_count[eng]
            self.dma_count[eng] += 1
        self.ops.append(o)
        return o

    def emit(self):
        nc = self.nc
        ops = self.ops

        def skip(d, o):
            return (not d.is_dma) and (not o.is_dma) and d.eng == o.eng and (d.eng == "pe" or not SAME_ENGINE_SYNC)

        for o in ops:
            for d in o.deps:
                if d.is_dma or skip(d, o):
                    continue
                d.signaled = True
        with contextlib.ExitStack() as es:
            esem = {e: es.enter_context(nc.semaphore("c_" + e)) for e in self.ENGS}
            dsem = {}
            for e in self.ENGS:
                if self.dma_count[e] > 0:
                    dsem[e] = [es.enter_context(nc.semaphore("d_%s_%d" % (e, i))) for i in range(NDMASEM)]
            cnt = {e: 0 for e in self.ENGS}
            for o in ops:
                if o.is_dma:
                    s = dsem[o.dq][o.dslot % NDMASEM]
                    o.ev = (s, 16 * (o.dslot // NDMASEM + 1))
                elif o.signaled:
                    cnt[o.eng] += 1
                    o.ev = (esem[o.eng], cnt[o.eng])
            self.stats = dict(cnt)
            block = es.enter_context(nc.Block())

            def body(engname, eng):
                seen = {}
                for o in ops:
                    if o.eng != engname:
                        continue
                    waits = []
                    for d in o.deps:
                        if d.ev is None or skip(d, o):
                            continue
                        waits.append(d.ev)
                    if o.is_dma and o.dslot >= NDMASEM:
                        s = dsem[o.dq][o.dslot % NDMASEM]
                        waits.append((s, 16 * (o.dslot // NDMASEM)))
                    mx = {}
                    for (s, v) in waits:
                        k = id(s)
                        if k not in mx or mx[k][1] < v:
                            mx[k] = (s, v)
                    for k, (s, v) in mx.items():
                        if seen.get(k, 0) >= v:
                            continue
                        seen[k] = v
                        eng.wait_ge(s, v)
                    if o.fn is None:
                        continue
                    inst = o.fn(eng)
                    if o.is_dma:
                        inst.then_inc(o.ev[0], 16)
                    elif o.signaled:
                        inst.then_inc(o.ev[0], 1)

            @block.sync
            def _(e):
                body("sp", e)

            @block.scalar
            def _(e):
                body("act", e)

            @block.vector
            def _(e):
                body("dve", e)

            @block.gpsimd
            def _(e):
                body("pool", e)

            @block.tensor
            def _(e):
                body("pe", e)


def build_nc(n_full=17, dbg=False, stage=99, n_ctx=FIRST_FULL):
    nc = bass.Bass("TRN2", target_bir_lowering=False)
    x_d = nc.dram_tensor("x", [SEQ, D], F32, kind="ExternalInput").ap()
    par_d = nc.dram_tensor("par", [128, NPAR], F32, kind="ExternalInput").ap()
    win_d = nc.dram_tensor("w_in", [D, 2568], F32, kind="ExternalInput").ap()
    wout_d = nc.dram_tensor("w_out", [D, D], F32, kind="ExternalInput").ap()
    wup_d = nc.dram_tensor("w_up", [D, 2 * DFF], F32, kind="ExternalInput").ap()
    wdn_d = nc.dram_tensor("w_down", [DFF, D], F32, kind="ExternalInput").ap()
    out_d = nc.dram_tensor("out", [4096, D], F32, kind="ExternalOutput").ap()
    wsc = nc.dram_tensor("wsc", [NSLOT, 128, 2048], BF16, kind="Internal").ap()
    ksc = nc.dram_tensor("ksc", [4, 128, SEQ], BF16, kind="Internal").ap()
    dbg_d = {}
    if dbg:
        for nm, shp in [("d_x1", [256, D]), ("d_outT", [128, 8 * 256]), ("d_ckb", [128, 64 * 8]),
                        ("d_qT", [128, 4 * 256]), ("d_oa", [128, 2 * 512])]:
            dbg_d[nm] = nc.dram_tensor(nm, shp, F32, kind="ExternalOutput").ap()

    with contextlib.ExitStack() as es:
        def T(name, shape, dt):
            return es.enter_context(nc.sbuf_tensor(name, shape, dt))

        def PS(name, shape, dt):
            return es.enter_context(nc.psum_tensor(name, shape, dt))

        P = Prog(nc)
        KSTG = T("KSTG", [128, 4, G], BF16)
        KCH = [T("KCH%d" % i, [128, 1024], BF16) for i in range(3)]
        V = T("V", [128, NB, 4, 132], BF16)
        XTs = [T("XT%d" % i, [128, 2, D], F32) for i in range(2)]
        XT = XTs[0]
        HBs = [T("HB%d" % i, [128, 2, D], BF16) for i in range(2)]
        HB = HBs[0]
        HT = T("HT", [128, 8, G], BF16)
        FA = T("FA", [128, 2560], F32)
        QT = FA[:, 0:512].bitcast(BF16).rearrange("p (a t) -> p a t", a=4)
        NPT = 6
        PT = [FA[:, 512 + 256 * i:512 + 256 * (i + 1)].bitcast(BF16) for i in range(NPT)]
        OSB = [FA[:, 2048 + 256 * i:2048 + 256 * (i + 1)] for i in range(2)]
        AGV = [FA[:, 516 * i:516 * (i + 1)].rearrange("p (a c) -> p a c", a=2) for i in range(2)]
        YGV = [FA[:, 1032 + 512 * i:1032 + 512 * (i + 1)].rearrange("p (a c) -> p a c", a=2) for i in range(2)]
        ARENA = T("ARENA", [128, 3072], F32)
        OUTF = ARENA[:, 0:2048].rearrange("p (b f) -> p b f", b=2)
        U = ARENA[:, 0:1024].rearrange("p (b f) -> p b f", b=2)
        VA = ARENA[:, 1024:2048].rearrange("p (b f) -> p b f", b=2)
        ABF = ARENA[:].bitcast(BF16)
        VN = ABF[:, 4096:5120].rearrange("p (b f) -> p b f", b=2)
        OA = ABF[:, 5120:6144].rearrange("p (b f) -> p b f", b=2)
        ACTT = ABF[:, 0:NCH * G].rearrange("p (c t) -> p c t", c=NCH)
        OUTT = HT
        NWP = 14
        WP = [T("WP%d" % i, [128, 2048], BF16) for i in range(NWP)]
        RBT = T("RBT", [128, 512], F32)
        RB = [RBT[:, i * G:(i + 1) * G] for i in range(2)]
        SQ = RBT
        CKB = T("CKB", [128, NB, 8], F32)
        WG = [T("WG%d" % i, [128, NB], F32) for i in range(2)]
        NRG = T("NRG", [128, 8], F32)
        VP = [[T("VP%d_%d" % (e_, i), [128, 8, 128], BF16) for i in range(2)] for e_ in range(2)]
        WF = T("WF", [128, 8, 8], BF16)
        CARRY = T("CARRY", [128, NCH, 2, 2], F32)
        ident = T("ident", [128, 128], BF16)
        CMASK = T("CMASK", [128, 2, G], BF16)
        WMT = T("WMT", [128, 8, 128], BF16)
        TRI = T("TRI", [128, 128], F32)
        SEL127 = T("SEL127", [128, 128], F32)
        SELA = T("SELA", [128, 128], F32)
        SELB = T("SELB", [128, 128], F32)
        FB = T("FB", [128, 8], F32)
        LNG = T("LNG", [128, 512], F32)
        GFIN = T("GFIN", [128, D], F32)
        GT = T("GT", [128, 16], F32)
        WC = T("WC", [128, 2 * NCH, 3], F32)
        BC = T("BC", [128, 2 * NCH], F32)
        BST = T("BST", [128, 8], F32)
        KM = T("KM", [128, NB], F32)
        ST = T("ST", [128, 64], F32)
        CCAR = T("CCAR", [128, 8], F32)
        RG = T("RG", [128, 8], F32)
        CB = T("CB", [128, 8], F32)
        SP_ = T("SP_", [128, 8], F32)
        EPSC = T("EPSC", [128, 2], F32)
        PGB = [PS("PGB%d" % i, [128, 512], F32) for i in range(3)]
        ACCB = [PS("ACCB%d" % i, [128, 512], F32) for i in range(4)]
        PSTB = PS("PSTB", [128, 1024], BF16)
        print("sbuf bytes remaining:", nc.sbuf_bytes_remaining)

        rot = {"pg": 0, "pt": 0}
        att_rot = [0]
        kch_rot = [0]
        PSTF = PSTB[:, :].bitcast(F32)

        def next_pg():
            i = rot["pg"] % 3
            rot["pg"] += 1
            return i

        def next_pt():
            i = rot["pt"] % NPT
            rot["pt"] += 1
            return i

        def pc(name):
            a, b = PC[name]
            return par_d[:, a:b]

        def dma(out, in_, reads, writes):
            return P.op("sp", lambda e: e.dma_start(out=out, in_=in_), reads=reads, writes=writes, dma=True)

        XT_ALL0 = [("XT", 0, 0), ("XT", 0, 1)]
        ARENA_ALL = ["U", "VA", "VN", "OA"] + [("ACTT", c) for c in range(NCH)]

        dma(TRI[:], pc("tri"), [], ["TRI"])
        dma(SEL127[:], pc("sel127"), [], ["SEL127"])
        dma(SELA[:], pc("selA"), [], ["SELA"])
        dma(SELB[:], pc("selB"), [], ["SELB"])
        dma(FB[:], pc("fb"), [], ["FB"])
        dma(LNG[:], pc("lng"), [], ["LNG"])
        dma(GFIN[:], pc("gfin"), [], ["GFIN"])
        dma(GT[:, 0:8], pc("g1T"), [], ["GT"])
        dma(GT[:, 8:16], pc("g2T"), [], ["GT"])
        dma(WC[:].rearrange("p c j -> p (c j)"), pc("wc"), [], ["WC"])
        dma(BC[:], pc("bc"), [], ["BC"])
        dma(BST[:], pc("bsT"), [], ["BST"])
        dma(KM[:], pc("kmask"), [], ["KM"])
        XTf = XT[:].rearrange("p b f -> p (b f)")
        HBf = HB[:].rearrange("p b f -> p (b f)")
        HTf = HT[:].rearrange("p k t -> p (k t)")
        a0 = PC["eye"][0]
        dma(XTf[:, 0:128], pc("eye"), [], XT_ALL0)
        dma(XTf[:, 128:640], par_d[:, PC["cm0"][0]:PC["cm1"][1]], [], XT_ALL0)
        dma(XTf[:, 640:1664], pc("sgwT"), [], XT_ALL0)
        P.op("dve", lambda e: e.tensor_copy(out=ident[:], in_=XTf[:, 0:128]), reads=XT_ALL0, writes=["ident"])
        P.op("dve", lambda e: e.tensor_copy(out=CMASK[:].rearrange("p a t -> p (a t)"), in_=XTf[:, 128:640]), reads=XT_ALL0, writes=["CMASK"])
        P.op("dve", lambda e: e.memset(XTf[64:128, 640:1664].rearrange("p (h t) -> p h t", h=8)[:, :, 0:64], 0.0), reads=[], writes=XT_ALL0)
        P.op("dve", lambda e: e.tensor_copy(out=WMT[:].rearrange("p h t -> p (h t)"), in_=XTf[:, 640:1664]), reads=XT_ALL0, writes=["WMT"])
        P.op("dve", lambda e: e.memset(EPSC[:, 0:1], EPS), writes=["EPSC"])
        P.op("dve", lambda e: e.memset(EPSC[:, 1:2], 1.0), writes=["EPSC"])
        P.op("dve", lambda e: e.memset(CCAR[:], 0.0), writes=["CCAR"])
        P.op("dve", lambda e: e.memset(CARRY[:].rearrange("p c a j -> p (c a j)"), 0.0), writes=["CARRY"])
        import os
        if os.environ.get("NOVMEM") != "1":
            P.op("dve", lambda e: e.memset(V[:].rearrange("p b a c -> p (b a) c")[:, :, 64:66], 1.0), writes=[("V", b, s) for b in range(NB) for s in range(2)])

        NCONV = 99 if stage >= 2 else 0
        HB_ALL = [("HB", 0, 0), ("HB", 0, 1)]
        HT_ALL = [("HT", 0), ("HT", 1)]
        stage_f = [(XTf, XT_ALL0), (ARENA[:, 0:2048], ARENA_ALL)]
        stage_b = [(HBf, HB_ALL), (HTf, HT_ALL)]
        conv_engs = ["dve", "act"]
        cnt = [0]

        def convert(slot, pieces, gain_col, bg=False):
            if cnt[0] >= NCONV:
                return
            i = cnt[0] % 2
            cnt[0] += 1
            if bg:
                sf, sfn = ARENA[:, 0:2048], ["cvF"]
                sb, sbn = ABF[:, 4096:6144], ["cvB"]
            else:
                sf, sfn = stage_f[i]
                sb, sbn = stage_b[i]
            wr = list(sfn)
            for (dst, src) in pieces:
                dma(dst(sf), src, [], wr)
            eng = conv_engs[i]
            if gain_col is None:
                if eng == "dve":
                    P.op(eng, lambda e: e.tensor_copy(out=sb[:, :], in_=sf[:, :]), reads=wr, writes=list(sbn))
                else:
                    P.op(eng, lambda e: e.copy(out=sb[:, :], in_=sf[:, :]), reads=wr, writes=list(sbn))
            else:
                for k in range(8):
                    if eng == "dve":
                        P.op(eng, lambda e, k=k: e.tensor_scalar(out=sb[:, k * 256:(k + 1) * 256], in0=sf[:, k * 256:(k + 1) * 256],
                                                               scalar1=GT[:, gain_col + k:gain_col + k + 1], scalar2=None, op0=ALU.mult),
                             reads=wr + ["GT"], writes=list(sbn))
                    else:
                        P.op(eng, lambda e, k=k: e.mul(out=sb[:, k * 256:(k + 1) * 256], in_=sf[:, k * 256:(k + 1) * 256],
                                                     mul=GT[:, gain_col + k:gain_col + k + 1]),
                             reads=wr + ["GT"], writes=list(sbn))
            out_fn = lambda: dma(wsc[slot], sb[:, :], list(sbn), [("wsc", slot)])
            if bg:
                return out_fn
            out_fn()
            return None

        def v3(sf, a, b):
            return sf[:, 0:2048].rearrange("p (a b) -> p a b", a=a)

        conv_jobs = {}
        for s in range(10):
            conv_jobs[S_WIN + s] = ([(lambda sf: v3(sf, 8, 256), win_d[:, s * 256:(s + 1) * 256].rearrange("(k p) c -> p k c", p=128))], 0)
        for s in range(4):
            conv_jobs[S_WOUT + s] = ([(lambda sf: v3(sf, 8, 256), wout_d[:, s * 256:(s + 1) * 256].rearrange("(k p) c -> p k c", p=128))], None)
        for j in range(NCH):
            conv_jobs[S_WUP + j] = ([(lambda sf: v3(sf, 8, 256)[:, :, 0:128], wup_d[:, j * 128:(j + 1) * 128].rearrange("(k p) c -> p k c", p=128)),
                                     (lambda sf: v3(sf, 8, 256)[:, :, 128:256], wup_d[:, DFF + j * 128:DFF + (j + 1) * 128].rearrange("(k p) c -> p k c", p=128))], 8)
        for i in range(11):
            conv_jobs[S_WDN + i] = ([(lambda sf: v3(sf, 2, 1024), wdn_d[i * 256:(i + 1) * 256, :].rearrange("(c p) n -> p c n", p=128))], None)
        for slot in (6, 7, 8, 9):
            convert(slot, *conv_jobs[slot])
        dma(XTf[:, 0:64].rearrange("p (k c) -> p k c", k=8), win_d[:, 2560:2568].rearrange("(k p) c -> p k c", p=128), [], XT_ALL0)
        for k in range(8):
            P.op("dve", lambda e, k=k: e.tensor_scalar(out=WF[:, k, :], in0=XTf[:, k * 8:(k + 1) * 8], scalar1=GT[:, k:k + 1], scalar2=None, op0=ALU.mult),
                 reads=XT_ALL0 + ["GT"], writes=["WF"])
        P.op("dve", lambda e: e.memset(ST[:, 60:61], 0.0), reads=[], writes=ARENA_ALL + ["cvF", "cvB"])
        bg_slots = [sl for sl in list(range(0, 6)) + list(range(10, NSLOT))]
        bg_state = {"i": 0, "pending_out": None}

        def conv_step():
            if bg_state["pending_out"] is not None:
                bg_state["pending_out"]()
                bg_state["pending_out"] = None
            if stage < 2 or bg_state["i"] >= len(bg_slots):
                return
            slot = bg_slots[bg_state["i"]]
            bg_state["i"] += 1
            bg_state["pending_out"] = convert(slot, *conv_jobs[slot], bg=True)

        def conv_flush():
            while bg_state["i"] < len(bg_slots) or bg_state["pending_out"] is not None:
                conv_step()
            P.op("dve", lambda e: e.memset(ST[:, 61:62], 0.0), reads=[], writes=ARENA_ALL + ["cvF", "cvB"])

        wseq = []
        for ci in range(NGRP):
            full = ci >= FIRST_FULL
            if stage < 3 or (not full and ci >= n_ctx):
                continue
            if full and (ci - FIRST_FULL) >= n_full:
                break
            if full:
                wseq += list(range(0, 10)) + list(range(10, 14)) + list(range(14, 36)) + list(range(36, 47))
            else:
                wseq += [6, 7, 8, 9]
        wstate = {"use": 0, "load": 0}

        def wload_next():
            n = wstate["load"]
            if n >= len(wseq):
                return
            slot = wseq[n]
            buf = n % NWP
            wstate["load"] += 1
            dma(WP[buf][:, :], wsc[slot], [("wsc", slot)], [("WP", buf)])

        def wacquire(slot):
            n = wstate["use"]
            assert wseq[n] == slot, (n, wseq[n], slot)
            return n % NWP

        def wrelease():
            wstate["use"] += 1
            wload_next()

        for _ in range(NWP):
            wload_next()

        evac_rr = [0]

        def rms_front(XT, HB, par, sc):
            for b in range(2):
                P.op("dve", lambda e, b=b: e.scalar_tensor_tensor(out=HB[:, b, :], in0=XT[:, b, :], scalar=1.0, in1=XT[:, b, :],
                                                                 op0=ALU.mult, op1=ALU.mult, accum_out=ST[:, sc + b:sc + b + 1]),
                     reads=[("XT", par, b)], writes=[("HB", par, b), ("ST", sc + b)])
                P.op("act", lambda e, b=b: e.activation(out=ST[:, sc + 2 + b:sc + 3 + b], in_=ST[:, sc + b:sc + b + 1], func=AF.Ln, bias=EPSC[:, 0:1], scale=1.0 / D),
                     reads=[("ST", sc + b), "EPSC"], writes=[("ST", sc + 2 + b)])
                P.op("act", lambda e, b=b: e.activation(out=ST[:, sc + 4 + b:sc + 5 + b], in_=ST[:, sc + 2 + b:sc + 3 + b], func=AF.Exp, scale=-0.5),
                     reads=[("ST", sc + 2 + b)], writes=[("ST", sc + 4 + b)])
                P.op("dve", lambda e, b=b: e.tensor_scalar(out=HB[:, b, :], in0=XT[:, b, :], scalar1=ST[:, sc + 4 + b:sc + 5 + b], scalar2=None, op0=ALU.mult),
                     reads=[("XT", par, b), ("ST", sc + 4 + b)], writes=[("HB", par, b)])

        def rms_back(HB, par):
            for b in range(2):
                for k in range(8):
                    P.op("pe", lambda e, b=b, k=k: e.transpose(out=PSTB[:, k * 128:(k + 1) * 128], in_=HB[:, b, k * 128:(k + 1) * 128], identity=ident[:]),
                         reads=[("HB", par, b), "ident"], writes=["PST"])
                P.op("act", lambda e, b=b: e.copy(out=HT[:, :, b * 128:(b + 1) * 128], in_=PSTB[:, :].rearrange("p (k t) -> p k t", k=8)),
                     reads=["PST"], writes=[("HT", b)] + [("OUTT", kq) for kq in range(8)])

        front_done = set()

        def emit_front(ci):
            front_done.add(ci)
            par = ci % 2
            T0 = ci * G
            dma(XTs[par][:], x_d[T0:T0 + G, :].rearrange("(b p) f -> p b f", p=128), [], [("XT", par, 0), ("XT", par, 1)])
            rms_front(XTs[par], HBs[par], par, 40)

        def HT_reads(b=None):
            if b is None:
                return [("HT", 0), ("HT", 1)]
            return [("HT", b)]

        def mm_feat(pi, hf, wbuf, c0):
            for k in range(8):
                P.op("pe", lambda e, k=k: e.matmul(PGB[pi][:, hf * G:(hf + 1) * G], lhsT=WP[wbuf][:, k * 256 + c0:k * 256 + c0 + 128], rhs=HT[:, k, :],
                                                  start=(k == 0), stop=(k == 7)),
                     reads=[("WP", wbuf)] + HT_reads(), writes=[("PG", pi)])

        def mm_tok(pi, hf, wbuf, b, lhs, lhs_reads):
            for k in range(8):
                P.op("pe", lambda e, k=k: e.matmul(PGB[pi][:, hf * G:(hf + 1) * G], lhsT=lhs[:, k, b * 128:(b + 1) * 128], rhs=WP[wbuf][:, k * 256:(k + 1) * 256],
                                                  start=(k == 0), stop=(k == 7)),
                     reads=[("WP", wbuf)] + lhs_reads, writes=[("PG", pi)])

        def emit_group(ci):
            full = ci >= FIRST_FULL
            gi = ci - FIRST_FULL
            b0 = 2 * ci
            T0 = ci * G
            par = ci % 2
            XT = XTs[par]
            HB = HBs[par]
            if gi == 0:
                conv_flush()
            if not full:
                conv_step()
            rms_back(HB, par)
            nxt = ci + 1
            has_next = (nxt < NGRP) and not (nxt >= FIRST_FULL and (nxt - FIRST_FULL) >= n_full) and not (stage < 3) and not (nxt < FIRST_FULL and nxt >= n_ctx)
            if not full and has_next:
                emit_front(nxt)
            if full:
                for s in range(4):
                    wb = wacquire(s)
                    dst, dn = (U, "U") if s < 2 else (VA, "VA")
                    hf = s % 2
                    pi = next_pg()
                    for b in range(2):
                        mm_tok(pi, b, wb, b, HT, HT_reads(b))
                    P.op("act", lambda e, pi=pi, dst=dst, hf=hf: e.activation(out=dst[:, :, hf * 256:(hf + 1) * 256], in_=PGB[pi][:, :].rearrange("p (b c) -> p b c", b=2), func=AF.Gelu),
                         reads=[("PG", pi)], writes=[dn] + [("ACTT", c) for c in range(NCH)])
                    wrelease()
                for s in range(2):
                    wb = wacquire(4 + s)
                    pi = next_pg()
                    for m in range(2):
                        mm_feat(pi, m, wb, m * 128)
                    P.op("act", lambda e, pi=pi, s=s: e.mul(out=QT[:, 2 * s:2 * s + 2, :].rearrange("p a t -> p (a t)"), in_=PGB[pi][:, :], mul=0.125),
                         reads=[("PG", pi)], writes=[("QT", 2 * s), ("QT", 2 * s + 1)])
                    wrelease()
            for s in range(2):
                wb = wacquire(6 + s)
                pi = next_pg()
                for m in range(2):
                    mm_feat(pi, m, wb, m * 128)
                for m in range(2):
                    hp = 2 * s + m
                    P.op("act", lambda e, pi=pi, hp=hp, m=m: e.copy(out=KSTG[:, hp, :], in_=PGB[pi][:, m * G:(m + 1) * G]),
                         reads=[("PG", pi)], writes=[("KSTG", hp)])
                wrelease()
            dma(ksc[:, :, T0:T0 + G].rearrange("h p t -> p h t"), KSTG[:, :, :], [("KSTG", hp) for hp in range(4)],
                [("KD", hp, bb) for hp in range(4) for bb in (b0, b0 + 1)])
            if not full:
                conv_step()
            for s in range(2):
                wb = wacquire(8 + s)
                pi = next_pg()
                for b in range(2):
                    mm_tok(pi, b, wb, b, HT, HT_reads(b))
                for b in range(2):
                    for a in range(2):
                        P.op("dve", lambda e, b=b, pi=pi, s=s, a=a: e.tensor_copy(
                            out=V[:, b0 + b, 2 * s + a, :].rearrange("p (e c) -> p e c", e=2)[:, :, 0:64],
                            in_=PGB[pi][:, b * G + a * 128:b * G + (a + 1) * 128].rearrange("p (e c) -> p e c", e=2)),
                             reads=[("PG", pi)], writes=[("V", b0 + b, s)])
                wrelease()
            if not full:
                conv_step()
            for b in range(2):
                pi = next_pg()
                for k in range(8):
                    P.op("pe", lambda e, k=k, b=b, pi=pi: e.matmul(PGB[pi][:, 0:8], lhsT=HT[:, k, b * 128:(b + 1) * 128], rhs=WF[:, k, :], start=(k == 0), stop=(k == 7)),
                         reads=["WF"] + HT_reads(b), writes=[("PG", pi)])
                P.op("dve", lambda e, pi=pi: e.tensor_tensor(out=SP_[:, :], in0=PGB[pi][:, 0:8], in1=FB[:, :], op=ALU.add),
                     reads=[("PG", pi), "FB"], writes=["SP_"])
                P.op("act", lambda e: e.activation(out=SP_[:, :], in_=SP_[:, :], func=AF.Exp, scale=-1.0), reads=["SP_"], writes=["SP_"])
                P.op("act", lambda e: e.activation(out=SP_[:, :], in_=SP_[:, :], func=AF.Ln, bias=EPSC[:, 1:2], scale=1.0), reads=["SP_", "EPSC"], writes=["SP_"])
                pi2 = next_pg()
                P.op("pe", lambda e, pi2=pi2: e.matmul(PGB[pi2][:, 0:8], lhsT=TRI[:, :], rhs=SP_[:, :], start=True, stop=True),
                     reads=["TRI", "SP_"], writes=[("PG", pi2)])
                P.op("dve", lambda e, pi2=pi2: e.tensor_tensor(out=CB[:, :], in0=PGB[pi2][:, 0:8], in1=CCAR[:, :], op=ALU.add),
                     reads=[("PG", pi2), "CCAR"], writes=["CB"])
                P.op("dve", lambda e, b=b: e.tensor_scalar(out=CKB[:, b0 + b, :], in0=CB[:, :], scalar1=KM[:, b0 + b:b0 + b + 1], scalar2=None, op0=ALU.add),
                     reads=["CB", "KM"], writes=[("CKB", b0 + b)])
                pi3 = next_pg()
                P.op("pe", lambda e, pi3=pi3: e.matmul(PGB[pi3][:, 0:8], lhsT=SEL127[:, :], rhs=CB[:, :], start=True, stop=True),
                     reads=["SEL127", "CB"], writes=[("PG", pi3)])
                P.op("dve", lambda e, pi3=pi3: e.tensor_copy(out=CCAR[:, :], in_=PGB[pi3][:, 0:8]), reads=[("PG", pi3)], writes=["CCAR"])
                if b == 0 and full:
                    P.op("dve", lambda e, pi3=pi3: e.tensor_scalar(out=NRG[:, :], in0=PGB[pi3][:, 0:8], scalar1=-1.0, scalar2=None, op0=ALU.mult), reads=[("PG", pi3)], writes=["RG"])
            if not full:
                return
            if dbg and gi == dbg_g:
                dma(dbg_d["d_qT"], QT[:].rearrange("p a t -> p (a t)"), [("QT", h) for h in range(4)], ["dq"])
            def emit_sg():
                SQR = [("RB", 0), ("RB", 1)]
                for b in range(2):
                    va3 = VA[:, b, :].rearrange("p (h d) -> p h d", h=8)
                    P.op("dve", lambda e, va3=va3: e.tensor_reduce(out=ST[:, 8:16], in_=va3, axis=AX.X, op=ALU.add), reads=["VA"], writes=[("ST", "s1")])
                    P.op("dve", lambda e, b=b: e.tensor_tensor(out=SQ[:, :], in0=VA[:, b, :], in1=VA[:, b, :], op=ALU.mult), reads=["VA"], writes=SQR)
                    P.op("dve", lambda e: e.tensor_reduce(out=ST[:, 16:24], in_=SQ[:, :].rearrange("p (h d) -> p h d", h=8), axis=AX.X, op=ALU.add),
                         reads=SQR, writes=[("ST", "s2")])
                    P.op("dve", lambda e: e.tensor_scalar(out=ST[:, 8:16], in0=ST[:, 8:16], scalar1=1.0 / 64, scalar2=None, op0=ALU.mult),
                         reads=[("ST", "s1")], writes=[("ST", "s1")])
                    P.op("dve", lambda e: e.tensor_tensor(out=ST[:, 24:32], in0=ST[:, 8:16], in1=ST[:, 8:16], op=ALU.mult),
                         reads=[("ST", "s1")], writes=[("ST", "msq")])
                    P.op("dve", lambda e: e.scalar_tensor_tensor(out=ST[:, 16:24], in0=ST[:, 16:24], scalar=1.0 / 64, in1=ST[:, 24:32], op0=ALU.mult, op1=ALU.subtract),
                         reads=[("ST", "s2"), ("ST", "msq")], writes=[("ST", "s2")])
                    P.op("act", lambda e: e.activation(out=ST[:, 16:24], in_=ST[:, 16:24], func=AF.Ln, bias=EPSC[:, 0:1], scale=1.0),
                         reads=[("ST", "s2"), "EPSC"], writes=[("ST", "s2")])
                    P.op("act", lambda e: e.activation(out=ST[:, 16:24], in_=ST[:, 16:24], func=AF.Exp, scale=-0.5),
                         reads=[("ST", "s2")], writes=[("ST", "s2")])
                    P.op("dve", lambda e, va3=va3: e.tensor_tensor(out=va3, in0=va3, in1=ST[:, 8:16].unsqueeze(2).to_broadcast([128, 8, 64]), op=ALU.subtract),
                         reads=["VA", ("ST", "s1")], writes=["VA"])
                    P.op("dve", lambda e, va3=va3: e.tensor_tensor(out=va3, in0=va3, in1=ST[:, 16:24].unsqueeze(2).to_broadcast([128, 8, 64]), op=ALU.mult),
                         reads=["VA", ("ST", "s2")], writes=["VA"])
                    P.op("dve", lambda e, b=b: e.tensor_tensor(out=VN[:, b, :], in0=VA[:, b, :], in1=LNG[:, :], op=ALU.mult),
                         reads=["VA", "LNG"], writes=["VN"])
                    pi = next_pg()
                    for h in range(8):
                        P.op("pe", lambda e, h=h, b=b, pi=pi: e.matmul(PGB[pi][:, h * 64:(h + 1) * 64], lhsT=WMT[:, h, :], rhs=VN[:, b, h * 64:(h + 1) * 64],
                                                                      start=True, stop=True),
                             reads=["WMT", "VN"], writes=[("PG", pi)])
                    P.op("dve", lambda e, pi=pi: e.tensor_tensor(out=SQ[:, :].rearrange("p (h d) -> p h d", h=8),
                                                                 in0=PGB[pi][:, :].rearrange("p (h d) -> p h d", h=8),
                                                                 in1=BST[:, :].unsqueeze(2).to_broadcast([128, 8, 64]), op=ALU.add),
                         reads=[("PG", pi), "BST"], writes=SQR)
                    P.op("dve", lambda e, b=b: e.tensor_tensor(out=OA[:, b, :], in0=SQ[:, :], in1=U[:, b, :], op=ALU.mult),
                         reads=SQR + ["U"], writes=["OA"])
                    for kk in range(4):
                        P.op("pe", lambda e, b=b, kk=kk: e.transpose(out=PSTB[:, kk * 128:(kk + 1) * 128], in_=OA[:, b, kk * 128:(kk + 1) * 128], identity=ident[:]),
                             reads=["OA", "ident"], writes=["PST"])
                    P.op("act", lambda e, b=b: e.copy(out=OUTT[:, 0:4, b * 128:(b + 1) * 128], in_=PSTB[:, 0:512].rearrange("p (k t) -> p k t", k=4)),
                         reads=["PST"], writes=[("OUTT", k) for k in range(4)] + [("HT", 0), ("HT", 1)])

            nj = b0 + 2
            nchunk = (nj + 7) // 8

            def kload(hp, c):
                nblk = min(8, nj - 8 * c)
                kb = kch_rot[0] % 3
                kch_rot[0] += 1
                dma(KCH[kb][:, 0:nblk * 128], ksc[hp][:, c * 1024:c * 1024 + nblk * 128],
                    [("KD", hp, 8 * c + q) for q in range(nblk)], [("KCH", kb)])
                return kb

            for hp in range(4):
                if hp == 1:
                    emit_sg()
                accs = [(2 * hp) % 4, (2 * hp + 1) % 4]
                kbufs = {0: kload(hp, 0)}
                if nchunk > 1:
                    kbufs[1] = kload(hp, 1)
                for ee in range(2):
                    h = 2 * hp + ee
                    P.op("act", lambda e, h=h, ee=ee: e.activation(out=WG[ee][:, 0:nj], in_=CKB[:, 0:nj, h], func=AF.Exp, bias=NRG[:, h:h + 1], scale=1.0),
                         reads=[("CKB", j) for j in range(nj)] + ["RG"], writes=[("WG", ee)])

                def vprep(c, hp=hp):
                    nblk = min(8, nj - 8 * c)
                    cb = c % 2
                    for ee in range(2):
                        Kc = 65 if ee == 0 else 128
                        c0 = 0 if ee == 0 else 2
                        P.op("dve", lambda e, ee=ee, Kc=Kc, c0=c0, cb=cb, nblk=nblk, c=c: e.tensor_tensor(
                            out=VP[ee][cb][:, 0:nblk, 0:Kc], in0=V[:, 8 * c:8 * c + nblk, hp, c0:c0 + Kc],
                            in1=WG[ee][:, 8 * c:8 * c + nblk].unsqueeze(2).to_broadcast([128, nblk, Kc]), op=ALU.mult),
                             reads=[("V", 8 * c + q_, hp // 2) for q_ in range(nblk)] + [("WG", ee)], writes=[("VP", ee, cb)])

                vprep(0)
                if nchunk > 1:
                    vprep(1)

                def qk_step(j, hp=hp):
                    bp = att_rot[0] % 2
                    att_rot[0] += 1
                    banks = [(PGB[0], ("PG", 0)), (PGB[1], ("PG", 1))] if bp == 0 else [(PGB[2], ("PG", 2)), (PSTF, "PST")]
                    for q in range(2):
                        jj = j + q
                        diag = jj >= b0
                        for ee in range(2):
                            kr = slice(ee * 64, (ee + 1) * 64)
                            bk, bkey = banks[ee]
                            kb_ = kbufs[jj // 8]
                            jo = jj % 8
                            P.op("pe", lambda e, ee=ee, kr=kr, bk=bk, jj=jj, q=q, diag=diag, kb_=kb_, jo=jo: e.matmul(bk[:, q * G:(q + 1) * G], lhsT=KCH[kb_][kr, jo * 128:(jo + 1) * 128], rhs=QT[kr, hp, :],
                                                                                                  start=True, stop=not diag),
                                 reads=[("KCH", kb_), ("QT", hp)], writes=[bkey])
                        if diag:
                            for ee in range(2):
                                bk, bkey = banks[ee]
                                P.op("pe", lambda e, bk=bk, jj=jj, q=q: e.matmul(bk[:, q * G:(q + 1) * G], lhsT=ident[:, :], rhs=CMASK[:, jj - b0, :], start=False, stop=True),
                                     reads=["ident", "CMASK"], writes=[bkey])
                    return banks

                def exp_step(j, banks):
                    tis = []
                    for ee in range(2):
                        bk, bkey = banks[ee]
                        ti = next_pt()
                        P.op("act", lambda e, ti=ti, bk=bk: e.activation(out=PT[ti][:, :], in_=bk[:, :], func=AF.Exp, scale=1.0),
                             reads=[bkey], writes=[("PT", ti)])
                        tis.append(ti)
                    return tis

                def pv_step(j, tis, hp=hp, accs=accs):
                    for q in range(2):
                        jj = j + q
                        cb = (jj // 8) % 2
                        jo = jj % 8
                        t0_, t1_ = tis[0], tis[1]
                        P.op("pe", lambda e, jj=jj, cb=cb, jo=jo, t0_=t0_, q=q: e.matmul(ACCB[accs[0]][0:65, 0:G], lhsT=VP[0][cb][:, jo, 0:65], rhs=PT[t0_][:, q * G:(q + 1) * G], start=(jj == 0), stop=(jj == nj - 1)),
                             reads=[("VP", 0, cb), ("PT", t0_)], writes=[("ACC", accs[0])])
                        P.op("pe", lambda e, jj=jj, cb=cb, jo=jo, t1_=t1_, q=q: e.matmul(ACCB[accs[1]][:, 0:G], lhsT=VP[1][cb][:, jo, :], rhs=PT[t1_][:, q * G:(q + 1) * G], start=(jj == 0), stop=(jj == nj - 1)),
                             reads=[("VP", 1, cb), ("PT", t1_)], writes=[("ACC", accs[1])])

                pend = None
                for j in range(0, nj, 2):
                    if j % 8 == 0 and j > 0 and (j // 8 + 1) < nchunk:
                        kbufs[j // 8 + 1] = kload(hp, j // 8 + 1)
                    if j % 8 == 2 and j > 8 - 8 and (j // 8 + 1) < nchunk and (j // 8 + 1) >= 2:
                        vprep(j // 8 + 1)
                    banks = qk_step(j)
                    tis = exp_step(j, banks)
                    if pend is not None:
                        pv_step(*pend)
                    pend = (j, tis)
                pv_step(*pend)
                for ee in range(2):
                    h = 2 * hp + ee
                    acc = accs[ee]
                    kr = slice(ee * 64, (ee + 1) * 64)
                    oi = ee
                    K = 65 if ee == 0 else 128
                    sel = SELA if ee == 0 else SELB
                    seln = "SELA" if ee == 0 else "SELB"
                    P.op("act", lambda e, oi=oi, K=K, acc=acc: e.copy(out=OSB[oi][0:K, :], in_=ACCB[acc][0:K, 0:G]), reads=[("ACC", acc)], writes=[("OSB", oi)])
                    pi = next_pg()
                    P.op("pe", lambda e, oi=oi, K=K, sel=sel, pi=pi: e.matmul(PGB[pi][:, 0:G], lhsT=sel[0:K, :], rhs=OSB[oi][0:K, :], start=True, stop=True),
                         reads=[seln, ("OSB", oi)], writes=[("PG", pi)])
                    P.op("dve", lambda e, oi=oi, pi=pi: e.tensor_scalar(out=RB[oi][:, :], in0=PGB[pi][:, 0:G], scalar1=1e-30, scalar2=None, op0=ALU.max),
                         reads=[("PG", pi)], writes=[("RB", oi)])
                    P.op("dve", lambda e, oi=oi: e.reciprocal(out=RB[oi][:, :], in_=RB[oi][:, :]), reads=[("RB", oi)], writes=[("RB", oi)])
                    P.op("dve", lambda e, oi=oi, kr=kr, hp=hp: e.tensor_tensor(out=OUTT[kr, 4 + hp, :], in0=OSB[oi][kr, :], in1=RB[oi][kr, :], op=ALU.mult),
                         reads=[("OSB", oi), ("RB", oi)], writes=[("OUTT", 4 + hp), ("HT", 0), ("HT", 1)])
            if dbg and gi == dbg_g:
                for k in range(8):
                    P.op("dve", lambda e, k=k: e.tensor_copy(out=YGV[0][:, 0, :], in_=OUTT[:, k, :]), reads=[("OUTT", k)], writes=["dtmp"])
                    dma(dbg_d["d_outT"][:, k * 256:(k + 1) * 256], YGV[0][:, 0, :], ["dtmp"], ["dtmp2"])
                dma(dbg_d["d_ckb"], CKB[:].rearrange("p b h -> p (b h)"), [("CKB", j) for j in range(NB)], ["dck"])
            if has_next:
                emit_front(nxt)
            for n in range(4):
                wb = wacquire(S_WOUT + n)
                pi = next_pg()
                for b in range(2):
                    mm_tok(pi, b, wb, b, OUTT, [("OUTT", k) for k in range(8)])
                P.op("dve", lambda e, n=n, pi=pi: e.tensor_tensor(out=XT[:, :, n * 256:(n + 1) * 256], in0=XT[:, :, n * 256:(n + 1) * 256],
                                                                  in1=PGB[pi][:, :].rearrange("p (b c) -> p b c", b=2), op=ALU.add),
                     reads=[("PG", pi), ("XT", par, 0), ("XT", par, 1)], writes=[("XT", par, 0), ("XT", par, 1)])
                wrelease()
            if dbg and gi == dbg_g:
                dma(dbg_d["d_x1"].rearrange("(b p) f -> p b f", p=128), XT[:], [("XT", par, 0), ("XT", par, 1)], ["dx1"])
            rms_front(XT, HB, par, 0)
            rms_back(HB, par)
            def ffn_tail(j):
                r = j % 2
                P.op("act", lambda e, r=r: e.activation(out=YGV[r][:, 0, :], in_=YGV[r][:, 0, :], func=AF.Silu), reads=[("YGV", r, 0)], writes=[("YGV", r, 0)])
                P.op("pool", lambda e, r=r, j=j: e.tensor_tensor(out=ACTT[:, j, :], in0=YGV[r][:, 0, :], in1=YGV[r][:, 1, :], op=ALU.mult),
                     reads=[("YGV", r, 0), ("YGV", r, 1)], writes=[("ACTT", j), "U", "VA", "VN", "OA"])

            for j in range(NCH):
                wb = wacquire(S_WUP + j)
                r = j % 2
                pi = next_pg()
                for part in range(2):
                    mm_feat(pi, part, wb, part * 128)
                P.op("dve", lambda e, r=r, j=j: e.tensor_copy(out=AGV[r][:, :, 0:2], in_=CARRY[:, j, :, :]),
                     reads=[("CARRY", j)], writes=[("AGVc", r)])
                P.op("act", lambda e, r=r, pi=pi: e.copy(out=AGV[r][:, :, 2:258], in_=PGB[pi][:, :].rearrange("p (a t) -> p a t", a=2)),
                     reads=[("PG", pi)], writes=[("AGV", r, 0), ("AGV", r, 1)])
                for part in range(2):
                    cj = part * NCH + j
                    P.op("act", lambda e, r=r, part=part, cj=cj, pi=pi: e.activation(out=YGV[r][:, part, :], in_=PGB[pi][:, part * G:(part + 1) * G], func=AF.Identity,
                                                                                 bias=BC[:, cj:cj + 1], scale=WC[:, cj, 2:3]),
                         reads=[("PG", pi), "WC", "BC"], writes=[("YGV", r, part)])
                if j >= 1:
                    ffn_tail(j - 1)
                P.op("dve", lambda e, r=r, j=j: e.tensor_copy(out=CARRY[:, j, :, :], in_=AGV[r][:, :, 256:258]),
                     reads=[("AGV", r, 0), ("AGV", r, 1)], writes=[("CARRY", j)])
                for tap in (1, 0):
                    for part in range(2):
                        cj = part * NCH + j
                        P.op("dve", lambda e, r=r, part=part, cj=cj, tap=tap: e.scalar_tensor_tensor(out=YGV[r][:, part, :], in0=AGV[r][:, part, tap:tap + 256], scalar=WC[:, cj, tap:tap + 1],
                                                                                                  in1=YGV[r][:, part, :], op0=ALU.mult, op1=ALU.add),
                             reads=[("AGV", r, part), ("AGVc", r), "WC", ("YGV", r, part)], writes=[("YGV", r, part)])
                wrelease()
            ffn_tail(NCH - 1)
            for i in range(11):
                wb = wacquire(S_WDN + i)
                for c2 in range(2):
                    ch = 2 * i + c2
                    for b in range(2):
                        for n2 in range(2):
                            acc = b * 2 + n2
                            P.op("pe", lambda e, ch=ch, c2=c2, b=b, n2=n2, acc=acc, wb=wb: e.matmul(ACCB[acc][:, :], lhsT=ACTT[:, ch, b * 128:(b + 1) * 128],
                                                                                                 rhs=WP[wb][:, c2 * 1024 + n2 * 512:c2 * 1024 + (n2 + 1) * 512],
                                                                                                 start=(ch == 0), stop=(ch == NCH - 1)),
                                 reads=[("WP", wb), ("ACTT", ch)], writes=[("ACC", acc)])
                wrelease()
            for b in range(2):
                for n2 in range(2):
                    acc = b * 2 + n2
                    P.op("dve", lambda e, b=b, n2=n2, acc=acc: e.tensor_tensor(out=XT[:, b, n2 * 512:(n2 + 1) * 512], in0=XT[:, b, n2 * 512:(n2 + 1) * 512], in1=ACCB[acc][:, :], op=ALU.add),
                         reads=[("ACC", acc), ("XT", par, b)], writes=[("XT", par, b)])
            for b in range(2):
                P.op("dve", lambda e, b=b: e.scalar_tensor_tensor(out=HB[:, b, :], in0=XT[:, b, :], scalar=1.0, in1=XT[:, b, :],
                                                                 op0=ALU.mult, op1=ALU.mult, accum_out=ST[:, 32 + b:33 + b]),
                     reads=[("XT", par, b)], writes=[("HB", par, b), ("ST", 32 + b)])
                P.op("act", lambda e, b=b: e.activation(out=ST[:, 34 + b:35 + b], in_=ST[:, 32 + b:33 + b], func=AF.Ln, bias=EPSC[:, 0:1], scale=1.0 / D),
                     reads=[("ST", 32 + b), "EPSC"], writes=[("ST", 34 + b)])
                P.op("act", lambda e, b=b: e.activation(out=ST[:, 36 + b:37 + b], in_=ST[:, 34 + b:35 + b], func=AF.Exp, scale=-0.5),
                     reads=[("ST", 34 + b)], writes=[("ST", 36 + b)])
                P.op("dve", lambda e, b=b: e.scalar_tensor_tensor(out=OUTF[:, b, :], in0=XT[:, b, :], scalar=ST[:, 36 + b:37 + b], in1=GFIN[:, :], op0=ALU.mult, op1=ALU.mult),
                     reads=[("XT", par, b), ("ST", 36 + b), "GFIN"], writes=ARENA_ALL)
            if gi >= 1:
                r0 = (gi - 1) * G
                dma(out_d[r0:r0 + G, :].rearrange("(b p) f -> p b f", p=128), OUTF, ARENA_ALL, [("out", gi)])

        dbg_g = 1
        first = True
        for ci in range(NGRP):
            if stage < 3 or (ci < FIRST_FULL and ci >= n_ctx):
                continue
            if ci >= FIRST_FULL and (ci - FIRST_FULL) >= n_full:
                break
            if ci not in front_done:
                emit_front(ci)
            emit_group(ci)
        fin = [("out", gi) for gi in range(1, n_full)]
        if dbg:
            fin += ["dq", "dtmp2", "dck", "dx1"]
        P.op("sp", None, reads=fin)
        P.emit()
        print("ops:", len(P.ops), "signals:", P.stats)
    return nc


def make_par(inputs, core):
    par = np.zeros((128, NPAR), np.float32)

    def put(name, arr):
        a, b = PC[name]
        par[:, a:b] = np.asarray(arr, np.float32).reshape(128, b - a)

    put("eye", np.eye(128, dtype=np.float32))
    s = np.arange(128)
    put("tri", (s[:, None] <= s[None, :]).astype(np.float32))
    sel = np.zeros((128, 128), np.float32); sel[127, :] = 1.0
    put("sel127", sel)
    sel = np.zeros((128, 128), np.float32); sel[64, :] = 1.0
    put("selA", sel)
    sel = np.zeros((128, 128), np.float32); sel[63, :] = 1.0
    put("selB", sel)
    tri_mask = np.where(s[:, None] <= s[None, :], 0.0, NEG).astype(np.float32)
    cm0 = np.concatenate([tri_mask, np.zeros((128, 128), np.float32)], axis=1)
    cm1 = np.concatenate([np.full((128, 128), NEG, np.float32), tri_mask], axis=1)
    put("cm0", cm0)
    put("cm1", cm1)
    put("fb", np.broadcast_to(inputs["f_bias"].reshape(1, 8), (128, 8)))
    put("lng", np.broadcast_to(inputs["sg_ln_g"].reshape(1, 512), (128, 512)))
    put("gfin", np.broadcast_to(inputs["norm_final_g"].reshape(1, D), (128, D)))
    put("g1T", inputs["norm_mix_g"].reshape(8, 128).T)
    put("g2T", inputs["norm_ffn_g"].reshape(8, 128).T)
    wc = inputs["w_conv"].reshape(3, 2 * NCH, 128)
    put("wc", np.transpose(wc, (2, 1, 0)))
    put("bc", inputs["b_conv"].reshape(2 * NCH, 128).T)
    put("bsT", inputs["sg_b"].reshape(8, 128).T)
    km = np.zeros((128, NB), np.float32)
    if core % 2 == 0:
        km[:, 0:32] = NEG
    put("kmask", km)
    sgw = inputs["sg_w"].reshape(8, 128, 128)
    put("sgwT", np.transpose(sgw, (2, 0, 1)))
    return par


_NC_CACHE = {}


def kernel(**inputs):
    inputs = {k: np.asarray(v) for k, v in inputs.items()}
    x = inputs["x"].astype(np.float32, copy=False)
    key = "full"
    if key not in _NC_CACHE:
        _NC_CACHE[key] = build_nc()
    nc = _NC_CACHE[key]
    in_maps = []
    for c in range(8):
        b, half = c // 2, c % 2
        if half == 0:
            xc = np.concatenate([np.zeros((4096, D), np.float32), x[b, 0:4096]], axis=0)
        else:
            xc = x[b]
        in_maps.append({
            "x": np.ascontiguousarray(xc),
            "par": make_par(inputs, c),
            "w_in": np.ascontiguousarray(inputs["w_in"][0]),
            "w_out": np.ascontiguousarray(inputs["w_out"][0]),
            "w_up": np.ascontiguousarray(inputs["w_up"][0]),
            "w_down": np.ascontiguousarray(inputs["w_down"][0]),
        })
    res = run_bass_kernel_spmd(nc, in_maps, core_ids=list(range(8)))
    out = np.empty((4, SEQ, D), np.float32)
    for c in range(8):
        b, half = c // 2, c % 2
        out[b, half * 4096:(half + 1) * 4096] = res.results[c]["out"]
    return out
```

```python
import contextlib
import numpy as np
import concourse.bass as bass
import concourse.mybir as mybir
from concourse.bass_utils import run_bass_kernel_spmd

F32 = mybir.dt.float32
BF16 = mybir.dt.bfloat16
AF = mybir.ActivationFunctionType
ALU = mybir.AluOpType
AX = mybir.AxisListType

SAME_ENGINE_SYNC = True
NDMASEM = 8

D = 1024
SEQ = 8192
NB = 64
G = 256
NGRP = 32
FIRST_FULL = 15
DFF = 2816
NCH = 22
EPS = 1e-6
NEG = -30000.0

PC = {}
_off = 0
for _n, _w in [("eye", 128), ("tri", 128), ("sel127", 128), ("selA", 128), ("selB", 128),
               ("cm0", 256), ("cm1", 256), ("fb", 8), ("lng", 512), ("gfin", 1024),
               ("g1T", 8), ("g2T", 8), ("wc", 132), ("bc", 44), ("bsT", 8), ("kmask", 64),
               ("sgwT", 1024)]:
    PC[_n] = (_off, _off + _w)
    _off += _w
NPAR = _off

S_WIN, S_WOUT, S_WUP, S_WDN = 0, 10, 14, 36
NSLOT = 47


class Op:
    __slots__ = ("eng", "fn", "deps", "idx", "signaled", "ev", "is_dma", "dq", "dslot")

    def __init__(self, eng, fn, idx, is_dma):
        self.eng = eng
        self.fn = fn
        self.idx = idx
        self.deps = []
        self.signaled = False
        self.ev = None
        self.is_dma = is_dma


class Prog:
    ENGS = ("sp", "act", "dve", "pool", "pe")

    def __init__(self, nc):
        self.nc = nc
        self.ops = []
        self.last_writer = {}
        self.readers = {}
        self.dma_count = {e: 0 for e in self.ENGS}

    def op(self, eng, fn, reads=(), writes=(), dma=False):
        o = Op(eng, fn, len(self.ops), dma)
        deps = {}
        for r in reads:
            w = self.last_writer.get(r)
            if w is not None:
                deps[w.idx] = w
        for r in writes:
            w = self.last_writer.get(r)
            if w is not None:
                deps[w.idx] = w
            for rd in self.readers.get(r, ()):
                deps[rd.idx] = rd
        o.deps = list(deps.values())
        for r in reads:
            self.readers.setdefault(r, []).append(o)
        for r in writes:
            self.last_writer[r] = o
            self.readers[r] = []
        if dma:
            o.dq = eng
            o.dslot = self.dma_count[eng]
            self.dma_count[eng] += 1
        self.ops.append(o)
        return o

    def emit(self):
        nc = self.nc
        ops = self.ops

        def skip(d, o):
            return (not d.is_dma) and (not o.is_dma) and d.eng == o.eng and (d.eng == "pe" or not SAME_ENGINE_SYNC)

        for o in ops:
            for d in o.deps:
                if d.is_dma or skip(d, o):
                    continue
                d.signaled = True
        with contextlib.ExitStack() as es:
            esem = {e: es.enter_context(nc.semaphore("c_" + e)) for e in self.ENGS}
            dsem = {}
            for e in self.ENGS:
                if self.dma_count[e] > 0:
                    dsem[e] = [es.enter_context(nc.semaphore("d_%s_%d" % (e, i))) for i in range(NDMASEM)]
            cnt = {e: 0 for e in self.ENGS}
            for o in ops:
                if o.is_dma:
                    s = dsem[o.dq][o.dslot % NDMASEM]
                    o.ev = (s, 16 * (o.dslot // NDMASEM + 1))
                elif o.signaled:
                    cnt[o.eng] += 1
                    o.ev = (esem[o.eng], cnt[o.eng])
            self.stats = dict(cnt)
            block = es.enter_context(nc.Block())

            def body(engname, eng):
                seen = {}
                for o in ops:
                    if o.eng != engname:
                        continue
                    waits = []
                    for d in o.deps:
                        if d.ev is None or skip(d, o):
                            continue
                        waits.append(d.ev)
                    if o.is_dma and o.dslot >= NDMASEM:
                        s = dsem[o.dq][o.dslot % NDMASEM]
                        waits.append((s, 16 * (o.dslot // NDMASEM)))
                    mx = {}
                    for (s, v) in waits:
                        k = id(s)
                        if k not in mx or mx[k][1] < v:
                            mx[k] = (s, v)
                    for k, (s, v) in mx.items():
                        if seen.get(k, 0) >= v:
                            continue
                        seen[k] = v
                        eng.wait_ge(s, v)
                    if o.fn is None:
                        continue
                    inst = o.fn(eng)
                    if o.is_dma:
                        inst.then_inc(o.ev[0], 16)
                    elif o.signaled:
                        inst.then_inc(o.ev[0], 1)

            @block.sync
            def _(e):
                body("sp", e)

            @block.scalar
            def _(e):
                body("act", e)

            @block.vector
            def _(e):
                body("dve", e)

            @block.gpsimd
            def _(e):
                body("pool", e)

            @block.tensor
            def _(e):
                body("pe", e)


def build_nc(n_full=17, dbg=False, stage=99, n_ctx=FIRST_FULL):
    nc = bass.Bass("TRN2", target_bir_lowering=False)
    x_d = nc.dram_tensor("x", [SEQ, D], F32, kind="ExternalInput").ap()
    par_d = nc.dram_tensor("par", [128, NPAR], F32, kind="ExternalInput").ap()
    win_d = nc.dram_tensor("w_in", [D, 2568], F32, kind="ExternalInput").ap()
    wout_d = nc.dram_tensor("w_out", [D, D], F32, kind="ExternalInput").ap()
    wup_d = nc.dram_tensor("w_up", [D, 2 * DFF], F32, kind="ExternalInput").ap()
    wdn_d = nc.dram_tensor("w_down", [DFF, D], F32, kind="ExternalInput").ap()
    out_d = nc.dram_tensor("out", [4096, D], F32, kind="ExternalOutput").ap()
    wsc = nc.dram_tensor("wsc", [NSLOT, 128, 2048], BF16, kind="Internal").ap()
    ksc = nc.dram_tensor("ksc", [4, 128, SEQ], BF16, kind="Internal").ap()
    dbg_d = {}
    if dbg:
        for nm, shp in [("d_x1", [256, D]), ("d_outT", [128, 8 * 256]), ("d_ckb", [128, 64 * 8]),
                        ("d_qT", [128, 4 * 256]), ("d_oa", [128, 2 * 512])]:
            dbg_d[nm] = nc.dram_tensor(nm, shp, F32, kind="ExternalOutput").ap()

    with contextlib.ExitStack() as es:
        def T(name, shape, dt):
            return es.enter_context(nc.sbuf_tensor(name, shape, dt))

        def PS(name, shape, dt):
            return es.enter_context(nc.psum_tensor(name, shape, dt))

        P = Prog(nc)
        KSTG = T("KSTG", [128, 4, G], BF16)
        KCH = [T("KCH%d" % i, [128, 1024], BF16) for i in range(3)]
        V = T("V", [128, NB, 4, 132], BF16)
        XTs = [T("XT%d" % i, [128, 2, D], F32) for i in range(2)]
        XT = XTs[0]
        HBs = [T("HB%d" % i, [128, 2, D], BF16) for i in range(2)]
        HB = HBs[0]
        HT = T("HT", [128, 8, G], BF16)
        FA = T("FA", [128, 2560], F32)
        QT = FA[:, 0:512].bitcast(BF16).rearrange("p (a t) -> p a t", a=4)
        NPT = 6
        PT = [FA[:, 512 + 256 * i:512 + 256 * (i + 1)].bitcast(BF16) for i in range(NPT)]
        OSB = [FA[:, 2048 + 256 * i:2048 + 256 * (i + 1)] for i in range(2)]
        AGV = [FA[:, 516 * i:516 * (i + 1)].rearrange("p (a c) -> p a c", a=2) for i in range(2)]
        YGV = [FA[:, 1032 + 512 * i:1032 + 512 * (i + 1)].rearrange("p (a c) -> p a c", a=2) for i in range(2)]
        ARENA = T("ARENA", [128, 3072], F32)
        OUTF = ARENA[:, 0:2048].rearrange("p (b f) -> p b f", b=2)
        U = ARENA[:, 0:1024].rearrange("p (b f) -> p b f", b=2)
        VA = ARENA[:, 1024:2048].rearrange("p (b f) -> p b f", b=2)
        ABF = ARENA[:].bitcast(BF16)
        VN = ABF[:, 4096:5120].rearrange("p (b f) -> p b f", b=2)
        OA = ABF[:, 5120:6144].rearrange("p (b f) -> p b f", b=2)
        ACTT = ABF[:, 0:NCH * G].rearrange("p (c t) -> p c t", c=NCH)
        OUTT = HT
        NWP = 10
        NPIN = 4
        WP = [T("WP%d" % i, [128, 2048], BF16) for i in range(NWP + NPIN)]
        RBT = T("RBT", [128, 512], F32)
        RB = [RBT[:, i * G:(i + 1) * G] for i in range(2)]
        SQ = RBT
        CKB = T("CKB", [128, NB, 8], F32)
        WG = [T("WG%d" % i, [128, NB], F32) for i in range(2)]
        NRG = T("NRG", [128, 8], F32)
        VP = [[T("VP%d_%d" % (e_, i), [128, 8, 128], BF16) for i in range(2)] for e_ in range(2)]
        WF = T("WF", [128, 8, 8], BF16)
        CARRY = T("CARRY", [128, NCH, 2, 2], F32)
        ident = T("ident", [128, 128], BF16)
        CMASK = T("CMASK", [128, 2, G], BF16)
        WMT = T("WMT", [128, 8, 128], BF16)
        TRI = T("TRI", [128, 128], F32)
        SEL127 = T("SEL127", [128, 128], F32)
        SELA = T("SELA", [128, 128], F32)
        SELB = T("SELB", [128, 128], F32)
        FB = T("FB", [128, 8], F32)
        LNG = T("LNG", [128, 512], F32)
        GFIN = T("GFIN", [128, D], F32)
        GT = T("GT", [128, 16], F32)
        WC = T("WC", [128, 2 * NCH, 3], F32)
        BC = T("BC", [128, 2 * NCH], F32)
        BST = T("BST", [128, 8], F32)
        KM = T("KM", [128, NB], F32)
        ST = T("ST", [128, 64], F32)
        CCAR = T("CCAR", [128, 8], F32)
        RG = T("RG", [128, 8], F32)
        CB = T("CB", [128, 8], F32)
        SP_ = T("SP_", [128, 8], F32)
        EPSC = T("EPSC", [128, 2], F32)
        PGB = [PS("PGB%d" % i, [128, 512], F32) for i in range(3)]
        ACCB = [PS("ACCB%d" % i, [128, 512], F32) for i in range(4)]
        PSTB = PS("PSTB", [128, 1024], BF16)
        print("sbuf bytes remaining:", nc.sbuf_bytes_remaining)

        rot = {"pg": 0, "pt": 0}
        att_rot = [0]
        kch_rot = [0]
        PSTF = PSTB[:, :].bitcast(F32)

        def next_pg():
            i = rot["pg"] % 3
            rot["pg"] += 1
            return i

        def next_pt():
            i = rot["pt"] % NPT
            rot["pt"] += 1
            return i

        def pc(name):
            a, b = PC[name]
            return par_d[:, a:b]

        def dma(out, in_, reads, writes):
            return P.op("sp", lambda e: e.dma_start(out=out, in_=in_), reads=reads, writes=writes, dma=True)

        XT_ALL0 = [("XT", 0, 0), ("XT", 0, 1)]
        ARENA_ALL = ["U", "VA", "VN", "OA"] + [("ACTT", c) for c in range(NCH)]

        dma(TRI[:], pc("tri"), [], ["TRI"])
        dma(SEL127[:], pc("sel127"), [], ["SEL127"])
        dma(SELA[:], pc("selA"), [], ["SELA"])
        dma(SELB[:], pc("selB"), [], ["SELB"])
        dma(FB[:], pc("fb"), [], ["FB"])
        dma(LNG[:], pc("lng"), [], ["LNG"])
        dma(GFIN[:], pc("gfin"), [], ["GFIN"])
        dma(GT[:, 0:8], pc("g1T"), [], ["GT"])
        dma(GT[:, 8:16], pc("g2T"), [], ["GT"])
        dma(WC[:].rearrange("p c j -> p (c j)"), pc("wc"), [], ["WC"])
        dma(BC[:], pc("bc"), [], ["BC"])
        dma(BST[:], pc("bsT"), [], ["BST"])
        dma(KM[:], pc("kmask"), [], ["KM"])
        XTf = XT[:].rearrange("p b f -> p (b f)")
        HBf = HB[:].rearrange("p b f -> p (b f)")
        HTf = HT[:].rearrange("p k t -> p (k t)")
        a0 = PC["eye"][0]
        dma(XTf[:, 0:128], pc("eye"), [], XT_ALL0)
        dma(XTf[:, 128:640], par_d[:, PC["cm0"][0]:PC["cm1"][1]], [], XT_ALL0)
        dma(XTf[:, 640:1664], pc("sgwT"), [], XT_ALL0)
        P.op("dve", lambda e: e.tensor_copy(out=ident[:], in_=XTf[:, 0:128]), reads=XT_ALL0, writes=["ident"])
        P.op("dve", lambda e: e.tensor_copy(out=CMASK[:].rearrange("p a t -> p (a t)"), in_=XTf[:, 128:640]), reads=XT_ALL0, writes=["CMASK"])
        P.op("dve", lambda e: e.memset(XTf[64:128, 640:1664].rearrange("p (h t) -> p h t", h=8)[:, :, 0:64], 0.0), reads=[], writes=XT_ALL0)
        P.op("dve", lambda e: e.tensor_copy(out=WMT[:].rearrange("p h t -> p (h t)"), in_=XTf[:, 640:1664]), reads=XT_ALL0, writes=["WMT"])
        P.op("dve", lambda e: e.memset(EPSC[:, 0:1], EPS), writes=["EPSC"])
        P.op("dve", lambda e: e.memset(EPSC[:, 1:2], 1.0), writes=["EPSC"])
        P.op("dve", lambda e: e.memset(CCAR[:], 0.0), writes=["CCAR"])
        P.op("dve", lambda e: e.memset(CARRY[:].rearrange("p c a j -> p (c a j)"), 0.0), writes=[("CARRY", jq) for jq in range(NCH)])
        import os
        if os.environ.get("NOVMEM") != "1":
            P.op("dve", lambda e: e.memset(V[:].rearrange("p b a c -> p (b a) c")[:, :, 64:66], 1.0), writes=[("V", b, s) for b in range(NB) for s in range(2)])

        NCONV = 99 if stage >= 2 else 0
        HB_ALL = [("HB", 0, 0), ("HB", 0, 1)]
        HT_ALL = [("HT", 0), ("HT", 1)]
        stage_f = [(XTf, XT_ALL0), (ARENA[:, 0:2048], ARENA_ALL)]
        stage_b = [(HBf, HB_ALL), (HTf, HT_ALL)]
        conv_engs = ["dve", "act"]
        cnt = [0]

        def convert(slot, pieces, gain_col, bg=False):
            if cnt[0] >= NCONV:
                return
            i = cnt[0] % 2
            cnt[0] += 1
            if bg:
                sf, sfn = ARENA[:, 0:2048], ["cvF"]
                sb, sbn = ABF[:, 4096:6144], ["cvB"]
            else:
                sf, sfn = stage_f[i]
                sb, sbn = stage_b[i]
            wr = list(sfn)
            for (dst, src) in pieces:
                dma(dst(sf), src, [], wr)
            eng = conv_engs[i]
            if gain_col is None:
                if eng == "dve":
                    P.op(eng, lambda e: e.tensor_copy(out=sb[:, :], in_=sf[:, :]), reads=wr, writes=list(sbn))
                else:
                    P.op(eng, lambda e: e.copy(out=sb[:, :], in_=sf[:, :]), reads=wr, writes=list(sbn))
            else:
                for k in range(8):
                    if eng == "dve":
                        P.op(eng, lambda e, k=k: e.tensor_scalar(out=sb[:, k * 256:(k + 1) * 256], in0=sf[:, k * 256:(k + 1) * 256],
                                                               scalar1=GT[:, gain_col + k:gain_col + k + 1], scalar2=None, op0=ALU.mult),
                             reads=wr + ["GT"], writes=list(sbn))
                    else:
                        P.op(eng, lambda e, k=k: e.mul(out=sb[:, k * 256:(k + 1) * 256], in_=sf[:, k * 256:(k + 1) * 256],
                                                     mul=GT[:, gain_col + k:gain_col + k + 1]),
                             reads=wr + ["GT"], writes=list(sbn))
            out_fn = lambda: dma(wsc[slot], sb[:, :], list(sbn), [("wsc", slot)])
            if bg:
                return out_fn
            out_fn()
            return None

        def v3(sf, a, b):
            return sf[:, 0:2048].rearrange("p (a b) -> p a b", a=a)

        conv_jobs = {}
        for s in range(10):
            conv_jobs[S_WIN + s] = ([(lambda sf: v3(sf, 8, 256), win_d[:, s * 256:(s + 1) * 256].rearrange("(k p) c -> p k c", p=128))], 0)
        for s in range(4):
            conv_jobs[S_WOUT + s] = ([(lambda sf: v3(sf, 8, 256), wout_d[:, s * 256:(s + 1) * 256].rearrange("(k p) c -> p k c", p=128))], None)
        for j in range(NCH):
            conv_jobs[S_WUP + j] = ([(lambda sf: v3(sf, 8, 256)[:, :, 0:128], wup_d[:, j * 128:(j + 1) * 128].rearrange("(k p) c -> p k c", p=128)),
                                     (lambda sf: v3(sf, 8, 256)[:, :, 128:256], wup_d[:, DFF + j * 128:DFF + (j + 1) * 128].rearrange("(k p) c -> p k c", p=128))], 8)
        for i in range(11):
            conv_jobs[S_WDN + i] = ([(lambda sf: v3(sf, 2, 1024), wdn_d[i * 256:(i + 1) * 256, :].rearrange("(c p) n -> p c n", p=128))], None)
        for slot in (6, 7, 8, 9):
            convert(slot, *conv_jobs[slot])
        dma(XTf[:, 0:64].rearrange("p (k c) -> p k c", k=8), win_d[:, 2560:2568].rearrange("(k p) c -> p k c", p=128), [], XT_ALL0)
        for k in range(8):
            P.op("dve", lambda e, k=k: e.tensor_scalar(out=WF[:, k, :], in0=XTf[:, k * 8:(k + 1) * 8], scalar1=GT[:, k:k + 1], scalar2=None, op0=ALU.mult),
                 reads=XT_ALL0 + ["GT"], writes=["WF"])
        P.op("dve", lambda e: e.memset(ST[:, 60:61], 0.0), reads=[], writes=ARENA_ALL + ["cvF", "cvB"])
        bg_slots = [sl for sl in list(range(0, 6)) + list(range(10, NSLOT))]
        bg_state = {"i": 0, "pending_out": None}

        def conv_step():
            if bg_state["pending_out"] is not None:
                bg_state["pending_out"]()
                bg_state["pending_out"] = None
            if stage < 2 or bg_state["i"] >= len(bg_slots):
                return
            slot = bg_slots[bg_state["i"]]
            bg_state["i"] += 1
            bg_state["pending_out"] = convert(slot, *conv_jobs[slot], bg=True)

        def conv_flush():
            while bg_state["i"] < len(bg_slots) or bg_state["pending_out"] is not None:
                conv_step()
            P.op("dve", lambda e: e.memset(ST[:, 61:62], 0.0), reads=[], writes=ARENA_ALL + ["cvF", "cvB"])

        wseq = []
        for ci in range(NGRP):
            full = ci >= FIRST_FULL
            if stage < 3 or (not full and ci >= n_ctx):
                continue
            if full and (ci - FIRST_FULL) >= n_full:
                break
            if full:
                wseq += list(range(0, 6)) + list(range(10, 14)) + list(range(14, 36)) + list(range(36, 47))
        wstate = {"use": 0, "load": 0}
        if stage >= 3:
            for sl in range(6, 10):
                dma(WP[NWP + sl - 6][:, :], wsc[sl], [("wsc", sl)], [("WP", NWP + sl - 6)])

        def wload_next():
            n = wstate["load"]
            if n >= len(wseq):
                return
            slot = wseq[n]
            buf = n % NWP
            wstate["load"] += 1
            dma(WP[buf][:, :], wsc[slot], [("wsc", slot)], [("WP", buf)])

        def wacquire(slot):
            if 6 <= slot < 10:
                return NWP + slot - 6
            n = wstate["use"]
            assert wseq[n] == slot, (n, wseq[n], slot)
            return n % NWP

        def wrelease(pinned=False):
            if pinned:
                return
            wstate["use"] += 1
            wload_next()


        evac_rr = [0]

        def rms_front(XT, HB, par, sc):
            for b in range(2):
                P.op("dve", lambda e, b=b: e.scalar_tensor_tensor(out=HB[:, b, :], in0=XT[:, b, :], scalar=1.0, in1=XT[:, b, :],
                                                                 op0=ALU.mult, op1=ALU.mult, accum_out=ST[:, sc + b:sc + b + 1]),
                     reads=[("XT", par, b)], writes=[("HB", par, b), ("ST", sc + b)])
                P.op("act", lambda e, b=b: e.activation(out=ST[:, sc + 2 + b:sc + 3 + b], in_=ST[:, sc + b:sc + b + 1], func=AF.Ln, bias=EPSC[:, 0:1], scale=1.0 / D),
                     reads=[("ST", sc + b), "EPSC"], writes=[("ST", sc + 2 + b)])
                P.op("act", lambda e, b=b: e.activation(out=ST[:, sc + 4 + b:sc + 5 + b], in_=ST[:, sc + 2 + b:sc + 3 + b], func=AF.Exp, scale=-0.5),
                     reads=[("ST", sc + 2 + b)], writes=[("ST", sc + 4 + b)])
                P.op("dve", lambda e, b=b: e.tensor_scalar(out=HB[:, b, :], in0=XT[:, b, :], scalar1=ST[:, sc + 4 + b:sc + 5 + b], scalar2=None, op0=ALU.mult),
                     reads=[("XT", par, b), ("ST", sc + 4 + b)], writes=[("HB", par, b)])

        def rms_back(HB, par):
            for b in range(2):
                for k in range(8):
                    P.op("pe", lambda e, b=b, k=k: e.transpose(out=PSTB[:, k * 128:(k + 1) * 128], in_=HB[:, b, k * 128:(k + 1) * 128], identity=ident[:]),
                         reads=[("HB", par, b), "ident"], writes=["PST"])
                P.op("act", lambda e, b=b: e.copy(out=HT[:, :, b * 128:(b + 1) * 128], in_=PSTB[:, :].rearrange("p (k t) -> p k t", k=8)),
                     reads=["PST"], writes=[("HT", b)] + [("OUTT", kq) for kq in range(8)])

        front_done = set()

        def emit_front(ci):
            front_done.add(ci)
            par = ci % 2
            T0 = ci * G
            dma(XTs[par][:], x_d[T0:T0 + G, :].rearrange("(b p) f -> p b f", p=128), [], [("XT", par, 0), ("XT", par, 1)])
            rms_front(XTs[par], HBs[par], par, 40)

        def HT_reads(b=None):
            if b is None:
                return [("HT", 0), ("HT", 1)]
            return [("HT", b)]

        def mm_feat(pi, hf, wbuf, c0):
            for k in range(8):
                P.op("pe", lambda e, k=k: e.matmul(PGB[pi][:, hf * G:(hf + 1) * G], lhsT=WP[wbuf][:, k * 256 + c0:k * 256 + c0 + 128], rhs=HT[:, k, :],
                                                  start=(k == 0), stop=(k == 7)),
                     reads=[("WP", wbuf)] + HT_reads(), writes=[("PG", pi)])

        def mm_tok(pi, hf, wbuf, b, lhs, lhs_reads):
            for k in range(8):
                P.op("pe", lambda e, k=k: e.matmul(PGB[pi][:, hf * G:(hf + 1) * G], lhsT=lhs[:, k, b * 128:(b + 1) * 128], rhs=WP[wbuf][:, k * 256:(k + 1) * 256],
                                                  start=(k == 0), stop=(k == 7)),
                     reads=[("WP", wbuf)] + lhs_reads, writes=[("PG", pi)])

        def emit_group(ci):
            full = ci >= FIRST_FULL
            gi = ci - FIRST_FULL
            b0 = 2 * ci
            T0 = ci * G
            par = ci % 2
            XT = XTs[par]
            HB = HBs[par]
            if gi == 0:
                conv_flush()
                for _ in range(NWP):
                    wload_next()
            if not full:
                conv_step()
            rms_back(HB, par)
            nxt = ci + 1
            has_next = (nxt < NGRP) and not (nxt >= FIRST_FULL and (nxt - FIRST_FULL) >= n_full) and not (stage < 3) and not (nxt < FIRST_FULL and nxt >= n_ctx)
            if not full and has_next:
                emit_front(nxt)
            if full:
                for s in range(4):
                    wb = wacquire(s)
                    dst, dn = (U, "U") if s < 2 else (VA, "VA")
                    hf = s % 2
                    pi = next_pg()
                    for b in range(2):
                        mm_tok(pi, b, wb, b, HT, HT_reads(b))
                    P.op("act", lambda e, pi=pi, dst=dst, hf=hf: e.activation(out=dst[:, :, hf * 256:(hf + 1) * 256], in_=PGB[pi][:, :].rearrange("p (b c) -> p b c", b=2), func=AF.Gelu),
                         reads=[("PG", pi)], writes=[dn] + [("ACTT", c) for c in range(NCH)])
                    wrelease()
                for s in range(2):
                    wb = wacquire(4 + s)
                    pi = next_pg()
                    for m in range(2):
                        mm_feat(pi, m, wb, m * 128)
                    P.op("act", lambda e, pi=pi, s=s: e.mul(out=QT[:, 2 * s:2 * s + 2, :].rearrange("p a t -> p (a t)"), in_=PGB[pi][:, :], mul=0.125),
                         reads=[("PG", pi)], writes=[("QT", 2 * s), ("QT", 2 * s + 1)])
                    wrelease()
            for s in range(2):
                wb = wacquire(6 + s)
                pi = next_pg()
                for m in range(2):
                    mm_feat(pi, m, wb, m * 128)
                for m in range(2):
                    hp = 2 * s + m
                    P.op("act", lambda e, pi=pi, hp=hp, m=m: e.copy(out=KSTG[:, hp, :], in_=PGB[pi][:, m * G:(m + 1) * G]),
                         reads=[("PG", pi)], writes=[("KSTG", hp)])
                wrelease(pinned=True)
            dma(ksc[:, :, T0:T0 + G].rearrange("h p t -> p h t"), KSTG[:, :, :], [("KSTG", hp) for hp in range(4)],
                [("KD", hp, bb) for hp in range(4) for bb in (b0, b0 + 1)])
            if not full:
                conv_step()
            for s in range(2):
                wb = wacquire(8 + s)
                pi = next_pg()
                for b in range(2):
                    mm_tok(pi, b, wb, b, HT, HT_reads(b))
                for b in range(2):
                    for a in range(2):
                        P.op("dve", lambda e, b=b, pi=pi, s=s, a=a: e.tensor_copy(
                            out=V[:, b0 + b, 2 * s + a, :].rearrange("p (e c) -> p e c", e=2)[:, :, 0:64],
                            in_=PGB[pi][:, b * G + a * 128:b * G + (a + 1) * 128].rearrange("p (e c) -> p e c", e=2)),
                             reads=[("PG", pi)], writes=[("V", b0 + b, s)])
                wrelease(pinned=True)
            if not full:
                conv_step()
            for b in range(2):
                pi = next_pg()
                for k in range(8):
                    P.op("pe", lambda e, k=k, b=b, pi=pi: e.matmul(PGB[pi][:, 0:8], lhsT=HT[:, k, b * 128:(b + 1) * 128], rhs=WF[:, k, :], start=(k == 0), stop=(k == 7)),
                         reads=["WF"] + HT_reads(b), writes=[("PG", pi)])
                P.op("dve", lambda e, pi=pi: e.tensor_tensor(out=SP_[:, :], in0=PGB[pi][:, 0:8], in1=FB[:, :], op=ALU.add),
                     reads=[("PG", pi), "FB"], writes=["SP_"])
                P.op("act", lambda e: e.activation(out=SP_[:, :], in_=SP_[:, :], func=AF.Exp, scale=-1.0), reads=["SP_"], writes=["SP_"])
                P.op("act", lambda e: e.activation(out=SP_[:, :], in_=SP_[:, :], func=AF.Ln, bias=EPSC[:, 1:2], scale=1.0), reads=["SP_", "EPSC"], writes=["SP_"])
                pi2 = next_pg()
                P.op("pe", lambda e, pi2=pi2: e.matmul(PGB[pi2][:, 0:8], lhsT=TRI[:, :], rhs=SP_[:, :], start=True, stop=True),
                     reads=["TRI", "SP_"], writes=[("PG", pi2)])
                P.op("dve", lambda e, pi2=pi2: e.tensor_tensor(out=CB[:, :], in0=PGB[pi2][:, 0:8], in1=CCAR[:, :], op=ALU.add),
                     reads=[("PG", pi2), "CCAR"], writes=["CB"])
                P.op("dve", lambda e, b=b: e.tensor_scalar(out=CKB[:, b0 + b, :], in0=CB[:, :], scalar1=KM[:, b0 + b:b0 + b + 1], scalar2=None, op0=ALU.add),
                     reads=["CB", "KM"], writes=[("CKB", b0 + b)])
                pi3 = next_pg()
                P.op("pe", lambda e, pi3=pi3: e.matmul(PGB[pi3][:, 0:8], lhsT=SEL127[:, :], rhs=CB[:, :], start=True, stop=True),
                     reads=["SEL127", "CB"], writes=[("PG", pi3)])
                P.op("dve", lambda e, pi3=pi3: e.tensor_copy(out=CCAR[:, :], in_=PGB[pi3][:, 0:8]), reads=[("PG", pi3)], writes=["CCAR"])
                if b == 0 and full:
                    P.op("dve", lambda e, pi3=pi3: e.tensor_scalar(out=NRG[:, :], in0=PGB[pi3][:, 0:8], scalar1=-1.0, scalar2=None, op0=ALU.mult), reads=[("PG", pi3)], writes=["RG"])
            if not full:
                return
            if dbg and gi == dbg_g:
                dma(dbg_d["d_qT"], QT[:].rearrange("p a t -> p (a t)"), [("QT", h) for h in range(4)], ["dq"])
            def emit_sg():
                SQR = [("RB", 0), ("RB", 1)]
                for b in range(2):
                    va3 = VA[:, b, :].rearrange("p (h d) -> p h d", h=8)
                    P.op("dve", lambda e, va3=va3: e.tensor_reduce(out=ST[:, 8:16], in_=va3, axis=AX.X, op=ALU.add), reads=["VA"], writes=[("ST", "s1")])
                    P.op("dve", lambda e, b=b: e.tensor_tensor(out=SQ[:, :], in0=VA[:, b, :], in1=VA[:, b, :], op=ALU.mult), reads=["VA"], writes=SQR)
                    P.op("dve", lambda e: e.tensor_reduce(out=ST[:, 16:24], in_=SQ[:, :].rearrange("p (h d) -> p h d", h=8), axis=AX.X, op=ALU.add),
                         reads=SQR, writes=[("ST", "s2")])
                    P.op("dve", lambda e: e.tensor_scalar(out=ST[:, 8:16], in0=ST[:, 8:16], scalar1=1.0 / 64, scalar2=None, op0=ALU.mult),
                         reads=[("ST", "s1")], writes=[("ST", "s1")])
                    P.op("dve", lambda e: e.tensor_tensor(out=ST[:, 24:32], in0=ST[:, 8:16], in1=ST[:, 8:16], op=ALU.mult),
                         reads=[("ST", "s1")], writes=[("ST", "msq")])
                    P.op("dve", lambda e: e.scalar_tensor_tensor(out=ST[:, 16:24], in0=ST[:, 16:24], scalar=1.0 / 64, in1=ST[:, 24:32], op0=ALU.mult, op1=ALU.subtract),
                         reads=[("ST", "s2"), ("ST", "msq")], writes=[("ST", "s2")])
                    P.op("act", lambda e: e.activation(out=ST[:, 16:24], in_=ST[:, 16:24], func=AF.Ln, bias=EPSC[:, 0:1], scale=1.0),
                         reads=[("ST", "s2"), "EPSC"], writes=[("ST", "s2")])
                    P.op("act", lambda e: e.activation(out=ST[:, 16:24], in_=ST[:, 16:24], func=AF.Exp, scale=-0.5),
                         reads=[("ST", "s2")], writes=[("ST", "s2")])
                    P.op("dve", lambda e, va3=va3: e.tensor_tensor(out=va3, in0=va3, in1=ST[:, 8:16].unsqueeze(2).to_broadcast([128, 8, 64]), op=ALU.subtract),
                         reads=["VA", ("ST", "s1")], writes=["VA"])
                    P.op("dve", lambda e, va3=va3: e.tensor_tensor(out=va3, in0=va3, in1=ST[:, 16:24].unsqueeze(2).to_broadcast([128, 8, 64]), op=ALU.mult),
                         reads=["VA", ("ST", "s2")], writes=["VA"])
                    P.op("dve", lambda e, b=b: e.tensor_tensor(out=VN[:, b, :], in0=VA[:, b, :], in1=LNG[:, :], op=ALU.mult),
                         reads=["VA", "LNG"], writes=["VN"])
                    pi = next_pg()
                    for h in range(8):
                        P.op("pe", lambda e, h=h, b=b, pi=pi: e.matmul(PGB[pi][:, h * 64:(h + 1) * 64], lhsT=WMT[:, h, :], rhs=VN[:, b, h * 64:(h + 1) * 64],
                                                                      start=True, stop=True),
                             reads=["WMT", "VN"], writes=[("PG", pi)])
                    P.op("dve", lambda e, pi=pi: e.tensor_tensor(out=SQ[:, :].rearrange("p (h d) -> p h d", h=8),
                                                                 in0=PGB[pi][:, :].rearrange("p (h d) -> p h d", h=8),
                                                                 in1=BST[:, :].unsqueeze(2).to_broadcast([128, 8, 64]), op=ALU.add),
                         reads=[("PG", pi), "BST"], writes=SQR)
                    P.op("dve", lambda e, b=b: e.tensor_tensor(out=OA[:, b, :], in0=SQ[:, :], in1=U[:, b, :], op=ALU.mult),
                         reads=SQR + ["U"], writes=["OA"])
                    for kk in range(4):
                        P.op("pe", lambda e, b=b, kk=kk: e.transpose(out=PSTB[:, kk * 128:(kk + 1) * 128], in_=OA[:, b, kk * 128:(kk + 1) * 128], identity=ident[:]),
                             reads=["OA", "ident"], writes=["PST"])
                    P.op("act", lambda e, b=b: e.copy(out=OUTT[:, 0:4, b * 128:(b + 1) * 128], in_=PSTB[:, 0:512].rearrange("p (k t) -> p k t", k=4)),
                         reads=["PST"], writes=[("OUTT", k) for k in range(4)] + [("HT", 0), ("HT", 1)])

            nj = b0 + 2
            nchunk = (nj + 7) // 8

            def kload(hp, c):
                nblk = min(8, nj - 8 * c)
                kb = kch_rot[0] % 3
                kch_rot[0] += 1
                dma(KCH[kb][:, 0:nblk * 128], ksc[hp][:, c * 1024:c * 1024 + nblk * 128],
                    [("KD", hp, 8 * c + q) for q in range(nblk)], [("KCH", kb)])
                return kb

            for hp in range(4):
                if hp == 1:
                    emit_sg()
                accs = [(2 * hp) % 4, (2 * hp + 1) % 4]
                kbufs = {0: kload(hp, 0)}
                if nchunk > 1:
                    kbufs[1] = kload(hp, 1)
                for ee in range(2):
                    h = 2 * hp + ee
                    P.op("act", lambda e, h=h, ee=ee: e.activation(out=WG[ee][:, 0:nj], in_=CKB[:, 0:nj, h], func=AF.Exp, bias=NRG[:, h:h + 1], scale=1.0),
                         reads=[("CKB", j) for j in range(nj)] + ["RG"], writes=[("WG", ee)])

                def vprep(c, hp=hp):
                    nblk = min(8, nj - 8 * c)
                    cb = c % 2
                    for ee in range(2):
                        Kc = 65 if ee == 0 else 128
                        c0 = 0 if ee == 0 else 2
                        P.op("dve", lambda e, ee=ee, Kc=Kc, c0=c0, cb=cb, nblk=nblk, c=c: e.tensor_tensor(
                            out=VP[ee][cb][:, 0:nblk, 0:Kc], in0=V[:, 8 * c:8 * c + nblk, hp, c0:c0 + Kc],
                            in1=WG[ee][:, 8 * c:8 * c + nblk].unsqueeze(2).to_broadcast([128, nblk, Kc]), op=ALU.mult),
                             reads=[("V", 8 * c + q_, hp // 2) for q_ in range(nblk)] + [("WG", ee)], writes=[("VP", ee, cb)])

                vprep(0)
                if nchunk > 1:
                    vprep(1)

                def qk_step(j, hp=hp):
                    bp = att_rot[0] % 2
                    att_rot[0] += 1
                    banks = [(PGB[0], ("PG", 0)), (PGB[1], ("PG", 1))] if bp == 0 else [(PGB[2], ("PG", 2)), (PSTF, "PST")]
                    for q in range(2):
                        jj = j + q
                        diag = jj >= b0
                        for ee in range(2):
                            kr = slice(ee * 64, (ee + 1) * 64)
                            bk, bkey = banks[ee]
                            kb_ = kbufs[jj // 8]
                            jo = jj % 8
                            P.op("pe", lambda e, ee=ee, kr=kr, bk=bk, jj=jj, q=q, diag=diag, kb_=kb_, jo=jo: e.matmul(bk[:, q * G:(q + 1) * G], lhsT=KCH[kb_][kr, jo * 128:(jo + 1) * 128], rhs=QT[kr, hp, :],
                                                                                                  start=True, stop=not diag),
                                 reads=[("KCH", kb_), ("QT", hp)], writes=[bkey])
                        if diag:
                            for ee in range(2):
                                bk, bkey = banks[ee]
                                P.op("pe", lambda e, bk=bk, jj=jj, q=q: e.matmul(bk[:, q * G:(q + 1) * G], lhsT=ident[:, :], rhs=CMASK[:, jj - b0, :], start=False, stop=True),
                                     reads=["ident", "CMASK"], writes=[bkey])
                    return banks

                def exp_step(j, banks):
                    tis = []
                    for ee in range(2):
                        bk, bkey = banks[ee]
                        ti = next_pt()
                        P.op("act", lambda e, ti=ti, bk=bk: e.activation(out=PT[ti][:, :], in_=bk[:, :], func=AF.Exp, scale=1.0),
                             reads=[bkey], writes=[("PT", ti)])
                        tis.append(ti)
                    return tis

                def pv_step(j, tis, hp=hp, accs=accs):
                    for q in range(2):
                        jj = j + q
                        cb = (jj // 8) % 2
                        jo = jj % 8
                        t0_, t1_ = tis[0], tis[1]
                        P.op("pe", lambda e, jj=jj, cb=cb, jo=jo, t0_=t0_, q=q: e.matmul(ACCB[accs[0]][0:65, 0:G], lhsT=VP[0][cb][:, jo, 0:65], rhs=PT[t0_][:, q * G:(q + 1) * G], start=(jj == 0), stop=(jj == nj - 1)),
                             reads=[("VP", 0, cb), ("PT", t0_)], writes=[("ACC", accs[0])])
                        P.op("pe", lambda e, jj=jj, cb=cb, jo=jo, t1_=t1_, q=q: e.matmul(ACCB[accs[1]][:, 0:G], lhsT=VP[1][cb][:, jo, :], rhs=PT[t1_][:, q * G:(q + 1) * G], start=(jj == 0), stop=(jj == nj - 1)),
                             reads=[("VP", 1, cb), ("PT", t1_)], writes=[("ACC", accs[1])])

                pend = None
                for j in range(0, nj, 2):
                    if j % 8 == 0 and j > 0 and (j // 8 + 1) < nchunk:
                        kbufs[j // 8 + 1] = kload(hp, j // 8 + 1)
                    if j % 8 == 2 and j > 8 - 8 and (j // 8 + 1) < nchunk and (j // 8 + 1) >= 2:
                        vprep(j // 8 + 1)
                    banks = qk_step(j)
                    tis = exp_step(j, banks)
                    if pend is not None:
                        pv_step(*pend)
                    pend = (j, tis)
                pv_step(*pend)
                for ee in range(2):
                    h = 2 * hp + ee
                    acc = accs[ee]
                    kr = slice(ee * 64, (ee + 1) * 64)
                    oi = ee
                    K = 65 if ee == 0 else 128
                    sel = SELA if ee == 0 else SELB
                    seln = "SELA" if ee == 0 else "SELB"
                    P.op("act", lambda e, oi=oi, K=K, acc=acc: e.copy(out=OSB[oi][0:K, :], in_=ACCB[acc][0:K, 0:G]), reads=[("ACC", acc)], writes=[("OSB", oi)])
                    pi = next_pg()
                    P.op("pe", lambda e, oi=oi, K=K, sel=sel, pi=pi: e.matmul(PGB[pi][:, 0:G], lhsT=sel[0:K, :], rhs=OSB[oi][0:K, :], start=True, stop=True),
                         reads=[seln, ("OSB", oi)], writes=[("PG", pi)])
                    P.op("dve", lambda e, oi=oi, pi=pi: e.tensor_scalar(out=RB[oi][:, :], in0=PGB[pi][:, 0:G], scalar1=1e-30, scalar2=None, op0=ALU.max),
                         reads=[("PG", pi)], writes=[("RB", oi)])
                    P.op("dve", lambda e, oi=oi: e.reciprocal(out=RB[oi][:, :], in_=RB[oi][:, :]), reads=[("RB", oi)], writes=[("RB", oi)])
                    P.op("dve", lambda e, oi=oi, kr=kr, hp=hp: e.tensor_tensor(out=OUTT[kr, 4 + hp, :], in0=OSB[oi][kr, :], in1=RB[oi][kr, :], op=ALU.mult),
                         reads=[("OSB", oi), ("RB", oi)], writes=[("OUTT", 4 + hp), ("HT", 0), ("HT", 1)])
            if dbg and gi == dbg_g:
                for k in range(8):
                    P.op("dve", lambda e, k=k: e.tensor_copy(out=YGV[0][:, 0, :], in_=OUTT[:, k, :]), reads=[("OUTT", k)], writes=["dtmp"])
                    dma(dbg_d["d_outT"][:, k * 256:(k + 1) * 256], YGV[0][:, 0, :], ["dtmp"], ["dtmp2"])
                dma(dbg_d["d_ckb"], CKB[:].rearrange("p b h -> p (b h)"), [("CKB", j) for j in range(NB)], ["dck"])
            if has_next:
                emit_front(nxt)
            for n in range(4):
                wb = wacquire(S_WOUT + n)
                pi = next_pg()
                for b in range(2):
                    mm_tok(pi, b, wb, b, OUTT, [("OUTT", k) for k in range(8)])
                P.op("dve", lambda e, n=n, pi=pi: e.tensor_tensor(out=XT[:, :, n * 256:(n + 1) * 256], in0=XT[:, :, n * 256:(n + 1) * 256],
                                                                  in1=PGB[pi][:, :].rearrange("p (b c) -> p b c", b=2), op=ALU.add),
                     reads=[("PG", pi), ("XT", par, 0), ("XT", par, 1)], writes=[("XT", par, 0), ("XT", par, 1)])
                wrelease()
            if dbg and gi == dbg_g:
                dma(dbg_d["d_x1"].rearrange("(b p) f -> p b f", p=128), XT[:], [("XT", par, 0), ("XT", par, 1)], ["dx1"])
            rms_front(XT, HB, par, 0)
            rms_back(HB, par)
            def ffn_tail(j):
                r = j % 2
                P.op("act", lambda e, r=r: e.activation(out=YGV[r][:, 0, :], in_=YGV[r][:, 0, :], func=AF.Silu), reads=[("YGV", r, 0)], writes=[("YGV", r, 0)])
                P.op("pool", lambda e, r=r, j=j: e.tensor_tensor(out=ACTT[:, j, :], in0=YGV[r][:, 0, :], in1=YGV[r][:, 1, :], op=ALU.mult),
                     reads=[("YGV", r, 0), ("YGV", r, 1)], writes=[("ACTT", j), "U", "VA", "VN", "OA"])

            for j in range(NCH):
                wb = wacquire(S_WUP + j)
                r = j % 2
                pi = next_pg()
                for part in range(2):
                    mm_feat(pi, part, wb, part * 128)
                P.op("pool", lambda e, r=r, j=j: e.tensor_copy(out=AGV[r][:, :, 0:2], in_=CARRY[:, j, :, :]),
                     reads=[("CARRY", j)] + [("OUTT", 4 + hq) for hq in range(4)], writes=[("AGVc", r)])
                P.op("act", lambda e, r=r, pi=pi: e.copy(out=AGV[r][:, :, 2:258], in_=PGB[pi][:, :].rearrange("p (a t) -> p a t", a=2)),
                     reads=[("PG", pi)], writes=[("AGV", r, 0), ("AGV", r, 1)])
                for part in range(2):
                    cj = part * NCH + j
                    P.op("act", lambda e, r=r, part=part, cj=cj, pi=pi: e.activation(out=YGV[r][:, part, :], in_=PGB[pi][:, part * G:(part + 1) * G], func=AF.Identity,
                                                                                 bias=BC[:, cj:cj + 1], scale=WC[:, cj, 2:3]),
                         reads=[("PG", pi), "WC", "BC"], writes=[("YGV", r, part)])
                if j >= 1:
                    ffn_tail(j - 1)
                P.op("pool", lambda e, r=r, j=j: e.tensor_copy(out=CARRY[:, j, :, :], in_=AGV[r][:, :, 256:258]),
                     reads=[("AGV", r, 0), ("AGV", r, 1)], writes=[("CARRY", j)])
                for tap in (1, 0):
                    for part in range(2):
                        cj = part * NCH + j
                        P.op("dve", lambda e, r=r, part=part, cj=cj, tap=tap: e.scalar_tensor_tensor(out=YGV[r][:, part, :], in0=AGV[r][:, part, tap:tap + 256], scalar=WC[:, cj, tap:tap + 1],
                                                                                                  in1=YGV[r][:, part, :], op0=ALU.mult, op1=ALU.add),
                             reads=[("AGV", r, part), ("AGVc", r), "WC", ("YGV", r, part)], writes=[("YGV", r, part)])
                wrelease()
            ffn_tail(NCH - 1)
            for i in range(11):
                wb = wacquire(S_WDN + i)
                for c2 in range(2):
                    ch = 2 * i + c2
                    for b in range(2):
                        for n2 in range(2):
                            acc = b * 2 + n2
                            P.op("pe", lambda e, ch=ch, c2=c2, b=b, n2=n2, acc=acc, wb=wb: e.matmul(ACCB[acc][:, :], lhsT=ACTT[:, ch, b * 128:(b + 1) * 128],
                                                                                                 rhs=WP[wb][:, c2 * 1024 + n2 * 512:c2 * 1024 + (n2 + 1) * 512],
                                                                                                 start=(ch == 0), stop=(ch == NCH - 1)),
                                 reads=[("WP", wb), ("ACTT", ch)], writes=[("ACC", acc)])
                wrelease()
            for b in range(2):
                for n2 in range(2):
                    acc = b * 2 + n2
                    P.op("dve", lambda e, b=b, n2=n2, acc=acc: e.tensor_tensor(out=XT[:, b, n2 * 512:(n2 + 1) * 512], in0=XT[:, b, n2 * 512:(n2 + 1) * 512], in1=ACCB[acc][:, :], op=ALU.add),
                         reads=[("ACC", acc), ("XT", par, b)], writes=[("XT", par, b)])
            for b in range(2):
                P.op("dve", lambda e, b=b: e.scalar_tensor_tensor(out=HB[:, b, :], in0=XT[:, b, :], scalar=1.0, in1=XT[:, b, :],
                                                                 op0=ALU.mult, op1=ALU.mult, accum_out=ST[:, 32 + b:33 + b]),
                     reads=[("XT", par, b)], writes=[("HB", par, b), ("ST", 32 + b)])
                P.op("act", lambda e, b=b: e.activation(out=ST[:, 34 + b:35 + b], in_=ST[:, 32 + b:33 + b], func=AF.Ln, bias=EPSC[:, 0:1], scale=1.0 / D),
                     reads=[("ST", 32 + b), "EPSC"], writes=[("ST", 34 + b)])
                P.op("act", lambda e, b=b: e.activation(out=ST[:, 36 + b:37 + b], in_=ST[:, 34 + b:35 + b], func=AF.Exp, scale=-0.5),
                     reads=[("ST", 34 + b)], writes=[("ST", 36 + b)])
                P.op("dve", lambda e, b=b: e.scalar_tensor_tensor(out=OUTF[:, b, :], in0=XT[:, b, :], scalar=ST[:, 36 + b:37 + b], in1=GFIN[:, :], op0=ALU.mult, op1=ALU.mult),
                     reads=[("XT", par, b), ("ST", 36 + b), "GFIN"], writes=ARENA_ALL)
            if gi >= 1:
                r0 = (gi - 1) * G
                dma(out_d[r0:r0 + G, :].rearrange("(b p) f -> p b f", p=128), OUTF, ARENA_ALL, [("out", gi)])

        dbg_g = 1
        first = True
        for ci in range(NGRP):
            if stage < 3 or (ci < FIRST_FULL and ci >= n_ctx):
                continue
            if ci >= FIRST_FULL and (ci - FIRST_FULL) >= n_full:
                break
            if ci not in front_done:
                emit_front(ci)
            emit_group(ci)
        fin = [("out", gi) for gi in range(1, n_full)]
        if dbg:
            fin += ["dq", "dtmp2", "dck", "dx1"]
        P.op("sp", None, reads=fin)
        P.emit()
        print("ops:", len(P.ops), "signals:", P.stats)
    return nc


def make_par(inputs, core):
    par = np.zeros((128, NPAR), np.float32)

    def put(name, arr):
        a, b = PC[name]
        par[:, a:b] = np.asarray(arr, np.float32).reshape(128, b - a)

    put("eye", np.eye(128, dtype=np.float32))
    s = np.arange(128)
    put("tri", (s[:, None] <= s[None, :]).astype(np.float32))
    sel = np.zeros((128, 128), np.float32); sel[127, :] = 1.0
    put("sel127", sel)
    sel = np.zeros((128, 128), np.float32); sel[64, :] = 1.0
    put("selA", sel)
    sel = np.zeros((128, 128), np.float32); sel[63, :] = 1.0
    put("selB", sel)
    tri_mask = np.where(s[:, None] <= s[None, :], 0.0, NEG).astype(np.float32)
    cm0 = np.concatenate([tri_mask, np.zeros((128, 128), np.float32)], axis=1)
    cm1 = np.concatenate([np.full((128, 128), NEG, np.float32), tri_mask], axis=1)
    put("cm0", cm0)
    put("cm1", cm1)
    put("fb", np.broadcast_to(inputs["f_bias"].reshape(1, 8), (128, 8)))
    put("lng", np.broadcast_to(inputs["sg_ln_g"].reshape(1, 512), (128, 512)))
    put("gfin", np.broadcast_to(inputs["norm_final_g"].reshape(1, D), (128, D)))
    put("g1T", inputs["norm_mix_g"].reshape(8, 128).T)
    put("g2T", inputs["norm_ffn_g"].reshape(8, 128).T)
    wc = inputs["w_conv"].reshape(3, 2 * NCH, 128)
    put("wc", np.transpose(wc, (2, 1, 0)))
    put("bc", inputs["b_conv"].reshape(2 * NCH, 128).T)
    put("bsT", inputs["sg_b"].reshape(8, 128).T)
    km = np.zeros((128, NB), np.float32)
    if core % 2 == 0:
        km[:, 0:32] = NEG
    put("kmask", km)
    sgw = inputs["sg_w"].reshape(8, 128, 128)
    put("sgwT", np.transpose(sgw, (2, 0, 1)))
    return par


_NC_CACHE = {}


def kernel(**inputs):
    inputs = {k: np.asarray(v) for k, v in inputs.items()}
    x = inputs["x"].astype(np.float32, copy=False)
    key = "full"
    if key not in _NC_CACHE:
        _NC_CACHE[key] = build_nc()
    nc = _NC_CACHE[key]
    in_maps = []
    for c in range(8):
        b, half = c // 2, c % 2
        if half == 0:
            xc = np.concatenate([np.zeros((4096, D), np.float32), x[b, 0:4096]], axis=0)
        else:
            xc = x[b]
        in_maps.append({
            "x": np.ascontiguousarray(xc),
            "par": make_par(inputs, c),
            "w_in": np.ascontiguousarray(inputs["w_in"][0]),
            "w_out": np.ascontiguousarray(inputs["w_out"][0]),
            "w_up": np.ascontiguousarray(inputs["w_up"][0]),
            "w_down": np.ascontiguousarray(inputs["w_down"][0]),
        })
    res = run_bass_kernel_spmd(nc, in_maps, core_ids=list(range(8)))
    out = np.empty((4, SEQ, D), np.float32)
    for c in range(8):
        b, half = c // 2, c % 2
        out[b, half * 4096:(half + 1) * 4096] = res.results[c]["out"]
    return out
```

```python
import contextlib
import numpy as np
import concourse.bass as bass
import concourse.mybir as mybir
from concourse.bass_utils import run_bass_kernel_spmd

F32 = mybir.dt.float32
BF16 = mybir.dt.bfloat16
AF = mybir.ActivationFunctionType
ALU = mybir.AluOpType
AX = mybir.AxisListType

SAME_ENGINE_SYNC = True
NDMASEM = 8

D = 1024
SEQ = 8192
NB = 64
G = 256
NGRP = 32
FIRST_FULL = 15
DFF = 2816
NCH = 22
EPS = 1e-6
NEG = -30000.0

PC = {}
_off = 0
for _n, _w in [("eye", 128), ("tri", 128), ("sel127", 128), ("selA", 128), ("selB", 128),
               ("cm0", 256), ("cm1", 256), ("fb", 8), ("lng", 512), ("gfin", 1024),
               ("g1T", 8), ("g2T", 8), ("wc", 132), ("bc", 44), ("bsT", 8), ("kmask", 64),
               ("sgwT", 1024)]:
    PC[_n] = (_off, _off + _w)
    _off += _w
NPAR = _off

S_WIN, S_WOUT, S_WUP, S_WDN = 0, 10, 14, 36
NSLOT = 47


class Op:
    __slots__ = ("eng", "fn", "deps", "idx", "signaled", "ev", "is_dma", "dq", "dslot")

    def __init__(self, eng, fn, idx, is_dma):
        self.eng = eng
        self.fn = fn
        self.idx = idx
        self.deps = []
        self.signaled = False
        self.ev = None
        self.is_dma = is_dma


class Prog:
    ENGS = ("sp", "act", "dve", "pool", "pe")

    def __init__(self, nc):
        self.nc = nc
        self.ops = []
        self.last_writer = {}
        self.readers = {}
        self.dma_count = {e: 0 for e in self.ENGS}

    def op(self, eng, fn, reads=(), writes=(), dma=False):
        o = Op(eng, fn, len(self.ops), dma)
        deps = {}
        for r in reads:
            w = self.last_writer.get(r)
            if w is not None:
                deps[w.idx] = w
        for r in writes:
            w = self.last_writer.get(r)
            if w is not None:
                deps[w.idx] = w
            for rd in self.readers.get(r, ()):
                deps[rd.idx] = rd
        o.deps = list(deps.values())
        for r in reads:
            self.readers.setdefault(r, []).append(o)
        for r in writes:
            self.last_writer[r] = o
            self.readers[r] = []
        if dma:
            o.dq = eng
            o.dslot = self.dma_count[eng]
            self.dma_count[eng] += 1
        self.ops.append(o)
        return o

    def emit(self):
        nc = self.nc
        ops = self.ops

        def skip(d, o):
            return (not d.is_dma) and (not o.is_dma) and d.eng == o.eng and (d.eng == "pe" or not SAME_ENGINE_SYNC)

        for o in ops:
            for d in o.deps:
                if d.is_dma or skip(d, o):
                    continue
                d.signaled = True
        with contextlib.ExitStack() as es:
            esem = {e: es.enter_context(nc.semaphore("c_" + e)) for e in self.ENGS}
            dsem = {}
            for e in self.ENGS:
                if self.dma_count[e] > 0:
                    dsem[e] = [es.enter_context(nc.semaphore("d_%s_%d" % (e, i))) for i in range(NDMASEM)]
            cnt = {e: 0 for e in self.ENGS}
            for o in ops:
                if o.is_dma:
                    s = dsem[o.dq][o.dslot % NDMASEM]
                    o.ev = (s, 16 * (o.dslot // NDMASEM + 1))
                elif o.signaled:
                    cnt[o.eng] += 1
                    o.ev = (esem[o.eng], cnt[o.eng])
            self.stats = dict(cnt)
            block = es.enter_context(nc.Block())

            def body(engname, eng):
                seen = {}
                for o in ops:
                    if o.eng != engname:
                        continue
                    waits = []
                    for d in o.deps:
                        if d.ev is None or skip(d, o):
                            continue
                        waits.append(d.ev)
                    if o.is_dma and o.dslot >= NDMASEM:
                        s = dsem[o.dq][o.dslot % NDMASEM]
                        waits.append((s, 16 * (o.dslot // NDMASEM)))
                    mx = {}
                    for (s, v) in waits:
                        k = id(s)
                        if k not in mx or mx[k][1] < v:
                            mx[k] = (s, v)
                    for k, (s, v) in mx.items():
                        if seen.get(k, 0) >= v:
                            continue
                        seen[k] = v
                        eng.wait_ge(s, v)
                    if o.fn is None:
                        continue
                    inst = o.fn(eng)
                    if o.is_dma:
                        inst.then_inc(o.ev[0], 16)
                    elif o.signaled:
                        inst.then_inc(o.ev[0], 1)

            @block.sync
            def _(e):
                body("sp", e)

            @block.scalar
            def _(e):
                body("act", e)

            @block.vector
            def _(e):
                body("dve", e)

            @block.gpsimd
            def _(e):
                body("pool", e)

            @block.tensor
            def _(e):
                body("pe", e)


def build_nc(n_full=17, dbg=False, stage=99, n_ctx=FIRST_FULL):
    nc = bass.Bass("TRN2", target_bir_lowering=False)
    x_d = nc.dram_tensor("x", [SEQ, D], F32, kind="ExternalInput").ap()
    par_d = nc.dram_tensor("par", [128, NPAR], F32, kind="ExternalInput").ap()
    win_d = nc.dram_tensor("w_in", [D, 2568], F32, kind="ExternalInput").ap()
    wout_d = nc.dram_tensor("w_out", [D, D], F32, kind="ExternalInput").ap()
    wup_d = nc.dram_tensor("w_up", [D, 2 * DFF], F32, kind="ExternalInput").ap()
    wdn_d = nc.dram_tensor("w_down", [DFF, D], F32, kind="ExternalInput").ap()
    out_d = nc.dram_tensor("out", [4096, D], F32, kind="ExternalOutput").ap()
    wsc = nc.dram_tensor("wsc", [NSLOT, 128, 2048], BF16, kind="Internal").ap()
    ksc = nc.dram_tensor("ksc", [4, 128, SEQ], BF16, kind="Internal").ap()
    dbg_d = {}
    if dbg:
        for nm, shp in [("d_x1", [256, D]), ("d_outT", [128, 8 * 256]), ("d_ckb", [128, 64 * 8]),
                        ("d_qT", [128, 4 * 256]), ("d_oa", [128, 2 * 512])]:
            dbg_d[nm] = nc.dram_tensor(nm, shp, F32, kind="ExternalOutput").ap()

    with contextlib.ExitStack() as es:
        def T(name, shape, dt):
            return es.enter_context(nc.sbuf_tensor(name, shape, dt))

        def PS(name, shape, dt):
            return es.enter_context(nc.psum_tensor(name, shape, dt))

        P = Prog(nc)
        KSTG = T("KSTG", [128, 4, G], BF16)
        KCH = [T("KCH%d" % i, [128, 1024], BF16) for i in range(3)]
        V = T("V", [128, NB, 4, 132], BF16)
        XTs = [T("XT%d" % i, [128, 2, D], F32) for i in range(2)]
        XT = XTs[0]
        HBs = [T("HB%d" % i, [128, 2, D], BF16) for i in range(2)]
        HB = HBs[0]
        HT = T("HT", [128, 8, G], BF16)
        FA = T("FA", [128, 2560], F32)
        QT = FA[:, 0:512].bitcast(BF16).rearrange("p (a t) -> p a t", a=4)
        NPT = 6
        PT = [FA[:, 512 + 256 * i:512 + 256 * (i + 1)].bitcast(BF16) for i in range(NPT)]
        OSB = [FA[:, 2048 + 256 * i:2048 + 256 * (i + 1)] for i in range(2)]
        AGV = [FA[:, 516 * i:516 * (i + 1)].rearrange("p (a c) -> p a c", a=2) for i in range(2)]
        YGV = [FA[:, 1032 + 512 * i:1032 + 512 * (i + 1)].rearrange("p (a c) -> p a c", a=2) for i in range(2)]
        ARENA = T("ARENA", [128, 3072], F32)
        OUTF = ARENA[:, 0:2048].rearrange("p (b f) -> p b f", b=2)
        U = ARENA[:, 0:1024].rearrange("p (b f) -> p b f", b=2)
        VA = ARENA[:, 1024:2048].rearrange("p (b f) -> p b f", b=2)
        ABF = ARENA[:].bitcast(BF16)
        VN = ABF[:, 4096:5120].rearrange("p (b f) -> p b f", b=2)
        OA = ABF[:, 5120:6144].rearrange("p (b f) -> p b f", b=2)
        ACTT = ABF[:, 0:NCH * G].rearrange("p (c t) -> p c t", c=NCH)
        OUTT = HT
        NWP = 10
        NPIN = 4
        WP = [T("WP%d" % i, [128, 2048], BF16) for i in range(NWP + NPIN)]
        RBT = T("RBT", [128, 512], F32)
        RB = [RBT[:, i * G:(i + 1) * G] for i in range(2)]
        SQ = RBT
        CKB = T("CKB", [128, NB, 8], F32)
        WG = [T("WG%d" % i, [128, NB], F32) for i in range(2)]
        NRG = T("NRG", [128, 8], F32)
        VP = [[T("VP%d_%d" % (e_, i), [128, 8, 128], BF16) for i in range(2)] for e_ in range(2)]
        WF = T("WF", [128, 8, 8], BF16)
        CARRY = T("CARRY", [128, NCH, 2, 2], F32)
        ident = T("ident", [128, 128], BF16)
        CMASK = T("CMASK", [128, 2, G], BF16)
        WMT = T("WMT", [128, 8, 128], BF16)
        TRI = T("TRI", [128, 128], F32)
        SEL127 = T("SEL127", [128, 128], F32)
        SELA = T("SELA", [128, 128], F32)
        SELB = T("SELB", [128, 128], F32)
        FB = T("FB", [128, 8], F32)
        LNG = T("LNG", [128, 512], F32)
        GFIN = T("GFIN", [128, D], F32)
        GT = T("GT", [128, 16], F32)
        WC = T("WC", [128, 2 * NCH, 3], F32)
        BC = T("BC", [128, 2 * NCH], F32)
        BST = T("BST", [128, 8], F32)
        KM = T("KM", [128, NB], F32)
        ST = T("ST", [128, 64], F32)
        CCAR = T("CCAR", [128, 8], F32)
        RG = T("RG", [128, 8], F32)
        CB = T("CB", [128, 8], F32)
        SP_ = T("SP_", [128, 8], F32)
        SPB = T("SPB", [128, 2, 8], F32)
        EPSC = T("EPSC", [128, 2], F32)
        PGB = [PS("PGB%d" % i, [128, 512], F32) for i in range(3)]
        ACCB = [PS("ACCB%d" % i, [128, 512], F32) for i in range(4)]
        PSTB = PS("PSTB", [128, 1024], BF16)
        print("sbuf bytes remaining:", nc.sbuf_bytes_remaining)

        rot = {"pg": 0, "pt": 0}
        att_rot = [0]
        kch_rot = [0]
        PSTF = PSTB[:, :].bitcast(F32)

        def next_pg():
            i = rot["pg"] % 3
            rot["pg"] += 1
            return i

        def next_pt():
            i = rot["pt"] % NPT
            rot["pt"] += 1
            return i

        def pc(name):
            a, b = PC[name]
            return par_d[:, a:b]

        def dma(out, in_, reads, writes):
            return P.op("sp", lambda e: e.dma_start(out=out, in_=in_), reads=reads, writes=writes, dma=True)

        XT_ALL0 = [("XT", 0, 0), ("XT", 0, 1)]
        ARENA_ALL = ["U", "VA", "VN", "OA"] + [("ACTT", c) for c in range(NCH)]

        dma(TRI[:], pc("tri"), [], ["TRI"])
        dma(SEL127[:], pc("sel127"), [], ["SEL127"])
        dma(SELA[:], pc("selA"), [], ["SELA"])
        dma(SELB[:], pc("selB"), [], ["SELB"])
        dma(FB[:], pc("fb"), [], ["FB"])
        dma(LNG[:], pc("lng"), [], ["LNG"])
        dma(GFIN[:], pc("gfin"), [], ["GFIN"])
        dma(GT[:, 0:8], pc("g1T"), [], ["GT"])
        dma(GT[:, 8:16], pc("g2T"), [], ["GT"])
        dma(WC[:].rearrange("p c j -> p (c j)"), pc("wc"), [], ["WC"])
        dma(BC[:], pc("bc"), [], ["BC"])
        dma(BST[:], pc("bsT"), [], ["BST"])
        dma(KM[:], pc("kmask"), [], ["KM"])
        XTf = XT[:].rearrange("p b f -> p (b f)")
        HBf = HB[:].rearrange("p b f -> p (b f)")
        HTf = HT[:].rearrange("p k t -> p (k t)")
        a0 = PC["eye"][0]
        dma(XTf[:, 0:128], pc("eye"), [], XT_ALL0)
        dma(XTf[:, 128:640], par_d[:, PC["cm0"][0]:PC["cm1"][1]], [], XT_ALL0)
        dma(XTf[:, 640:1664], pc("sgwT"), [], XT_ALL0)
        P.op("dve", lambda e: e.tensor_copy(out=ident[:], in_=XTf[:, 0:128]), reads=XT_ALL0, writes=["ident"])
        P.op("dve", lambda e: e.tensor_copy(out=CMASK[:].rearrange("p a t -> p (a t)"), in_=XTf[:, 128:640]), reads=XT_ALL0, writes=["CMASK"])
        P.op("dve", lambda e: e.memset(XTf[64:128, 640:1664].rearrange("p (h t) -> p h t", h=8)[:, :, 0:64], 0.0), reads=[], writes=XT_ALL0)
        P.op("dve", lambda e: e.tensor_copy(out=WMT[:].rearrange("p h t -> p (h t)"), in_=XTf[:, 640:1664]), reads=XT_ALL0, writes=["WMT"])
        P.op("dve", lambda e: e.memset(EPSC[:, 0:1], EPS), writes=["EPSC"])
        P.op("dve", lambda e: e.memset(EPSC[:, 1:2], 1.0), writes=["EPSC"])
        P.op("dve", lambda e: e.memset(CCAR[:], 0.0), writes=["CCAR"])
        P.op("dve", lambda e: e.memset(CARRY[:].rearrange("p c a j -> p (c a j)"), 0.0), writes=[("CARRY", jq) for jq in range(NCH)])
        import os
        if os.environ.get("NOVMEM") != "1":
            P.op("dve", lambda e: e.memset(V[:].rearrange("p b a c -> p (b a) c")[:, :, 64:66], 1.0), writes=[("V", b, s) for b in range(NB) for s in range(2)])

        NCONV = 99 if stage >= 2 else 0
        HB_ALL = [("HB", 0, 0), ("HB", 0, 1)]
        HT_ALL = [("HT", 0), ("HT", 1)]
        stage_f = [(XTf, XT_ALL0), (ARENA[:, 0:2048], ARENA_ALL)]
        stage_b = [(HBf, HB_ALL), (HTf, HT_ALL)]
        conv_engs = ["dve", "act"]
        cnt = [0]

        def convert(slot, pieces, gain_col, bg=False):
            if cnt[0] >= NCONV:
                return
            i = cnt[0] % 2
            cnt[0] += 1
            if bg:
                sf, sfn = ARENA[:, 0:2048], ["cvF"]
                sb, sbn = ABF[:, 4096:6144], ["cvB"]
            else:
                sf, sfn = stage_f[i]
                sb, sbn = stage_b[i]
            wr = list(sfn)
            for (dst, src) in pieces:
                dma(dst(sf), src, [], wr)
            eng = conv_engs[i]
            if gain_col is None:
                if eng == "dve":
                    P.op(eng, lambda e: e.tensor_copy(out=sb[:, :], in_=sf[:, :]), reads=wr, writes=list(sbn))
                else:
                    P.op(eng, lambda e: e.copy(out=sb[:, :], in_=sf[:, :]), reads=wr, writes=list(sbn))
            else:
                for k in range(8):
                    if eng == "dve":
                        P.op(eng, lambda e, k=k: e.tensor_scalar(out=sb[:, k * 256:(k + 1) * 256], in0=sf[:, k * 256:(k + 1) * 256],
                                                               scalar1=GT[:, gain_col + k:gain_col + k + 1], scalar2=None, op0=ALU.mult),
                             reads=wr + ["GT"], writes=list(sbn))
                    else:
                        P.op(eng, lambda e, k=k: e.mul(out=sb[:, k * 256:(k + 1) * 256], in_=sf[:, k * 256:(k + 1) * 256],
                                                     mul=GT[:, gain_col + k:gain_col + k + 1]),
                             reads=wr + ["GT"], writes=list(sbn))
            out_fn = lambda: dma(wsc[slot], sb[:, :], list(sbn), [("wsc", slot)])
            if bg:
                return out_fn
            out_fn()
            return None

        def v3(sf, a, b):
            return sf[:, 0:2048].rearrange("p (a b) -> p a b", a=a)

        conv_jobs = {}
        for s in range(10):
            conv_jobs[S_WIN + s] = ([(lambda sf: v3(sf, 8, 256), win_d[:, s * 256:(s + 1) * 256].rearrange("(k p) c -> p k c", p=128))], 0)
        for s in range(4):
            conv_jobs[S_WOUT + s] = ([(lambda sf: v3(sf, 8, 256), wout_d[:, s * 256:(s + 1) * 256].rearrange("(k p) c -> p k c", p=128))], None)
        for j in range(NCH):
            conv_jobs[S_WUP + j] = ([(lambda sf: v3(sf, 8, 256)[:, :, 0:128], wup_d[:, j * 128:(j + 1) * 128].rearrange("(k p) c -> p k c", p=128)),
                                     (lambda sf: v3(sf, 8, 256)[:, :, 128:256], wup_d[:, DFF + j * 128:DFF + (j + 1) * 128].rearrange("(k p) c -> p k c", p=128))], 8)
        for i in range(11):
            conv_jobs[S_WDN + i] = ([(lambda sf: v3(sf, 2, 1024), wdn_d[i * 256:(i + 1) * 256, :].rearrange("(c p) n -> p c n", p=128))], None)
        for slot in (6, 7, 8, 9):
            convert(slot, *conv_jobs[slot])
        dma(XTf[:, 0:64].rearrange("p (k c) -> p k c", k=8), win_d[:, 2560:2568].rearrange("(k p) c -> p k c", p=128), [], XT_ALL0)
        for k in range(8):
            P.op("dve", lambda e, k=k: e.tensor_scalar(out=WF[:, k, :], in0=XTf[:, k * 8:(k + 1) * 8], scalar1=GT[:, k:k + 1], scalar2=None, op0=ALU.mult),
                 reads=XT_ALL0 + ["GT"], writes=["WF"])
        P.op("dve", lambda e: e.memset(ST[:, 60:61], 0.0), reads=[], writes=ARENA_ALL + ["cvF", "cvB"])
        bg_slots = [sl for sl in list(range(0, 6)) + list(range(10, NSLOT))]
        bg_state = {"i": 0, "pending_out": None}

        def conv_step():
            if bg_state["pending_out"] is not None:
                bg_state["pending_out"]()
                bg_state["pending_out"] = None
            if stage < 2 or bg_state["i"] >= len(bg_slots):
                return
            slot = bg_slots[bg_state["i"]]
            bg_state["i"] += 1
            bg_state["pending_out"] = convert(slot, *conv_jobs[slot], bg=True)

        def conv_flush():
            while bg_state["i"] < len(bg_slots) or bg_state["pending_out"] is not None:
                conv_step()
            P.op("dve", lambda e: e.memset(ST[:, 61:62], 0.0), reads=[], writes=ARENA_ALL + ["cvF", "cvB"])

        wseq = []
        for ci in range(NGRP):
            full = ci >= FIRST_FULL
            if stage < 3 or (not full and ci >= n_ctx):
                continue
            if full and (ci - FIRST_FULL) >= n_full:
                break
            if full:
                wseq += list(range(0, 6)) + list(range(10, 14)) + list(range(14, 36)) + list(range(36, 47))
        wstate = {"use": 0, "load": 0}
        if stage >= 3:
            for sl in range(6, 10):
                dma(WP[NWP + sl - 6][:, :], wsc[sl], [("wsc", sl)], [("WP", NWP + sl - 6)])

        def wload_next():
            n = wstate["load"]
            if n >= len(wseq):
                return
            slot = wseq[n]
            buf = n % NWP
            wstate["load"] += 1
            dma(WP[buf][:, :], wsc[slot], [("wsc", slot)], [("WP", buf)])

        def wacquire(slot):
            if 6 <= slot < 10:
                return NWP + slot - 6
            n = wstate["use"]
            assert wseq[n] == slot, (n, wseq[n], slot)
            return n % NWP

        def wrelease(pinned=False):
            if pinned:
                return
            wstate["use"] += 1
            wload_next()


        evac_rr = [0]

        def rms_front(XT, HB, par, sc):
            for b in range(2):
                P.op("dve", lambda e, b=b: e.scalar_tensor_tensor(out=HB[:, b, :], in0=XT[:, b, :], scalar=1.0, in1=XT[:, b, :],
                                                                 op0=ALU.mult, op1=ALU.mult, accum_out=ST[:, sc + b:sc + b + 1]),
                     reads=[("XT", par, b)], writes=[("HB", par, b), ("ST", sc + b)])
                P.op("act", lambda e, b=b: e.activation(out=ST[:, sc + 2 + b:sc + 3 + b], in_=ST[:, sc + b:sc + b + 1], func=AF.Ln, bias=EPSC[:, 0:1], scale=1.0 / D),
                     reads=[("ST", sc + b), "EPSC"], writes=[("ST", sc + 2 + b)])
                P.op("act", lambda e, b=b: e.activation(out=ST[:, sc + 4 + b:sc + 5 + b], in_=ST[:, sc + 2 + b:sc + 3 + b], func=AF.Exp, scale=-0.5),
                     reads=[("ST", sc + 2 + b)], writes=[("ST", sc + 4 + b)])
                P.op("dve", lambda e, b=b: e.tensor_scalar(out=HB[:, b, :], in0=XT[:, b, :], scalar1=ST[:, sc + 4 + b:sc + 5 + b], scalar2=None, op0=ALU.mult),
                     reads=[("XT", par, b), ("ST", sc + 4 + b)], writes=[("HB", par, b)])

        def rms_back(HB, par):
            for b in range(2):
                for k in range(8):
                    P.op("pe", lambda e, b=b, k=k: e.transpose(out=PSTB[:, k * 128:(k + 1) * 128], in_=HB[:, b, k * 128:(k + 1) * 128], identity=ident[:]),
                         reads=[("HB", par, b), "ident"], writes=["PST"])
                P.op("act", lambda e, b=b: e.copy(out=HT[:, :, b * 128:(b + 1) * 128], in_=PSTB[:, :].rearrange("p (k t) -> p k t", k=8)),
                     reads=["PST"], writes=[("HT", b)] + [("OUTT", kq) for kq in range(8)])

        front_done = set()

        def emit_front(ci):
            front_done.add(ci)
            par = ci % 2
            T0 = ci * G
            dma(XTs[par][:], x_d[T0:T0 + G, :].rearrange("(b p) f -> p b f", p=128), [], [("XT", par, 0), ("XT", par, 1)])
            rms_front(XTs[par], HBs[par], par, 40)

        def HT_reads(b=None):
            if b is None:
                return [("HT", 0), ("HT", 1)]
            return [("HT", b)]

        def mm_feat(pi, hf, wbuf, c0):
            for k in range(8):
                P.op("pe", lambda e, k=k: e.matmul(PGB[pi][:, hf * G:(hf + 1) * G], lhsT=WP[wbuf][:, k * 256 + c0:k * 256 + c0 + 128], rhs=HT[:, k, :],
                                                  start=(k == 0), stop=(k == 7)),
                     reads=[("WP", wbuf)] + HT_reads(), writes=[("PG", pi)])

        def mm_tok(pi, hf, wbuf, b, lhs, lhs_reads):
            for k in range(8):
                P.op("pe", lambda e, k=k: e.matmul(PGB[pi][:, hf * G:(hf + 1) * G], lhsT=lhs[:, k, b * 128:(b + 1) * 128], rhs=WP[wbuf][:, k * 256:(k + 1) * 256],
                                                  start=(k == 0), stop=(k == 7)),
                     reads=[("WP", wbuf)] + lhs_reads, writes=[("PG", pi)])

        def emit_group(ci):
            full = ci >= FIRST_FULL
            gi = ci - FIRST_FULL
            b0 = 2 * ci
            T0 = ci * G
            par = ci % 2
            XT = XTs[par]
            HB = HBs[par]
            if gi == 0:
                conv_flush()
                for _ in range(NWP):
                    wload_next()
            if not full:
                conv_step()
            rms_back(HB, par)

            def cs1(b):
                pi = next_pg()
                for k in range(8):
                    P.op("pe", lambda e, k=k, b=b, pi=pi: e.matmul(PGB[pi][:, 0:8], lhsT=HT[:, k, b * 128:(b + 1) * 128], rhs=WF[:, k, :], start=(k == 0), stop=(k == 7)),
                         reads=["WF"] + HT_reads(b), writes=[("PG", pi)])
                P.op("dve", lambda e, pi=pi, b=b: e.tensor_tensor(out=SPB[:, b, :], in0=PGB[pi][:, 0:8], in1=FB[:, :], op=ALU.add),
                     reads=[("PG", pi), "FB"], writes=[("SPB", b)])
                P.op("act", lambda e, b=b: e.activation(out=SPB[:, b, :], in_=SPB[:, b, :], func=AF.Exp, scale=-1.0), reads=[("SPB", b)], writes=[("SPB", b)])
                P.op("act", lambda e, b=b: e.activation(out=SPB[:, b, :], in_=SPB[:, b, :], func=AF.Ln, bias=EPSC[:, 1:2], scale=1.0), reads=[("SPB", b), "EPSC"], writes=[("SPB", b)])

            def cs2(b):
                pi2 = next_pg()
                P.op("pe", lambda e, pi2=pi2, b=b: e.matmul(PGB[pi2][:, 0:8], lhsT=TRI[:, :], rhs=SPB[:, b, :], start=True, stop=True),
                     reads=["TRI", ("SPB", b)], writes=[("PG", pi2)])
                P.op("dve", lambda e, pi2=pi2: e.tensor_tensor(out=CB[:, :], in0=PGB[pi2][:, 0:8], in1=CCAR[:, :], op=ALU.add),
                     reads=[("PG", pi2), "CCAR"], writes=["CB"])
                P.op("dve", lambda e, b=b: e.tensor_scalar(out=CKB[:, b0 + b, :], in0=CB[:, :], scalar1=KM[:, b0 + b:b0 + b + 1], scalar2=None, op0=ALU.add),
                     reads=["CB", "KM"], writes=[("CKB", b0 + b)])

            def cs3(b):
                pi3 = next_pg()
                P.op("pe", lambda e, pi3=pi3: e.matmul(PGB[pi3][:, 0:8], lhsT=SEL127[:, :], rhs=CB[:, :], start=True, stop=True),
                     reads=["SEL127", "CB"], writes=[("PG", pi3)])
                P.op("dve", lambda e, pi3=pi3: e.tensor_copy(out=CCAR[:, :], in_=PGB[pi3][:, 0:8]), reads=[("PG", pi3)], writes=["CCAR"])
                if b == 0 and full:
                    P.op("dve", lambda e, pi3=pi3: e.tensor_scalar(out=NRG[:, :], in0=PGB[pi3][:, 0:8], scalar1=-1.0, scalar2=None, op0=ALU.mult), reads=[("PG", pi3)], writes=["RG"])

            cstages = [lambda: cs1(0), lambda: cs1(1), lambda: cs2(0), lambda: cs3(0), lambda: cs2(1), lambda: cs3(1)]

            def cstage():
                if cstages:
                    cstages.pop(0)()

            if not full:
                cstage()
                cstage()
            nxt = ci + 1
            has_next = (nxt < NGRP) and not (nxt >= FIRST_FULL and (nxt - FIRST_FULL) >= n_full) and not (stage < 3) and not (nxt < FIRST_FULL and nxt >= n_ctx)
            if not full and has_next:
                emit_front(nxt)
            if full:
                for s in range(4):
                    wb = wacquire(s)
                    dst, dn = (U, "U") if s < 2 else (VA, "VA")
                    hf = s % 2
                    pi = next_pg()
                    for b in range(2):
                        mm_tok(pi, b, wb, b, HT, HT_reads(b))
                    P.op("act", lambda e, pi=pi, dst=dst, hf=hf: e.activation(out=dst[:, :, hf * 256:(hf + 1) * 256], in_=PGB[pi][:, :].rearrange("p (b c) -> p b c", b=2), func=AF.Gelu),
                         reads=[("PG", pi)], writes=[dn] + [("ACTT", c) for c in range(NCH)])
                    wrelease()
                    cstage()
                for s in range(2):
                    wb = wacquire(4 + s)
                    pi = next_pg()
                    for m in range(2):
                        mm_feat(pi, m, wb, m * 128)
                    P.op("act", lambda e, pi=pi, s=s: e.mul(out=QT[:, 2 * s:2 * s + 2, :].rearrange("p a t -> p (a t)"), in_=PGB[pi][:, :], mul=0.125),
                         reads=[("PG", pi)], writes=[("QT", 2 * s), ("QT", 2 * s + 1)])
                    wrelease()
                    cstage()
            for s in range(2):
                wb = wacquire(6 + s)
                pi = next_pg()
                for m in range(2):
                    mm_feat(pi, m, wb, m * 128)
                for m in range(2):
                    hp = 2 * s + m
                    P.op("act", lambda e, pi=pi, hp=hp, m=m: e.copy(out=KSTG[:, hp, :], in_=PGB[pi][:, m * G:(m + 1) * G]),
                         reads=[("PG", pi)], writes=[("KSTG", hp)])
                wrelease(pinned=True)
                if not full:
                    cstage()
            dma(ksc[:, :, T0:T0 + G].rearrange("h p t -> p h t"), KSTG[:, :, :], [("KSTG", hp) for hp in range(4)],
                [("KD", hp, bb) for hp in range(4) for bb in (b0, b0 + 1)])
            if not full:
                conv_step()
            for s in range(2):
                wb = wacquire(8 + s)
                pi = next_pg()
                for b in range(2):
                    mm_tok(pi, b, wb, b, HT, HT_reads(b))
                for b in range(2):
                    for a in range(2):
                        P.op("dve", lambda e, b=b, pi=pi, s=s, a=a: e.tensor_copy(
                            out=V[:, b0 + b, 2 * s + a, :].rearrange("p (e c) -> p e c", e=2)[:, :, 0:64],
                            in_=PGB[pi][:, b * G + a * 128:b * G + (a + 1) * 128].rearrange("p (e c) -> p e c", e=2)),
                             reads=[("PG", pi)], writes=[("V", b0 + b, s)])
                wrelease(pinned=True)
                if not full:
                    cstage()
            if not full:
                conv_step()
            while cstages:
                cstage()
            if not full:
                return
            if dbg and gi == dbg_g:
                dma(dbg_d["d_qT"], QT[:].rearrange("p a t -> p (a t)"), [("QT", h) for h in range(4)], ["dq"])
            def emit_sg():
                SQR = [("RB", 0), ("RB", 1)]
                for b in range(2):
                    va3 = VA[:, b, :].rearrange("p (h d) -> p h d", h=8)
                    P.op("dve", lambda e, va3=va3: e.tensor_reduce(out=ST[:, 8:16], in_=va3, axis=AX.X, op=ALU.add), reads=["VA"], writes=[("ST", "s1")])
                    P.op("dve", lambda e, b=b: e.tensor_tensor(out=SQ[:, :], in0=VA[:, b, :], in1=VA[:, b, :], op=ALU.mult), reads=["VA"], writes=SQR)
                    P.op("dve", lambda e: e.tensor_reduce(out=ST[:, 16:24], in_=SQ[:, :].rearrange("p (h d) -> p h d", h=8), axis=AX.X, op=ALU.add),
                         reads=SQR, writes=[("ST", "s2")])
                    P.op("dve", lambda e: e.tensor_scalar(out=ST[:, 8:16], in0=ST[:, 8:16], scalar1=1.0 / 64, scalar2=None, op0=ALU.mult),
                         reads=[("ST", "s1")], writes=[("ST", "s1")])
                    P.op("dve", lambda e: e.tensor_tensor(out=ST[:, 24:32], in0=ST[:, 8:16], in1=ST[:, 8:16], op=ALU.mult),
                         reads=[("ST", "s1")], writes=[("ST", "msq")])
                    P.op("dve", lambda e: e.scalar_tensor_tensor(out=ST[:, 16:24], in0=ST[:, 16:24], scalar=1.0 / 64, in1=ST[:, 24:32], op0=ALU.mult, op1=ALU.subtract),
                         reads=[("ST", "s2"), ("ST", "msq")], writes=[("ST", "s2")])
                    P.op("act", lambda e: e.activation(out=ST[:, 16:24], in_=ST[:, 16:24], func=AF.Ln, bias=EPSC[:, 0:1], scale=1.0),
                         reads=[("ST", "s2"), "EPSC"], writes=[("ST", "s2")])
                    P.op("act", lambda e: e.activation(out=ST[:, 16:24], in_=ST[:, 16:24], func=AF.Exp, scale=-0.5),
                         reads=[("ST", "s2")], writes=[("ST", "s2")])
                    P.op("dve", lambda e, va3=va3: e.tensor_tensor(out=va3, in0=va3, in1=ST[:, 8:16].unsqueeze(2).to_broadcast([128, 8, 64]), op=ALU.subtract),
                         reads=["VA", ("ST", "s1")], writes=["VA"])
                    P.op("dve", lambda e, va3=va3: e.tensor_tensor(out=va3, in0=va3, in1=ST[:, 16:24].unsqueeze(2).to_broadcast([128, 8, 64]), op=ALU.mult),
                         reads=["VA", ("ST", "s2")], writes=["VA"])
                    P.op("dve", lambda e, b=b: e.tensor_tensor(out=VN[:, b, :], in0=VA[:, b, :], in1=LNG[:, :], op=ALU.mult),
                         reads=["VA", "LNG"], writes=["VN"])
                    pi = next_pg()
                    for h in range(8):
                        P.op("pe", lambda e, h=h, b=b, pi=pi: e.matmul(PGB[pi][:, h * 64:(h + 1) * 64], lhsT=WMT[:, h, :], rhs=VN[:, b, h * 64:(h + 1) * 64],
                                                                      start=True, stop=True),
                             reads=["WMT", "VN"], writes=[("PG", pi)])
                    P.op("dve", lambda e, pi=pi: e.tensor_tensor(out=SQ[:, :].rearrange("p (h d) -> p h d", h=8),
                                                                 in0=PGB[pi][:, :].rearrange("p (h d) -> p h d", h=8),
                                                                 in1=BST[:, :].unsqueeze(2).to_broadcast([128, 8, 64]), op=ALU.add),
                         reads=[("PG", pi), "BST"], writes=SQR)
                    P.op("dve", lambda e, b=b: e.tensor_tensor(out=OA[:, b, :], in0=SQ[:, :], in1=U[:, b, :], op=ALU.mult),
                         reads=SQR + ["U"], writes=["OA"])
                    for kk in range(4):
                        P.op("pe", lambda e, b=b, kk=kk: e.transpose(out=PSTB[:, kk * 128:(kk + 1) * 128], in_=OA[:, b, kk * 128:(kk + 1) * 128], identity=ident[:]),
                             reads=["OA", "ident"], writes=["PST"])
                    P.op("act", lambda e, b=b: e.copy(out=OUTT[:, 0:4, b * 128:(b + 1) * 128], in_=PSTB[:, 0:512].rearrange("p (k t) -> p k t", k=4)),
                         reads=["PST"], writes=[("OUTT", k) for k in range(4)] + [("HT", 0), ("HT", 1)])

            nj = b0 + 2
            nchunk = (nj + 7) // 8

            def kload(hp, c):
                nblk = min(8, nj - 8 * c)
                kb = kch_rot[0] % 3
                kch_rot[0] += 1
                dma(KCH[kb][:, 0:nblk * 128], ksc[hp][:, c * 1024:c * 1024 + nblk * 128],
                    [("KD", hp, 8 * c + q) for q in range(nblk)], [("KCH", kb)])
                return kb

            for hp in range(4):
                if hp == 1:
                    emit_sg()
                accs = [(2 * hp) % 4, (2 * hp + 1) % 4]
                kbufs = {0: kload(hp, 0)}
                if nchunk > 1:
                    kbufs[1] = kload(hp, 1)
                for ee in range(2):
                    h = 2 * hp + ee
                    P.op("act", lambda e, h=h, ee=ee: e.activation(out=WG[ee][:, 0:nj], in_=CKB[:, 0:nj, h], func=AF.Exp, bias=NRG[:, h:h + 1], scale=1.0),
                         reads=[("CKB", j) for j in range(nj)] + ["RG"], writes=[("WG", ee)])

                def vprep(c, hp=hp):
                    nblk = min(8, nj - 8 * c)
                    cb = c % 2
                    for ee in range(2):
                        Kc = 65 if ee == 0 else 128
                        c0 = 0 if ee == 0 else 2
                        P.op("dve", lambda e, ee=ee, Kc=Kc, c0=c0, cb=cb, nblk=nblk, c=c: e.tensor_tensor(
                            out=VP[ee][cb][:, 0:nblk, 0:Kc], in0=V[:, 8 * c:8 * c + nblk, hp, c0:c0 + Kc],
                            in1=WG[ee][:, 8 * c:8 * c + nblk].unsqueeze(2).to_broadcast([128, nblk, Kc]), op=ALU.mult),
                             reads=[("V", 8 * c + q_, hp // 2) for q_ in range(nblk)] + [("WG", ee)], writes=[("VP", ee, cb)])

                vprep(0)
                if nchunk > 1:
                    vprep(1)

                def qk_step(j, hp=hp):
                    bp = att_rot[0] % 2
                    att_rot[0] += 1
                    banks = [(PGB[0], ("PG", 0)), (PGB[1], ("PG", 1))] if bp == 0 else [(PGB[2], ("PG", 2)), (PSTF, "PST")]
                    for q in range(2):
                        jj = j + q
                        diag = jj >= b0
                        for ee in range(2):
                            kr = slice(ee * 64, (ee + 1) * 64)
                            bk, bkey = banks[ee]
                            kb_ = kbufs[jj // 8]
                            jo = jj % 8
                            P.op("pe", lambda e, ee=ee, kr=kr, bk=bk, jj=jj, q=q, diag=diag, kb_=kb_, jo=jo: e.matmul(bk[:, q * G:(q + 1) * G], lhsT=KCH[kb_][kr, jo * 128:(jo + 1) * 128], rhs=QT[kr, hp, :],
                                                                                                  start=True, stop=not diag),
                                 reads=[("KCH", kb_), ("QT", hp)], writes=[bkey])
                        if diag:
                            for ee in range(2):
                                bk, bkey = banks[ee]
                                P.op("pe", lambda e, bk=bk, jj=jj, q=q: e.matmul(bk[:, q * G:(q + 1) * G], lhsT=ident[:, :], rhs=CMASK[:, jj - b0, :], start=False, stop=True),
                                     reads=["ident", "CMASK"], writes=[bkey])
                    return banks

                def exp_step(j, banks):
                    tis = []
                    for ee in range(2):
                        bk, bkey = banks[ee]
                        ti = next_pt()
                        P.op("act", lambda e, ti=ti, bk=bk: e.activation(out=PT[ti][:, :], in_=bk[:, :], func=AF.Exp, scale=1.0),
                             reads=[bkey], writes=[("PT", ti)])
                        tis.append(ti)
                    return tis

                def pv_step(j, tis, hp=hp, accs=accs):
                    for q in range(2):
                        jj = j + q
                        cb = (jj // 8) % 2
                        jo = jj % 8
                        t0_, t1_ = tis[0], tis[1]
                        P.op("pe", lambda e, jj=jj, cb=cb, jo=jo, t0_=t0_, q=q: e.matmul(ACCB[accs[0]][0:65, 0:G], lhsT=VP[0][cb][:, jo, 0:65], rhs=PT[t0_][:, q * G:(q + 1) * G], start=(jj == 0), stop=(jj == nj - 1)),
                             reads=[("VP", 0, cb), ("PT", t0_)], writes=[("ACC", accs[0])])
                        P.op("pe", lambda e, jj=jj, cb=cb, jo=jo, t1_=t1_, q=q: e.matmul(ACCB[accs[1]][:, 0:G], lhsT=VP[1][cb][:, jo, :], rhs=PT[t1_][:, q * G:(q + 1) * G], start=(jj == 0), stop=(jj == nj - 1)),
                             reads=[("VP", 1, cb), ("PT", t1_)], writes=[("ACC", accs[1])])

                pend = None
                for j in range(0, nj, 2):
                    if j % 8 == 0 and j > 0 and (j // 8 + 1) < nchunk:
                        kbufs[j // 8 + 1] = kload(hp, j // 8 + 1)
                    if j % 8 == 2 and j > 8 - 8 and (j // 8 + 1) < nchunk and (j // 8 + 1) >= 2:
                        vprep(j // 8 + 1)
                    banks = qk_step(j)
                    tis = exp_step(j, banks)
                    if pend is not None:
                        pv_step(*pend)
                    pend = (j, tis)
                pv_step(*pend)
                for ee in range(2):
                    h = 2 * hp + ee
                    acc = accs[ee]
                    kr = slice(ee * 64, (ee + 1) * 64)
                    oi = ee
                    K = 65 if ee == 0 else 128
                    sel = SELA if ee == 0 else SELB
                    seln = "SELA" if ee == 0 else "SELB"
                    P.op("act", lambda e, oi=oi, K=K, acc=acc: e.copy(out=OSB[oi][0:K, :], in_=ACCB[acc][0:K, 0:G]), reads=[("ACC", acc)], writes=[("OSB", oi)])
                    pi = next_pg()
                    P.op("pe", lambda e, oi=oi, K=K, sel=sel, pi=pi: e.matmul(PGB[pi][:, 0:G], lhsT=sel[0:K, :], rhs=OSB[oi][0:K, :], start=True, stop=True),
                         reads=[seln, ("OSB", oi)], writes=[("PG", pi)])
                    P.op("dve", lambda e, oi=oi, pi=pi: e.tensor_scalar(out=RB[oi][:, :], in0=PGB[pi][:, 0:G], scalar1=1e-30, scalar2=None, op0=ALU.max),
                         reads=[("PG", pi)], writes=[("RB", oi)])
                    P.op("dve", lambda e, oi=oi: e.reciprocal(out=RB[oi][:, :], in_=RB[oi][:, :]), reads=[("RB", oi)], writes=[("RB", oi)])
                    P.op("dve", lambda e, oi=oi, kr=kr, hp=hp: e.tensor_tensor(out=OUTT[kr, 4 + hp, :], in0=OSB[oi][kr, :], in1=RB[oi][kr, :], op=ALU.mult),
                         reads=[("OSB", oi), ("RB", oi)], writes=[("OUTT", 4 + hp), ("HT", 0), ("HT", 1)])
            if dbg and gi == dbg_g:
                for k in range(8):
                    P.op("dve", lambda e, k=k: e.tensor_copy(out=YGV[0][:, 0, :], in_=OUTT[:, k, :]), reads=[("OUTT", k)], writes=["dtmp"])
                    dma(dbg_d["d_outT"][:, k * 256:(k + 1) * 256], YGV[0][:, 0, :], ["dtmp"], ["dtmp2"])
                dma(dbg_d["d_ckb"], CKB[:].rearrange("p b h -> p (b h)"), [("CKB", j) for j in range(NB)], ["dck"])
            if has_next:
                emit_front(nxt)
            for n in range(4):
                wb = wacquire(S_WOUT + n)
                pi = next_pg()
                for b in range(2):
                    mm_tok(pi, b, wb, b, OUTT, [("OUTT", k) for k in range(8)])
                P.op("dve", lambda e, n=n, pi=pi: e.tensor_tensor(out=XT[:, :, n * 256:(n + 1) * 256], in0=XT[:, :, n * 256:(n + 1) * 256],
                                                                  in1=PGB[pi][:, :].rearrange("p (b c) -> p b c", b=2), op=ALU.add),
                     reads=[("PG", pi), ("XT", par, 0), ("XT", par, 1)], writes=[("XT", par, 0), ("XT", par, 1)])
                wrelease()
            if dbg and gi == dbg_g:
                dma(dbg_d["d_x1"].rearrange("(b p) f -> p b f", p=128), XT[:], [("XT", par, 0), ("XT", par, 1)], ["dx1"])
            rms_front(XT, HB, par, 0)
            rms_back(HB, par)
            def ffn_tail(j):
                r = j % 2
                P.op("act", lambda e, r=r: e.activation(out=YGV[r][:, 0, :], in_=YGV[r][:, 0, :], func=AF.Silu), reads=[("YGV", r, 0)], writes=[("YGV", r, 0)])
                P.op("pool", lambda e, r=r, j=j: e.tensor_tensor(out=ACTT[:, j, :], in0=YGV[r][:, 0, :], in1=YGV[r][:, 1, :], op=ALU.mult),
                     reads=[("YGV", r, 0), ("YGV", r, 1)], writes=[("ACTT", j), "U", "VA", "VN", "OA"])

            for j in range(NCH):
                wb = wacquire(S_WUP + j)
                r = j % 2
                pi = next_pg()
                for part in range(2):
                    mm_feat(pi, part, wb, part * 128)
                P.op("pool", lambda e, r=r, j=j: e.tensor_copy(out=AGV[r][:, :, 0:2], in_=CARRY[:, j, :, :]),
                     reads=[("CARRY", j)] + [("OUTT", 4 + hq) for hq in range(4)], writes=[("AGVc", r)])
                P.op("act", lambda e, r=r, pi=pi: e.copy(out=AGV[r][:, :, 2:258], in_=PGB[pi][:, :].rearrange("p (a t) -> p a t", a=2)),
                     reads=[("PG", pi)], writes=[("AGV", r, 0), ("AGV", r, 1)])
                for part in range(2):
                    cj = part * NCH + j
                    P.op("act", lambda e, r=r, part=part, cj=cj, pi=pi: e.activation(out=YGV[r][:, part, :], in_=PGB[pi][:, part * G:(part + 1) * G], func=AF.Identity,
                                                                                 bias=BC[:, cj:cj + 1], scale=WC[:, cj, 2:3]),
                         reads=[("PG", pi), "WC", "BC"], writes=[("YGV", r, part)])
                if j >= 1:
                    ffn_tail(j - 1)
                P.op("pool", lambda e, r=r, j=j: e.tensor_copy(out=CARRY[:, j, :, :], in_=AGV[r][:, :, 256:258]),
                     reads=[("AGV", r, 0), ("AGV", r, 1)], writes=[("CARRY", j)])
                for tap in (1, 0):
                    for part in range(2):
                        cj = part * NCH + j
                        P.op("dve", lambda e, r=r, part=part, cj=cj, tap=tap: e.scalar_tensor_tensor(out=YGV[r][:, part, :], in0=AGV[r][:, part, tap:tap + 256], scalar=WC[:, cj, tap:tap + 1],
                                                                                                  in1=YGV[r][:, part, :], op0=ALU.mult, op1=ALU.add),
                             reads=[("AGV", r, part), ("AGVc", r), "WC", ("YGV", r, part)], writes=[("YGV", r, part)])
                wrelease()
            ffn_tail(NCH - 1)
            for i in range(11):
                wb = wacquire(S_WDN + i)
                for c2 in range(2):
                    ch = 2 * i + c2
                    for b in range(2):
                        for n2 in range(2):
                            acc = b * 2 + n2
                            P.op("pe", lambda e, ch=ch, c2=c2, b=b, n2=n2, acc=acc, wb=wb: e.matmul(ACCB[acc][:, :], lhsT=ACTT[:, ch, b * 128:(b + 1) * 128],
                                                                                                 rhs=WP[wb][:, c2 * 1024 + n2 * 512:c2 * 1024 + (n2 + 1) * 512],
                                                                                                 start=(ch == 0), stop=(ch == NCH - 1)),
                                 reads=[("WP", wb), ("ACTT", ch)], writes=[("ACC", acc)])
                wrelease()
            for b in range(2):
                for n2 in range(2):
                    acc = b * 2 + n2
                    P.op("dve", lambda e, b=b, n2=n2, acc=acc: e.tensor_tensor(out=XT[:, b, n2 * 512:(n2 + 1) * 512], in0=XT[:, b, n2 * 512:(n2 + 1) * 512], in1=ACCB[acc][:, :], op=ALU.add),
                         reads=[("ACC", acc), ("XT", par, b)], writes=[("XT", par, b)])
            for b in range(2):
                P.op("dve", lambda e, b=b: e.scalar_tensor_tensor(out=HB[:, b, :], in0=XT[:, b, :], scalar=1.0, in1=XT[:, b, :],
                                                                 op0=ALU.mult, op1=ALU.mult, accum_out=ST[:, 32 + b:33 + b]),
                     reads=[("XT", par, b)], writes=[("HB", par, b), ("ST", 32 + b)])
                P.op("act", lambda e, b=b: e.activation(out=ST[:, 34 + b:35 + b], in_=ST[:, 32 + b:33 + b], func=AF.Ln, bias=EPSC[:, 0:1], scale=1.0 / D),
                     reads=[("ST", 32 + b), "EPSC"], writes=[("ST", 34 + b)])
                P.op("act", lambda e, b=b: e.activation(out=ST[:, 36 + b:37 + b], in_=ST[:, 34 + b:35 + b], func=AF.Exp, scale=-0.5),
                     reads=[("ST", 34 + b)], writes=[("ST", 36 + b)])
                P.op("dve", lambda e, b=b: e.scalar_tensor_tensor(out=OUTF[:, b, :], in0=XT[:, b, :], scalar=ST[:, 36 + b:37 + b], in1=GFIN[:, :], op0=ALU.mult, op1=ALU.mult),
                     reads=[("XT", par, b), ("ST", 36 + b), "GFIN"], writes=ARENA_ALL)
            if gi >= 1:
                r0 = (gi - 1) * G
                dma(out_d[r0:r0 + G, :].rearrange("(b p) f -> p b f", p=128), OUTF, ARENA_ALL, [("out", gi)])

        dbg_g = 1
        first = True
        for ci in range(NGRP):
            if stage < 3 or (ci < FIRST_FULL and ci >= n_ctx):
                continue
            if ci >= FIRST_FULL and (ci - FIRST_FULL) >= n_full:
                break
            if ci not in front_done:
                emit_front(ci)
            emit_group(ci)
        fin = [("out", gi) for gi in range(1, n_full)]
        if dbg:
            fin += ["dq", "dtmp2", "dck", "dx1"]
        P.op("sp", None, reads=fin)
        P.emit()
        print("ops:", len(P.ops), "signals:", P.stats)
    return nc


def make_par(inputs, core):
    par = np.zeros((128, NPAR), np.float32)

    def put(name, arr):
        a, b = PC[name]
        par[:, a:b] = np.asarray(arr, np.float32).reshape(128, b - a)

    put("eye", np.eye(128, dtype=np.float32))
    s = np.arange(128)
    put("tri", (s[:, None] <= s[None, :]).astype(np.float32))
    sel = np.zeros((128, 128), np.float32); sel[127, :] = 1.0
    put("sel127", sel)
    sel = np.zeros((128, 128), np.float32); sel[64, :] = 1.0
    put("selA", sel)
    sel = np.zeros((128, 128), np.float32); sel[63, :] = 1.0
    put("selB", sel)
    tri_mask = np.where(s[:, None] <= s[None, :], 0.0, NEG).astype(np.float32)
    cm0 = np.concatenate([tri_mask, np.zeros((128, 128), np.float32)], axis=1)
    cm1 = np.concatenate([np.full((128, 128), NEG, np.float32), tri_mask], axis=1)
    put("cm0", cm0)
    put("cm1", cm1)
    put("fb", np.broadcast_to(inputs["f_bias"].reshape(1, 8), (128, 8)))
    put("lng", np.broadcast_to(inputs["sg_ln_g"].reshape(1, 512), (128, 512)))
    put("gfin", np.broadcast_to(inputs["norm_final_g"].reshape(1, D), (128, D)))
    put("g1T", inputs["norm_mix_g"].reshape(8, 128).T)
    put("g2T", inputs["norm_ffn_g"].reshape(8, 128).T)
    wc = inputs["w_conv"].reshape(3, 2 * NCH, 128)
    put("wc", np.transpose(wc, (2, 1, 0)))
    put("bc", inputs["b_conv"].reshape(2 * NCH, 128).T)
    put("bsT", inputs["sg_b"].reshape(8, 128).T)
    km = np.zeros((128, NB), np.float32)
    if core % 2 == 0:
        km[:, 0:32] = NEG
    put("kmask", km)
    sgw = inputs["sg_w"].reshape(8, 128, 128)
    put("sgwT", np.transpose(sgw, (2, 0, 1)))
    return par


_NC_CACHE = {}


def kernel(**inputs):
    inputs = {k: np.asarray(v) for k, v in inputs.items()}
    x = inputs["x"].astype(np.float32, copy=False)
    key = "full"
    if key not in _NC_CACHE:
        _NC_CACHE[key] = build_nc()
    nc = _NC_CACHE[key]
    in_maps = []
    for c in range(8):
        b, half = c // 2, c % 2
        if half == 0:
            xc = np.concatenate([np.zeros((4096, D), np.float32), x[b, 0:4096]], axis=0)
        else:
            xc = x[b]
        in_maps.append({
            "x": np.ascontiguousarray(xc),
            "par": make_par(inputs, c),
            "w_in": np.ascontiguousarray(inputs["w_in"][0]),
            "w_out": np.ascontiguousarray(inputs["w_out"][0]),
            "w_up": np.ascontiguousarray(inputs["w_up"][0]),
            "w_down": np.ascontiguousarray(inputs["w_down"][0]),
        })
    res = run_bass_kernel_spmd(nc, in_maps, core_ids=list(range(8)))
    out = np.empty((4, SEQ, D), np.float32)
    for c in range(8):
        b, half = c // 2, c % 2
        out[b, half * 4096:(half + 1) * 4096] = res.results[c]["out"]
    return out
```

```python
import contextlib
import numpy as np
import concourse.bass as bass
import concourse.mybir as mybir
from concourse.bass_utils import run_bass_kernel_spmd

F32 = mybir.dt.float32
BF16 = mybir.dt.bfloat16
AF = mybir.ActivationFunctionType
ALU = mybir.AluOpType
AX = mybir.AxisListType

SAME_ENGINE_SYNC = True
NDMASEM = 8

D = 1024
SEQ = 8192
NB = 64
G = 256
NGRP = 32
FIRST_FULL = 15
DFF = 2816
NCH = 22
EPS = 1e-6
NEG = -30000.0

PC = {}
_off = 0
for _n, _w in [("eye", 128), ("tri", 128), ("sel127", 128), ("selA", 128), ("selB", 128),
               ("cm0", 256), ("cm1", 256), ("fb", 8), ("lng", 512), ("gfin", 1024),
               ("g1T", 8), ("g2T", 8), ("wc", 132), ("bc", 44), ("bsT", 8), ("kmask", 64),
               ("sgwT", 1024)]:
    PC[_n] = (_off, _off + _w)
    _off += _w
NPAR = _off

S_WIN, S_WOUT, S_WUP, S_WDN = 0, 10, 14, 36
NSLOT = 47


class Op:
    __slots__ = ("eng", "fn", "deps", "idx", "signaled", "ev", "is_dma", "dq", "dslot")

    def __init__(self, eng, fn, idx, is_dma):
        self.eng = eng
        self.fn = fn
        self.idx = idx
        self.deps = []
        self.signaled = False
        self.ev = None
        self.is_dma = is_dma


class Prog:
    ENGS = ("sp", "act", "dve", "pool", "pe")

    def __init__(self, nc):
        self.nc = nc
        self.ops = []
        self.last_writer = {}
        self.readers = {}
        self.dma_count = {e: 0 for e in self.ENGS}

    def op(self, eng, fn, reads=(), writes=(), dma=False):
        o = Op(eng, fn, len(self.ops), dma)
        deps = {}
        for r in reads:
            w = self.last_writer.get(r)
            if w is not None:
                deps[w.idx] = w
        for r in writes:
            w = self.last_writer.get(r)
            if w is not None:
                deps[w.idx] = w
            for rd in self.readers.get(r, ()):
                deps[rd.idx] = rd
        o.deps = list(deps.values())
        for r in reads:
            self.readers.setdefault(r, []).append(o)
        for r in writes:
            self.last_writer[r] = o
            self.readers[r] = []
        if dma:
            o.dq = eng
            o.dslot = self.dma_count[eng]
            self.dma_count[eng] += 1
        self.ops.append(o)
        return o

    def emit(self):
        nc = self.nc
        ops = self.ops

        def skip(d, o):
            return (not d.is_dma) and (not o.is_dma) and d.eng == o.eng and (d.eng == "pe" or not SAME_ENGINE_SYNC)

        for o in ops:
            for d in o.deps:
                if d.is_dma or skip(d, o):
                    continue
                d.signaled = True
        with contextlib.ExitStack() as es:
            esem = {e: es.enter_context(nc.semaphore("c_" + e)) for e in self.ENGS}
            dsem = {}
            for e in self.ENGS:
                if self.dma_count[e] > 0:
                    dsem[e] = [es.enter_context(nc.semaphore("d_%s_%d" % (e, i))) for i in range(NDMASEM)]
            cnt = {e: 0 for e in self.ENGS}
            for o in ops:
                if o.is_dma:
                    s = dsem[o.dq][o.dslot % NDMASEM]
                    o.ev = (s, 16 * (o.dslot // NDMASEM + 1))
                elif o.signaled:
                    cnt[o.eng] += 1
                    o.ev = (esem[o.eng], cnt[o.eng])
            self.stats = dict(cnt)
            block = es.enter_context(nc.Block())

            def body(engname, eng):
                seen = {}
                for o in ops:
                    if o.eng != engname:
                        continue
                    waits = []
                    for d in o.deps:
                        if d.ev is None or skip(d, o):
                            continue
                        waits.append(d.ev)
                    if o.is_dma and o.dslot >= NDMASEM:
                        s = dsem[o.dq][o.dslot % NDMASEM]
                        waits.append((s, 16 * (o.dslot // NDMASEM)))
                    mx = {}
                    for (s, v) in waits:
                        k = id(s)
                        if k not in mx or mx[k][1] < v:
                            mx[k] = (s, v)
                    for k, (s, v) in mx.items():
                        if seen.get(k, 0) >= v:
                            continue
                        seen[k] = v
                        eng.wait_ge(s, v)
                    if o.fn is None:
                        continue
                    inst = o.fn(eng)
                    if o.is_dma:
                        inst.then_inc(o.ev[0], 16)
                    elif o.signaled:
                        inst.then_inc(o.ev[0], 1)

            @block.sync
            def _(e):
                body("sp", e)

            @block.scalar
            def _(e):
                body("act", e)

            @block.vector
            def _(e):
                body("dve", e)

            @block.gpsimd
            def _(e):
                body("pool", e)

            @block.tensor
            def _(e):
                body("pe", e)


def build_nc(n_full=17, dbg=False, stage=99, n_ctx=FIRST_FULL):
    nc = bass.Bass("TRN2", target_bir_lowering=False)
    x_d = nc.dram_tensor("x", [SEQ, D], F32, kind="ExternalInput").ap()
    par_d = nc.dram_tensor("par", [128, NPAR], F32, kind="ExternalInput").ap()
    win_d = nc.dram_tensor("w_in", [D, 2568], F32, kind="ExternalInput").ap()
    wout_d = nc.dram_tensor("w_out", [D, D], F32, kind="ExternalInput").ap()
    wup_d = nc.dram_tensor("w_up", [D, 2 * DFF], F32, kind="ExternalInput").ap()
    wdn_d = nc.dram_tensor("w_down", [DFF, D], F32, kind="ExternalInput").ap()
    out_d = nc.dram_tensor("out", [4096, D], F32, kind="ExternalOutput").ap()
    wsc = nc.dram_tensor("wsc", [NSLOT, 128, 2048], BF16, kind="Internal").ap()
    ksc = nc.dram_tensor("ksc", [4, 128, SEQ], BF16, kind="Internal").ap()
    dbg_d = {}
    if dbg:
        for nm, shp in [("d_x1", [256, D]), ("d_outT", [128, 8 * 256]), ("d_ckb", [128, 64 * 8]),
                        ("d_qT", [128, 4 * 256]), ("d_oa", [128, 2 * 512])]:
            dbg_d[nm] = nc.dram_tensor(nm, shp, F32, kind="ExternalOutput").ap()

    with contextlib.ExitStack() as es:
        def T(name, shape, dt):
            return es.enter_context(nc.sbuf_tensor(name, shape, dt))

        def PS(name, shape, dt):
            return es.enter_context(nc.psum_tensor(name, shape, dt))

        P = Prog(nc)
        KSTG = T("KSTG", [128, 4, G], BF16)
        KCH = [T("KCH%d" % i, [128, 1024], BF16) for i in range(3)]
        V = T("V", [128, NB, 4, 132], BF16)
        XTs = [T("XT%d" % i, [128, 2, D], F32) for i in range(2)]
        XT = XTs[0]
        HBs = [T("HB%d" % i, [128, 2, D], BF16) for i in range(2)]
        HB = HBs[0]
        HT = T("HT", [128, 8, G], BF16)
        FA = T("FA", [128, 3072], F32)
        QT = FA[:, 0:512].bitcast(BF16).rearrange("p (a t) -> p a t", a=4)
        NPT = 8
        PT = [FA[:, 512 + 256 * i:512 + 256 * (i + 1)].bitcast(BF16) for i in range(NPT)]
        OSB = [FA[:, 2560 + 256 * i:2560 + 256 * (i + 1)] for i in range(2)]
        AGV = [FA[:, 516 * i:516 * (i + 1)].rearrange("p (a c) -> p a c", a=2) for i in range(2)]
        YGV = [FA[:, 1032 + 512 * i:1032 + 512 * (i + 1)].rearrange("p (a c) -> p a c", a=2) for i in range(2)]
        ARENA = T("ARENA", [128, 3072], F32)
        OUTF = ARENA[:, 0:2048].rearrange("p (b f) -> p b f", b=2)
        U = ARENA[:, 0:1024].rearrange("p (b f) -> p b f", b=2)
        VA = ARENA[:, 1024:2048].rearrange("p (b f) -> p b f", b=2)
        ABF = ARENA[:].bitcast(BF16)
        VN = ABF[:, 4096:5120].rearrange("p (b f) -> p b f", b=2)
        OA = ABF[:, 5120:6144].rearrange("p (b f) -> p b f", b=2)
        ACTT = ABF[:, 0:NCH * G].rearrange("p (c t) -> p c t", c=NCH)
        OUTT = HT
        NWP = 10
        NPIN = 4
        WP = [T("WP%d" % i, [128, 2048], BF16) for i in range(NWP + NPIN)]
        RBT = T("RBT", [128, 512], F32)
        RB = [RBT[:, i * G:(i + 1) * G] for i in range(2)]
        SQ = RBT
        CKB = T("CKB", [128, NB, 8], F32)
        WG = [T("WG%d" % i, [128, NB], F32) for i in range(2)]
        NRG = T("NRG", [128, 8], F32)
        VP = [[T("VP%d_%d" % (e_, i), [128, 8, 128], BF16) for i in range(2)] for e_ in range(2)]
        WF = T("WF", [128, 8, 8], BF16)
        CARRY = T("CARRY", [128, NCH, 2, 2], F32)
        ident = T("ident", [128, 128], BF16)
        CMASK = T("CMASK", [128, 2, G], BF16)
        WMT = T("WMT", [128, 8, 128], BF16)
        TRI = T("TRI", [128, 128], F32)
        SEL127 = T("SEL127", [128, 128], F32)
        SELA = T("SELA", [128, 128], F32)
        SELB = T("SELB", [128, 128], F32)
        FB = T("FB", [128, 8], F32)
        LNG = T("LNG", [128, 512], F32)
        GFIN = T("GFIN", [128, D], F32)
        GT = T("GT", [128, 16], F32)
        WC = T("WC", [128, 2 * NCH, 3], F32)
        BC = T("BC", [128, 2 * NCH], F32)
        BST = T("BST", [128, 8], F32)
        KM = T("KM", [128, NB], F32)
        ST = T("ST", [128, 64], F32)
        CCAR = T("CCAR", [128, 8], F32)
        RG = T("RG", [128, 8], F32)
        CB = T("CB", [128, 8], F32)
        SP_ = T("SP_", [128, 8], F32)
        SPB = T("SPB", [128, 2, 8], F32)
        EPSC = T("EPSC", [128, 2], F32)
        PGB = [PS("PGB%d" % i, [128, 512], F32) for i in range(3)]
        ACCB = [PS("ACCB%d" % i, [128, 512], F32) for i in range(4)]
        PSTB = PS("PSTB", [128, 1024], BF16)
        print("sbuf bytes remaining:", nc.sbuf_bytes_remaining)

        rot = {"pg": 0, "pt": 0}
        att_rot = [0]
        kch_rot = [0]
        PSTF = PSTB[:, :].bitcast(F32)

        def next_pg():
            i = rot["pg"] % 3
            rot["pg"] += 1
            return i

        def next_pt():
            i = rot["pt"] % NPT
            rot["pt"] += 1
            return i

        def pc(name):
            a, b = PC[name]
            return par_d[:, a:b]

        def dma(out, in_, reads, writes):
            return P.op("sp", lambda e: e.dma_start(out=out, in_=in_), reads=reads, writes=writes, dma=True)

        XT_ALL0 = [("XT", 0, 0), ("XT", 0, 1)]
        ARENA_ALL = ["U", "VA", "VN", "OA"] + [("ACTT", c) for c in range(NCH)]

        dma(TRI[:], pc("tri"), [], ["TRI"])
        dma(SEL127[:], pc("sel127"), [], ["SEL127"])
        dma(SELA[:], pc("selA"), [], ["SELA"])
        dma(SELB[:], pc("selB"), [], ["SELB"])
        dma(FB[:], pc("fb"), [], ["FB"])
        dma(LNG[:], pc("lng"), [], ["LNG"])
        dma(GFIN[:], pc("gfin"), [], ["GFIN"])
        dma(GT[:, 0:8], pc("g1T"), [], ["GT"])
        dma(GT[:, 8:16], pc("g2T"), [], ["GT"])
        dma(WC[:].rearrange("p c j -> p (c j)"), pc("wc"), [], ["WC"])
        dma(BC[:], pc("bc"), [], ["BC"])
        dma(BST[:], pc("bsT"), [], ["BST"])
        dma(KM[:], pc("kmask"), [], ["KM"])
        XTf = XT[:].rearrange("p b f -> p (b f)")
        HBf = HB[:].rearrange("p b f -> p (b f)")
        HTf = HT[:].rearrange("p k t -> p (k t)")
        a0 = PC["eye"][0]
        dma(XTf[:, 0:128], pc("eye"), [], XT_ALL0)
        dma(XTf[:, 128:640], par_d[:, PC["cm0"][0]:PC["cm1"][1]], [], XT_ALL0)
        dma(XTf[:, 640:1664], pc("sgwT"), [], XT_ALL0)
        P.op("dve", lambda e: e.tensor_copy(out=ident[:], in_=XTf[:, 0:128]), reads=XT_ALL0, writes=["ident"])
        P.op("dve", lambda e: e.tensor_copy(out=CMASK[:].rearrange("p a t -> p (a t)"), in_=XTf[:, 128:640]), reads=XT_ALL0, writes=["CMASK"])
        P.op("dve", lambda e: e.memset(XTf[64:128, 640:1664].rearrange("p (h t) -> p h t", h=8)[:, :, 0:64], 0.0), reads=[], writes=XT_ALL0)
        P.op("dve", lambda e: e.tensor_copy(out=WMT[:].rearrange("p h t -> p (h t)"), in_=XTf[:, 640:1664]), reads=XT_ALL0, writes=["WMT"])
        P.op("dve", lambda e: e.memset(EPSC[:, 0:1], EPS), writes=["EPSC"])
        P.op("dve", lambda e: e.memset(EPSC[:, 1:2], 1.0), writes=["EPSC"])
        P.op("dve", lambda e: e.memset(CCAR[:], 0.0), writes=["CCAR"])
        P.op("dve", lambda e: e.memset(CARRY[:].rearrange("p c a j -> p (c a j)"), 0.0), writes=[("CARRY", jq) for jq in range(NCH)])
        import os
        if os.environ.get("NOVMEM") != "1":
            P.op("dve", lambda e: e.memset(V[:].rearrange("p b a c -> p (b a) c")[:, :, 64:66], 1.0), writes=[("V", b, s) for b in range(NB) for s in range(2)])

        NCONV = 99 if stage >= 2 else 0
        HB_ALL = [("HB", 0, 0), ("HB", 0, 1)]
        HT_ALL = [("HT", 0), ("HT", 1)]
        stage_f = [(XTf, XT_ALL0), (ARENA[:, 0:2048], ARENA_ALL)]
        stage_b = [(HBf, HB_ALL), (HTf, HT_ALL)]
        conv_engs = ["dve", "act"]
        cnt = [0]

        def convert(slot, pieces, gain_col, bg=False):
            if cnt[0] >= NCONV:
                return
            i = cnt[0] % 2
            cnt[0] += 1
            if bg:
                sf, sfn = ARENA[:, 0:2048], ["cvF"]
                sb, sbn = ABF[:, 4096:6144], ["cvB"]
            else:
                sf, sfn = stage_f[i]
                sb, sbn = stage_b[i]
            wr = list(sfn)
            for (dst, src) in pieces:
                dma(dst(sf), src, [], wr)
            eng = conv_engs[i]
            if gain_col is None:
                if eng == "dve":
                    P.op(eng, lambda e: e.tensor_copy(out=sb[:, :], in_=sf[:, :]), reads=wr, writes=list(sbn))
                else:
                    P.op(eng, lambda e: e.copy(out=sb[:, :], in_=sf[:, :]), reads=wr, writes=list(sbn))
            else:
                for k in range(8):
                    if eng == "dve":
                        P.op(eng, lambda e, k=k: e.tensor_scalar(out=sb[:, k * 256:(k + 1) * 256], in0=sf[:, k * 256:(k + 1) * 256],
                                                               scalar1=GT[:, gain_col + k:gain_col + k + 1], scalar2=None, op0=ALU.mult),
                             reads=wr + ["GT"], writes=list(sbn))
                    else:
                        P.op(eng, lambda e, k=k: e.mul(out=sb[:, k * 256:(k + 1) * 256], in_=sf[:, k * 256:(k + 1) * 256],
                                                     mul=GT[:, gain_col + k:gain_col + k + 1]),
                             reads=wr + ["GT"], writes=list(sbn))
            out_fn = lambda: dma(wsc[slot], sb[:, :], list(sbn), [("wsc", slot)])
            if bg:
                return out_fn
            out_fn()
            return None

        def v3(sf, a, b):
            return sf[:, 0:2048].rearrange("p (a b) -> p a b", a=a)

        conv_jobs = {}
        for s in range(10):
            conv_jobs[S_WIN + s] = ([(lambda sf: v3(sf, 8, 256), win_d[:, s * 256:(s + 1) * 256].rearrange("(k p) c -> p k c", p=128))], 0)
        for s in range(4):
            conv_jobs[S_WOUT + s] = ([(lambda sf: v3(sf, 8, 256), wout_d[:, s * 256:(s + 1) * 256].rearrange("(k p) c -> p k c", p=128))], None)
        for j in range(NCH):
            conv_jobs[S_WUP + j] = ([(lambda sf: v3(sf, 8, 256)[:, :, 0:128], wup_d[:, j * 128:(j + 1) * 128].rearrange("(k p) c -> p k c", p=128)),
                                     (lambda sf: v3(sf, 8, 256)[:, :, 128:256], wup_d[:, DFF + j * 128:DFF + (j + 1) * 128].rearrange("(k p) c -> p k c", p=128))], 8)
        for i in range(11):
            conv_jobs[S_WDN + i] = ([(lambda sf: v3(sf, 2, 1024), wdn_d[i * 256:(i + 1) * 256, :].rearrange("(c p) n -> p c n", p=128))], None)
        for slot in (6, 7, 8, 9):
            convert(slot, *conv_jobs[slot])
        dma(XTf[:, 0:64].rearrange("p (k c) -> p k c", k=8), win_d[:, 2560:2568].rearrange("(k p) c -> p k c", p=128), [], XT_ALL0)
        for k in range(8):
            P.op("dve", lambda e, k=k: e.tensor_scalar(out=WF[:, k, :], in0=XTf[:, k * 8:(k + 1) * 8], scalar1=GT[:, k:k + 1], scalar2=None, op0=ALU.mult),
                 reads=XT_ALL0 + ["GT"], writes=["WF"])
        P.op("dve", lambda e: e.memset(ST[:, 60:61], 0.0), reads=[], writes=ARENA_ALL + ["cvF", "cvB"])
        bg_slots = [sl for sl in list(range(0, 6)) + list(range(10, NSLOT))]
        bg_state = {"i": 0, "pending_out": None}

        def conv_step():
            if bg_state["pending_out"] is not None:
                bg_state["pending_out"]()
                bg_state["pending_out"] = None
            if stage < 2 or bg_state["i"] >= len(bg_slots):
                return
            slot = bg_slots[bg_state["i"]]
            bg_state["i"] += 1
            bg_state["pending_out"] = convert(slot, *conv_jobs[slot], bg=True)

        def conv_flush():
            while bg_state["i"] < len(bg_slots) or bg_state["pending_out"] is not None:
                conv_step()
            P.op("dve", lambda e: e.memset(ST[:, 61:62], 0.0), reads=[], writes=ARENA_ALL + ["cvF", "cvB"])

        wseq = []
        for ci in range(NGRP):
            full = ci >= FIRST_FULL
            if stage < 3 or (not full and ci >= n_ctx):
                continue
            if full and (ci - FIRST_FULL) >= n_full:
                break
            if full:
                wseq += list(range(0, 6)) + list(range(10, 14)) + list(range(14, 36)) + list(range(36, 47))
        wstate = {"use": 0, "load": 0}
        if stage >= 3:
            for sl in range(6, 10):
                dma(WP[NWP + sl - 6][:, :], wsc[sl], [("wsc", sl)], [("WP", NWP + sl - 6)])

        def wload_next():
            n = wstate["load"]
            if n >= len(wseq):
                return
            slot = wseq[n]
            buf = n % NWP
            wstate["load"] += 1
            dma(WP[buf][:, :], wsc[slot], [("wsc", slot)], [("WP", buf)])

        def wacquire(slot):
            if 6 <= slot < 10:
                return NWP + slot - 6
            n = wstate["use"]
            assert wseq[n] == slot, (n, wseq[n], slot)
            return n % NWP

        def wrelease(pinned=False):
            if pinned:
                return
            wstate["use"] += 1
            wload_next()


        evac_rr = [0]

        def rms_front(XT, HB, par, sc):
            for b in range(2):
                P.op("dve", lambda e, b=b: e.scalar_tensor_tensor(out=HB[:, b, :], in0=XT[:, b, :], scalar=1.0, in1=XT[:, b, :],
                                                                 op0=ALU.mult, op1=ALU.mult, accum_out=ST[:, sc + b:sc + b + 1]),
                     reads=[("XT", par, b)], writes=[("HB", par, b), ("ST", sc + b)])
                P.op("act", lambda e, b=b: e.activation(out=ST[:, sc + 2 + b:sc + 3 + b], in_=ST[:, sc + b:sc + b + 1], func=AF.Ln, bias=EPSC[:, 0:1], scale=1.0 / D),
                     reads=[("ST", sc + b), "EPSC"], writes=[("ST", sc + 2 + b)])
                P.op("act", lambda e, b=b: e.activation(out=ST[:, sc + 4 + b:sc + 5 + b], in_=ST[:, sc + 2 + b:sc + 3 + b], func=AF.Exp, scale=-0.5),
                     reads=[("ST", sc + 2 + b)], writes=[("ST", sc + 4 + b)])
                P.op("dve", lambda e, b=b: e.tensor_scalar(out=HB[:, b, :], in0=XT[:, b, :], scalar1=ST[:, sc + 4 + b:sc + 5 + b], scalar2=None, op0=ALU.mult),
                     reads=[("XT", par, b), ("ST", sc + 4 + b)], writes=[("HB", par, b)])

        def rms_back(HB, par):
            for b in range(2):
                for k in range(8):
                    P.op("pe", lambda e, b=b, k=k: e.transpose(out=PSTB[:, k * 128:(k + 1) * 128], in_=HB[:, b, k * 128:(k + 1) * 128], identity=ident[:]),
                         reads=[("HB", par, b), "ident"], writes=["PST"])
                P.op("act", lambda e, b=b: e.copy(out=HT[:, :, b * 128:(b + 1) * 128], in_=PSTB[:, :].rearrange("p (k t) -> p k t", k=8)),
                     reads=["PST"], writes=[("HT", b)] + [("OUTT", kq) for kq in range(8)])

        front_done = set()

        def emit_front(ci):
            front_done.add(ci)
            par = ci % 2
            T0 = ci * G
            dma(XTs[par][:], x_d[T0:T0 + G, :].rearrange("(b p) f -> p b f", p=128), [], [("XT", par, 0), ("XT", par, 1)])
            rms_front(XTs[par], HBs[par], par, 40)

        def HT_reads(b=None):
            if b is None:
                return [("HT", 0), ("HT", 1)]
            return [("HT", b)]

        def mm_feat(pi, hf, wbuf, c0):
            for k in range(8):
                P.op("pe", lambda e, k=k: e.matmul(PGB[pi][:, hf * G:(hf + 1) * G], lhsT=WP[wbuf][:, k * 256 + c0:k * 256 + c0 + 128], rhs=HT[:, k, :],
                                                  start=(k == 0), stop=(k == 7)),
                     reads=[("WP", wbuf)] + HT_reads(), writes=[("PG", pi)])

        def mm_tok(pi, hf, wbuf, b, lhs, lhs_reads):
            for k in range(8):
                P.op("pe", lambda e, k=k: e.matmul(PGB[pi][:, hf * G:(hf + 1) * G], lhsT=lhs[:, k, b * 128:(b + 1) * 128], rhs=WP[wbuf][:, k * 256:(k + 1) * 256],
                                                  start=(k == 0), stop=(k == 7)),
                     reads=[("WP", wbuf)] + lhs_reads, writes=[("PG", pi)])

        def emit_group(ci):
            full = ci >= FIRST_FULL
            gi = ci - FIRST_FULL
            b0 = 2 * ci
            T0 = ci * G
            par = ci % 2
            XT = XTs[par]
            HB = HBs[par]
            if gi == 0:
                conv_flush()
                for _ in range(NWP):
                    wload_next()
            if not full:
                conv_step()
            rms_back(HB, par)

            def cs1(b):
                pi = next_pg()
                for k in range(8):
                    P.op("pe", lambda e, k=k, b=b, pi=pi: e.matmul(PGB[pi][:, 0:8], lhsT=HT[:, k, b * 128:(b + 1) * 128], rhs=WF[:, k, :], start=(k == 0), stop=(k == 7)),
                         reads=["WF"] + HT_reads(b), writes=[("PG", pi)])
                P.op("dve", lambda e, pi=pi, b=b: e.tensor_tensor(out=SPB[:, b, :], in0=PGB[pi][:, 0:8], in1=FB[:, :], op=ALU.add),
                     reads=[("PG", pi), "FB"], writes=[("SPB", b)])
                P.op("act", lambda e, b=b: e.activation(out=SPB[:, b, :], in_=SPB[:, b, :], func=AF.Exp, scale=-1.0), reads=[("SPB", b)], writes=[("SPB", b)])
                P.op("act", lambda e, b=b: e.activation(out=SPB[:, b, :], in_=SPB[:, b, :], func=AF.Ln, bias=EPSC[:, 1:2], scale=1.0), reads=[("SPB", b), "EPSC"], writes=[("SPB", b)])

            def cs2(b):
                pi2 = next_pg()
                P.op("pe", lambda e, pi2=pi2, b=b: e.matmul(PGB[pi2][:, 0:8], lhsT=TRI[:, :], rhs=SPB[:, b, :], start=True, stop=True),
                     reads=["TRI", ("SPB", b)], writes=[("PG", pi2)])
                P.op("dve", lambda e, pi2=pi2: e.tensor_tensor(out=CB[:, :], in0=PGB[pi2][:, 0:8], in1=CCAR[:, :], op=ALU.add),
                     reads=[("PG", pi2), "CCAR"], writes=["CB"])
                P.op("dve", lambda e, b=b: e.tensor_scalar(out=CKB[:, b0 + b, :], in0=CB[:, :], scalar1=KM[:, b0 + b:b0 + b + 1], scalar2=None, op0=ALU.add),
                     reads=["CB", "KM"], writes=[("CKB", b0 + b)])

            def cs3(b):
                pi3 = next_pg()
                P.op("pe", lambda e, pi3=pi3: e.matmul(PGB[pi3][:, 0:8], lhsT=SEL127[:, :], rhs=CB[:, :], start=True, stop=True),
                     reads=["SEL127", "CB"], writes=[("PG", pi3)])
                P.op("dve", lambda e, pi3=pi3: e.tensor_copy(out=CCAR[:, :], in_=PGB[pi3][:, 0:8]), reads=[("PG", pi3)], writes=["CCAR"])
                if b == 0 and full:
                    P.op("dve", lambda e, pi3=pi3: e.tensor_scalar(out=NRG[:, :], in0=PGB[pi3][:, 0:8], scalar1=-1.0, scalar2=None, op0=ALU.mult), reads=[("PG", pi3)], writes=["RG"])

            cstages = [lambda: cs1(0), lambda: cs1(1), lambda: cs2(0), lambda: cs3(0), lambda: cs2(1), lambda: cs3(1)]

            def cstage():
                if cstages:
                    cstages.pop(0)()

            if not full:
                cstage()
                cstage()
            nxt = ci + 1
            has_next = (nxt < NGRP) and not (nxt >= FIRST_FULL and (nxt - FIRST_FULL) >= n_full) and not (stage < 3) and not (nxt < FIRST_FULL and nxt >= n_ctx)
            if not full and has_next:
                emit_front(nxt)
            if full:
                for s in range(4):
                    wb = wacquire(s)
                    dst, dn = (U, "U") if s < 2 else (VA, "VA")
                    hf = s % 2
                    pi = next_pg()
                    for b in range(2):
                        mm_tok(pi, b, wb, b, HT, HT_reads(b))
                    P.op("act", lambda e, pi=pi, dst=dst, hf=hf: e.activation(out=dst[:, :, hf * 256:(hf + 1) * 256], in_=PGB[pi][:, :].rearrange("p (b c) -> p b c", b=2), func=AF.Gelu),
                         reads=[("PG", pi)], writes=[dn] + [("ACTT", c) for c in range(NCH)])
                    wrelease()
                    cstage()
                for s in range(2):
                    wb = wacquire(4 + s)
                    pi = next_pg()
                    for m in range(2):
                        mm_feat(pi, m, wb, m * 128)
                    P.op("act", lambda e, pi=pi, s=s: e.mul(out=QT[:, 2 * s:2 * s + 2, :].rearrange("p a t -> p (a t)"), in_=PGB[pi][:, :], mul=0.125),
                         reads=[("PG", pi)], writes=[("QT", 2 * s), ("QT", 2 * s + 1)])
                    wrelease()
                    cstage()
            for s in range(2):
                wb = wacquire(6 + s)
                pi = next_pg()
                for m in range(2):
                    mm_feat(pi, m, wb, m * 128)
                for m in range(2):
                    hp = 2 * s + m
                    P.op("act", lambda e, pi=pi, hp=hp, m=m: e.copy(out=KSTG[:, hp, :], in_=PGB[pi][:, m * G:(m + 1) * G]),
                         reads=[("PG", pi)], writes=[("KSTG", hp)])
                wrelease(pinned=True)
                if not full:
                    cstage()
            dma(ksc[:, :, T0:T0 + G].rearrange("h p t -> p h t"), KSTG[:, :, :], [("KSTG", hp) for hp in range(4)],
                [("KD", hp, bb) for hp in range(4) for bb in (b0, b0 + 1)])
            if not full:
                conv_step()
            for s in range(2):
                wb = wacquire(8 + s)
                pi = next_pg()
                for b in range(2):
                    mm_tok(pi, b, wb, b, HT, HT_reads(b))
                for b in range(2):
                    for a in range(2):
                        P.op("dve", lambda e, b=b, pi=pi, s=s, a=a: e.tensor_copy(
                            out=V[:, b0 + b, 2 * s + a, :].rearrange("p (e c) -> p e c", e=2)[:, :, 0:64],
                            in_=PGB[pi][:, b * G + a * 128:b * G + (a + 1) * 128].rearrange("p (e c) -> p e c", e=2)),
                             reads=[("PG", pi)], writes=[("V", b0 + b, s)])
                wrelease(pinned=True)
                if not full:
                    cstage()
            if not full:
                conv_step()
            while cstages:
                cstage()
            if not full:
                return
            if dbg and gi == dbg_g:
                dma(dbg_d["d_qT"], QT[:].rearrange("p a t -> p (a t)"), [("QT", h) for h in range(4)], ["dq"])
            def emit_sg():
                SQR = [("RB", 0), ("RB", 1)]
                for b in range(2):
                    va3 = VA[:, b, :].rearrange("p (h d) -> p h d", h=8)
                    P.op("dve", lambda e, va3=va3: e.tensor_reduce(out=ST[:, 8:16], in_=va3, axis=AX.X, op=ALU.add), reads=["VA"], writes=[("ST", "s1")])
                    P.op("dve", lambda e, b=b: e.tensor_tensor(out=SQ[:, :], in0=VA[:, b, :], in1=VA[:, b, :], op=ALU.mult), reads=["VA"], writes=SQR)
                    P.op("dve", lambda e: e.tensor_reduce(out=ST[:, 16:24], in_=SQ[:, :].rearrange("p (h d) -> p h d", h=8), axis=AX.X, op=ALU.add),
                         reads=SQR, writes=[("ST", "s2")])
                    P.op("dve", lambda e: e.tensor_scalar(out=ST[:, 8:16], in0=ST[:, 8:16], scalar1=1.0 / 64, scalar2=None, op0=ALU.mult),
                         reads=[("ST", "s1")], writes=[("ST", "s1")])
                    P.op("dve", lambda e: e.tensor_tensor(out=ST[:, 24:32], in0=ST[:, 8:16], in1=ST[:, 8:16], op=ALU.mult),
                         reads=[("ST", "s1")], writes=[("ST", "msq")])
                    P.op("dve", lambda e: e.scalar_tensor_tensor(out=ST[:, 16:24], in0=ST[:, 16:24], scalar=1.0 / 64, in1=ST[:, 24:32], op0=ALU.mult, op1=ALU.subtract),
                         reads=[("ST", "s2"), ("ST", "msq")], writes=[("ST", "s2")])
                    P.op("act", lambda e: e.activation(out=ST[:, 16:24], in_=ST[:, 16:24], func=AF.Ln, bias=EPSC[:, 0:1], scale=1.0),
                         reads=[("ST", "s2"), "EPSC"], writes=[("ST", "s2")])
                    P.op("act", lambda e: e.activation(out=ST[:, 16:24], in_=ST[:, 16:24], func=AF.Exp, scale=-0.5),
                         reads=[("ST", "s2")], writes=[("ST", "s2")])
                    P.op("dve", lambda e, va3=va3: e.tensor_tensor(out=va3, in0=va3, in1=ST[:, 8:16].unsqueeze(2).to_broadcast([128, 8, 64]), op=ALU.subtract),
                         reads=["VA", ("ST", "s1")], writes=["VA"])
                    P.op("dve", lambda e, va3=va3: e.tensor_tensor(out=va3, in0=va3, in1=ST[:, 16:24].unsqueeze(2).to_broadcast([128, 8, 64]), op=ALU.mult),
                         reads=["VA", ("ST", "s2")], writes=["VA"])
                    P.op("dve", lambda e, b=b: e.tensor_tensor(out=VN[:, b, :], in0=VA[:, b, :], in1=LNG[:, :], op=ALU.mult),
                         reads=["VA", "LNG"], writes=["VN"])
                    pi = next_pg()
                    for h in range(8):
                        P.op("pe", lambda e, h=h, b=b, pi=pi: e.matmul(PGB[pi][:, h * 64:(h + 1) * 64], lhsT=WMT[:, h, :], rhs=VN[:, b, h * 64:(h + 1) * 64],
                                                                      start=True, stop=True),
                             reads=["WMT", "VN"], writes=[("PG", pi)])
                    P.op("dve", lambda e, pi=pi: e.tensor_tensor(out=SQ[:, :].rearrange("p (h d) -> p h d", h=8),
                                                                 in0=PGB[pi][:, :].rearrange("p (h d) -> p h d", h=8),
                                                                 in1=BST[:, :].unsqueeze(2).to_broadcast([128, 8, 64]), op=ALU.add),
                         reads=[("PG", pi), "BST"], writes=SQR)
                    P.op("dve", lambda e, b=b: e.tensor_tensor(out=OA[:, b, :], in0=SQ[:, :], in1=U[:, b, :], op=ALU.mult),
                         reads=SQR + ["U"], writes=["OA"])
                    for kk in range(4):
                        P.op("pe", lambda e, b=b, kk=kk: e.transpose(out=PSTB[:, kk * 128:(kk + 1) * 128], in_=OA[:, b, kk * 128:(kk + 1) * 128], identity=ident[:]),
                             reads=["OA", "ident"], writes=["PST"])
                    P.op("act", lambda e, b=b: e.copy(out=OUTT[:, 0:4, b * 128:(b + 1) * 128], in_=PSTB[:, 0:512].rearrange("p (k t) -> p k t", k=4)),
                         reads=["PST"], writes=[("OUTT", k) for k in range(4)] + [("HT", 0), ("HT", 1)])

            nj = b0 + 2
            nchunk = (nj + 7) // 8

            def kload(hp, c):
                nblk = min(8, nj - 8 * c)
                kb = kch_rot[0] % 3
                kch_rot[0] += 1
                dma(KCH[kb][:, 0:nblk * 128], ksc[hp][:, c * 1024:c * 1024 + nblk * 128],
                    [("KD", hp, 8 * c + q) for q in range(nblk)], [("KCH", kb)])
                return kb

            for hp in range(4):
                if hp == 1:
                    emit_sg()
                accs = [0, 1]
                kbufs = {0: kload(hp, 0)}
                if nchunk > 1:
                    kbufs[1] = kload(hp, 1)
                for ee in range(2):
                    h = 2 * hp + ee
                    P.op("act", lambda e, h=h, ee=ee: e.activation(out=WG[ee][:, 0:nj], in_=CKB[:, 0:nj, h], func=AF.Exp, bias=NRG[:, h:h + 1], scale=1.0),
                         reads=[("CKB", j) for j in range(nj)] + ["RG"], writes=[("WG", ee)])

                def vprep(c, hp=hp):
                    nblk = min(8, nj - 8 * c)
                    cb = c % 2
                    for ee in range(2):
                        Kc = 65 if ee == 0 else 128
                        c0 = 0 if ee == 0 else 2
                        P.op("dve", lambda e, ee=ee, Kc=Kc, c0=c0, cb=cb, nblk=nblk, c=c: e.tensor_tensor(
                            out=VP[ee][cb][:, 0:nblk, 0:Kc], in0=V[:, 8 * c:8 * c + nblk, hp, c0:c0 + Kc],
                            in1=WG[ee][:, 8 * c:8 * c + nblk].unsqueeze(2).to_broadcast([128, nblk, Kc]), op=ALU.mult),
                             reads=[("V", 8 * c + q_, hp // 2) for q_ in range(nblk)] + [("WG", ee)], writes=[("VP", ee, cb)])

                vprep(0)
                if nchunk > 1:
                    vprep(1)

                def qk_step(j, hp=hp):
                    bp = att_rot[0] % 3
                    att_rot[0] += 1
                    banks = [[(PGB[0], ("PG", 0)), (PGB[1], ("PG", 1))], [(PGB[2], ("PG", 2)), (PSTF, "PST")],
                             [(ACCB[2][:, :], ("ACC", 2)), (ACCB[3][:, :], ("ACC", 3))]][bp]
                    for q in range(2):
                        jj = j + q
                        diag = jj >= b0
                        for ee in range(2):
                            kr = slice(ee * 64, (ee + 1) * 64)
                            bk, bkey = banks[ee]
                            kb_ = kbufs[jj // 8]
                            jo = jj % 8
                            P.op("pe", lambda e, ee=ee, kr=kr, bk=bk, jj=jj, q=q, diag=diag, kb_=kb_, jo=jo: e.matmul(bk[:, q * G:(q + 1) * G], lhsT=KCH[kb_][kr, jo * 128:(jo + 1) * 128], rhs=QT[kr, hp, :],
                                                                                                  start=True, stop=not diag),
                                 reads=[("KCH", kb_), ("QT", hp)], writes=[bkey])
                        if diag:
                            for ee in range(2):
                                bk, bkey = banks[ee]
                                P.op("pe", lambda e, bk=bk, jj=jj, q=q: e.matmul(bk[:, q * G:(q + 1) * G], lhsT=ident[:, :], rhs=CMASK[:, jj - b0, :], start=False, stop=True),
                                     reads=["ident", "CMASK"], writes=[bkey])
                    return banks

                def exp_step(j, banks):
                    tis = []
                    for ee in range(2):
                        bk, bkey = banks[ee]
                        ti = next_pt()
                        P.op("act", lambda e, ti=ti, bk=bk: e.activation(out=PT[ti][:, :], in_=bk[:, :], func=AF.Exp, scale=1.0),
                             reads=[bkey], writes=[("PT", ti)])
                        tis.append(ti)
                    return tis

                def pv_step(j, tis, hp=hp, accs=accs):
                    for q in range(2):
                        jj = j + q
                        cb = (jj // 8) % 2
                        jo = jj % 8
                        t0_, t1_ = tis[0], tis[1]
                        P.op("pe", lambda e, jj=jj, cb=cb, jo=jo, t0_=t0_, q=q: e.matmul(ACCB[accs[0]][0:65, 0:G], lhsT=VP[0][cb][:, jo, 0:65], rhs=PT[t0_][:, q * G:(q + 1) * G], start=(jj == 0), stop=(jj == nj - 1)),
                             reads=[("VP", 0, cb), ("PT", t0_)], writes=[("ACC", accs[0])])
                        P.op("pe", lambda e, jj=jj, cb=cb, jo=jo, t1_=t1_, q=q: e.matmul(ACCB[accs[1]][:, 0:G], lhsT=VP[1][cb][:, jo, :], rhs=PT[t1_][:, q * G:(q + 1) * G], start=(jj == 0), stop=(jj == nj - 1)),
                             reads=[("VP", 1, cb), ("PT", t1_)], writes=[("ACC", accs[1])])

                pend = []
                for j in range(0, nj, 2):
                    if j % 8 == 0 and j > 0 and (j // 8 + 1) < nchunk:
                        kbufs[j // 8 + 1] = kload(hp, j // 8 + 1)
                    if j % 8 == 4 and (j // 8 + 1) < nchunk and (j // 8 + 1) >= 2:
                        vprep(j // 8 + 1)
                    banks = qk_step(j)
                    tis = exp_step(j, banks)
                    pend.append((j, tis))
                    if len(pend) > 2:
                        pv_step(*pend.pop(0))
                while pend:
                    pv_step(*pend.pop(0))
                for ee in range(2):
                    h = 2 * hp + ee
                    acc = accs[ee]
                    kr = slice(ee * 64, (ee + 1) * 64)
                    oi = ee
                    K = 65 if ee == 0 else 128
                    sel = SELA if ee == 0 else SELB
                    seln = "SELA" if ee == 0 else "SELB"
                    P.op("act", lambda e, oi=oi, K=K, acc=acc: e.copy(out=OSB[oi][0:K, :], in_=ACCB[acc][0:K, 0:G]), reads=[("ACC", acc)], writes=[("OSB", oi)])
                    pi = next_pg()
                    P.op("pe", lambda e, oi=oi, K=K, sel=sel, pi=pi: e.matmul(PGB[pi][:, 0:G], lhsT=sel[0:K, :], rhs=OSB[oi][0:K, :], start=True, stop=True),
                         reads=[seln, ("OSB", oi)], writes=[("PG", pi)])
                    P.op("dve", lambda e, oi=oi, pi=pi: e.tensor_scalar(out=RB[oi][:, :], in0=PGB[pi][:, 0:G], scalar1=1e-30, scalar2=None, op0=ALU.max),
                         reads=[("PG", pi)], writes=[("RB", oi)])
                    P.op("dve", lambda e, oi=oi: e.reciprocal(out=RB[oi][:, :], in_=RB[oi][:, :]), reads=[("RB", oi)], writes=[("RB", oi)])
                    P.op("dve", lambda e, oi=oi, kr=kr, hp=hp: e.tensor_tensor(out=OUTT[kr, 4 + hp, :], in0=OSB[oi][kr, :], in1=RB[oi][kr, :], op=ALU.mult),
                         reads=[("OSB", oi), ("RB", oi)], writes=[("OUTT", 4 + hp), ("HT", 0), ("HT", 1)])
            if dbg and gi == dbg_g:
                for k in range(8):
                    P.op("dve", lambda e, k=k: e.tensor_copy(out=YGV[0][:, 0, :], in_=OUTT[:, k, :]), reads=[("OUTT", k)], writes=["dtmp"])
                    dma(dbg_d["d_outT"][:, k * 256:(k + 1) * 256], YGV[0][:, 0, :], ["dtmp"], ["dtmp2"])
                dma(dbg_d["d_ckb"], CKB[:].rearrange("p b h -> p (b h)"), [("CKB", j) for j in range(NB)], ["dck"])
            if has_next:
                emit_front(nxt)
            for n in range(4):
                wb = wacquire(S_WOUT + n)
                pi = next_pg()
                for b in range(2):
                    mm_tok(pi, b, wb, b, OUTT, [("OUTT", k) for k in range(8)])
                P.op("dve", lambda e, n=n, pi=pi: e.tensor_tensor(out=XT[:, :, n * 256:(n + 1) * 256], in0=XT[:, :, n * 256:(n + 1) * 256],
                                                                  in1=PGB[pi][:, :].rearrange("p (b c) -> p b c", b=2), op=ALU.add),
                     reads=[("PG", pi), ("XT", par, 0), ("XT", par, 1)], writes=[("XT", par, 0), ("XT", par, 1)])
                wrelease()
            if dbg and gi == dbg_g:
                dma(dbg_d["d_x1"].rearrange("(b p) f -> p b f", p=128), XT[:], [("XT", par, 0), ("XT", par, 1)], ["dx1"])
            rms_front(XT, HB, par, 0)
            rms_back(HB, par)
            def ffn_tail(j):
                r = j % 2
                P.op("act", lambda e, r=r: e.activation(out=YGV[r][:, 0, :], in_=YGV[r][:, 0, :], func=AF.Silu), reads=[("YGV", r, 0)], writes=[("YGV", r, 0)])
                P.op("pool", lambda e, r=r, j=j: e.tensor_tensor(out=ACTT[:, j, :], in0=YGV[r][:, 0, :], in1=YGV[r][:, 1, :], op=ALU.mult),
                     reads=[("YGV", r, 0), ("YGV", r, 1)], writes=[("ACTT", j), "U", "VA", "VN", "OA"])

            for j in range(NCH):
                wb = wacquire(S_WUP + j)
                r = j % 2
                pi = next_pg()
                for part in range(2):
                    mm_feat(pi, part, wb, part * 128)
                P.op("pool", lambda e, r=r, j=j: e.tensor_copy(out=AGV[r][:, :, 0:2], in_=CARRY[:, j, :, :]),
                     reads=[("CARRY", j)] + [("OUTT", 4 + hq) for hq in range(4)], writes=[("AGVc", r)])
                P.op("act", lambda e, r=r, pi=pi: e.copy(out=AGV[r][:, :, 2:258], in_=PGB[pi][:, :].rearrange("p (a t) -> p a t", a=2)),
                     reads=[("PG", pi)], writes=[("AGV", r, 0), ("AGV", r, 1)])
                for part in range(2):
                    cj = part * NCH + j
                    P.op("act", lambda e, r=r, part=part, cj=cj, pi=pi: e.activation(out=YGV[r][:, part, :], in_=PGB[pi][:, part * G:(part + 1) * G], func=AF.Identity,
                                                                                 bias=BC[:, cj:cj + 1], scale=WC[:, cj, 2:3]),
                         reads=[("PG", pi), "WC", "BC"], writes=[("YGV", r, part)])
                if j >= 1:
                    ffn_tail(j - 1)
                P.op("pool", lambda e, r=r, j=j: e.tensor_copy(out=CARRY[:, j, :, :], in_=AGV[r][:, :, 256:258]),
                     reads=[("AGV", r, 0), ("AGV", r, 1)], writes=[("CARRY", j)])
                for tap in (1, 0):
                    for part in range(2):
                        cj = part * NCH + j
                        P.op("dve", lambda e, r=r, part=part, cj=cj, tap=tap: e.scalar_tensor_tensor(out=YGV[r][:, part, :], in0=AGV[r][:, part, tap:tap + 256], scalar=WC[:, cj, tap:tap + 1],
                                                                                                  in1=YGV[r][:, part, :], op0=ALU.mult, op1=ALU.add),
                             reads=[("AGV", r, part), ("AGVc", r), "WC", ("YGV", r, part)], writes=[("YGV", r, part)])
                wrelease()
            ffn_tail(NCH - 1)
            for i in range(11):
                wb = wacquire(S_WDN + i)
                for c2 in range(2):
                    ch = 2 * i + c2
                    for b in range(2):
                        for n2 in range(2):
                            acc = b * 2 + n2
                            P.op("pe", lambda e, ch=ch, c2=c2, b=b, n2=n2, acc=acc, wb=wb: e.matmul(ACCB[acc][:, :], lhsT=ACTT[:, ch, b * 128:(b + 1) * 128],
                                                                                                 rhs=WP[wb][:, c2 * 1024 + n2 * 512:c2 * 1024 + (n2 + 1) * 512],
                                                                                                 start=(ch == 0), stop=(ch == NCH - 1)),
                                 reads=[("WP", wb), ("ACTT", ch)], writes=[("ACC", acc)])
                wrelease()
            for b in range(2):
                for n2 in range(2):
                    acc = b * 2 + n2
                    P.op("dve", lambda e, b=b, n2=n2, acc=acc: e.tensor_tensor(out=XT[:, b, n2 * 512:(n2 + 1) * 512], in0=XT[:, b, n2 * 512:(n2 + 1) * 512], in1=ACCB[acc][:, :], op=ALU.add),
                         reads=[("ACC", acc), ("XT", par, b)], writes=[("XT", par, b)])
            for b in range(2):
                P.op("dve", lambda e, b=b: e.scalar_tensor_tensor(out=HB[:, b, :], in0=XT[:, b, :], scalar=1.0, in1=XT[:, b, :],
                                                                 op0=ALU.mult, op1=ALU.mult, accum_out=ST[:, 32 + b:33 + b]),
                     reads=[("XT", par, b)], writes=[("HB", par, b), ("ST", 32 + b)])
                P.op("act", lambda e, b=b: e.activation(out=ST[:, 34 + b:35 + b], in_=ST[:, 32 + b:33 + b], func=AF.Ln, bias=EPSC[:, 0:1], scale=1.0 / D),
                     reads=[("ST", 32 + b), "EPSC"], writes=[("ST", 34 + b)])
                P.op("act", lambda e, b=b: e.activation(out=ST[:, 36 + b:37 + b], in_=ST[:, 34 + b:35 + b], func=AF.Exp, scale=-0.5),
                     reads=[("ST", 34 + b)], writes=[("ST", 36 + b)])
                P.op("dve", lambda e, b=b: e.scalar_tensor_tensor(out=OUTF[:, b, :], in0=XT[:, b, :], scalar=ST[:, 36 + b:37 + b], in1=GFIN[:, :], op0=ALU.mult, op1=ALU.mult),
                     reads=[("XT", par, b), ("ST", 36 + b), "GFIN"], writes=ARENA_ALL)
            if gi >= 1:
                r0 = (gi - 1) * G
                dma(out_d[r0:r0 + G, :].rearrange("(b p) f -> p b f", p=128), OUTF, ARENA_ALL, [("out", gi)])

        dbg_g = 1
        first = True
        for ci in range(NGRP):
            if stage < 3 or (ci < FIRST_FULL and ci >= n_ctx):
                continue
            if ci >= FIRST_FULL and (ci - FIRST_FULL) >= n_full:
                break
            if ci not in front_done:
                emit_front(ci)
            emit_group(ci)
        fin = [("out", gi) for gi in range(1, n_full)]
        if dbg:
            fin += ["dq", "dtmp2", "dck", "dx1"]
        P.op("sp", None, reads=fin)
        P.emit()
        print("ops:", len(P.ops), "signals:", P.stats)
    return nc


def make_par(inputs, core):
    par = np.zeros((128, NPAR), np.float32)

    def put(name, arr):
        a, b = PC[name]
        par[:, a:b] = np.asarray(arr, np.float32).reshape(128, b - a)

    put("eye", np.eye(128, dtype=np.float32))
    s = np.arange(128)
    put("tri", (s[:, None] <= s[None, :]).astype(np.float32))
    sel = np.zeros((128, 128), np.float32); sel[127, :] = 1.0
    put("sel127", sel)
    sel = np.zeros((128, 128), np.float32); sel[64, :] = 1.0
    put("selA", sel)
    sel = np.zeros((128, 128), np.float32); sel[63, :] = 1.0
    put("selB", sel)
    tri_mask = np.where(s[:, None] <= s[None, :], 0.0, NEG).astype(np.float32)
    cm0 = np.concatenate([tri_mask, np.zeros((128, 128), np.float32)], axis=1)
    cm1 = np.concatenate([np.full((128, 128), NEG, np.float32), tri_mask], axis=1)
    put("cm0", cm0)
    put("cm1", cm1)
    put("fb", np.broadcast_to(inputs["f_bias"].reshape(1, 8), (128, 8)))
    put("lng", np.broadcast_to(inputs["sg_ln_g"].reshape(1, 512), (128, 512)))
    put("gfin", np.broadcast_to(inputs["norm_final_g"].reshape(1, D), (128, D)))
    put("g1T", inputs["norm_mix_g"].reshape(8, 128).T)
    put("g2T", inputs["norm_ffn_g"].reshape(8, 128).T)
    wc = inputs["w_conv"].reshape(3, 2 * NCH, 128)
    put("wc", np.transpose(wc, (2, 1, 0)))
    put("bc", inputs["b_conv"].reshape(2 * NCH, 128).T)
    put("bsT", inputs["sg_b"].reshape(8, 128).T)
    km = np.zeros((128, NB), np.float32)
    if core % 2 == 0:
        km[:, 0:32] = NEG
    put("kmask", km)
    sgw = inputs["sg_w"].reshape(8, 128, 128)
    put("sgwT", np.transpose(sgw, (2, 0, 1)))
    return par


_NC_CACHE = {}


def kernel(**inputs):
    inputs = {k: np.asarray(v) for k, v in inputs.items()}
    x = inputs["x"].astype(np.float32, copy=False)
    key = "full"
    if key not in _NC_CACHE:
        _NC_CACHE[key] = build_nc()
    nc = _NC_CACHE[key]
    in_maps = []
    for c in range(8):
        b, half = c // 2, c % 2
        if half == 0:
            xc = np.concatenate([np.zeros((4096, D), np.float32), x[b, 0:4096]], axis=0)
        else:
            xc = x[b]
        in_maps.append({
            "x": np.ascontiguousarray(xc),
            "par": make_par(inputs, c),
            "w_in": np.ascontiguousarray(inputs["w_in"][0]),
            "w_out": np.ascontiguousarray(inputs["w_out"][0]),
            "w_up": np.ascontiguousarray(inputs["w_up"][0]),
            "w_down": np.ascontiguousarray(inputs["w_down"][0]),
        })
    res = run_bass_kernel_spmd(nc, in_maps, core_ids=list(range(8)))
    out = np.empty((4, SEQ, D), np.float32)
    for c in range(8):
        b, half = c // 2, c % 2
        out[b, half * 4096:(half + 1) * 4096] = res.results[c]["out"]
    return out
```

```python
import contextlib
import numpy as np
import concourse.bass as bass
import concourse.mybir as mybir
from concourse.bass_utils import run_bass_kernel_spmd

F32 = mybir.dt.float32
BF16 = mybir.dt.bfloat16
AF = mybir.ActivationFunctionType
ALU = mybir.AluOpType
AX = mybir.AxisListType

SAME_ENGINE_SYNC = True
NDMASEM = 8

D = 1024
SEQ = 8192
NB = 64
G = 256
NGRP = 32
FIRST_FULL = 15
DFF = 2816
NCH = 22
EPS = 1e-6
NEG = -30000.0

PC = {}
_off = 0
for _n, _w in [("eye", 128), ("tri", 128), ("sel127", 128), ("selA", 128), ("selB", 128),
               ("cm0", 256), ("cm1", 256), ("fb", 8), ("lng", 512), ("gfin", 1024),
               ("g1T", 8), ("g2T", 8), ("wc", 132), ("bc", 44), ("bsT", 8), ("kmask", 64),
               ("sgwT", 1024)]:
    PC[_n] = (_off, _off + _w)
    _off += _w
NPAR = _off

S_WIN, S_WOUT, S_WUP, S_WDN = 0, 10, 14, 36
NSLOT = 47


class Op:
    __slots__ = ("eng", "fn", "deps", "idx", "signaled", "ev", "is_dma", "dq", "dslot")

    def __init__(self, eng, fn, idx, is_dma):
        self.eng = eng
        self.fn = fn
        self.idx = idx
        self.deps = []
        self.signaled = False
        self.ev = None
        self.is_dma = is_dma


class Prog:
    ENGS = ("sp", "act", "dve", "pool", "pe")

    def __init__(self, nc):
        self.nc = nc
        self.ops = []
        self.last_writer = {}
        self.readers = {}
        self.dma_count = {e: 0 for e in self.ENGS}

    def op(self, eng, fn, reads=(), writes=(), dma=False):
        o = Op(eng, fn, len(self.ops), dma)
        deps = {}
        for r in reads:
            w = self.last_writer.get(r)
            if w is not None:
                deps[w.idx] = w
        for r in writes:
            w = self.last_writer.get(r)
            if w is not None:
                deps[w.idx] = w
            for rd in self.readers.get(r, ()):
                deps[rd.idx] = rd
        o.deps = list(deps.values())
        for r in reads:
            self.readers.setdefault(r, []).append(o)
        for r in writes:
            self.last_writer[r] = o
            self.readers[r] = []
        if dma:
            o.dq = eng
            o.dslot = self.dma_count[eng]
            self.dma_count[eng] += 1
        self.ops.append(o)
        return o

    def emit(self):
        nc = self.nc
        ops = self.ops

        def skip(d, o):
            return (not d.is_dma) and (not o.is_dma) and d.eng == o.eng and (d.eng == "pe" or not SAME_ENGINE_SYNC)

        for o in ops:
            for d in o.deps:
                if d.is_dma or skip(d, o):
                    continue
                d.signaled = True
        with contextlib.ExitStack() as es:
            esem = {e: es.enter_context(nc.semaphore("c_" + e)) for e in self.ENGS}
            dsem = {}
            for e in self.ENGS:
                if self.dma_count[e] > 0:
                    dsem[e] = [es.enter_context(nc.semaphore("d_%s_%d" % (e, i))) for i in range(NDMASEM)]
            cnt = {e: 0 for e in self.ENGS}
            for o in ops:
                if o.is_dma:
                    s = dsem[o.dq][o.dslot % NDMASEM]
                    o.ev = (s, 16 * (o.dslot // NDMASEM + 1))
                elif o.signaled:
                    cnt[o.eng] += 1
                    o.ev = (esem[o.eng], cnt[o.eng])
            self.stats = dict(cnt)
            block = es.enter_context(nc.Block())

            def body(engname, eng):
                seen = {}
                for o in ops:
                    if o.eng != engname:
                        continue
                    waits = []
                    for d in o.deps:
                        if d.ev is None or skip(d, o):
                            continue
                        waits.append(d.ev)
                    if o.is_dma and o.dslot >= NDMASEM:
                        s = dsem[o.dq][o.dslot % NDMASEM]
                        waits.append((s, 16 * (o.dslot // NDMASEM)))
                    mx = {}
                    for (s, v) in waits:
                        k = id(s)
                        if k not in mx or mx[k][1] < v:
                            mx[k] = (s, v)
                    for k, (s, v) in mx.items():
                        if seen.get(k, 0) >= v:
                            continue
                        seen[k] = v
                        eng.wait_ge(s, v)
                    if o.fn is None:
                        continue
                    inst = o.fn(eng)
                    if o.is_dma:
                        inst.then_inc(o.ev[0], 16)
                    elif o.signaled:
                        inst.then_inc(o.ev[0], 1)

            @block.sync
            def _(e):
                body("sp", e)

            @block.scalar
            def _(e):
                body("act", e)

            @block.vector
            def _(e):
                body("dve", e)

            @block.gpsimd
            def _(e):
                body("pool", e)

            @block.tensor
            def _(e):
                body("pe", e)


def build_nc(n_full=17, dbg=False, stage=99, n_ctx=FIRST_FULL):
    nc = bass.Bass("TRN2", target_bir_lowering=False)
    x_d = nc.dram_tensor("x", [SEQ, D], F32, kind="ExternalInput").ap()
    par_d = nc.dram_tensor("par", [128, NPAR], F32, kind="ExternalInput").ap()
    win_d = nc.dram_tensor("w_in", [D, 2568], F32, kind="ExternalInput").ap()
    wout_d = nc.dram_tensor("w_out", [D, D], F32, kind="ExternalInput").ap()
    wup_d = nc.dram_tensor("w_up", [D, 2 * DFF], F32, kind="ExternalInput").ap()
    wdn_d = nc.dram_tensor("w_down", [DFF, D], F32, kind="ExternalInput").ap()
    out_d = nc.dram_tensor("out", [4096, D], F32, kind="ExternalOutput").ap()
    wsc = nc.dram_tensor("wsc", [NSLOT, 128, 2048], BF16, kind="Internal").ap()
    ksc = nc.dram_tensor("ksc", [4, 128, SEQ], BF16, kind="Internal").ap()
    dbg_d = {}
    if dbg:
        for nm, shp in [("d_x1", [256, D]), ("d_outT", [128, 8 * 256]), ("d_ckb", [128, 64 * 8]),
                        ("d_qT", [128, 4 * 256]), ("d_oa", [128, 2 * 512])]:
            dbg_d[nm] = nc.dram_tensor(nm, shp, F32, kind="ExternalOutput").ap()

    with contextlib.ExitStack() as es:
        def T(name, shape, dt):
            return es.enter_context(nc.sbuf_tensor(name, shape, dt))

        def PS(name, shape, dt):
            return es.enter_context(nc.psum_tensor(name, shape, dt))

        P = Prog(nc)
        KSTG = T("KSTG", [128, 4, G], BF16)
        KCH = [T("KCH%d" % i, [128, 1024], BF16) for i in range(3)]
        V = T("V", [128, NB, 4, 132], BF16)
        XTs = [T("XT%d" % i, [128, 2, D], F32) for i in range(2)]
        XT = XTs[0]
        HBs = [T("HB%d" % i, [128, 2, D], BF16) for i in range(2)]
        HB = HBs[0]
        HT = T("HT", [128, 8, G], BF16)
        FA = T("FA", [128, 3072], F32)
        QT = FA[:, 0:512].bitcast(BF16).rearrange("p (a t) -> p a t", a=4)
        NPT = 8
        PT = [FA[:, 512 + 256 * i:512 + 256 * (i + 1)].bitcast(BF16) for i in range(NPT)]
        OSB = [FA[:, 2560 + 256 * i:2560 + 256 * (i + 1)] for i in range(2)]
        AGV = [FA[:, 516 * i:516 * (i + 1)].rearrange("p (a c) -> p a c", a=2) for i in range(2)]
        YGV = [FA[:, 1032 + 512 * i:1032 + 512 * (i + 1)].rearrange("p (a c) -> p a c", a=2) for i in range(2)]
        ARENA = T("ARENA", [128, 3072], F32)
        OUTF = ARENA[:, 0:2048].rearrange("p (b f) -> p b f", b=2)
        U = ARENA[:, 0:1024].rearrange("p (b f) -> p b f", b=2)
        VA = ARENA[:, 1024:2048].rearrange("p (b f) -> p b f", b=2)
        ABF = ARENA[:].bitcast(BF16)
        VN = ABF[:, 4096:5120].rearrange("p (b f) -> p b f", b=2)
        OA = ABF[:, 5120:6144].rearrange("p (b f) -> p b f", b=2)
        ACTT = ABF[:, 0:NCH * G].rearrange("p (c t) -> p c t", c=NCH)
        OUTT = HT
        NWP = 10
        NPIN = 4
        WP = [T("WP%d" % i, [128, 2048], BF16) for i in range(NWP + NPIN)]
        RBT = T("RBT", [128, 512], F32)
        RB = [RBT[:, i * G:(i + 1) * G] for i in range(2)]
        SQ = RBT
        CKB = T("CKB", [128, NB, 8], F32)
        WG = [T("WG%d" % i, [128, NB], F32) for i in range(2)]
        NRG = T("NRG", [128, 8], F32)
        VP = [[T("VP%d_%d" % (e_, i), [128, 8, 128], BF16) for i in range(2)] for e_ in range(2)]
        WF = T("WF", [128, 8, 8], BF16)
        CARRY = T("CARRY", [128, NCH, 2, 2], F32)
        ident = T("ident", [128, 128], BF16)
        CMASK = T("CMASK", [128, 2, G], BF16)
        WMT = T("WMT", [128, 8, 128], BF16)
        TRI = T("TRI", [128, 128], F32)
        SEL127 = T("SEL127", [128, 128], F32)
        SELA = T("SELA", [128, 128], F32)
        SELB = T("SELB", [128, 128], F32)
        FB = T("FB", [128, 8], F32)
        LNG = T("LNG", [128, 512], F32)
        GFIN = T("GFIN", [128, D], F32)
        GT = T("GT", [128, 16], F32)
        WC = T("WC", [128, 2 * NCH, 3], F32)
        BC = T("BC", [128, 2 * NCH], F32)
        BST = T("BST", [128, 8], F32)
        KM = T("KM", [128, NB], F32)
        ST = T("ST", [128, 64], F32)
        CCAR = T("CCAR", [128, 8], F32)
        RG = T("RG", [128, 8], F32)
        CB = T("CB", [128, 8], F32)
        SP_ = T("SP_", [128, 8], F32)
        SPB = T("SPB", [128, 2, 8], F32)
        EPSC = T("EPSC", [128, 2], F32)
        PGB = [PS("PGB%d" % i, [128, 512], F32) for i in range(3)]
        ACCB = [PS("ACCB%d" % i, [128, 512], F32) for i in range(4)]
        PSTB = PS("PSTB", [128, 1024], BF16)
        print("sbuf bytes remaining:", nc.sbuf_bytes_remaining)

        rot = {"pg": 0, "pt": 0}
        att_rot = [0]
        kch_rot = [0]
        PSTF = PSTB[:, :].bitcast(F32)

        def next_pg():
            i = rot["pg"] % 3
            rot["pg"] += 1
            return i

        def next_pt():
            i = rot["pt"] % NPT
            rot["pt"] += 1
            return i

        def pc(name):
            a, b = PC[name]
            return par_d[:, a:b]

        def dma(out, in_, reads, writes):
            return P.op("sp", lambda e: e.dma_start(out=out, in_=in_), reads=reads, writes=writes, dma=True)

        XT_ALL0 = [("XT", 0, 0), ("XT", 0, 1)]
        ARENA_ALL = ["U", "VA", "VN", "OA"] + [("ACTT", c) for c in range(NCH)]

        dma(TRI[:], pc("tri"), [], ["TRI"])
        dma(SEL127[:], pc("sel127"), [], ["SEL127"])
        dma(SELA[:], pc("selA"), [], ["SELA"])
        dma(SELB[:], pc("selB"), [], ["SELB"])
        dma(FB[:], pc("fb"), [], ["FB"])
        dma(LNG[:], pc("lng"), [], ["LNG"])
        dma(GFIN[:], pc("gfin"), [], ["GFIN"])
        dma(GT[:, 0:8], pc("g1T"), [], ["GT"])
        dma(GT[:, 8:16], pc("g2T"), [], ["GT"])
        dma(WC[:].rearrange("p c j -> p (c j)"), pc("wc"), [], ["WC"])
        dma(BC[:], pc("bc"), [], ["BC"])
        dma(BST[:], pc("bsT"), [], ["BST"])
        dma(KM[:], pc("kmask"), [], ["KM"])
        XTf = XT[:].rearrange("p b f -> p (b f)")
        HBf = HB[:].rearrange("p b f -> p (b f)")
        HTf = HT[:].rearrange("p k t -> p (k t)")
        a0 = PC["eye"][0]
        dma(XTf[:, 0:128], pc("eye"), [], XT_ALL0)
        dma(XTf[:, 128:640], par_d[:, PC["cm0"][0]:PC["cm1"][1]], [], XT_ALL0)
        dma(XTf[:, 640:1664], pc("sgwT"), [], XT_ALL0)
        P.op("dve", lambda e: e.tensor_copy(out=ident[:], in_=XTf[:, 0:128]), reads=XT_ALL0, writes=["ident"])
        P.op("dve", lambda e: e.tensor_copy(out=CMASK[:].rearrange("p a t -> p (a t)"), in_=XTf[:, 128:640]), reads=XT_ALL0, writes=["CMASK"])
        P.op("dve", lambda e: e.memset(XTf[64:128, 640:1664].rearrange("p (h t) -> p h t", h=8)[:, :, 0:64], 0.0), reads=[], writes=XT_ALL0)
        P.op("dve", lambda e: e.tensor_copy(out=WMT[:].rearrange("p h t -> p (h t)"), in_=XTf[:, 640:1664]), reads=XT_ALL0, writes=["WMT"])
        P.op("dve", lambda e: e.memset(EPSC[:, 0:1], EPS), writes=["EPSC"])
        P.op("dve", lambda e: e.memset(EPSC[:, 1:2], 1.0), writes=["EPSC"])
        P.op("dve", lambda e: e.memset(CCAR[:], 0.0), writes=["CCAR"])
        P.op("dve", lambda e: e.memset(CARRY[:].rearrange("p c a j -> p (c a j)"), 0.0), writes=[("CARRY", jq) for jq in range(NCH)])
        import os
        if os.environ.get("NOVMEM") != "1":
            P.op("dve", lambda e: e.memset(V[:].rearrange("p b a c -> p (b a) c")[:, :, 64:66], 1.0), writes=[("V", b, s) for b in range(NB) for s in range(2)])

        NCONV = 99 if stage >= 2 else 0
        HB_ALL = [("HB", 0, 0), ("HB", 0, 1)]
        HT_ALL = [("HT", 0), ("HT", 1)]
        stage_f = [(XTf, XT_ALL0), (ARENA[:, 0:2048], ARENA_ALL)]
        stage_b = [(HBf, HB_ALL), (HTf, HT_ALL)]
        conv_engs = ["dve", "act"]
        cnt = [0]

        def convert(slot, pieces, gain_col, bg=False):
            if cnt[0] >= NCONV:
                return
            i = cnt[0] % 2
            cnt[0] += 1
            if bg:
                sf, sfn = ARENA[:, 0:2048], ["cvF"]
                sb, sbn = ABF[:, 4096:6144], ["cvB"]
            else:
                sf, sfn = stage_f[i]
                sb, sbn = stage_b[i]
            wr = list(sfn)
            for (dst, src) in pieces:
                dma(dst(sf), src, [], wr)
            eng = conv_engs[i]
            if gain_col is None:
                if eng == "dve":
                    P.op(eng, lambda e: e.tensor_copy(out=sb[:, :], in_=sf[:, :]), reads=wr, writes=list(sbn))
                else:
                    P.op(eng, lambda e: e.copy(out=sb[:, :], in_=sf[:, :]), reads=wr, writes=list(sbn))
            else:
                for k in range(8):
                    if eng == "dve":
                        P.op(eng, lambda e, k=k: e.tensor_scalar(out=sb[:, k * 256:(k + 1) * 256], in0=sf[:, k * 256:(k + 1) * 256],
                                                               scalar1=GT[:, gain_col + k:gain_col + k + 1], scalar2=None, op0=ALU.mult),
                             reads=wr + ["GT"], writes=list(sbn))
                    else:
                        P.op(eng, lambda e, k=k: e.mul(out=sb[:, k * 256:(k + 1) * 256], in_=sf[:, k * 256:(k + 1) * 256],
                                                     mul=GT[:, gain_col + k:gain_col + k + 1]),
                             reads=wr + ["GT"], writes=list(sbn))
            out_fn = lambda: dma(wsc[slot], sb[:, :], list(sbn), [("wsc", slot)])
            if bg:
                return out_fn
            out_fn()
            return None

        def v3(sf, a, b):
            return sf[:, 0:2048].rearrange("p (a b) -> p a b", a=a)

        conv_jobs = {}
        for s in range(10):
            conv_jobs[S_WIN + s] = ([(lambda sf: v3(sf, 8, 256), win_d[:, s * 256:(s + 1) * 256].rearrange("(k p) c -> p k c", p=128))], 0)
        for s in range(4):
            conv_jobs[S_WOUT + s] = ([(lambda sf: v3(sf, 8, 256), wout_d[:, s * 256:(s + 1) * 256].rearrange("(k p) c -> p k c", p=128))], None)
        for j in range(NCH):
            conv_jobs[S_WUP + j] = ([(lambda sf: v3(sf, 8, 256)[:, :, 0:128], wup_d[:, j * 128:(j + 1) * 128].rearrange("(k p) c -> p k c", p=128)),
                                     (lambda sf: v3(sf, 8, 256)[:, :, 128:256], wup_d[:, DFF + j * 128:DFF + (j + 1) * 128].rearrange("(k p) c -> p k c", p=128))], 8)
        for i in range(11):
            conv_jobs[S_WDN + i] = ([(lambda sf: v3(sf, 2, 1024), wdn_d[i * 256:(i + 1) * 256, :].rearrange("(c p) n -> p c n", p=128))], None)
        for slot in (6, 7, 8, 9):
            convert(slot, *conv_jobs[slot])
        dma(XTf[:, 0:64].rearrange("p (k c) -> p k c", k=8), win_d[:, 2560:2568].rearrange("(k p) c -> p k c", p=128), [], XT_ALL0)
        for k in range(8):
            P.op("dve", lambda e, k=k: e.tensor_scalar(out=WF[:, k, :], in0=XTf[:, k * 8:(k + 1) * 8], scalar1=GT[:, k:k + 1], scalar2=None, op0=ALU.mult),
                 reads=XT_ALL0 + ["GT"], writes=["WF"])
        P.op("dve", lambda e: e.memset(ST[:, 60:61], 0.0), reads=[], writes=ARENA_ALL + ["cvF", "cvB"])
        bg_slots = [sl for sl in list(range(0, 6)) + list(range(10, NSLOT))]
        bg_state = {"i": 0, "pending_out": None}

        def conv_step():
            if bg_state["pending_out"] is not None:
                bg_state["pending_out"]()
                bg_state["pending_out"] = None
            if stage < 2 or bg_state["i"] >= len(bg_slots):
                return
            slot = bg_slots[bg_state["i"]]
            bg_state["i"] += 1
            bg_state["pending_out"] = convert(slot, *conv_jobs[slot], bg=True)

        def conv_flush():
            while bg_state["i"] < len(bg_slots) or bg_state["pending_out"] is not None:
                conv_step()
            P.op("dve", lambda e: e.memset(ST[:, 61:62], 0.0), reads=[], writes=ARENA_ALL + ["cvF", "cvB"])

        wseq = []
        for ci in range(NGRP):
            full = ci >= FIRST_FULL
            if stage < 3 or (not full and ci >= n_ctx):
                continue
            if full and (ci - FIRST_FULL) >= n_full:
                break
            if full:
                wseq += list(range(0, 6)) + list(range(10, 14)) + list(range(14, 36)) + list(range(36, 47))
        wstate = {"use": 0, "load": 0}
        if stage >= 3:
            for sl in range(6, 10):
                dma(WP[NWP + sl - 6][:, :], wsc[sl], [("wsc", sl)], [("WP", NWP + sl - 6)])

        def wload_next():
            n = wstate["load"]
            if n >= len(wseq):
                return
            slot = wseq[n]
            buf = n % NWP
            wstate["load"] += 1
            dma(WP[buf][:, :], wsc[slot], [("wsc", slot)], [("WP", buf)])

        def wacquire(slot):
            if 6 <= slot < 10:
                return NWP + slot - 6
            n = wstate["use"]
            assert wseq[n] == slot, (n, wseq[n], slot)
            return n % NWP

        def wrelease(pinned=False):
            if pinned:
                return
            wstate["use"] += 1
            wload_next()


        evac_rr = [0]

        def rms_front(XT, HB, par, sc):
            for b in range(2):
                P.op("dve", lambda e, b=b: e.scalar_tensor_tensor(out=HB[:, b, :], in0=XT[:, b, :], scalar=1.0, in1=XT[:, b, :],
                                                                 op0=ALU.mult, op1=ALU.mult, accum_out=ST[:, sc + b:sc + b + 1]),
                     reads=[("XT", par, b)], writes=[("HB", par, b), ("ST", sc + b)])
                P.op("act", lambda e, b=b: e.activation(out=ST[:, sc + 2 + b:sc + 3 + b], in_=ST[:, sc + b:sc + b + 1], func=AF.Ln, bias=EPSC[:, 0:1], scale=1.0 / D),
                     reads=[("ST", sc + b), "EPSC"], writes=[("ST", sc + 2 + b)])
                P.op("act", lambda e, b=b: e.activation(out=ST[:, sc + 4 + b:sc + 5 + b], in_=ST[:, sc + 2 + b:sc + 3 + b], func=AF.Exp, scale=-0.5),
                     reads=[("ST", sc + 2 + b)], writes=[("ST", sc + 4 + b)])
                P.op("dve", lambda e, b=b: e.tensor_scalar(out=HB[:, b, :], in0=XT[:, b, :], scalar1=ST[:, sc + 4 + b:sc + 5 + b], scalar2=None, op0=ALU.mult),
                     reads=[("XT", par, b), ("ST", sc + 4 + b)], writes=[("HB", par, b)])

        def rms_back(HB, par):
            for b in range(2):
                for k in range(8):
                    P.op("pe", lambda e, b=b, k=k: e.transpose(out=PSTB[:, k * 128:(k + 1) * 128], in_=HB[:, b, k * 128:(k + 1) * 128], identity=ident[:]),
                         reads=[("HB", par, b), "ident"], writes=["PST"])
                P.op("act", lambda e, b=b: e.copy(out=HT[:, :, b * 128:(b + 1) * 128], in_=PSTB[:, :].rearrange("p (k t) -> p k t", k=8)),
                     reads=["PST"], writes=[("HT", b)] + [("OUTT", kq) for kq in range(8)])

        front_done = set()
        back_done = set()

        def emit_front(ci):
            front_done.add(ci)
            par = ci % 2
            T0 = ci * G
            dma(XTs[par][:], x_d[T0:T0 + G, :].rearrange("(b p) f -> p b f", p=128), [], [("XT", par, 0), ("XT", par, 1)])
            rms_front(XTs[par], HBs[par], par, 40)

        def HT_reads(b=None):
            if b is None:
                return [("HT", 0), ("HT", 1)]
            return [("HT", b)]

        def mm_feat(pi, hf, wbuf, c0):
            for k in range(8):
                P.op("pe", lambda e, k=k: e.matmul(PGB[pi][:, hf * G:(hf + 1) * G], lhsT=WP[wbuf][:, k * 256 + c0:k * 256 + c0 + 128], rhs=HT[:, k, :],
                                                  start=(k == 0), stop=(k == 7)),
                     reads=[("WP", wbuf)] + HT_reads(), writes=[("PG", pi)])

        def mm_tok(pi, hf, wbuf, b, lhs, lhs_reads):
            for k in range(8):
                P.op("pe", lambda e, k=k: e.matmul(PGB[pi][:, hf * G:(hf + 1) * G], lhsT=lhs[:, k, b * 128:(b + 1) * 128], rhs=WP[wbuf][:, k * 256:(k + 1) * 256],
                                                  start=(k == 0), stop=(k == 7)),
                     reads=[("WP", wbuf)] + lhs_reads, writes=[("PG", pi)])

        def emit_group(ci):
            full = ci >= FIRST_FULL
            gi = ci - FIRST_FULL
            b0 = 2 * ci
            T0 = ci * G
            par = ci % 2
            XT = XTs[par]
            HB = HBs[par]
            if gi == 0:
                conv_flush()
                for _ in range(NWP):
                    wload_next()
            if not full:
                conv_step()
            if ci not in back_done:
                rms_back(HB, par)

            def cs1(b):
                pi = next_pg()
                for k in range(8):
                    P.op("pe", lambda e, k=k, b=b, pi=pi: e.matmul(PGB[pi][:, 0:8], lhsT=HT[:, k, b * 128:(b + 1) * 128], rhs=WF[:, k, :], start=(k == 0), stop=(k == 7)),
                         reads=["WF"] + HT_reads(b), writes=[("PG", pi)])
                P.op("dve", lambda e, pi=pi, b=b: e.tensor_tensor(out=SPB[:, b, :], in0=PGB[pi][:, 0:8], in1=FB[:, :], op=ALU.add),
                     reads=[("PG", pi), "FB"], writes=[("SPB", b)])
                P.op("act", lambda e, b=b: e.activation(out=SPB[:, b, :], in_=SPB[:, b, :], func=AF.Exp, scale=-1.0), reads=[("SPB", b)], writes=[("SPB", b)])
                P.op("act", lambda e, b=b: e.activation(out=SPB[:, b, :], in_=SPB[:, b, :], func=AF.Ln, bias=EPSC[:, 1:2], scale=1.0), reads=[("SPB", b), "EPSC"], writes=[("SPB", b)])

            def cs2(b):
                pi2 = next_pg()
                P.op("pe", lambda e, pi2=pi2, b=b: e.matmul(PGB[pi2][:, 0:8], lhsT=TRI[:, :], rhs=SPB[:, b, :], start=True, stop=True),
                     reads=["TRI", ("SPB", b)], writes=[("PG", pi2)])
                P.op("dve", lambda e, pi2=pi2: e.tensor_tensor(out=CB[:, :], in0=PGB[pi2][:, 0:8], in1=CCAR[:, :], op=ALU.add),
                     reads=[("PG", pi2), "CCAR"], writes=["CB"])
                P.op("dve", lambda e, b=b: e.tensor_scalar(out=CKB[:, b0 + b, :], in0=CB[:, :], scalar1=KM[:, b0 + b:b0 + b + 1], scalar2=None, op0=ALU.add),
                     reads=["CB", "KM"], writes=[("CKB", b0 + b)])

            def cs3(b):
                pi3 = next_pg()
                P.op("pe", lambda e, pi3=pi3: e.matmul(PGB[pi3][:, 0:8], lhsT=SEL127[:, :], rhs=CB[:, :], start=True, stop=True),
                     reads=["SEL127", "CB"], writes=[("PG", pi3)])
                P.op("dve", lambda e, pi3=pi3: e.tensor_copy(out=CCAR[:, :], in_=PGB[pi3][:, 0:8]), reads=[("PG", pi3)], writes=["CCAR"])
                if b == 0 and full:
                    P.op("dve", lambda e, pi3=pi3: e.tensor_scalar(out=NRG[:, :], in0=PGB[pi3][:, 0:8], scalar1=-1.0, scalar2=None, op0=ALU.mult), reads=[("PG", pi3)], writes=["RG"])

            cstages = [lambda: cs1(0), lambda: cs1(1), lambda: cs2(0), lambda: cs3(0), lambda: cs2(1), lambda: cs3(1)]

            def cstage():
                if cstages:
                    cstages.pop(0)()

            if not full:
                cstage()
                cstage()
            nxt = ci + 1
            has_next = (nxt < NGRP) and not (nxt >= FIRST_FULL and (nxt - FIRST_FULL) >= n_full) and not (stage < 3) and not (nxt < FIRST_FULL and nxt >= n_ctx)
            if not full and has_next:
                emit_front(nxt)
            if full:
                for s in range(4):
                    wb = wacquire(s)
                    dst, dn = (U, "U") if s < 2 else (VA, "VA")
                    hf = s % 2
                    pi = next_pg()
                    for b in range(2):
                        mm_tok(pi, b, wb, b, HT, HT_reads(b))
                    P.op("act", lambda e, pi=pi, dst=dst, hf=hf: e.activation(out=dst[:, :, hf * 256:(hf + 1) * 256], in_=PGB[pi][:, :].rearrange("p (b c) -> p b c", b=2), func=AF.Gelu),
                         reads=[("PG", pi)], writes=[dn] + [("ACTT", c) for c in range(NCH)])
                    wrelease()
                    cstage()
                for s in range(2):
                    wb = wacquire(4 + s)
                    pi = next_pg()
                    for m in range(2):
                        mm_feat(pi, m, wb, m * 128)
                    P.op("act", lambda e, pi=pi, s=s: e.mul(out=QT[:, 2 * s:2 * s + 2, :].rearrange("p a t -> p (a t)"), in_=PGB[pi][:, :], mul=0.125),
                         reads=[("PG", pi)], writes=[("QT", 2 * s), ("QT", 2 * s + 1)])
                    wrelease()
                    cstage()
            for s in range(2):
                wb = wacquire(6 + s)
                pi = next_pg()
                for m in range(2):
                    mm_feat(pi, m, wb, m * 128)
                for m in range(2):
                    hp = 2 * s + m
                    P.op("act", lambda e, pi=pi, hp=hp, m=m: e.copy(out=KSTG[:, hp, :], in_=PGB[pi][:, m * G:(m + 1) * G]),
                         reads=[("PG", pi)], writes=[("KSTG", hp)])
                wrelease(pinned=True)
                if not full:
                    cstage()
            dma(ksc[:, :, T0:T0 + G].rearrange("h p t -> p h t"), KSTG[:, :, :], [("KSTG", hp) for hp in range(4)],
                [("KD", hp, bb) for hp in range(4) for bb in (b0, b0 + 1)])
            if not full:
                conv_step()
            for s in range(2):
                wb = wacquire(8 + s)
                pi = next_pg()
                for b in range(2):
                    mm_tok(pi, b, wb, b, HT, HT_reads(b))
                for b in range(2):
                    for a in range(2):
                        P.op("dve", lambda e, b=b, pi=pi, s=s, a=a: e.tensor_copy(
                            out=V[:, b0 + b, 2 * s + a, :].rearrange("p (e c) -> p e c", e=2)[:, :, 0:64],
                            in_=PGB[pi][:, b * G + a * 128:b * G + (a + 1) * 128].rearrange("p (e c) -> p e c", e=2)),
                             reads=[("PG", pi)], writes=[("V", b0 + b, s)])
                wrelease(pinned=True)
                if not full:
                    cstage()
            if not full:
                conv_step()
            while cstages:
                cstage()
            if not full:
                return
            if dbg and gi == dbg_g:
                dma(dbg_d["d_qT"], QT[:].rearrange("p a t -> p (a t)"), [("QT", h) for h in range(4)], ["dq"])
            def emit_sg():
                SQR = [("RB", 0), ("RB", 1)]
                for b in range(2):
                    va3 = VA[:, b, :].rearrange("p (h d) -> p h d", h=8)
                    P.op("dve", lambda e, va3=va3: e.tensor_reduce(out=ST[:, 8:16], in_=va3, axis=AX.X, op=ALU.add), reads=["VA"], writes=[("ST", "s1")])
                    P.op("dve", lambda e, b=b: e.tensor_tensor(out=SQ[:, :], in0=VA[:, b, :], in1=VA[:, b, :], op=ALU.mult), reads=["VA"], writes=SQR)
                    P.op("dve", lambda e: e.tensor_reduce(out=ST[:, 16:24], in_=SQ[:, :].rearrange("p (h d) -> p h d", h=8), axis=AX.X, op=ALU.add),
                         reads=SQR, writes=[("ST", "s2")])
                    P.op("dve", lambda e: e.tensor_scalar(out=ST[:, 8:16], in0=ST[:, 8:16], scalar1=1.0 / 64, scalar2=None, op0=ALU.mult),
                         reads=[("ST", "s1")], writes=[("ST", "s1")])
                    P.op("dve", lambda e: e.tensor_tensor(out=ST[:, 24:32], in0=ST[:, 8:16], in1=ST[:, 8:16], op=ALU.mult),
                         reads=[("ST", "s1")], writes=[("ST", "msq")])
                    P.op("dve", lambda e: e.scalar_tensor_tensor(out=ST[:, 16:24], in0=ST[:, 16:24], scalar=1.0 / 64, in1=ST[:, 24:32], op0=ALU.mult, op1=ALU.subtract),
                         reads=[("ST", "s2"), ("ST", "msq")], writes=[("ST", "s2")])
                    P.op("act", lambda e: e.activation(out=ST[:, 16:24], in_=ST[:, 16:24], func=AF.Ln, bias=EPSC[:, 0:1], scale=1.0),
                         reads=[("ST", "s2"), "EPSC"], writes=[("ST", "s2")])
                    P.op("act", lambda e: e.activation(out=ST[:, 16:24], in_=ST[:, 16:24], func=AF.Exp, scale=-0.5),
                         reads=[("ST", "s2")], writes=[("ST", "s2")])
                    P.op("dve", lambda e, va3=va3: e.tensor_tensor(out=va3, in0=va3, in1=ST[:, 8:16].unsqueeze(2).to_broadcast([128, 8, 64]), op=ALU.subtract),
                         reads=["VA", ("ST", "s1")], writes=["VA"])
                    P.op("dve", lambda e, va3=va3: e.tensor_tensor(out=va3, in0=va3, in1=ST[:, 16:24].unsqueeze(2).to_broadcast([128, 8, 64]), op=ALU.mult),
                         reads=["VA", ("ST", "s2")], writes=["VA"])
                    P.op("dve", lambda e, b=b: e.tensor_tensor(out=VN[:, b, :], in0=VA[:, b, :], in1=LNG[:, :], op=ALU.mult),
                         reads=["VA", "LNG"], writes=["VN"])
                    pi = next_pg()
                    for h in range(8):
                        P.op("pe", lambda e, h=h, b=b, pi=pi: e.matmul(PGB[pi][:, h * 64:(h + 1) * 64], lhsT=WMT[:, h, :], rhs=VN[:, b, h * 64:(h + 1) * 64],
                                                                      start=True, stop=True),
                             reads=["WMT", "VN"], writes=[("PG", pi)])
                    P.op("dve", lambda e, pi=pi: e.tensor_tensor(out=SQ[:, :].rearrange("p (h d) -> p h d", h=8),
                                                                 in0=PGB[pi][:, :].rearrange("p (h d) -> p h d", h=8),
                                                                 in1=BST[:, :].unsqueeze(2).to_broadcast([128, 8, 64]), op=ALU.add),
                         reads=[("PG", pi), "BST"], writes=SQR)
                    P.op("dve", lambda e, b=b: e.tensor_tensor(out=OA[:, b, :], in0=SQ[:, :], in1=U[:, b, :], op=ALU.mult),
                         reads=SQR + ["U"], writes=["OA"])
                    for kk in range(4):
                        P.op("pe", lambda e, b=b, kk=kk: e.transpose(out=PSTB[:, kk * 128:(kk + 1) * 128], in_=OA[:, b, kk * 128:(kk + 1) * 128], identity=ident[:]),
                             reads=["OA", "ident"], writes=["PST"])
                    P.op("act", lambda e, b=b: e.copy(out=OUTT[:, 0:4, b * 128:(b + 1) * 128], in_=PSTB[:, 0:512].rearrange("p (k t) -> p k t", k=4)),
                         reads=["PST"], writes=[("OUTT", k) for k in range(4)] + [("HT", 0), ("HT", 1)])

            nj = b0 + 2
            nchunk = (nj + 7) // 8

            def kload(hp, c):
                nblk = min(8, nj - 8 * c)
                kb = kch_rot[0] % 3
                kch_rot[0] += 1
                dma(KCH[kb][:, 0:nblk * 128], ksc[hp][:, c * 1024:c * 1024 + nblk * 128],
                    [("KD", hp, 8 * c + q) for q in range(nblk)], [("KCH", kb)])
                return kb

            for hp in range(4):
                if hp == 1:
                    emit_sg()
                accs = [0, 1]
                kbufs = {0: kload(hp, 0)}
                if nchunk > 1:
                    kbufs[1] = kload(hp, 1)
                for ee in range(2):
                    h = 2 * hp + ee
                    P.op("act", lambda e, h=h, ee=ee: e.activation(out=WG[ee][:, 0:nj], in_=CKB[:, 0:nj, h], func=AF.Exp, bias=NRG[:, h:h + 1], scale=1.0),
                         reads=[("CKB", j) for j in range(nj)] + ["RG"], writes=[("WG", ee)])

                def vprep(c, hp=hp):
                    nblk = min(8, nj - 8 * c)
                    cb = c % 2
                    for ee in range(2):
                        Kc = 65 if ee == 0 else 128
                        c0 = 0 if ee == 0 else 2
                        P.op("dve", lambda e, ee=ee, Kc=Kc, c0=c0, cb=cb, nblk=nblk, c=c: e.tensor_tensor(
                            out=VP[ee][cb][:, 0:nblk, 0:Kc], in0=V[:, 8 * c:8 * c + nblk, hp, c0:c0 + Kc],
                            in1=WG[ee][:, 8 * c:8 * c + nblk].unsqueeze(2).to_broadcast([128, nblk, Kc]), op=ALU.mult),
                             reads=[("V", 8 * c + q_, hp // 2) for q_ in range(nblk)] + [("WG", ee)], writes=[("VP", ee, cb)])

                vprep(0)
                if nchunk > 1:
                    vprep(1)

                def qk_step(j, hp=hp):
                    bp = att_rot[0] % 3
                    att_rot[0] += 1
                    banks = [[(PGB[0], ("PG", 0)), (PGB[1], ("PG", 1))], [(PGB[2], ("PG", 2)), (PSTF, "PST")],
                             [(ACCB[2][:, :], ("ACC", 2)), (ACCB[3][:, :], ("ACC", 3))]][bp]
                    for q in range(2):
                        jj = j + q
                        diag = jj >= b0
                        for ee in range(2):
                            kr = slice(ee * 64, (ee + 1) * 64)
                            bk, bkey = banks[ee]
                            kb_ = kbufs[jj // 8]
                            jo = jj % 8
                            P.op("pe", lambda e, ee=ee, kr=kr, bk=bk, jj=jj, q=q, diag=diag, kb_=kb_, jo=jo: e.matmul(bk[:, q * G:(q + 1) * G], lhsT=KCH[kb_][kr, jo * 128:(jo + 1) * 128], rhs=QT[kr, hp, :],
                                                                                                  start=True, stop=not diag),
                                 reads=[("KCH", kb_), ("QT", hp)], writes=[bkey])
                        if diag:
                            for ee in range(2):
                                bk, bkey = banks[ee]
                                P.op("pe", lambda e, bk=bk, jj=jj, q=q: e.matmul(bk[:, q * G:(q + 1) * G], lhsT=ident[:, :], rhs=CMASK[:, jj - b0, :], start=False, stop=True),
                                     reads=["ident", "CMASK"], writes=[bkey])
                    return banks

                def exp_step(j, banks):
                    tis = []
                    for ee in range(2):
                        bk, bkey = banks[ee]
                        ti = next_pt()
                        P.op("act", lambda e, ti=ti, bk=bk: e.activation(out=PT[ti][:, :], in_=bk[:, :], func=AF.Exp, scale=1.0),
                             reads=[bkey], writes=[("PT", ti)])
                        tis.append(ti)
                    return tis

                def pv_step(j, tis, hp=hp, accs=accs):
                    for q in range(2):
                        jj = j + q
                        cb = (jj // 8) % 2
                        jo = jj % 8
                        t0_, t1_ = tis[0], tis[1]
                        P.op("pe", lambda e, jj=jj, cb=cb, jo=jo, t0_=t0_, q=q: e.matmul(ACCB[accs[0]][0:65, 0:G], lhsT=VP[0][cb][:, jo, 0:65], rhs=PT[t0_][:, q * G:(q + 1) * G], start=(jj == 0), stop=(jj == nj - 1)),
                             reads=[("VP", 0, cb), ("PT", t0_)], writes=[("ACC", accs[0])])
                        P.op("pe", lambda e, jj=jj, cb=cb, jo=jo, t1_=t1_, q=q: e.matmul(ACCB[accs[1]][:, 0:G], lhsT=VP[1][cb][:, jo, :], rhs=PT[t1_][:, q * G:(q + 1) * G], start=(jj == 0), stop=(jj == nj - 1)),
                             reads=[("VP", 1, cb), ("PT", t1_)], writes=[("ACC", accs[1])])

                pend = []
                for j in range(0, nj, 2):
                    if j % 8 == 0 and j > 0 and (j // 8 + 1) < nchunk:
                        kbufs[j // 8 + 1] = kload(hp, j // 8 + 1)
                    if j % 8 == 4 and (j // 8 + 1) < nchunk and (j // 8 + 1) >= 2:
                        vprep(j // 8 + 1)
                    banks = qk_step(j)
                    tis = exp_step(j, banks)
                    pend.append((j, tis))
                    if len(pend) > 2:
                        pv_step(*pend.pop(0))
                while pend:
                    pv_step(*pend.pop(0))
                for ee in range(2):
                    h = 2 * hp + ee
                    acc = accs[ee]
                    kr = slice(ee * 64, (ee + 1) * 64)
                    oi = ee
                    K = 65 if ee == 0 else 128
                    sel = SELA if ee == 0 else SELB
                    seln = "SELA" if ee == 0 else "SELB"
                    P.op("act", lambda e, oi=oi, K=K, acc=acc: e.copy(out=OSB[oi][0:K, :], in_=ACCB[acc][0:K, 0:G]), reads=[("ACC", acc)], writes=[("OSB", oi)])
                    pi = next_pg()
                    P.op("pe", lambda e, oi=oi, K=K, sel=sel, pi=pi: e.matmul(PGB[pi][:, 0:G], lhsT=sel[0:K, :], rhs=OSB[oi][0:K, :], start=True, stop=True),
                         reads=[seln, ("OSB", oi)], writes=[("PG", pi)])
                    P.op("dve", lambda e, oi=oi, pi=pi: e.tensor_scalar(out=RB[oi][:, :], in0=PGB[pi][:, 0:G], scalar1=1e-30, scalar2=None, op0=ALU.max),
                         reads=[("PG", pi)], writes=[("RB", oi)])
                    P.op("dve", lambda e, oi=oi: e.reciprocal(out=RB[oi][:, :], in_=RB[oi][:, :]), reads=[("RB", oi)], writes=[("RB", oi)])
                    P.op("dve", lambda e, oi=oi, kr=kr, hp=hp: e.tensor_tensor(out=OUTT[kr, 4 + hp, :], in0=OSB[oi][kr, :], in1=RB[oi][kr, :], op=ALU.mult),
                         reads=[("OSB", oi), ("RB", oi)], writes=[("OUTT", 4 + hp), ("HT", 0), ("HT", 1)])
            if dbg and gi == dbg_g:
                for k in range(8):
                    P.op("dve", lambda e, k=k: e.tensor_copy(out=YGV[0][:, 0, :], in_=OUTT[:, k, :]), reads=[("OUTT", k)], writes=["dtmp"])
                    dma(dbg_d["d_outT"][:, k * 256:(k + 1) * 256], YGV[0][:, 0, :], ["dtmp"], ["dtmp2"])
                dma(dbg_d["d_ckb"], CKB[:].rearrange("p b h -> p (b h)"), [("CKB", j) for j in range(NB)], ["dck"])
            if has_next:
                emit_front(nxt)
            for n in range(4):
                wb = wacquire(S_WOUT + n)
                pi = next_pg()
                for b in range(2):
                    mm_tok(pi, b, wb, b, OUTT, [("OUTT", k) for k in range(8)])
                P.op("dve", lambda e, n=n, pi=pi: e.tensor_tensor(out=XT[:, :, n * 256:(n + 1) * 256], in0=XT[:, :, n * 256:(n + 1) * 256],
                                                                  in1=PGB[pi][:, :].rearrange("p (b c) -> p b c", b=2), op=ALU.add),
                     reads=[("PG", pi), ("XT", par, 0), ("XT", par, 1)], writes=[("XT", par, 0), ("XT", par, 1)])
                wrelease()
            if dbg and gi == dbg_g:
                dma(dbg_d["d_x1"].rearrange("(b p) f -> p b f", p=128), XT[:], [("XT", par, 0), ("XT", par, 1)], ["dx1"])
            rms_front(XT, HB, par, 0)
            rms_back(HB, par)
            def ffn_tail(j):
                r = j % 2
                P.op("act", lambda e, r=r: e.activation(out=YGV[r][:, 0, :], in_=YGV[r][:, 0, :], func=AF.Silu), reads=[("YGV", r, 0)], writes=[("YGV", r, 0)])
                P.op("pool", lambda e, r=r, j=j: e.tensor_tensor(out=ACTT[:, j, :], in0=YGV[r][:, 0, :], in1=YGV[r][:, 1, :], op=ALU.mult),
                     reads=[("YGV", r, 0), ("YGV", r, 1)], writes=[("ACTT", j), "U", "VA", "VN", "OA"])

            for j in range(NCH):
                wb = wacquire(S_WUP + j)
                r = j % 2
                pi = next_pg()
                for part in range(2):
                    mm_feat(pi, part, wb, part * 128)
                P.op("pool", lambda e, r=r, j=j: e.tensor_copy(out=AGV[r][:, :, 0:2], in_=CARRY[:, j, :, :]),
                     reads=[("CARRY", j)] + [("OUTT", 4 + hq) for hq in range(4)], writes=[("AGVc", r)])
                P.op("act", lambda e, r=r, pi=pi: e.copy(out=AGV[r][:, :, 2:258], in_=PGB[pi][:, :].rearrange("p (a t) -> p a t", a=2)),
                     reads=[("PG", pi)], writes=[("AGV", r, 0), ("AGV", r, 1)])
                for part in range(2):
                    cj = part * NCH + j
                    P.op("act", lambda e, r=r, part=part, cj=cj, pi=pi: e.activation(out=YGV[r][:, part, :], in_=PGB[pi][:, part * G:(part + 1) * G], func=AF.Identity,
                                                                                 bias=BC[:, cj:cj + 1], scale=WC[:, cj, 2:3]),
                         reads=[("PG", pi), "WC", "BC"], writes=[("YGV", r, part)])
                if j >= 1:
                    ffn_tail(j - 1)
                P.op("pool", lambda e, r=r, j=j: e.tensor_copy(out=CARRY[:, j, :, :], in_=AGV[r][:, :, 256:258]),
                     reads=[("AGV", r, 0), ("AGV", r, 1)], writes=[("CARRY", j)])
                for tap in (1, 0):
                    for part in range(2):
                        cj = part * NCH + j
                        P.op("dve", lambda e, r=r, part=part, cj=cj, tap=tap: e.scalar_tensor_tensor(out=YGV[r][:, part, :], in0=AGV[r][:, part, tap:tap + 256], scalar=WC[:, cj, tap:tap + 1],
                                                                                                  in1=YGV[r][:, part, :], op0=ALU.mult, op1=ALU.add),
                             reads=[("AGV", r, part), ("AGVc", r), "WC", ("YGV", r, part)], writes=[("YGV", r, part)])
                wrelease()
            ffn_tail(NCH - 1)
            for i in range(11):
                wb = wacquire(S_WDN + i)
                for c2 in range(2):
                    ch = 2 * i + c2
                    for b in range(2):
                        for n2 in range(2):
                            acc = b * 2 + n2
                            P.op("pe", lambda e, ch=ch, c2=c2, b=b, n2=n2, acc=acc, wb=wb: e.matmul(ACCB[acc][:, :], lhsT=ACTT[:, ch, b * 128:(b + 1) * 128],
                                                                                                 rhs=WP[wb][:, c2 * 1024 + n2 * 512:c2 * 1024 + (n2 + 1) * 512],
                                                                                                 start=(ch == 0), stop=(ch == NCH - 1)),
                                 reads=[("WP", wb), ("ACTT", ch)], writes=[("ACC", acc)])
                wrelease()
            if has_next:
                rms_back(HBs[nxt % 2], nxt % 2)
                back_done.add(nxt)
            for b in range(2):
                for n2 in range(2):
                    acc = b * 2 + n2
                    P.op("dve", lambda e, b=b, n2=n2, acc=acc: e.tensor_tensor(out=XT[:, b, n2 * 512:(n2 + 1) * 512], in0=XT[:, b, n2 * 512:(n2 + 1) * 512], in1=ACCB[acc][:, :], op=ALU.add),
                         reads=[("ACC", acc), ("XT", par, b)], writes=[("XT", par, b)])
            for b in range(2):
                P.op("dve", lambda e, b=b: e.scalar_tensor_tensor(out=HB[:, b, :], in0=XT[:, b, :], scalar=1.0, in1=XT[:, b, :],
                                                                 op0=ALU.mult, op1=ALU.mult, accum_out=ST[:, 32 + b:33 + b]),
                     reads=[("XT", par, b)], writes=[("HB", par, b), ("ST", 32 + b)])
                P.op("act", lambda e, b=b: e.activation(out=ST[:, 34 + b:35 + b], in_=ST[:, 32 + b:33 + b], func=AF.Ln, bias=EPSC[:, 0:1], scale=1.0 / D),
                     reads=[("ST", 32 + b), "EPSC"], writes=[("ST", 34 + b)])
                P.op("act", lambda e, b=b: e.activation(out=ST[:, 36 + b:37 + b], in_=ST[:, 34 + b:35 + b], func=AF.Exp, scale=-0.5),
                     reads=[("ST", 34 + b)], writes=[("ST", 36 + b)])
                P.op("dve", lambda e, b=b: e.scalar_tensor_tensor(out=OUTF[:, b, :], in0=XT[:, b, :], scalar=ST[:, 36 + b:37 + b], in1=GFIN[:, :], op0=ALU.mult, op1=ALU.mult),
                     reads=[("XT", par, b), ("ST", 36 + b), "GFIN"], writes=ARENA_ALL)
            if gi >= 1:
                r0 = (gi - 1) * G
                dma(out_d[r0:r0 + G, :].rearrange("(b p) f -> p b f", p=128), OUTF, ARENA_ALL, [("out", gi)])

        dbg_g = 1
        first = True
        for ci in range(NGRP):
            if stage < 3 or (ci < FIRST_FULL and ci >= n_ctx):
                continue
            if ci >= FIRST_FULL and (ci - FIRST_FULL) >= n_full:
                break
            if ci not in front_done:
                emit_front(ci)
            emit_group(ci)
        fin = [("out", gi) for gi in range(1, n_full)]
        if dbg:
            fin += ["dq", "dtmp2", "dck", "dx1"]
        P.op("sp", None, reads=fin)
        P.emit()
        print("ops:", len(P.ops), "signals:", P.stats)
    return nc


def make_par(inputs, core):
    par = np.zeros((128, NPAR), np.float32)

    def put(name, arr):
        a, b = PC[name]
        par[:, a:b] = np.asarray(arr, np.float32).reshape(128, b - a)

    put("eye", np.eye(128, dtype=np.float32))
    s = np.arange(128)
    put("tri", (s[:, None] <= s[None, :]).astype(np.float32))
    sel = np.zeros((128, 128), np.float32); sel[127, :] = 1.0
    put("sel127", sel)
    sel = np.zeros((128, 128), np.float32); sel[64, :] = 1.0
    put("selA", sel)
    sel = np.zeros((128, 128), np.float32); sel[63, :] = 1.0
    put("selB", sel)
    tri_mask = np.where(s[:, None] <= s[None, :], 0.0, NEG).astype(np.float32)
    cm0 = np.concatenate([tri_mask, np.zeros((128, 128), np.float32)], axis=1)
    cm1 = np.concatenate([np.full((128, 128), NEG, np.float32), tri_mask], axis=1)
    put("cm0", cm0)
    put("cm1", cm1)
    put("fb", np.broadcast_to(inputs["f_bias"].reshape(1, 8), (128, 8)))
    put("lng", np.broadcast_to(inputs["sg_ln_g"].reshape(1, 512), (128, 512)))
    put("gfin", np.broadcast_to(inputs["norm_final_g"].reshape(1, D), (128, D)))
    put("g1T", inputs["norm_mix_g"].reshape(8, 128).T)
    put("g2T", inputs["norm_ffn_g"].reshape(8, 128).T)
    wc = inputs["w_conv"].reshape(3, 2 * NCH, 128)
    put("wc", np.transpose(wc, (2, 1, 0)))
    put("bc", inputs["b_conv"].reshape(2 * NCH, 128).T)
    put("bsT", inputs["sg_b"].reshape(8, 128).T)
    km = np.zeros((128, NB), np.float32)
    if core % 2 == 0:
        km[:, 0:32] = NEG
    put("kmask", km)
    sgw = inputs["sg_w"].reshape(8, 128, 128)
    put("sgwT", np.transpose(sgw, (2, 0, 1)))
    return par


_NC_CACHE = {}


def kernel(**inputs):
    inputs = {k: np.asarray(v) for k, v in inputs.items()}
    x = inputs["x"].astype(np.float32, copy=False)
    key = "full"
    if key not in _NC_CACHE:
        _NC_CACHE[key] = build_nc()
    nc = _NC_CACHE[key]
    in_maps = []
    for c in range(8):
        b, half = c // 2, c % 2
        if half == 0:
            xc = np.concatenate([np.zeros((4096, D), np.float32), x[b, 0:4096]], axis=0)
        else:
            xc = x[b]
        in_maps.append({
            "x": np.ascontiguousarray(xc),
            "par": make_par(inputs, c),
            "w_in": np.ascontiguousarray(inputs["w_in"][0]),
            "w_out": np.ascontiguousarray(inputs["w_out"][0]),
            "w_up": np.ascontiguousarray(inputs["w_up"][0]),
            "w_down": np.ascontiguousarray(inputs["w_down"][0]),
        })
    res = run_bass_kernel_spmd(nc, in_maps, core_ids=list(range(8)))
    out = np.empty((4, SEQ, D), np.float32)
    for c in range(8):
        b, half = c // 2, c % 2
        out[b, half * 4096:(half + 1) * 4096] = res.results[c]["out"]
    return out
```

```python
import contextlib
import numpy as np
import concourse.bass as bass
import concourse.mybir as mybir
from concourse.bass_utils import run_bass_kernel_spmd

F32 = mybir.dt.float32
BF16 = mybir.dt.bfloat16
AF = mybir.ActivationFunctionType
ALU = mybir.AluOpType
AX = mybir.AxisListType

SAME_ENGINE_SYNC = True
NDMASEM = 8

D = 1024
SEQ = 8192
NB = 64
G = 256
NGRP = 32
FIRST_FULL = 15
DFF = 2816
NCH = 22
EPS = 1e-6
NEG = -30000.0

PC = {}
_off = 0
for _n, _w in [("eye", 128), ("tri", 128), ("sel127", 128), ("selA", 128), ("selB", 128),
               ("cm0", 256), ("cm1", 256), ("fb", 8), ("lng", 512), ("gfin", 1024),
               ("g1T", 8), ("g2T", 8), ("wc", 132), ("bc", 44), ("bsT", 8), ("kmask", 64),
               ("sgwT", 1024)]:
    PC[_n] = (_off, _off + _w)
    _off += _w
NPAR = _off

S_WIN, S_WOUT, S_WUP, S_WDN = 0, 10, 14, 36
NSLOT = 47


class Op:
    __slots__ = ("eng", "fn", "deps", "idx", "signaled", "ev", "is_dma", "dq", "dslot")

    def __init__(self, eng, fn, idx, is_dma):
        self.eng = eng
        self.fn = fn
        self.idx = idx
        self.deps = []
        self.signaled = False
        self.ev = None
        self.is_dma = is_dma


class Prog:
    ENGS = ("sp", "act", "dve", "pool", "pe")

    def __init__(self, nc):
        self.nc = nc
        self.ops = []
        self.last_writer = {}
        self.readers = {}
        self.dma_count = {e: 0 for e in self.ENGS}

    def op(self, eng, fn, reads=(), writes=(), dma=False):
        o = Op(eng, fn, len(self.ops), dma)
        deps = {}
        for r in reads:
            w = self.last_writer.get(r)
            if w is not None:
                deps[w.idx] = w
        for r in writes:
            w = self.last_writer.get(r)
            if w is not None:
                deps[w.idx] = w
            for rd in self.readers.get(r, ()):
                deps[rd.idx] = rd
        o.deps = list(deps.values())
        for r in reads:
            self.readers.setdefault(r, []).append(o)
        for r in writes:
            self.last_writer[r] = o
            self.readers[r] = []
        if dma:
            o.dq = eng
            o.dslot = self.dma_count[eng]
            self.dma_count[eng] += 1
        self.ops.append(o)
        return o

    def emit(self):
        nc = self.nc
        ops = self.ops

        def skip(d, o):
            return (not d.is_dma) and (not o.is_dma) and d.eng == o.eng and (d.eng == "pe" or not SAME_ENGINE_SYNC)

        for o in ops:
            for d in o.deps:
                if d.is_dma or skip(d, o):
                    continue
                d.signaled = True
        with contextlib.ExitStack() as es:
            esem = {e: es.enter_context(nc.semaphore("c_" + e)) for e in self.ENGS}
            dsem = {}
            for e in self.ENGS:
                if self.dma_count[e] > 0:
                    dsem[e] = [es.enter_context(nc.semaphore("d_%s_%d" % (e, i))) for i in range(NDMASEM)]
            cnt = {e: 0 for e in self.ENGS}
            for o in ops:
                if o.is_dma:
                    s = dsem[o.dq][o.dslot % NDMASEM]
                    o.ev = (s, 16 * (o.dslot // NDMASEM + 1))
                elif o.signaled:
                    cnt[o.eng] += 1
                    o.ev = (esem[o.eng], cnt[o.eng])
            self.stats = dict(cnt)
            block = es.enter_context(nc.Block())

            def body(engname, eng):
                seen = {}
                for o in ops:
                    if o.eng != engname:
                        continue
                    waits = []
                    for d in o.deps:
                        if d.ev is None or skip(d, o):
                            continue
                        waits.append(d.ev)
                    if o.is_dma and o.dslot >= NDMASEM:
                        s = dsem[o.dq][o.dslot % NDMASEM]
                        waits.append((s, 16 * (o.dslot // NDMASEM)))
                    mx = {}
                    for (s, v) in waits:
                        k = id(s)
                        if k not in mx or mx[k][1] < v:
                            mx[k] = (s, v)
                    for k, (s, v) in mx.items():
                        if seen.get(k, 0) >= v:
                            continue
                        seen[k] = v
                        eng.wait_ge(s, v)
                    if o.fn is None:
                        continue
                    inst = o.fn(eng)
                    if o.is_dma:
                        inst.then_inc(o.ev[0], 16)
                    elif o.signaled:
                        inst.then_inc(o.ev[0], 1)

            @block.sync
            def _(e):
                body("sp", e)

            @block.scalar
            def _(e):
                body("act", e)

            @block.vector
            def _(e):
                body("dve", e)

            @block.gpsimd
            def _(e):
                body("pool", e)

            @block.tensor
            def _(e):
                body("pe", e)


def build_nc(n_full=17, dbg=False, stage=99, n_ctx=FIRST_FULL):
    nc = bass.Bass("TRN2", target_bir_lowering=False)
    x_d = nc.dram_tensor("x", [SEQ, D], F32, kind="ExternalInput").ap()
    par_d = nc.dram_tensor("par", [128, NPAR], F32, kind="ExternalInput").ap()
    win_d = nc.dram_tensor("w_in", [D, 2568], F32, kind="ExternalInput").ap()
    wout_d = nc.dram_tensor("w_out", [D, D], F32, kind="ExternalInput").ap()
    wup_d = nc.dram_tensor("w_up", [D, 2 * DFF], F32, kind="ExternalInput").ap()
    wdn_d = nc.dram_tensor("w_down", [DFF, D], F32, kind="ExternalInput").ap()
    out_d = nc.dram_tensor("out", [4096, D], F32, kind="ExternalOutput").ap()
    wsc = nc.dram_tensor("wsc", [NSLOT, 128, 2048], BF16, kind="Internal").ap()
    ksc = nc.dram_tensor("ksc", [4, 128, SEQ], BF16, kind="Internal").ap()
    dbg_d = {}
    if dbg:
        for nm, shp in [("d_x1", [256, D]), ("d_outT", [128, 8 * 256]), ("d_ckb", [128, 64 * 8]),
                        ("d_qT", [128, 4 * 256]), ("d_oa", [128, 2 * 512])]:
            dbg_d[nm] = nc.dram_tensor(nm, shp, F32, kind="ExternalOutput").ap()

    with contextlib.ExitStack() as es:
        def T(name, shape, dt):
            return es.enter_context(nc.sbuf_tensor(name, shape, dt))

        def PS(name, shape, dt):
            return es.enter_context(nc.psum_tensor(name, shape, dt))

        P = Prog(nc)
        KSTG = T("KSTG", [128, 4, G], BF16)
        KCH = [T("KCH%d" % i, [128, 1024], BF16) for i in range(3)]
        V = T("V", [128, NB, 4, 132], BF16)
        XTs = [T("XT%d" % i, [128, 2, D], F32) for i in range(2)]
        XT = XTs[0]
        HBs = [T("HB%d" % i, [128, 2, D], BF16) for i in range(2)]
        HB = HBs[0]
        HT = T("HT", [128, 8, G], BF16)
        FA = T("FA", [128, 3072], F32)
        QT = FA[:, 0:512].bitcast(BF16).rearrange("p (a t) -> p a t", a=4)
        NPT = 8
        PT = [FA[:, 512 + 256 * i:512 + 256 * (i + 1)].bitcast(BF16) for i in range(NPT)]
        OSB = [FA[:, 2560 + 256 * i:2560 + 256 * (i + 1)] for i in range(2)]
        AGV = [FA[:, 516 * i:516 * (i + 1)].rearrange("p (a c) -> p a c", a=2) for i in range(2)]
        YGV = [FA[:, 1032 + 512 * i:1032 + 512 * (i + 1)].rearrange("p (a c) -> p a c", a=2) for i in range(2)]
        ARENA = T("ARENA", [128, 3072], F32)
        OUTF = ARENA[:, 0:2048].rearrange("p (b f) -> p b f", b=2)
        U = ARENA[:, 0:1024].rearrange("p (b f) -> p b f", b=2)
        VA = ARENA[:, 1024:2048].rearrange("p (b f) -> p b f", b=2)
        ABF = ARENA[:].bitcast(BF16)
        VN = ABF[:, 4096:5120].rearrange("p (b f) -> p b f", b=2)
        OA = ABF[:, 5120:6144].rearrange("p (b f) -> p b f", b=2)
        ACTT = ABF[:, 0:NCH * G].rearrange("p (c t) -> p c t", c=NCH)
        OUTT = HT
        NWP = 10
        NPIN = 4
        WP = [T("WP%d" % i, [128, 2048], BF16) for i in range(NWP + NPIN)]
        RBT = T("RBT", [128, 512], F32)
        RB = [RBT[:, i * G:(i + 1) * G] for i in range(2)]
        SQ = RBT
        CKB = T("CKB", [128, NB, 8], F32)
        WG = [T("WG%d" % i, [128, NB], F32) for i in range(2)]
        NRG = T("NRG", [128, 8], F32)
        VP = [[T("VP%d_%d" % (e_, i), [128, 8, 128], BF16) for i in range(2)] for e_ in range(2)]
        WF = T("WF", [128, 8, 8], BF16)
        CARRY = T("CARRY", [128, NCH, 2, 2], F32)
        ident = T("ident", [128, 128], BF16)
        CMASK = T("CMASK", [128, 2, G], BF16)
        WMT = T("WMT", [128, 8, 128], BF16)
        TRI = T("TRI", [128, 128], F32)
        SEL127 = T("SEL127", [128, 128], F32)
        SELA = T("SELA", [128, 128], F32)
        SELB = T("SELB", [128, 128], F32)
        FB = T("FB", [128, 8], F32)
        LNG = T("LNG", [128, 512], F32)
        GFIN = T("GFIN", [128, D], F32)
        GT = T("GT", [128, 16], F32)
        WC = T("WC", [128, 2 * NCH, 3], F32)
        BC = T("BC", [128, 2 * NCH], F32)
        BST = T("BST", [128, 8], F32)
        KM = T("KM", [128, NB], F32)
        ST = T("ST", [128, 64], F32)
        CCAR = T("CCAR", [128, 8], F32)
        RG = T("RG", [128, 8], F32)
        CB = T("CB", [128, 8], F32)
        SP_ = T("SP_", [128, 8], F32)
        SPB = T("SPB", [128, 2, 8], F32)
        EPSC = T("EPSC", [128, 2], F32)
        PGB = [PS("PGB%d" % i, [128, 512], F32) for i in range(3)]
        ACCB = [PS("ACCB%d" % i, [128, 512], F32) for i in range(4)]
        PSTB = PS("PSTB", [128, 1024], BF16)
        print("sbuf bytes remaining:", nc.sbuf_bytes_remaining)

        rot = {"pg": 0, "pt": 0}
        att_rot = [0]
        kch_rot = [0]
        PSTF = PSTB[:, :].bitcast(F32)

        def next_pg():
            i = rot["pg"] % 3
            rot["pg"] += 1
            return i

        def next_pt():
            i = rot["pt"] % NPT
            rot["pt"] += 1
            return i

        def pc(name):
            a, b = PC[name]
            return par_d[:, a:b]

        def dma(out, in_, reads, writes):
            return P.op("sp", lambda e: e.dma_start(out=out, in_=in_), reads=reads, writes=writes, dma=True)

        XT_ALL0 = [("XT", 0, 0), ("XT", 0, 1)]
        ARENA_ALL = ["U", "VA", "VN", "OA"] + [("ACTT", c) for c in range(NCH)]

        dma(TRI[:], pc("tri"), [], ["TRI"])
        dma(SEL127[:], pc("sel127"), [], ["SEL127"])
        dma(SELA[:], pc("selA"), [], ["SELA"])
        dma(SELB[:], pc("selB"), [], ["SELB"])
        dma(FB[:], pc("fb"), [], ["FB"])
        dma(LNG[:], pc("lng"), [], ["LNG"])
        dma(GFIN[:], pc("gfin"), [], ["GFIN"])
        dma(GT[:, 0:8], pc("g1T"), [], ["GT"])
        dma(GT[:, 8:16], pc("g2T"), [], ["GT"])
        dma(WC[:].rearrange("p c j -> p (c j)"), pc("wc"), [], ["WC"])
        dma(BC[:], pc("bc"), [], ["BC"])
        dma(BST[:], pc("bsT"), [], ["BST"])
        dma(KM[:], pc("kmask"), [], ["KM"])
        XTf = XT[:].rearrange("p b f -> p (b f)")
        HBf = HB[:].rearrange("p b f -> p (b f)")
        HTf = HT[:].rearrange("p k t -> p (k t)")
        a0 = PC["eye"][0]
        dma(XTf[:, 0:128], pc("eye"), [], XT_ALL0)
        dma(XTf[:, 128:640], par_d[:, PC["cm0"][0]:PC["cm1"][1]], [], XT_ALL0)
        dma(XTf[:, 640:1664], pc("sgwT"), [], XT_ALL0)
        P.op("dve", lambda e: e.tensor_copy(out=ident[:], in_=XTf[:, 0:128]), reads=XT_ALL0, writes=["ident"])
        P.op("dve", lambda e: e.tensor_copy(out=CMASK[:].rearrange("p a t -> p (a t)"), in_=XTf[:, 128:640]), reads=XT_ALL0, writes=["CMASK"])
        P.op("dve", lambda e: e.memset(XTf[64:128, 640:1664].rearrange("p (h t) -> p h t", h=8)[:, :, 0:64], 0.0), reads=[], writes=XT_ALL0)
        P.op("dve", lambda e: e.tensor_copy(out=WMT[:].rearrange("p h t -> p (h t)"), in_=XTf[:, 640:1664]), reads=XT_ALL0, writes=["WMT"])
        P.op("dve", lambda e: e.memset(EPSC[:, 0:1], EPS), writes=["EPSC"])
        P.op("dve", lambda e: e.memset(EPSC[:, 1:2], 1.0), writes=["EPSC"])
        P.op("dve", lambda e: e.memset(CCAR[:], 0.0), writes=["CCAR"])
        P.op("dve", lambda e: e.memset(CARRY[:].rearrange("p c a j -> p (c a j)"), 0.0), writes=[("CARRY", jq) for jq in range(NCH)])
        import os
        if os.environ.get("NOVMEM") != "1":
            P.op("dve", lambda e: e.memset(V[:].rearrange("p b a c -> p (b a) c")[:, :, 64:66], 1.0), writes=[("V", b, s) for b in range(NB) for s in range(2)])

        NCONV = 99 if stage >= 2 else 0
        HB_ALL = [("HB", 0, 0), ("HB", 0, 1)]
        HT_ALL = [("HT", 0), ("HT", 1)]
        stage_f = [(XTf, XT_ALL0), (ARENA[:, 0:2048], ARENA_ALL)]
        stage_b = [(HBf, HB_ALL), (HTf, HT_ALL)]
        conv_engs = ["dve", "act"]
        cnt = [0]

        def convert(slot, pieces, gain_col, bg=False):
            if cnt[0] >= NCONV:
                return
            i = cnt[0] % 2
            cnt[0] += 1
            if bg:
                sf, sfn = ARENA[:, 0:2048], ["cvF"]
                sb, sbn = ABF[:, 4096:6144], ["cvB"]
            else:
                sf, sfn = stage_f[i]
                sb, sbn = stage_b[i]
            wr = list(sfn)
            for (dst, src) in pieces:
                dma(dst(sf), src, [], wr)
            eng = conv_engs[i]
            if gain_col is None:
                if eng == "dve":
                    P.op(eng, lambda e: e.tensor_copy(out=sb[:, :], in_=sf[:, :]), reads=wr, writes=list(sbn))
                else:
                    P.op(eng, lambda e: e.copy(out=sb[:, :], in_=sf[:, :]), reads=wr, writes=list(sbn))
            else:
                for k in range(8):
                    if eng == "dve":
                        P.op(eng, lambda e, k=k: e.tensor_scalar(out=sb[:, k * 256:(k + 1) * 256], in0=sf[:, k * 256:(k + 1) * 256],
                                                               scalar1=GT[:, gain_col + k:gain_col + k + 1], scalar2=None, op0=ALU.mult),
                             reads=wr + ["GT"], writes=list(sbn))
                    else:
                        P.op(eng, lambda e, k=k: e.mul(out=sb[:, k * 256:(k + 1) * 256], in_=sf[:, k * 256:(k + 1) * 256],
                                                     mul=GT[:, gain_col + k:gain_col + k + 1]),
                             reads=wr + ["GT"], writes=list(sbn))
            out_fn = lambda: dma(wsc[slot], sb[:, :], list(sbn), [("wsc", slot)])
            if bg:
                return out_fn
            out_fn()
            return None

        def v3(sf, a, b):
            return sf[:, 0:2048].rearrange("p (a b) -> p a b", a=a)

        conv_jobs = {}
        for s in range(10):
            conv_jobs[S_WIN + s] = ([(lambda sf: v3(sf, 8, 256), win_d[:, s * 256:(s + 1) * 256].rearrange("(k p) c -> p k c", p=128))], 0)
        for s in range(4):
            conv_jobs[S_WOUT + s] = ([(lambda sf: v3(sf, 8, 256), wout_d[:, s * 256:(s + 1) * 256].rearrange("(k p) c -> p k c", p=128))], None)
        for j in range(NCH):
            conv_jobs[S_WUP + j] = ([(lambda sf: v3(sf, 8, 256)[:, :, 0:128], wup_d[:, j * 128:(j + 1) * 128].rearrange("(k p) c -> p k c", p=128)),
                                     (lambda sf: v3(sf, 8, 256)[:, :, 128:256], wup_d[:, DFF + j * 128:DFF + (j + 1) * 128].rearrange("(k p) c -> p k c", p=128))], 8)
        for i in range(11):
            conv_jobs[S_WDN + i] = ([(lambda sf: v3(sf, 2, 1024), wdn_d[i * 256:(i + 1) * 256, :].rearrange("(c p) n -> p c n", p=128))], None)
        for slot in (6, 7, 8, 9):
            convert(slot, *conv_jobs[slot])
        dma(XTf[:, 0:64].rearrange("p (k c) -> p k c", k=8), win_d[:, 2560:2568].rearrange("(k p) c -> p k c", p=128), [], XT_ALL0)
        for k in range(8):
            P.op("dve", lambda e, k=k: e.tensor_scalar(out=WF[:, k, :], in0=XTf[:, k * 8:(k + 1) * 8], scalar1=GT[:, k:k + 1], scalar2=None, op0=ALU.mult),
                 reads=XT_ALL0 + ["GT"], writes=["WF"])
        P.op("dve", lambda e: e.memset(ST[:, 60:61], 0.0), reads=[], writes=ARENA_ALL + ["cvF", "cvB"])
        bg_slots = [sl for sl in list(range(0, 6)) + list(range(10, NSLOT))]
        bg_state = {"i": 0, "pending_out": None}

        def conv_step():
            if bg_state["pending_out"] is not None:
                bg_state["pending_out"]()
                bg_state["pending_out"] = None
            if stage < 2 or bg_state["i"] >= len(bg_slots):
                return
            slot = bg_slots[bg_state["i"]]
            bg_state["i"] += 1
            bg_state["pending_out"] = convert(slot, *conv_jobs[slot], bg=True)

        def conv_flush():
            while bg_state["i"] < len(bg_slots) or bg_state["pending_out"] is not None:
                conv_step()
            P.op("dve", lambda e: e.memset(ST[:, 61:62], 0.0), reads=[], writes=ARENA_ALL + ["cvF", "cvB"])

        wseq = []
        for ci in range(NGRP):
            full = ci >= FIRST_FULL
            if stage < 3 or (not full and ci >= n_ctx):
                continue
            if full and (ci - FIRST_FULL) >= n_full:
                break
            if full:
                wseq += list(range(0, 6)) + list(range(10, 14)) + list(range(14, 36)) + list(range(36, 47))
        wstate = {"use": 0, "load": 0}
        if stage >= 3:
            for sl in range(6, 10):
                dma(WP[NWP + sl - 6][:, :], wsc[sl], [("wsc", sl)], [("WP", NWP + sl - 6)])

        def wload_next():
            n = wstate["load"]
            if n >= len(wseq):
                return
            slot = wseq[n]
            buf = n % NWP
            wstate["load"] += 1
            dma(WP[buf][:, :], wsc[slot], [("wsc", slot)], [("WP", buf)])

        def wacquire(slot):
            if 6 <= slot < 10:
                return NWP + slot - 6
            n = wstate["use"]
            assert wseq[n] == slot, (n, wseq[n], slot)
            return n % NWP

        def wrelease(pinned=False):
            if pinned:
                return
            wstate["use"] += 1
            wload_next()


        evac_rr = [0]

        def rms_front(XT, HB, par, sc):
            for b in range(2):
                P.op("dve", lambda e, b=b: e.scalar_tensor_tensor(out=HB[:, b, :], in0=XT[:, b, :], scalar=1.0, in1=XT[:, b, :],
                                                                 op0=ALU.mult, op1=ALU.mult, accum_out=ST[:, sc + b:sc + b + 1]),
                     reads=[("XT", par, b)], writes=[("HB", par, b), ("ST", sc + b)])
                P.op("act", lambda e, b=b: e.activation(out=ST[:, sc + 2 + b:sc + 3 + b], in_=ST[:, sc + b:sc + b + 1], func=AF.Ln, bias=EPSC[:, 0:1], scale=1.0 / D),
                     reads=[("ST", sc + b), "EPSC"], writes=[("ST", sc + 2 + b)])
                P.op("act", lambda e, b=b: e.activation(out=ST[:, sc + 4 + b:sc + 5 + b], in_=ST[:, sc + 2 + b:sc + 3 + b], func=AF.Exp, scale=-0.5),
                     reads=[("ST", sc + 2 + b)], writes=[("ST", sc + 4 + b)])
                P.op("dve", lambda e, b=b: e.tensor_scalar(out=HB[:, b, :], in0=XT[:, b, :], scalar1=ST[:, sc + 4 + b:sc + 5 + b], scalar2=None, op0=ALU.mult),
                     reads=[("XT", par, b), ("ST", sc + 4 + b)], writes=[("HB", par, b)])

        def rms_back(HB, par):
            for b in range(2):
                for k in range(8):
                    P.op("pe", lambda e, b=b, k=k: e.transpose(out=PSTB[:, k * 128:(k + 1) * 128], in_=HB[:, b, k * 128:(k + 1) * 128], identity=ident[:]),
                         reads=[("HB", par, b), "ident"], writes=["PST"])
                P.op("act", lambda e, b=b: e.copy(out=HT[:, :, b * 128:(b + 1) * 128], in_=PSTB[:, :].rearrange("p (k t) -> p k t", k=8)),
                     reads=["PST"], writes=[("HT", b)] + [("OUTT", kq) for kq in range(8)])

        front_done = set()
        back_done = set()

        def emit_front(ci):
            front_done.add(ci)
            par = ci % 2
            T0 = ci * G
            dma(XTs[par][:], x_d[T0:T0 + G, :].rearrange("(b p) f -> p b f", p=128), [], [("XT", par, 0), ("XT", par, 1)])
            rms_front(XTs[par], HBs[par], par, 40)

        def HT_reads(b=None):
            if b is None:
                return [("HT", 0), ("HT", 1)]
            return [("HT", b)]

        def mm_feat(pi, hf, wbuf, c0):
            for k in range(8):
                P.op("pe", lambda e, k=k: e.matmul(PGB[pi][:, hf * G:(hf + 1) * G], lhsT=WP[wbuf][:, k * 256 + c0:k * 256 + c0 + 128], rhs=HT[:, k, :],
                                                  start=(k == 0), stop=(k == 7)),
                     reads=[("WP", wbuf)] + HT_reads(), writes=[("PG", pi)])

        def mm_tok(pi, hf, wbuf, b, lhs, lhs_reads):
            for k in range(8):
                P.op("pe", lambda e, k=k: e.matmul(PGB[pi][:, hf * G:(hf + 1) * G], lhsT=lhs[:, k, b * 128:(b + 1) * 128], rhs=WP[wbuf][:, k * 256:(k + 1) * 256],
                                                  start=(k == 0), stop=(k == 7)),
                     reads=[("WP", wbuf)] + lhs_reads, writes=[("PG", pi)])

        def emit_group(ci):
            full = ci >= FIRST_FULL
            gi = ci - FIRST_FULL
            b0 = 2 * ci
            T0 = ci * G
            par = ci % 2
            XT = XTs[par]
            HB = HBs[par]
            if gi == 0:
                conv_flush()
                for _ in range(NWP):
                    wload_next()
            if not full:
                conv_step()
            if ci not in back_done:
                rms_back(HB, par)

            def cs1(b):
                pi = next_pg()
                for k in range(8):
                    P.op("pe", lambda e, k=k, b=b, pi=pi: e.matmul(PGB[pi][:, 0:8], lhsT=HT[:, k, b * 128:(b + 1) * 128], rhs=WF[:, k, :], start=(k == 0), stop=(k == 7)),
                         reads=["WF"] + HT_reads(b), writes=[("PG", pi)])
                P.op("dve", lambda e, pi=pi, b=b: e.tensor_tensor(out=SPB[:, b, :], in0=PGB[pi][:, 0:8], in1=FB[:, :], op=ALU.add),
                     reads=[("PG", pi), "FB"], writes=[("SPB", b)])
                P.op("act", lambda e, b=b: e.activation(out=SPB[:, b, :], in_=SPB[:, b, :], func=AF.Exp, scale=-1.0), reads=[("SPB", b)], writes=[("SPB", b)])
                P.op("act", lambda e, b=b: e.activation(out=SPB[:, b, :], in_=SPB[:, b, :], func=AF.Ln, bias=EPSC[:, 1:2], scale=1.0), reads=[("SPB", b), "EPSC"], writes=[("SPB", b)])

            def cs2(b):
                pi2 = next_pg()
                P.op("pe", lambda e, pi2=pi2, b=b: e.matmul(PGB[pi2][:, 0:8], lhsT=TRI[:, :], rhs=SPB[:, b, :], start=True, stop=True),
                     reads=["TRI", ("SPB", b)], writes=[("PG", pi2)])
                P.op("dve", lambda e, pi2=pi2: e.tensor_tensor(out=CB[:, :], in0=PGB[pi2][:, 0:8], in1=CCAR[:, :], op=ALU.add),
                     reads=[("PG", pi2), "CCAR"], writes=["CB"])
                P.op("dve", lambda e, b=b: e.tensor_scalar(out=CKB[:, b0 + b, :], in0=CB[:, :], scalar1=KM[:, b0 + b:b0 + b + 1], scalar2=None, op0=ALU.add),
                     reads=["CB", "KM"], writes=[("CKB", b0 + b)])

            def cs3(b):
                pi3 = next_pg()
                P.op("pe", lambda e, pi3=pi3: e.matmul(PGB[pi3][:, 0:8], lhsT=SEL127[:, :], rhs=CB[:, :], start=True, stop=True),
                     reads=["SEL127", "CB"], writes=[("PG", pi3)])
                P.op("dve", lambda e, pi3=pi3: e.tensor_copy(out=CCAR[:, :], in_=PGB[pi3][:, 0:8]), reads=[("PG", pi3)], writes=["CCAR"])
                if b == 0 and full:
                    P.op("dve", lambda e, pi3=pi3: e.tensor_scalar(out=NRG[:, :], in0=PGB[pi3][:, 0:8], scalar1=-1.0, scalar2=None, op0=ALU.mult), reads=[("PG", pi3)], writes=["RG"])

            cstages = [lambda: cs1(0), lambda: cs1(1), lambda: cs2(0), lambda: cs3(0), lambda: cs2(1), lambda: cs3(1)]

            def cstage():
                if cstages:
                    cstages.pop(0)()

            if not full:
                cstage()
                cstage()
            nxt = ci + 1
            has_next = (nxt < NGRP) and not (nxt >= FIRST_FULL and (nxt - FIRST_FULL) >= n_full) and not (stage < 3) and not (nxt < FIRST_FULL and nxt >= n_ctx)
            if not full and has_next:
                emit_front(nxt)
            if full:
                for s in range(4):
                    wb = wacquire(s)
                    dst, dn = (U, "U") if s < 2 else (VA, "VA")
                    hf = s % 2
                    pi = next_pg()
                    for b in range(2):
                        mm_tok(pi, b, wb, b, HT, HT_reads(b))
                    P.op("act", lambda e, pi=pi, dst=dst, hf=hf: e.activation(out=dst[:, :, hf * 256:(hf + 1) * 256], in_=PGB[pi][:, :].rearrange("p (b c) -> p b c", b=2), func=AF.Gelu),
                         reads=[("PG", pi)], writes=[dn] + [("ACTT", c) for c in range(NCH)])
                    wrelease()
                    cstage()
                for s in range(2):
                    wb = wacquire(4 + s)
                    pi = next_pg()
                    for m in range(2):
                        mm_feat(pi, m, wb, m * 128)
                    P.op("act", lambda e, pi=pi, s=s: e.mul(out=QT[:, 2 * s:2 * s + 2, :].rearrange("p a t -> p (a t)"), in_=PGB[pi][:, :], mul=0.125),
                         reads=[("PG", pi)], writes=[("QT", 2 * s), ("QT", 2 * s + 1)])
                    wrelease()
                    cstage()
            for s in range(2):
                wb = wacquire(6 + s)
                pi = next_pg()
                for m in range(2):
                    mm_feat(pi, m, wb, m * 128)
                for m in range(2):
                    hp = 2 * s + m
                    P.op("act", lambda e, pi=pi, hp=hp, m=m: e.copy(out=KSTG[:, hp, :], in_=PGB[pi][:, m * G:(m + 1) * G]),
                         reads=[("PG", pi)], writes=[("KSTG", hp)])
                wrelease(pinned=True)
                if not full:
                    cstage()
            dma(ksc[:, :, T0:T0 + G].rearrange("h p t -> p h t"), KSTG[:, :, :], [("KSTG", hp) for hp in range(4)],
                [("KD", hp, bb) for hp in range(4) for bb in (b0, b0 + 1)])
            if not full:
                conv_step()
            for s in range(2):
                wb = wacquire(8 + s)
                pi = next_pg()
                for b in range(2):
                    mm_tok(pi, b, wb, b, HT, HT_reads(b))
                for b in range(2):
                    for a in range(2):
                        P.op("dve", lambda e, b=b, pi=pi, s=s, a=a: e.tensor_copy(
                            out=V[:, b0 + b, 2 * s + a, :].rearrange("p (e c) -> p e c", e=2)[:, :, 0:64],
                            in_=PGB[pi][:, b * G + a * 128:b * G + (a + 1) * 128].rearrange("p (e c) -> p e c", e=2)),
                             reads=[("PG", pi)], writes=[("V", b0 + b, s)])
                wrelease(pinned=True)
                if not full:
                    cstage()
            if not full:
                conv_step()
            while cstages:
                cstage()
            if not full:
                return
            if dbg and gi == dbg_g:
                dma(dbg_d["d_qT"], QT[:].rearrange("p a t -> p (a t)"), [("QT", h) for h in range(4)], ["dq"])
            def emit_sg():
                SQR = [("RB", 0), ("RB", 1)]
                for b in range(2):
                    va3 = VA[:, b, :].rearrange("p (h d) -> p h d", h=8)
                    P.op("dve", lambda e, va3=va3: e.tensor_reduce(out=ST[:, 8:16], in_=va3, axis=AX.X, op=ALU.add), reads=["VA"], writes=[("ST", "s1")])
                    P.op("dve", lambda e, b=b: e.tensor_tensor(out=SQ[:, :], in0=VA[:, b, :], in1=VA[:, b, :], op=ALU.mult), reads=["VA"], writes=SQR)
                    P.op("dve", lambda e: e.tensor_reduce(out=ST[:, 16:24], in_=SQ[:, :].rearrange("p (h d) -> p h d", h=8), axis=AX.X, op=ALU.add),
                         reads=SQR, writes=[("ST", "s2")])
                    P.op("dve", lambda e: e.tensor_scalar(out=ST[:, 8:16], in0=ST[:, 8:16], scalar1=1.0 / 64, scalar2=None, op0=ALU.mult),
                         reads=[("ST", "s1")], writes=[("ST", "s1")])
                    P.op("dve", lambda e: e.tensor_tensor(out=ST[:, 24:32], in0=ST[:, 8:16], in1=ST[:, 8:16], op=ALU.mult),
                         reads=[("ST", "s1")], writes=[("ST", "msq")])
                    P.op("dve", lambda e: e.scalar_tensor_tensor(out=ST[:, 16:24], in0=ST[:, 16:24], scalar=1.0 / 64, in1=ST[:, 24:32], op0=ALU.mult, op1=ALU.subtract),
                         reads=[("ST", "s2"), ("ST", "msq")], writes=[("ST", "s2")])
                    P.op("act", lambda e: e.activation(out=ST[:, 16:24], in_=ST[:, 16:24], func=AF.Ln, bias=EPSC[:, 0:1], scale=1.0),
                         reads=[("ST", "s2"), "EPSC"], writes=[("ST", "s2")])
                    P.op("act", lambda e: e.activation(out=ST[:, 16:24], in_=ST[:, 16:24], func=AF.Exp, scale=-0.5),
                         reads=[("ST", "s2")], writes=[("ST", "s2")])
                    P.op("dve", lambda e, va3=va3: e.tensor_tensor(out=va3, in0=va3, in1=ST[:, 8:16].unsqueeze(2).to_broadcast([128, 8, 64]), op=ALU.subtract),
                         reads=["VA", ("ST", "s1")], writes=["VA"])
                    P.op("dve", lambda e, va3=va3: e.tensor_tensor(out=va3, in0=va3, in1=ST[:, 16:24].unsqueeze(2).to_broadcast([128, 8, 64]), op=ALU.mult),
                         reads=["VA", ("ST", "s2")], writes=["VA"])
                    P.op("dve", lambda e, b=b: e.tensor_tensor(out=VN[:, b, :], in0=VA[:, b, :], in1=LNG[:, :], op=ALU.mult),
                         reads=["VA", "LNG"], writes=["VN"])
                    pi = next_pg()
                    for h in range(8):
                        P.op("pe", lambda e, h=h, b=b, pi=pi: e.matmul(PGB[pi][:, h * 64:(h + 1) * 64], lhsT=WMT[:, h, :], rhs=VN[:, b, h * 64:(h + 1) * 64],
                                                                      start=True, stop=True),
                             reads=["WMT", "VN"], writes=[("PG", pi)])
                    P.op("dve", lambda e, pi=pi: e.tensor_tensor(out=SQ[:, :].rearrange("p (h d) -> p h d", h=8),
                                                                 in0=PGB[pi][:, :].rearrange("p (h d) -> p h d", h=8),
                                                                 in1=BST[:, :].unsqueeze(2).to_broadcast([128, 8, 64]), op=ALU.add),
                         reads=[("PG", pi), "BST"], writes=SQR)
                    P.op("dve", lambda e, b=b: e.tensor_tensor(out=OA[:, b, :], in0=SQ[:, :], in1=U[:, b, :], op=ALU.mult),
                         reads=SQR + ["U"], writes=["OA"])
                    for kk in range(4):
                        P.op("pe", lambda e, b=b, kk=kk: e.transpose(out=PSTB[:, kk * 128:(kk + 1) * 128], in_=OA[:, b, kk * 128:(kk + 1) * 128], identity=ident[:]),
                             reads=["OA", "ident"], writes=["PST"])
                    P.op("act", lambda e, b=b: e.copy(out=OUTT[:, 0:4, b * 128:(b + 1) * 128], in_=PSTB[:, 0:512].rearrange("p (k t) -> p k t", k=4)),
                         reads=["PST"], writes=[("OUTT", k) for k in range(4)] + [("HT", 0), ("HT", 1)])

            nj = b0 + 2
            nchunk = (nj + 7) // 8

            def kload(hp, c):
                nblk = min(8, nj - 8 * c)
                kb = kch_rot[0] % 3
                kch_rot[0] += 1
                dma(KCH[kb][:, 0:nblk * 128], ksc[hp][:, c * 1024:c * 1024 + nblk * 128],
                    [("KD", hp, 8 * c + q) for q in range(nblk)], [("KCH", kb)])
                return kb

            def wg_exps(hp):
                for ee in range(2):
                    h = 2 * hp + ee
                    P.op("act", lambda e, h=h, ee=ee: e.activation(out=WG[ee][:, 0:nj], in_=CKB[:, 0:nj, h], func=AF.Exp, bias=NRG[:, h:h + 1], scale=1.0),
                         reads=[("CKB", j) for j in range(nj)] + ["RG"], writes=[("WG", ee)])

            kb_next = None
            for hp in range(4):
                accs = [0, 1]
                if hp == 0:
                    kbufs = {0: kload(0, 0), 1: kload(0, 1)}
                    wg_exps(0)
                else:
                    kbufs = kb_next

                def vprep(c, hp=hp):
                    nblk = min(8, nj - 8 * c)
                    cb = c % 2
                    for ee in range(2):
                        Kc = 65 if ee == 0 else 128
                        c0 = 0 if ee == 0 else 2
                        P.op("dve", lambda e, ee=ee, Kc=Kc, c0=c0, cb=cb, nblk=nblk, c=c: e.tensor_tensor(
                            out=VP[ee][cb][:, 0:nblk, 0:Kc], in0=V[:, 8 * c:8 * c + nblk, hp, c0:c0 + Kc],
                            in1=WG[ee][:, 8 * c:8 * c + nblk].unsqueeze(2).to_broadcast([128, nblk, Kc]), op=ALU.mult),
                             reads=[("V", 8 * c + q_, hp // 2) for q_ in range(nblk)] + [("WG", ee)], writes=[("VP", ee, cb)])

                if hp == 0:
                    vprep(0)
                    vprep(1)

                def qk_step(j, hp=hp):
                    bp = att_rot[0] % 3
                    att_rot[0] += 1
                    banks = [[(PGB[0], ("PG", 0)), (PGB[1], ("PG", 1))], [(PGB[2], ("PG", 2)), (PSTF, "PST")],
                             [(ACCB[2][:, :], ("ACC", 2)), (ACCB[3][:, :], ("ACC", 3))]][bp]
                    for q in range(2):
                        jj = j + q
                        diag = jj >= b0
                        for ee in range(2):
                            kr = slice(ee * 64, (ee + 1) * 64)
                            bk, bkey = banks[ee]
                            kb_ = kbufs[jj // 8]
                            jo = jj % 8
                            P.op("pe", lambda e, ee=ee, kr=kr, bk=bk, jj=jj, q=q, diag=diag, kb_=kb_, jo=jo: e.matmul(bk[:, q * G:(q + 1) * G], lhsT=KCH[kb_][kr, jo * 128:(jo + 1) * 128], rhs=QT[kr, hp, :],
                                                                                                  start=True, stop=not diag),
                                 reads=[("KCH", kb_), ("QT", hp)], writes=[bkey])
                        if diag:
                            for ee in range(2):
                                bk, bkey = banks[ee]
                                P.op("pe", lambda e, bk=bk, jj=jj, q=q: e.matmul(bk[:, q * G:(q + 1) * G], lhsT=ident[:, :], rhs=CMASK[:, jj - b0, :], start=False, stop=True),
                                     reads=["ident", "CMASK"], writes=[bkey])
                    return banks

                def exp_step(j, banks):
                    tis = []
                    for ee in range(2):
                        bk, bkey = banks[ee]
                        ti = next_pt()
                        P.op("act", lambda e, ti=ti, bk=bk: e.activation(out=PT[ti][:, :], in_=bk[:, :], func=AF.Exp, scale=1.0),
                             reads=[bkey], writes=[("PT", ti)])
                        tis.append(ti)
                    return tis

                def pv_step(j, tis, hp=hp, accs=accs):
                    for q in range(2):
                        jj = j + q
                        cb = (jj // 8) % 2
                        jo = jj % 8
                        t0_, t1_ = tis[0], tis[1]
                        P.op("pe", lambda e, jj=jj, cb=cb, jo=jo, t0_=t0_, q=q: e.matmul(ACCB[accs[0]][0:65, 0:G], lhsT=VP[0][cb][:, jo, 0:65], rhs=PT[t0_][:, q * G:(q + 1) * G], start=(jj == 0), stop=(jj == nj - 1)),
                             reads=[("VP", 0, cb), ("PT", t0_)], writes=[("ACC", accs[0])])
                        P.op("pe", lambda e, jj=jj, cb=cb, jo=jo, t1_=t1_, q=q: e.matmul(ACCB[accs[1]][:, 0:G], lhsT=VP[1][cb][:, jo, :], rhs=PT[t1_][:, q * G:(q + 1) * G], start=(jj == 0), stop=(jj == nj - 1)),
                             reads=[("VP", 1, cb), ("PT", t1_)], writes=[("ACC", accs[1])])

                pend = []
                for j in range(0, nj, 2):
                    if j % 8 == 0 and j > 0 and (j // 8 + 1) < nchunk:
                        kbufs[j // 8 + 1] = kload(hp, j // 8 + 1)
                    if j % 8 == 4 and (j // 8 + 1) < nchunk and (j // 8 + 1) >= 2:
                        vprep(j // 8 + 1)
                    if hp == 1 and j == 16:
                        emit_sg()
                    banks = qk_step(j)
                    tis = exp_step(j, banks)
                    pend.append((j, tis))
                    if len(pend) > 2:
                        pv_step(*pend.pop(0))
                while pend:
                    pv_step(*pend.pop(0))
                if hp < 3:
                    kb_next = {0: kload(hp + 1, 0), 1: kload(hp + 1, 1)}
                    wg_exps(hp + 1)
                    vprep(0, hp=hp + 1)
                    vprep(1, hp=hp + 1)
                for ee in range(2):
                    h = 2 * hp + ee
                    acc = accs[ee]
                    kr = slice(ee * 64, (ee + 1) * 64)
                    oi = ee
                    K = 65 if ee == 0 else 128
                    sel = SELA if ee == 0 else SELB
                    seln = "SELA" if ee == 0 else "SELB"
                    P.op("act", lambda e, oi=oi, K=K, acc=acc: e.copy(out=OSB[oi][0:K, :], in_=ACCB[acc][0:K, 0:G]), reads=[("ACC", acc)], writes=[("OSB", oi)])
                    pi = next_pg()
                    P.op("pe", lambda e, oi=oi, K=K, sel=sel, pi=pi: e.matmul(PGB[pi][:, 0:G], lhsT=sel[0:K, :], rhs=OSB[oi][0:K, :], start=True, stop=True),
                         reads=[seln, ("OSB", oi)], writes=[("PG", pi)])
                    P.op("dve", lambda e, oi=oi, pi=pi: e.tensor_scalar(out=RB[oi][:, :], in0=PGB[pi][:, 0:G], scalar1=1e-30, scalar2=None, op0=ALU.max),
                         reads=[("PG", pi)], writes=[("RB", oi)])
                    P.op("dve", lambda e, oi=oi: e.reciprocal(out=RB[oi][:, :], in_=RB[oi][:, :]), reads=[("RB", oi)], writes=[("RB", oi)])
                    P.op("dve", lambda e, oi=oi, kr=kr, hp=hp: e.tensor_tensor(out=OUTT[kr, 4 + hp, :], in0=OSB[oi][kr, :], in1=RB[oi][kr, :], op=ALU.mult),
                         reads=[("OSB", oi), ("RB", oi)], writes=[("OUTT", 4 + hp), ("HT", 0), ("HT", 1)])
            if dbg and gi == dbg_g:
                for k in range(8):
                    P.op("dve", lambda e, k=k: e.tensor_copy(out=YGV[0][:, 0, :], in_=OUTT[:, k, :]), reads=[("OUTT", k)], writes=["dtmp"])
                    dma(dbg_d["d_outT"][:, k * 256:(k + 1) * 256], YGV[0][:, 0, :], ["dtmp"], ["dtmp2"])
                dma(dbg_d["d_ckb"], CKB[:].rearrange("p b h -> p (b h)"), [("CKB", j) for j in range(NB)], ["dck"])
            if has_next:
                emit_front(nxt)
            for n in range(4):
                wb = wacquire(S_WOUT + n)
                pi = next_pg()
                for b in range(2):
                    mm_tok(pi, b, wb, b, OUTT, [("OUTT", k) for k in range(8)])
                P.op("dve", lambda e, n=n, pi=pi: e.tensor_tensor(out=XT[:, :, n * 256:(n + 1) * 256], in0=XT[:, :, n * 256:(n + 1) * 256],
                                                                  in1=PGB[pi][:, :].rearrange("p (b c) -> p b c", b=2), op=ALU.add),
                     reads=[("PG", pi), ("XT", par, 0), ("XT", par, 1)], writes=[("XT", par, 0), ("XT", par, 1)])
                wrelease()
            if dbg and gi == dbg_g:
                dma(dbg_d["d_x1"].rearrange("(b p) f -> p b f", p=128), XT[:], [("XT", par, 0), ("XT", par, 1)], ["dx1"])
            rms_front(XT, HB, par, 0)
            rms_back(HB, par)
            def ffn_tail(j):
                r = j % 2
                P.op("act", lambda e, r=r: e.activation(out=YGV[r][:, 0, :], in_=YGV[r][:, 0, :], func=AF.Silu), reads=[("YGV", r, 0)], writes=[("YGV", r, 0)])
                P.op("pool", lambda e, r=r, j=j: e.tensor_tensor(out=ACTT[:, j, :], in0=YGV[r][:, 0, :], in1=YGV[r][:, 1, :], op=ALU.mult),
                     reads=[("YGV", r, 0), ("YGV", r, 1)], writes=[("ACTT", j), "U", "VA", "VN", "OA"])

            for j in range(NCH):
                wb = wacquire(S_WUP + j)
                r = j % 2
                pi = next_pg()
                for part in range(2):
                    mm_feat(pi, part, wb, part * 128)
                P.op("pool", lambda e, r=r, j=j: e.tensor_copy(out=AGV[r][:, :, 0:2], in_=CARRY[:, j, :, :]),
                     reads=[("CARRY", j)] + [("OUTT", 4 + hq) for hq in range(4)], writes=[("AGVc", r)])
                P.op("act", lambda e, r=r, pi=pi: e.copy(out=AGV[r][:, :, 2:258], in_=PGB[pi][:, :].rearrange("p (a t) -> p a t", a=2)),
                     reads=[("PG", pi)], writes=[("AGV", r, 0), ("AGV", r, 1)])
                for part in range(2):
                    cj = part * NCH + j
                    P.op("act", lambda e, r=r, part=part, cj=cj, pi=pi: e.activation(out=YGV[r][:, part, :], in_=PGB[pi][:, part * G:(part + 1) * G], func=AF.Identity,
                                                                                 bias=BC[:, cj:cj + 1], scale=WC[:, cj, 2:3]),
                         reads=[("PG", pi), "WC", "BC"], writes=[("YGV", r, part)])
                if j >= 1:
                    ffn_tail(j - 1)
                P.op("pool", lambda e, r=r, j=j: e.tensor_copy(out=CARRY[:, j, :, :], in_=AGV[r][:, :, 256:258]),
                     reads=[("AGV", r, 0), ("AGV", r, 1)], writes=[("CARRY", j)])
                for tap in (1, 0):
                    for part in range(2):
                        cj = part * NCH + j
                        P.op("dve", lambda e, r=r, part=part, cj=cj, tap=tap: e.scalar_tensor_tensor(out=YGV[r][:, part, :], in0=AGV[r][:, part, tap:tap + 256], scalar=WC[:, cj, tap:tap + 1],
                                                                                                  in1=YGV[r][:, part, :], op0=ALU.mult, op1=ALU.add),
                             reads=[("AGV", r, part), ("AGVc", r), "WC", ("YGV", r, part)], writes=[("YGV", r, part)])
                wrelease()
            ffn_tail(NCH - 1)
            for i in range(11):
                wb = wacquire(S_WDN + i)
                for c2 in range(2):
                    ch = 2 * i + c2
                    for b in range(2):
                        for n2 in range(2):
                            acc = b * 2 + n2
                            P.op("pe", lambda e, ch=ch, c2=c2, b=b, n2=n2, acc=acc, wb=wb: e.matmul(ACCB[acc][:, :], lhsT=ACTT[:, ch, b * 128:(b + 1) * 128],
                                                                                                 rhs=WP[wb][:, c2 * 1024 + n2 * 512:c2 * 1024 + (n2 + 1) * 512],
                                                                                                 start=(ch == 0), stop=(ch == NCH - 1)),
                                 reads=[("WP", wb), ("ACTT", ch)], writes=[("ACC", acc)])
                wrelease()
            if has_next:
                rms_back(HBs[nxt % 2], nxt % 2)
                back_done.add(nxt)
            for b in range(2):
                for n2 in range(2):
                    acc = b * 2 + n2
                    P.op("dve", lambda e, b=b, n2=n2, acc=acc: e.tensor_tensor(out=XT[:, b, n2 * 512:(n2 + 1) * 512], in0=XT[:, b, n2 * 512:(n2 + 1) * 512], in1=ACCB[acc][:, :], op=ALU.add),
                         reads=[("ACC", acc), ("XT", par, b)], writes=[("XT", par, b)])
            for b in range(2):
                P.op("dve", lambda e, b=b: e.scalar_tensor_tensor(out=HB[:, b, :], in0=XT[:, b, :], scalar=1.0, in1=XT[:, b, :],
                                                                 op0=ALU.mult, op1=ALU.mult, accum_out=ST[:, 32 + b:33 + b]),
                     reads=[("XT", par, b)], writes=[("HB", par, b), ("ST", 32 + b)])
                P.op("act", lambda e, b=b: e.activation(out=ST[:, 34 + b:35 + b], in_=ST[:, 32 + b:33 + b], func=AF.Ln, bias=EPSC[:, 0:1], scale=1.0 / D),
                     reads=[("ST", 32 + b), "EPSC"], writes=[("ST", 34 + b)])
                P.op("act", lambda e, b=b: e.activation(out=ST[:, 36 + b:37 + b], in_=ST[:, 34 + b:35 + b], func=AF.Exp, scale=-0.5),
                     reads=[("ST", 34 + b)], writes=[("ST", 36 + b)])
                P.op("dve", lambda e, b=b: e.scalar_tensor_tensor(out=OUTF[:, b, :], in0=XT[:, b, :], scalar=ST[:, 36 + b:37 + b], in1=GFIN[:, :], op0=ALU.mult, op1=ALU.mult),
                     reads=[("XT", par, b), ("ST", 36 + b), "GFIN"], writes=ARENA_ALL)
            if gi >= 1:
                r0 = (gi - 1) * G
                dma(out_d[r0:r0 + G, :].rearrange("(b p) f -> p b f", p=128), OUTF, ARENA_ALL, [("out", gi)])

        dbg_g = 1
        first = True
        for ci in range(NGRP):
            if stage < 3 or (ci < FIRST_FULL and ci >= n_ctx):
                continue
            if ci >= FIRST_FULL and (ci - FIRST_FULL) >= n_full:
                break
            if ci not in front_done:
                emit_front(ci)
            emit_group(ci)
        fin = [("out", gi) for gi in range(1, n_full)]
        if dbg:
            fin += ["dq", "dtmp2", "dck", "dx1"]
        P.op("sp", None, reads=fin)
        P.emit()
        print("ops:", len(P.ops), "signals:", P.stats)
    return nc


def make_par(inputs, core):
    par = np.zeros((128, NPAR), np.float32)

    def put(name, arr):
        a, b = PC[name]
        par[:, a:b] = np.asarray(arr, np.float32).reshape(128, b - a)

    put("eye", np.eye(128, dtype=np.float32))
    s = np.arange(128)
    put("tri", (s[:, None] <= s[None, :]).astype(np.float32))
    sel = np.zeros((128, 128), np.float32); sel[127, :] = 1.0
    put("sel127", sel)
    sel = np.zeros((128, 128), np.float32); sel[64, :] = 1.0
    put("selA", sel)
    sel = np.zeros((128, 128), np.float32); sel[63, :] = 1.0
    put("selB", sel)
    tri_mask = np.where(s[:, None] <= s[None, :], 0.0, NEG).astype(np.float32)
    cm0 = np.concatenate([tri_mask, np.zeros((128, 128), np.float32)], axis=1)
    cm1 = np.concatenate([np.full((128, 128), NEG, np.float32), tri_mask], axis=1)
    put("cm0", cm0)
    put("cm1", cm1)
    put("fb", np.broadcast_to(inputs["f_bias"].reshape(1, 8), (128, 8)))
    put("lng", np.broadcast_to(inputs["sg_ln_g"].reshape(1, 512), (128, 512)))
    put("gfin", np.broadcast_to(inputs["norm_final_g"].reshape(1, D), (128, D)))
    put("g1T", inputs["norm_mix_g"].reshape(8, 128).T)
    put("g2T", inputs["norm_ffn_g"].reshape(8, 128).T)
    wc = inputs["w_conv"].reshape(3, 2 * NCH, 128)
    put("wc", np.transpose(wc, (2, 1, 0)))
    put("bc", inputs["b_conv"].reshape(2 * NCH, 128).T)
    put("bsT", inputs["sg_b"].reshape(8, 128).T)
    km = np.zeros((128, NB), np.float32)
    if core % 2 == 0:
        km[:, 0:32] = NEG
    put("kmask", km)
    sgw = inputs["sg_w"].reshape(8, 128, 128)
    put("sgwT", np.transpose(sgw, (2, 0, 1)))
    return par


_NC_CACHE = {}


def kernel(**inputs):
    inputs = {k: np.asarray(v) for k, v in inputs.items()}
    x = inputs["x"].astype(np.float32, copy=False)
    key = "full"
    if key not in _NC_CACHE:
        _NC_CACHE[key] = build_nc()
    nc = _NC_CACHE[key]
    in_maps = []
    for c in range(8):
        b, half = c // 2, c % 2
        if half == 0:
            xc = np.concatenate([np.zeros((4096, D), np.float32), x[b, 0:4096]], axis=0)
        else:
            xc = x[b]
        in_maps.append({
            "x": np.ascontiguousarray(xc),
            "par": make_par(inputs, c),
            "w_in": np.ascontiguousarray(inputs["w_in"][0]),
            "w_out": np.ascontiguousarray(inputs["w_out"][0]),
            "w_up": np.ascontiguousarray(inputs["w_up"][0]),
            "w_down": np.ascontiguousarray(inputs["w_down"][0]),
        })
    res = run_bass_kernel_spmd(nc, in_maps, core_ids=list(range(8)))
    out = np.empty((4, SEQ, D), np.float32)
    for c in range(8):
        b, half = c // 2, c % 2
        out[b, half * 4096:(half + 1) * 4096] = res.results[c]["out"]
    return out
```
